# Optimizing a Trainium2 kernel written in Bass

```python
import math
import jax, jax.numpy as jnp
from jax import lax
import numpy as np

D_MODEL = 1024
BATCH = 4
SEQ = 4096
DEPTH = 4
DEC_BATCH = 32
DEC_SEQ = 4
PAST_LEN = 8192
PAGE_SIZE = 128

F32 = jnp.float32
MIX_WIDTH = D_MODEL
GROUP_WIDTH = MIX_WIDTH // 4
S5_WIDTH = GROUP_WIDTH
S5_GROUP = 16
S5_GROUPS = S5_WIDTH // S5_GROUP
S5_STATE = 64
FOX_WIDTH = GROUP_WIDTH
FOX_HEAD_DIM = 64
FOX_HEADS = FOX_WIDTH // FOX_HEAD_DIM
FOX_SCALE = FOX_HEAD_DIM ** -0.5
Q_BLOCK = 128
SSD_WIDTH = GROUP_WIDTH
SSD_HEAD_DIM = 64
SSD_HEADS = SSD_WIDTH // SSD_HEAD_DIM
SSD_GROUPS = 2
SSD_STATE = 128
SSD_CONV = 4
SSD_CHUNK = 128
SSD_XBC = SSD_WIDTH + 2 * SSD_GROUPS * SSD_STATE
SC_WIDTH = GROUP_WIDTH
SC_CONV = 3
FFN_WIDTH = 4 * D_MODEL
EPS = 1e-5
POOL_NUM = 5
POOL_DEN = 4
IN_SPLITS = (S5_WIDTH, FOX_WIDTH, FOX_WIDTH, FOX_WIDTH, FOX_HEADS,
             SSD_WIDTH, SSD_XBC, SSD_HEADS, SC_WIDTH, SC_WIDTH, SC_WIDTH)
IN_COLS = S5_WIDTH + 3 * FOX_WIDTH + FOX_HEADS + SSD_WIDTH + SSD_XBC + SSD_HEADS + 3 * SC_WIDTH

kernel_name = 'hybrid_s5_fox_ssd_shortconv_step'


def rmsnorm(x, g):
    xf = x.astype(F32)
    y = xf * lax.rsqrt(jnp.mean(xf * xf, axis=-1, keepdims=True) + EPS)
    return (y * g.astype(F32)).astype(x.dtype)


def split_cols(proj):
    outs, start = [], 0
    for n in IN_SPLITS:
        outs.append(proj[..., start:start + n])
        start += n
    return outs


def causal_dwconv(x, buf, w):
    width, T = w.shape[0], x.shape[1]
    xp = jnp.concatenate([buf.astype(x.dtype), x], axis=1)
    y = xp[:, 0:T] * w[0]
    for j in range(1, width):
        y = y + xp[:, j:j + T] * w[j]
    return y, xp[:, T:]


def _cplx_combine(e1, e2):
    a1r, a1i, b1r, b1i = e1
    a2r, a2i, b2r, b2i = e2
    return (a2r * a1r - a2i * a1i, a2r * a1i + a2i * a1r,
            a2r * b1r - a2i * b1i + b2r, a2r * b1i + a2i * b1r + b2i)


def s5_mix(u, h0, lam_re, lam_im, log_dt, b_re, b_im, c_re, c_im, d_skip, w_glu):
    bsz, T, _ = u.shape
    ug = u.astype(F32).reshape(bsz, T, S5_GROUPS, S5_GROUP)
    lr, li = lam_re.astype(F32), lam_im.astype(F32)
    dt = jnp.exp(log_dt.astype(F32))[:, None]
    mag = jnp.exp(lr * dt)
    abr, abi = mag * jnp.cos(li * dt), mag * jnp.sin(li * dt)
    den = lr * lr + li * li
    qr = ((abr - 1.0) * lr + abi * li) / den
    qi = (abi * lr - (abr - 1.0) * li) / den
    br, bi = b_re.astype(F32), b_im.astype(F32)
    bbr = qr[..., None] * br - qi[..., None] * bi
    bbi = qr[..., None] * bi + qi[..., None] * br
    bur = jnp.einsum('btgh,gph->btgp', ug, bbr)
    bui = jnp.einsum('btgh,gph->btgp', ug, bbi)
    ar = jnp.broadcast_to(abr, bur.shape)
    ai = jnp.broadcast_to(abi, bui.shape)
    acr, aci, xr, xi = lax.associative_scan(_cplx_combine, (ar, ai, bur, bui), axis=1)
    h0r = h0[..., 0].astype(F32)[:, None]
    h0i = h0[..., 1].astype(F32)[:, None]
    xr, xi = xr + acr * h0r - aci * h0i, xi + acr * h0i + aci * h0r
    y = (jnp.einsum('btgp,ghp->btgh', xr, c_re.astype(F32))
         - jnp.einsum('btgp,ghp->btgh', xi, c_im.astype(F32))
         + d_skip.astype(F32) * ug)
    y = jax.nn.gelu(y.reshape(bsz, T, S5_WIDTH))
    y = y * jax.nn.sigmoid(y @ w_glu.astype(F32))
    return y.astype(u.dtype), jnp.stack([xr[:, -1], xi[:, -1]], axis=-1)


def fox_attend(q, k, v, cum_q, cum_k, q_pos, k_pos):
    s = jnp.einsum('bqhd,bkhd->bhqk', q, k, preferred_element_type=F32) * FOX_SCALE
    s = s + jnp.transpose(cum_q, (0, 2, 1))[..., :, None] - jnp.transpose(cum_k, (0, 2, 1))[:, :, None, :]
    s = jnp.where(k_pos[None, :] <= q_pos[:, None], s, -jnp.inf)
    p = jax.nn.softmax(s, axis=-1)
    return jnp.einsum('bhqk,bkhd->bqhd', p.astype(v.dtype), v)


def fox_prompt(q, k, v, logf):
    bsz, T, H, Dh = q.shape
    cum = jnp.cumsum(logf, axis=1)
    k_pos = jnp.arange(T)

    def block(i):
        s0 = i * Q_BLOCK
        qb = lax.dynamic_slice_in_dim(q, s0, Q_BLOCK, axis=1)
        cb = lax.dynamic_slice_in_dim(cum, s0, Q_BLOCK, axis=1)
        return fox_attend(qb, k, v, cb, cum, s0 + jnp.arange(Q_BLOCK), k_pos)

    o = lax.map(block, jnp.arange(T // Q_BLOCK))
    return jnp.transpose(o, (1, 0, 2, 3, 4)).reshape(bsz, T, H, Dh)


def fox_sample(q, k, v, logf, k_past, v_past, logf_past):
    Lp, T = k_past.shape[1], q.shape[1]
    k_all = jnp.concatenate([k_past.astype(k.dtype), k], axis=1)
    v_all = jnp.concatenate([v_past.astype(v.dtype), v], axis=1)
    cum = jnp.cumsum(jnp.concatenate([logf_past.astype(F32), logf], axis=1), axis=1)
    return fox_attend(q, k_all, v_all, cum[:, Lp:], cum, Lp + jnp.arange(T), jnp.arange(Lp + T))


def ssd_scan(x, a, bm, cm, h0):
    bsz, T, H, P = x.shape
    chunk = SSD_CHUNK if T % SSD_CHUNK == 0 else T
    nc = T // chunk
    rep = H // bm.shape[2]
    bh = jnp.repeat(bm, rep, axis=2).reshape(bsz, nc, chunk, H, -1)
    ch = jnp.repeat(cm, rep, axis=2).reshape(bsz, nc, chunk, H, -1)
    xc = x.reshape(bsz, nc, chunk, H, P)
    acum = jnp.cumsum(jnp.transpose(a.reshape(bsz, nc, chunk, H), (0, 3, 1, 2)), axis=-1)
    li = jnp.arange(chunk)
    lmat = jnp.exp(jnp.where(li[:, None] >= li[None, :],
                             acum[..., :, None] - acum[..., None, :], -jnp.inf))
    scores = jnp.einsum('bclhn,bcshn->bhcls', ch, bh) * lmat
    y_diag = jnp.einsum('bhcls,bcshp->bclhp', scores, xc)
    decay_states = jnp.exp(acum[..., -1:] - acum)
    states = jnp.einsum('bclhn,bhcl,bclhp->bchpn', bh, decay_states, xc)
    states = jnp.concatenate([h0[:, None], states], axis=1)
    cs = jnp.cumsum(jnp.pad(acum[..., -1], ((0, 0), (0, 0), (1, 0))), axis=-1)
    ci = jnp.arange(nc + 1)
    decay_chunk = jnp.exp(jnp.where(ci[:, None] >= ci[None, :],
                                    cs[..., :, None] - cs[..., None, :], -jnp.inf))
    new_states = jnp.einsum('bhzc,bchpn->bzhpn', decay_chunk, states)
    y_off = jnp.einsum('bclhn,bchpn,bhcl->bclhp', ch, new_states[:, :-1], jnp.exp(acum))
    return (y_diag + y_off).reshape(bsz, T, H, P), new_states[:, -1]


def ssd_mix(z, xbc_raw, dt_raw, conv_buf, h0, conv_w, conv_b, dt_bias, a_log, d_skip, norm_g):
    xbc, new_buf = causal_dwconv(xbc_raw, conv_buf, conv_w)
    xbc = jax.nn.silu(xbc + conv_b).astype(F32)
    bsz, T, _ = xbc.shape
    nb = SSD_GROUPS * SSD_STATE
    xs = xbc[..., :SSD_WIDTH].reshape(bsz, T, SSD_HEADS, SSD_HEAD_DIM)
    bm = xbc[..., SSD_WIDTH:SSD_WIDTH + nb].reshape(bsz, T, SSD_GROUPS, SSD_STATE)
    cm = xbc[..., SSD_WIDTH + nb:].reshape(bsz, T, SSD_GROUPS, SSD_STATE)
    dt = jax.nn.softplus(dt_raw.astype(F32) + dt_bias.astype(F32))
    a = -jnp.exp(a_log.astype(F32))
    y, h_new = ssd_scan(xs * dt[..., None], dt * a, bm, cm, h0.astype(F32))
    y = y + d_skip.astype(F32)[:, None] * xs
    y = y.reshape(bsz, T, SSD_WIDTH).astype(z.dtype) * jax.nn.silu(z)
    return rmsnorm(y, norm_g), h_new, new_buf


def short_conv_mix(bg, cg, hh, buf, w):
    yc, new_buf = causal_dwconv(cg * hh, buf, w)
    return bg * yc, new_buf


def hybrid_layer(x, P, l, h_s5, h_ssd, buf_ssd, buf_sc, past):
    bsz, T, _ = x.shape
    xn = rmsnorm(x, P['norm_mix_g'][l])
    u, q, k, v, fr, z, xbc, dtr, sb, sc, sh = split_cols(xn @ P['w_in'][l])
    ya, h_s5n = s5_mix(u, h_s5, P['s5_lam_re'][l], P['s5_lam_im'][l], P['s5_log_dt'][l],
                       P['s5_b_re'][l], P['s5_b_im'][l], P['s5_c_re'][l], P['s5_c_im'][l],
                       P['s5_d'][l], P['s5_w_glu'][l])
    ya = rmsnorm(ya, P['s5_norm_g'][l])
    q = q.reshape(bsz, T, FOX_HEADS, FOX_HEAD_DIM)
    k = k.reshape(bsz, T, FOX_HEADS, FOX_HEAD_DIM)
    v = v.reshape(bsz, T, FOX_HEADS, FOX_HEAD_DIM)
    logf = jax.nn.log_sigmoid(fr.astype(F32) + P['fox_b_f'][l].astype(F32))
    if past is None:
        yb = fox_prompt(q, k, v, logf)
    else:
        yb = fox_sample(q, k, v, logf, past[0], past[1], past[2])
    yb = rmsnorm(yb.reshape(bsz, T, FOX_WIDTH), P['fox_norm_g'][l])
    yc, h_ssdn, buf_ssdn = ssd_mix(z, xbc, dtr, buf_ssd, h_ssd, P['ssd_conv_w'][l], P['ssd_conv_b'][l],
                                   P['ssd_dt_bias'][l], P['ssd_a_log'][l], P['ssd_d'][l],
                                   P['ssd_norm_g'][l])
    yd, buf_scn = short_conv_mix(sb, sc, sh, buf_sc, P['sc_conv_w'][l])
    yd = rmsnorm(yd, P['sc_norm_g'][l])
    x = x + jnp.concatenate([ya, yb, yc, yd], axis=-1) @ P['w_out'][l]
    hn = rmsnorm(x, P['norm_ffn_g'][l])
    x = x + jnp.square(jax.nn.relu(hn @ P['w_up'][l])) @ P['w_down'][l]
    return x, (k, v, logf, h_s5n, h_ssdn, buf_ssdn, buf_scn)


def gather_pages(cache, l, page_table):
    g = cache[l, page_table]
    return g.reshape((g.shape[0], g.shape[1] * g.shape[2]) + g.shape[3:])


def stack_layers(states):
    return tuple(jnp.stack(col, axis=0) for col in zip(*states))


def setup_inputs(seed: int = 0) -> dict:
    key = jax.random.key(seed)
    ks = iter(jax.random.split(key, 48))

    def nrm(shape, scale):
        return jax.random.normal(next(ks), shape, F32) * scale

    def unif(shape, lo, hi):
        return jax.random.uniform(next(ks), shape, F32, lo, hi)

    def gain(shape):
        return 1.0 + nrm(shape, 0.02)

    n_pages = PAST_LEN // PAGE_SIZE
    n_pool = (DEC_BATCH * n_pages * POOL_NUM) // POOL_DEN
    x_prompt = nrm((BATCH, SEQ, D_MODEL), 1.0)
    x_sample = nrm((DEC_BATCH, DEC_SEQ, D_MODEL), 1.0)
    cache_k = nrm((DEPTH, n_pool, PAGE_SIZE, FOX_HEADS, FOX_HEAD_DIM), 1.0)
    cache_v = nrm((DEPTH, n_pool, PAGE_SIZE, FOX_HEADS, FOX_HEAD_DIM), 1.0)
    cache_logf = jax.nn.log_sigmoid(nrm((DEPTH, n_pool, PAGE_SIZE, FOX_HEADS), 1.0) + 4.0)
    state_s5 = nrm((DEPTH, DEC_BATCH, S5_GROUPS, S5_STATE, 2), 0.1)
    state_ssd = nrm((DEPTH, DEC_BATCH, SSD_HEADS, SSD_HEAD_DIM, SSD_STATE), 0.1)
    state_ssd_conv = nrm((DEPTH, DEC_BATCH, SSD_CONV - 1, SSD_XBC), 1.0)
    state_sconv = nrm((DEPTH, DEC_BATCH, SC_CONV - 1, SC_WIDTH), 1.0)
    page_table = jax.random.permutation(next(ks), n_pool)[:DEC_BATCH * n_pages].reshape(
        DEC_BATCH, n_pages).astype(jnp.int32)
    norm_mix_g = gain((DEPTH, D_MODEL))
    w_in = nrm((DEPTH, D_MODEL, IN_COLS), D_MODEL ** -0.5)
    s5_lam_re = -0.5 + nrm((DEPTH, S5_GROUPS, S5_STATE), 0.01)
    s5_lam_im = math.pi * jnp.arange(S5_STATE, dtype=F32) + nrm((DEPTH, S5_GROUPS, S5_STATE), 0.01)
    s5_log_dt = unif((DEPTH, S5_GROUPS), math.log(1e-3), math.log(1e-1))
    s5_b_re = nrm((DEPTH, S5_GROUPS, S5_STATE, S5_GROUP), (2 * S5_GROUP) ** -0.5)
    s5_b_im = nrm((DEPTH, S5_GROUPS, S5_STATE, S5_GROUP), (2 * S5_GROUP) ** -0.5)
    s5_c_re = nrm((DEPTH, S5_GROUPS, S5_GROUP, S5_STATE), S5_STATE ** -0.5)
    s5_c_im = nrm((DEPTH, S5_GROUPS, S5_GROUP, S5_STATE), S5_STATE ** -0.5)
    s5_d = nrm((DEPTH, S5_GROUPS, S5_GROUP), 1.0)
    s5_w_glu = nrm((DEPTH, S5_WIDTH, S5_WIDTH), S5_WIDTH ** -0.5)
    s5_norm_g = gain((DEPTH, S5_WIDTH))
    fox_b_f = 3.0 + nrm((DEPTH, FOX_HEADS), 0.5)
    fox_norm_g = gain((DEPTH, FOX_WIDTH))
    ssd_conv_w = nrm((DEPTH, SSD_CONV, SSD_XBC), SSD_CONV ** -0.5)
    ssd_conv_b = nrm((DEPTH, SSD_XBC), 0.02)
    dt0 = jnp.exp(unif((DEPTH, SSD_HEADS), math.log(1e-3), math.log(1e-1)))
    ssd_dt_bias = dt0 + jnp.log(-jnp.expm1(-dt0))
    ssd_a_log = jnp.log(unif((DEPTH, SSD_HEADS), 1.0, 16.0))
    ssd_d = 1.0 + nrm((DEPTH, SSD_HEADS), 0.1)
    ssd_norm_g = gain((DEPTH, SSD_WIDTH))
    sc_conv_w = nrm((DEPTH, SC_CONV, SC_WIDTH), SC_CONV ** -0.5)
    sc_norm_g = gain((DEPTH, SC_WIDTH))
    w_out = nrm((DEPTH, MIX_WIDTH, D_MODEL), MIX_WIDTH ** -0.5)
    norm_ffn_g = gain((DEPTH, D_MODEL))
    w_up = nrm((DEPTH, D_MODEL, FFN_WIDTH), D_MODEL ** -0.5)
    w_down = nrm((DEPTH, FFN_WIDTH, D_MODEL), FFN_WIDTH ** -0.5)
    norm_final_g = gain((D_MODEL,))
    return {'x_prompt': x_prompt, 'x_sample': x_sample, 'cache_k': cache_k, 'cache_v': cache_v,
            'cache_logf': cache_logf, 'state_s5': state_s5, 'state_ssd': state_ssd,
            'state_ssd_conv': state_ssd_conv, 'state_sconv': state_sconv, 'page_table': page_table,
            'norm_mix_g': norm_mix_g, 'w_in': w_in, 's5_lam_re': s5_lam_re, 's5_lam_im': s5_lam_im,
            's5_log_dt': s5_log_dt, 's5_b_re': s5_b_re, 's5_b_im': s5_b_im, 's5_c_re': s5_c_re,
            's5_c_im': s5_c_im, 's5_d': s5_d, 's5_w_glu': s5_w_glu, 's5_norm_g': s5_norm_g,
            'fox_b_f': fox_b_f, 'fox_norm_g': fox_norm_g, 'ssd_conv_w': ssd_conv_w,
            'ssd_conv_b': ssd_conv_b, 'ssd_dt_bias': ssd_dt_bias, 'ssd_a_log': ssd_a_log,
            'ssd_d': ssd_d, 'ssd_norm_g': ssd_norm_g, 'sc_conv_w': sc_conv_w, 'sc_norm_g': sc_norm_g,
            'w_out': w_out, 'norm_ffn_g': norm_ffn_g, 'w_up': w_up, 'w_down': w_down,
            'norm_final_g': norm_final_g}


def reference(x_prompt, x_sample, cache_k, cache_v, cache_logf, state_s5, state_ssd,
              state_ssd_conv, state_sconv, page_table, norm_mix_g, w_in, s5_lam_re, s5_lam_im,
              s5_log_dt, s5_b_re, s5_b_im, s5_c_re, s5_c_im, s5_d, s5_w_glu, s5_norm_g,
              fox_b_f, fox_norm_g, ssd_conv_w, ssd_conv_b, ssd_dt_bias, ssd_a_log, ssd_d,
              ssd_norm_g, sc_conv_w, sc_norm_g, w_out, norm_ffn_g, w_up, w_down, norm_final_g):
    P = dict(norm_mix_g=norm_mix_g, w_in=w_in, s5_lam_re=s5_lam_re, s5_lam_im=s5_lam_im,
             s5_log_dt=s5_log_dt, s5_b_re=s5_b_re, s5_b_im=s5_b_im, s5_c_re=s5_c_re,
             s5_c_im=s5_c_im, s5_d=s5_d, s5_w_glu=s5_w_glu, s5_norm_g=s5_norm_g,
             fox_b_f=fox_b_f, fox_norm_g=fox_norm_g, ssd_conv_w=ssd_conv_w, ssd_conv_b=ssd_conv_b,
             ssd_dt_bias=ssd_dt_bias, ssd_a_log=ssd_a_log, ssd_d=ssd_d, ssd_norm_g=ssd_norm_g,
             sc_conv_w=sc_conv_w, sc_norm_g=sc_norm_g, w_out=w_out, norm_ffn_g=norm_ffn_g,
             w_up=w_up, w_down=w_down)

    bp = x_prompt.shape[0]
    h = x_prompt
    prompt_states = []
    for l in range(DEPTH):
        h, st = hybrid_layer(
            h, P, l,
            jnp.zeros((bp, S5_GROUPS, S5_STATE, 2), F32),
            jnp.zeros((bp, SSD_HEADS, SSD_HEAD_DIM, SSD_STATE), F32),
            jnp.zeros((bp, SSD_CONV - 1, SSD_XBC), x_prompt.dtype),
            jnp.zeros((bp, SC_CONV - 1, SC_WIDTH), x_prompt.dtype),
            None)
        prompt_states.append(st)
    y_prompt = rmsnorm(h, norm_final_g)
    (k_prompt, v_prompt, logf_prompt, s5_prompt, ssd_prompt,
     ssd_conv_prompt, sconv_prompt) = stack_layers(prompt_states)

    h = x_sample
    sample_states = []
    for l in range(DEPTH):
        past = (gather_pages(cache_k, l, page_table),
                gather_pages(cache_v, l, page_table),
                gather_pages(cache_logf, l, page_table))
        h, st = hybrid_layer(h, P, l, state_s5[l], state_ssd[l], state_ssd_conv[l],
                             state_sconv[l], past)
        sample_states.append(st)
    y_sample = rmsnorm(h, norm_final_g)
    (k_sample, v_sample, logf_sample, s5_sample, ssd_sample,
     ssd_conv_sample, sconv_sample) = stack_layers(sample_states)

    return (y_prompt, y_sample,
            k_prompt, v_prompt, logf_prompt, s5_prompt, ssd_prompt, ssd_conv_prompt, sconv_prompt,
            k_sample, v_sample, logf_sample, s5_sample, ssd_sample, ssd_conv_sample, sconv_sample)
```

```python
import os
import numpy as np
import ml_dtypes
from contextlib import ExitStack
import concourse.bass as bass
import concourse.mybir as mybir
from concourse.bass_utils import run_bass_kernel_spmd

F32 = mybir.dt.float32
BF16 = mybir.dt.bfloat16
I32 = mybir.dt.int32
ALU = mybir.AluOpType
AF = mybir.ActivationFunctionType

D = 1024
INC = 2824
EPS = 1e-5
NEG = -30000.0
MT_IN = ([("u", 0 + 128 * i, 128) for i in range(2)] + [("q", 256 + 128 * i, 128) for i in range(2)]
         + [("k", 512 + 128 * i, 128) for i in range(2)] + [("v", 768 + 128 * i, 128) for i in range(2)]
         + [("fr", 1024, 4)] + [("z", 1028 + 128 * i, 128) for i in range(2)]
         + [("xbc", 1284 + 128 * i, 128) for i in range(6)] + [("dt", 2052, 4)]
         + [("sb", 2056 + 128 * i, 128) for i in range(2)] + [("sc", 2312 + 128 * i, 128) for i in range(2)]
         + [("sh", 2568 + 128 * i, 128) for i in range(2)])
NMT_IN = len(MT_IN)


class Buf:
    __slots__ = ("name", "w", "r", "sem", "excl")

    def __init__(self, name, excl=False):
        self.name, self.w, self.r, self.sem, self.excl = name, None, [], None, excl


class Op:
    __slots__ = ("eng", "fn", "deps", "sem", "val", "needed", "isdma", "line")

    def __init__(self, eng, fn, deps, isdma=False, sem=None):
        self.eng, self.fn, self.deps = eng, fn, deps
        self.sem, self.val, self.needed, self.isdma = sem, None, False, isdma
        self.line = 0


def _flat(xs):
    out = []
    for x in xs:
        if isinstance(x, (list, tuple)):
            out.extend(_flat(x))
        elif x is not None:
            out.append(x)
    return out


class Sched:
    ENGS = ("pe", "act", "dve", "pool", "sp")

    def __init__(self, nc, es):
        self.nc, self.es = nc, es
        self.ops = {e: [] for e in self.ENGS}
        self.esem = {e: es.enter_context(nc.semaphore("s_" + e)) for e in self.ENGS}
        self.dcnt = {}
        self.nsem = 0
        self.limit = int(os.environ.get("K_MAXOPS", "1000000000"))
        self.nrec = 0
        self.lastline = None

    def _deps(self, eng, r, w, extra):
        deps = []
        for b in r:
            if b.w is not None:
                deps.append(b.w)
        for b in w:
            if b.w is not None:
                deps.append(b.w)
            deps.extend(b.r)
        deps.extend([x for x in extra if x is not None])
        if eng == "pe":
            deps = [d for d in deps if d.eng != "pe" or d.isdma]
        out, seen = [], set()
        for d in deps:
            if id(d) not in seen:
                seen.add(id(d))
                out.append(d)
                d.needed = True
        return out

    def _upd(self, o, r, w):
        for b in r:
            b.r.append(o)
        for b in w:
            b.w = o
            b.r = []

    def _skip(self):
        import sys as _sys
        self.nrec += 1
        if self.nrec > self.limit:
            return True
        self.lastline = (_sys._getframe(2).f_lineno, _sys._getframe(3).f_lineno)
        return False

    def op(self, eng, fn, r=(), w=(), extra=()):
        r, w, extra = _flat(r), _flat(w), _flat(extra)
        w = w + [b for b in r if b.excl]
        r = [b for b in r if not b.excl]
        if self._skip():
            return None
        o = Op(eng, fn, self._deps(eng, r, w, extra))
        self._upd(o, r, w)
        self.ops[eng].append(o)
        return o

    def dma(self, eng, key, fn, r=(), w=(), extra=()):
        r, w, extra = _flat(r), _flat(w), _flat(extra)
        if self._skip():
            return None
        if key.sem is None:
            key.sem = self.es.enter_context(self.nc.semaphore("d%d" % self.nsem))
            self.nsem += 1
            self.dcnt[id(key)] = 0
        o = Op(eng, fn, self._deps("dma", r, w, extra), isdma=True, sem=key.sem)
        self.dcnt[id(key)] += 16
        o.val = self.dcnt[id(key)]
        o.needed = True
        self._upd(o, r, w)
        self.ops[eng].append(o)
        return o

    def emit(self, final_ops):
        nc = self.nc
        for e in self.ENGS:
            c = 0
            for o in self.ops[e]:
                if o.isdma:
                    continue
                if o.needed:
                    c += 1
                    o.sem, o.val = self.esem[e], c
        engobj = {"pe": "tensor", "act": "scalar", "dve": "vector", "pool": "gpsimd", "sp": "sync"}
        with nc.Block() as block:
            for e in self.ENGS:
                ops = self.ops[e]

                def body(eng, ops=ops, e=e):
                    waited = {}
                    for o in ops:
                        need = {}
                        for d in o.deps:
                            k = id(d.sem)
                            if waited.get(k, 0) >= d.val:
                                continue
                            if k not in need or need[k][1] < d.val:
                                need[k] = (d.sem, d.val)
                        for k, (s, v) in need.items():
                            eng.wait_ge(s, v)
                            waited[k] = v
                        ins = o.fn(eng)
                        if o.isdma:
                            ins.then_inc(o.sem, 16)
                        elif o.needed:
                            ins.then_inc(o.sem, 1)
                    if e == "sp":
                        fin = {}
                        for d in final_ops:
                            if d is None:
                                continue
                            k = id(d.sem)
                            if k not in fin or fin[k][1] < d.val:
                                fin[k] = (d.sem, d.val)
                        for k, (s, v) in fin.items():
                            eng.wait_ge(s, v)

                getattr(block, engobj[e])(body)


def host_consts():
    c = {}
    c["ident"] = np.eye(128, dtype=np.float32)
    s = np.arange(128)
    c["tri"] = (s[:, None] <= s[None, :]).astype(np.float32)
    c["tristrict"] = (s[:, None] > s[None, :]).astype(np.float32)
    c["maskb"] = np.where(s[:, None] <= s[None, :], 0.0, NEG).astype(np.float32)
    sel = np.zeros((36, 4, 128), np.float32)
    for h in range(4):
        sel[h, h, :] = 1.0
        sel[32 + h, h, :] = 1.0
    c["sel"] = sel
    t = np.arange(16)
    same = (t[:, None] // 4) == (t[None, :] // 4)
    c["triS"] = (same & (t[:, None] <= t[None, :])).astype(np.float32)
    mS = np.zeros((128, 16), np.float32)
    mS[:16] = np.where(same & (t[:, None] <= t[None, :]), 0.0, NEG)
    c["maskS"] = mS
    p4 = np.ones((4, 16), np.float32); p4[:, 0::4] = 0.0
    c["pat4"] = p4
    c["iota"] = np.arange(128, dtype=np.float32).reshape(128, 1)
    return c


class Cfg:
    def __init__(self, T=4096, depth=4, ns=4, ts=4, npg=64, npool=2560, sample=True):
        self.T, self.depth, self.ns, self.ts, self.npg, self.npool, self.sample = T, depth, ns, ts, npg, npool, sample
        self.nch = T // 512


def build_program(cfg):
    nc = bass.Bass("TRN2", target_bir_lowering=False)
    T, L, NS, TS, NPG = cfg.T, cfg.depth, cfg.ns, cfg.ts, cfg.npg
    NSAMP = NS * TS
    NCH = cfg.nch
    NKT = T // 128
    es = ExitStack()

    def din(name, shape, dt=F32):
        return nc.dram_tensor(name, list(shape), dt, kind="ExternalInput").ap()

    def dout(name, shape, dt=F32):
        return nc.dram_tensor(name, list(shape), dt, kind="ExternalOutput").ap()

    xp = din("xp", [T, D])
    w_in = din("w_in", [L, D, INC]); w_out = din("w_out", [L, D, D])
    w_up = din("w_up", [L, D, 4096]); w_down = din("w_down", [L, 4096, D])
    prm = {}
    for nm, shp in [("norm_mix_g", [L, D]), ("norm_ffn_g", [L, D]), ("norm_final_g", [D]),
                    ("s5_lam_re", [L, 16, 64]), ("s5_lam_im", [L, 16, 64]), ("s5_log_dt", [L, 16]),
                    ("s5_b_re", [L, 16, 64, 16]), ("s5_b_im", [L, 16, 64, 16]),
                    ("s5_c_re", [L, 16, 16, 64]), ("s5_c_im", [L, 16, 16, 64]), ("s5_d", [L, 16, 16]),
                    ("s5_w_glu", [L, 256, 256]), ("s5_norm_g", [L, 256]), ("fox_b_f", [L, 4]),
                    ("fox_norm_g", [L, 256]), ("ssd_conv_w", [L, 4, 768]), ("ssd_conv_b", [L, 768]),
                    ("ssd_dt_bias", [L, 4]), ("ssd_a_log", [L, 4]), ("ssd_d", [L, 4]),
                    ("ssd_norm_g", [L, 256]), ("sc_conv_w", [L, 3, 256]), ("sc_norm_g", [L, 256])]:
        prm[nm] = din(nm, shp)
    cst = {k: din("c_" + k, v.shape) for k, v in host_consts().items()}
    y_p = dout("y_p", [T, D]); k_p = dout("k_p", [L, T, 256]); v_p = dout("v_p", [L, T, 256])
    logf_p = dout("logf_p", [L, T, 4]); s5_p = dout("s5_p", [L, 16, 64, 2]); ssd_p = dout("ssd_p", [L, 256, 128])
    ssdconv_p = dout("ssdconv_p", [L, 3, 768]); sconv_p = dout("sconv_p", [L, 2, 256])
    if cfg.sample:
        xs = din("xs", [NSAMP, D])
        cache_k = din("cache_k", [L * cfg.npool * 128, 256]); cache_v = din("cache_v", [L * cfg.npool * 128, 256])
        cache_logf = din("cache_logf", [L * cfg.npool * 128, 4])
        st_s5 = din("state_s5", [L, NS, 16, 64, 2]); st_ssd = din("state_ssd", [L, NS, 256, 128])
        st_ssdconv = din("state_ssd_conv", [L, NS, 3, 768]); st_sconv = din("state_sconv", [L, NS, 2, 256])
        page_table = din("page_table", [NS, NPG], I32)
        y_s = dout("y_s", [NSAMP, D]); k_s = dout("k_s", [L, NSAMP, 256]); v_s = dout("v_s", [L, NSAMP, 256])
        logf_s = dout("logf_s", [L, NSAMP, 4]); s5_s = dout("s5_s", [L, NS, 16, 64, 2])
        ssd_s = dout("ssd_s", [L, NS, 256, 128]); ssdconv_s = dout("ssdconv_s", [L, NS, 3, 768])
        sconv_s = dout("sconv_s", [L, NS, 2, 256])
    wt_in = nc.dram_tensor("wt_in", [L, NMT_IN, 128, 8, 128], BF16).ap()
    wt_out = nc.dram_tensor("wt_out", [L, 8, 128, 8, 128], BF16).ap()
    wt_up = nc.dram_tensor("wt_up", [L, 32, 128, 8, 128], BF16).ap()
    wt_down = nc.dram_tensor("wt_down", [L, 8, 128, 32, 128], BF16).ap()
    x_scr = nc.dram_tensor("x_scr", [NCH, 128, 8, 512], F32).ap()
    kt_scr = nc.dram_tensor("kt_scr", [NCH, 128, 2, 512], BF16).ap()
    v_scr = nc.dram_tensor("v_scr", [NCH, 128, 4, 4, 64], BF16).ap()
    DBG = bool(os.environ.get("K_DBG"))
    if DBG:
        dbg = dout("dbg", [128, 16384])
    dbg_state = {"off": 0, "items": []}

    with es:
        S = Sched(nc, es)
        finals = []

        def sb(name, shape, dt=F32):
            return es.enter_context(nc.sbuf_tensor(name, list(shape), dt))

        def dbg_dump(name, ap2d, ncols, bufs):
            if not DBG:
                return
            o = dbg_state["off"]
            npart = ap2d.shape[0]
            b = Buf("dbg_" + name)
            finals.append(S.dma("pool", b, lambda e: e.dma_start(out=dbg[0:npart, o:o + ncols], in_=ap2d), r=bufs, w=[b]))
            dbg_state["items"].append((name, o, ncols, npart))
            dbg_state["off"] = o + ncols
        nc._dbg_items = dbg_state["items"]

        PS = [es.enter_context(nc.psum_tensor("ps%d" % i, [128, 512], F32)) for i in range(8)]
        PSB = [Buf("ps%d" % i, excl=True) for i in range(8)]
        mm_rr = [0]

        def mmbank():
            i = mm_rr[0] % 3
            mm_rr[0] += 1
            return PS[i], PSB[i]

        ident = sb("ident", [128, 128]); b_ident = Buf("ident")
        identb = sb("identb", [128, 128], BF16); b_identb = Buf("identb")
        tri = sb("tri", [128, 128]); b_tri = Buf("tri")
        tristrict = sb("tristrict", [128, 128]); b_tristrict = Buf("tristrict")
        maskb = sb("maskb", [128, 128], BF16); b_maskb = Buf("maskb")
        sel = sb("sel", [36, 4, 128], BF16); b_sel = Buf("sel")
        onesf = sb("onesf", [128, 128]); b_onesf = Buf("onesf")
        onesb = sb("onesb", [128, 128], BF16); b_onesb = Buf("onesb")
        S.dma("pool", b_ident, lambda e: e.dma_start(out=ident[:], in_=cst["ident"]), w=[b_ident])
        S.dma("pool", b_identb, lambda e: e.dma_start(out=identb[:], in_=cst["ident"]), w=[b_identb])
        S.dma("pool", b_tri, lambda e: e.dma_start(out=tri[:], in_=cst["tri"]), w=[b_tri])
        S.dma("pool", b_tristrict, lambda e: e.dma_start(out=tristrict[:], in_=cst["tristrict"]), w=[b_tristrict])
        S.dma("pool", b_maskb, lambda e: e.dma_start(out=maskb[:], in_=cst["maskb"]), w=[b_maskb])
        S.dma("pool", b_sel, lambda e: e.dma_start(out=sel[:], in_=cst["sel"]), w=[b_sel])
        S.op("dve", lambda e: e.memset(onesf[:], 1.0), w=[b_onesf])
        S.op("dve", lambda e: e.memset(onesb[:], 1.0), w=[b_onesb])

        b_wt = [Buf("wt%d" % l) for l in range(L)]
        big16 = sb("big16", [128, 4096], F32); b_big = [Buf("big%d" % k) for k in range(16)]
        pstg = big16[:, 0:2048].bitcast(BF16); b_pstg = b_big[0:8]

        pst2 = [sb("pst2_%d" % i, [128, 4, 128], BF16) for i in range(2)]; b_pst2 = [Buf("pst2_%d" % i) for i in range(2)]
        pc_rr = [0]
        for l in range(L):
            b_wt[l].sem = es.enter_context(nc.semaphore("wt%d" % l)); S.dcnt[id(b_wt[l])] = 0

        def pc_jobs(l):
            jobs = []
            v = lambda w_: w_[l].rearrange("(k p) m -> p k m", p=128)
            for mi, (nm, c0, wd) in enumerate(MT_IN):
                for kh in range(2):
                    jobs.append((v(w_in)[:, kh * 4:(kh + 1) * 4, c0:c0 + wd], wt_in[l, mi, :, kh * 4:(kh + 1) * 4, 0:wd], wd))
            for mi in range(8):
                for kh in range(2):
                    jobs.append((v(w_out)[:, kh * 4:(kh + 1) * 4, mi * 128:(mi + 1) * 128], wt_out[l, mi, :, kh * 4:(kh + 1) * 4, :], 128))
            for mi in range(32):
                for kh in range(2):
                    jobs.append((v(w_up)[:, kh * 4:(kh + 1) * 4, mi * 128:(mi + 1) * 128], wt_up[l, mi, :, kh * 4:(kh + 1) * 4, :], 128))
            for mi in range(8):
                for kq in range(8):
                    jobs.append((v(w_down)[:, kq * 4:(kq + 1) * 4, mi * 128:(mi + 1) * 128], wt_down[l, mi, :, kq * 4:(kq + 1) * 4, :], 128))
            return jobs

        def pc_issue(l, job):
            src, dst, wd = job
            i = pc_rr[0] % 2; pc_rr[0] += 1
            stg = pst2[i][:, :, 0:wd]
            S.dma("pool", b_pst2[i], lambda e: e.dma_start(out=stg, in_=src), w=[b_pst2[i]])
            S.dma("pool", b_wt[l], lambda e: e.dma_start(out=dst, in_=stg), r=[b_pst2[i]], w=[])
        for job in pc_jobs(0):
            pc_issue(0, job)
        pc_pending = {"l": 1, "jobs": pc_jobs(1) if L > 1 else []}

        def pc_drip(n):
            for _ in range(n):
                if not pc_pending["jobs"]:
                    return
                pc_issue(pc_pending["l"], pc_pending["jobs"].pop(0))

        def pc_flush_and_next():
            pc_drip(10 ** 6)
            pc_pending["l"] += 1
            pc_pending["jobs"] = pc_jobs(pc_pending["l"]) if pc_pending["l"] < L else []
        wt_ready = []
        for l in range(L):
            o = Op("pool", None, [], isdma=True, sem=b_wt[l].sem)
            o.val = 0
            wt_ready.append(o)

        NSLOT = 2
        wslot = [sb("wslot%d" % i, [128, 4096], BF16) for i in range(NSLOT)]
        b_wslot = [Buf("wslot%d" % i) for i in range(NSLOT)]
        def groups_for(l):
            g = []
            for i in range(6):
                g.append(("in", l, i * 4, 4))
            for i in range(2):
                g.append(("out", l, i * 4, 4))
            for half in range(2):
                for i in range(4 * half, 4 * half + 4):
                    g.append(("up", l, i * 4, 4))
                for m in range(8):
                    g.append(("down", l, m * 2 + half, 1))
            return g
        nblk = NCH + (1 if cfg.sample else 0)
        glist = []
        for l in range(L):
            for b in range(nblk):
                glist.extend(groups_for(l))
        gstate = {"issued": 0, "next": 0}

        def gview(idx):
            kind, l, m0, n = glist[idx]
            slot = idx % NSLOT
            kk = 16 if kind == "down" else 8
            return wslot[slot][:, 0:n * kk * 128].rearrange("p (m k c) -> p m k c", m=n, k=kk), b_wslot[slot]

        def issue_group(idx):
            kind, l, m0, n = glist[idx]
            dst, b_dst = gview(idx)
            if kind == "down":
                m, half = m0 // 2, m0 % 2
                s_ap = wt_down[l, m:m + 1, :, half * 16:(half + 1) * 16, :].rearrange("m p k c -> p m k c")
            else:
                src = {"in": wt_in, "out": wt_out, "up": wt_up}[kind]
                s_ap = src[l, m0:m0 + n].rearrange("m p k c -> p m k c")
            S.dma("sp", b_dst, lambda e, dst=dst, s_ap=s_ap: e.dma_start(out=dst, in_=s_ap),
                  w=[b_dst], extra=[wt_ready[l]])

        def next_group(kind, l, m0):
            idx = gstate["next"]
            assert glist[idx][:3] == (kind, l, m0), (glist[idx], kind, l, m0)
            while gstate["issued"] < min(len(glist), idx + NSLOT - 1) or gstate["issued"] <= idx:
                issue_group(gstate["issued"])
                gstate["issued"] += 1
            gstate["next"] += 1
            pc_drip(1)
            return gview(idx)

        gmix = sb("gmix", [128, L, 8]); gffn = sb("gffn", [128, L, 8]); gfin = sb("gfin", [128, 8])
        b_gains = Buf("gains")
        for l in range(L):
            S.dma("pool", b_gains, lambda e, l=l: e.dma_start(out=gmix[:, l, :], in_=prm["norm_mix_g"][l].rearrange("(k p) -> p k", p=128), allow_slow_non_contiguous=True), w=[])
            S.dma("pool", b_gains, lambda e, l=l: e.dma_start(out=gffn[:, l, :], in_=prm["norm_ffn_g"][l].rearrange("(k p) -> p k", p=128), allow_slow_non_contiguous=True), w=[])
        o_g = S.dma("pool", b_gains, lambda e: e.dma_start(out=gfin[:], in_=prm["norm_final_g"].rearrange("(k p) -> p k", p=128), allow_slow_non_contiguous=True), w=[b_gains])
        ggrp = sb("ggrp", [128, L, 4, 2]); b_ggrp = Buf("ggrp")
        for l in range(L):
            for gi, nm in enumerate(["s5_norm_g", "fox_norm_g", "ssd_norm_g", "sc_norm_g"]):
                S.dma("pool", b_ggrp, lambda e, l=l, gi=gi, nm=nm: e.dma_start(out=ggrp[:, l, gi, :], in_=prm[nm][l].rearrange("(k p) -> p k", p=128), allow_slow_non_contiguous=True), w=[b_ggrp])

        epsc = sb("epsc", [128, 1]); b_epsc = Buf("epsc")
        S.op("dve", lambda e: e.memset(epsc[:], EPS), w=[b_epsc])
        halfpi = sb("halfpi", [128, 1]); b_halfpi = Buf("halfpi")
        S.op("dve", lambda e: e.memset(halfpi[:], float(np.pi / 2)), w=[b_halfpi])
        maskf = sb("maskf", [128, 128]); b_maskf = Buf("maskf")
        S.dma("pool", b_maskf, lambda e: e.dma_start(out=maskf[:], in_=cst["maskb"]), w=[b_maskf])
        rs = sb("rs", [128, 512]); b_rs = Buf("rs")

        def rmsnorm_fm(src, b_src, nk, N, gain_fn, dst, b_dst):
            ps, b_ps = mmbank()
            for k in range(nk):
                S.op("act", lambda e, k=k: e.activation(out=sq[:, k, 0:N], in_=src[:, k, 0:N], func=AF.Square),
                     r=[b_src[k]], w=[b_sq[k]])
            for k in range(nk):
                S.op("pe", lambda e, k=k: e.matmul(ps[:, 0:N], onesb[:, :], sq[:, k, 0:N], start=(k == 0), stop=(k == nk - 1)),
                     r=[b_sq[k], b_onesb], w=[b_ps])
            S.op("act", lambda e: e.activation(out=rs[:, 0:N], in_=ps[:, 0:N], func=AF.Sqrt, scale=1.0 / (nk * 128), bias=epsc[:, 0:1]),
                 r=[b_ps, b_epsc], w=[b_rs])
            S.op("dve", lambda e: e.reciprocal(rs[:, 0:N], rs[:, 0:N]), r=[b_rs], w=[b_rs])
            for k in range(nk):
                S.op("dve", lambda e, k=k: e.scalar_tensor_tensor(
                    out=dst[:, k, 0:N], in0=src[:, k, 0:N], scalar=gain_fn(k), in1=rs[:, 0:N],
                    op0=ALU.mult, op1=ALU.mult), r=[b_src[k], b_rs, b_gains, b_ggrp], w=[b_dst[k]])

        NB = 512
        xT = sb("xT", [128, 8, NB]); b_xT = [Buf("xT%d" % k) for k in range(8)]
        xnT = sb("xnT", [128, 8, NB], BF16); b_xnT = [Buf("xnT%d" % k) for k in range(8)]
        uTb = sb("uTb", [128, 2, NB], BF16); b_uTb = Buf("uTb")
        qT = sb("qT", [128, 2, NB], BF16); b_qT = Buf("qT")
        KTb = sb("KTb", [128, 2, NB], BF16); b_KTb = Buf("KTb")
        Vpb = sb("Vpb", [128, 4, 4, 128], BF16); b_Vpb = Buf("Vpb")
        S.op("pool", lambda e: e.memset(Vpb[:], 1.0), w=[b_Vpb])
        KTc = [sb("KTc%d" % i, [128, 2, NB], BF16) for i in range(2)]; b_KTc = [Buf("KTc%d" % i) for i in range(2)]
        Vpc = [sb("Vpc%d" % i, [128, 4, 4, 128], BF16) for i in range(2)]; b_Vpc = [Buf("Vpc%d" % i) for i in range(2)]
        for i in range(2):
            S.op("pool", lambda e, i=i: e.memset(Vpc[i][:], 1.0), w=[b_Vpc[i]])
        b_ktscr = [Buf("ktscr%d" % c) for c in range(NCH)]; b_vscr = [Buf("vscr%d" % c) for c in range(NCH)]
        cumT = sb("cumT", [4, NB]); b_cumT = Buf("cumT")
        cumc = sb("cumc", [4, 1]); b_cumc = Buf("cumc")
        augT = sb("augT", [36, NB], BF16); b_augT = Buf("augT")
        S.op("pool", lambda e: e.memset(augT[:], 0.0), w=[b_augT])
        zerob = sb("zerob", [128, 128], BF16)
        S.op("pool", lambda e: e.memset(zerob[:], 0.0), w=[b_maskb])
        ones4 = sb("ones4", [4, NB], BF16); b_ones4 = Buf("ones4")
        S.op("pool", lambda e: e.memset(ones4[:], 1.0), w=[b_ones4])
        PT = [sb("PT%d" % i, [128, NB], BF16) for i in range(2)]; b_PT = [Buf("PT%d" % i) for i in range(2)]
        rdn = rs; b_rdn = b_rs
        kc_rr = [0]; pt_rr = [0]; sbank_rr = [0]
        negcum = sb("negcum", [128, NKT, 4]); b_negcum = Buf("negcum")
        carry_bc = sb("carry_bc", [128, 4]); b_carry = Buf("carry_bc")
        logfT = sb("logfT", [4, NB]); b_logfT = Buf("logfT")
        xbcT = sb("xbcT", [128, 6, 3 + NB], BF16); b_xbcT = Buf("xbcT")
        xbcA = sb("xbcA", [128, 6, NB], BF16); b_xbcA = Buf("xbcA")
        szT = sb("szT", [128, 2, NB], BF16); b_szT = Buf("szT")
        daT = sb("daT", [36, NB]); b_daT = Buf("daT")
        S.op("pool", lambda e: e.memset(daT[:], 0.0), w=[b_daT])
        xs_tok = sb("xs_tok", [128, 256]); B_tok = sb("B_tok", [128, 256], BF16); da_tok = sb("da_tok", [128, 36]); b_tok = Buf("tok")
        xdtz = sb("xdtz", [128, 4, 128], BF16); xdtd = sb("xdtd", [128, 256], BF16); b_xdt = Buf("xdt")
        S.op("pool", lambda e: e.memset(xdtz[:], 0.0), w=[b_xdt])
        arep = sb("arep", [128, 4, 128]); b_arep = Buf("arep")
        sm = sb("sm", [128, 16]); b_sm = Buf("sm")
        dec = sb("dec", [128, 4, 128]); b_dec = Buf("dec")
        ea = sb("ea", [128, 4, 128], BF16); b_ea = Buf("ea")
        MTt = sb("MTt", [128, 4, 128], BF16); b_MT = Buf("MT")
        CdT = sb("CdT", [128, 4, 128], BF16); b_CdT = Buf("CdT")
        HT = sb("HT", [128, 256]); HTz = sb("HTz", [128, 4, 128], BF16); b_HT = Buf("HT"); b_HTz = Buf("HTz")
        S.op("pool", lambda e: e.memset(HTz[:], 0.0), w=[b_HTz])
        ysd = sb("ysd", [128, 128]); b_ysd = Buf("ysd")
        sbT = sb("sbT", [128, 2, NB], BF16); b_sbT = Buf("sbT")
        cshT = sb("cshT", [128, 2, 2 + NB], BF16); b_cshT = Buf("cshT")
        catT = sb("catT", [128, 8, NB], BF16); b_catT = [Buf("catT%d" % k) for k in range(8)]
        mixT = sb("mixT", [128, 8, NB], BF16); b_mixT = [Buf("mixT%d" % k) for k in range(8)]
        hT = big16[:, :].bitcast(BF16).rearrange("p (k n) -> p k n", k=16); b_hT = b_big
        sq = hT; b_sq = b_hT
        kvst = [sb("kvst%d" % i, [128, 512]) for i in range(1)]*2; b_kvst = [Buf("kvst0")]*2
        b_kvst_st = [Buf("kvst_st0")]*2
        ostage = [sb("ostage%d" % i, [128, 1024]) for i in range(1)]*2; b_ostage = [Buf("ost0")]*2
        b_ostage_st = [Buf("ost_st0")]*2
        b_xscr = [Buf("xscr%d" % c) for c in range(NCH)]
        lf_tok = sb("lf_tok", [128, 4, 4]); b_lf_tok = Buf("lf_tok")
        b_lf_st = Buf("lf_st")

        bfcol = sb("bfcol", [4, L]); dtbcol = sb("dtbcol", [4, L]); alogcol = sb("alogcol", [4, L]); b_pcol = Buf("pcol")
        S.dma("pool", b_pcol, lambda e: e.dma_start(out=bfcol[:], in_=prm["fox_b_f"].rearrange("l h -> h l"), allow_slow_non_contiguous=True), w=[])
        S.dma("pool", b_pcol, lambda e: e.dma_start(out=dtbcol[:], in_=prm["ssd_dt_bias"].rearrange("l h -> h l"), allow_slow_non_contiguous=True), w=[])
        S.dma("pool", b_pcol, lambda e: e.dma_start(out=alogcol[:], in_=prm["ssd_a_log"].rearrange("l h -> h l"), allow_slow_non_contiguous=True), w=[b_pcol])
        nbfcol = sb("nbfcol", [4, L]); acol = sb("acol", [4, L])
        S.op("dve", lambda e: e.tensor_scalar(out=nbfcol[:], in0=bfcol[:], scalar1=-1.0, scalar2=None, op0=ALU.mult), r=[b_pcol], w=[b_pcol])
        S.op("act", lambda e: e.activation(out=acol[:], in_=alogcol[:], func=AF.Exp), r=[b_pcol], w=[b_pcol])
        S.op("dve", lambda e: e.tensor_scalar(out=acol[:], in0=acol[:], scalar1=-1.0, scalar2=None, op0=ALU.mult), r=[b_pcol], w=[b_pcol])
        bf_bc = sb("bf_bc", [128, L, 4]); b_bfbc = Buf("bf_bc")
        for l in range(L):
            S.dma("pool", b_bfbc, lambda e, l=l: e.dma_start(out=bf_bc[:, l, :], in_=prm["fox_b_f"][l:l + 1, :].partition_broadcast(128)), w=[b_bfbc])
        convw = sb("convw", [128, L, 6, 4]); convb = sb("convb", [128, L, 6]); scw = sb("scw", [128, L, 2, 3]); b_convp = Buf("convp")
        dcol = sb("dcol", [128, L, 2]); s5dcol = sb("s5dcol", [128, L, 2])
        for l in range(L):
            for k in range(6):
                S.dma("pool", b_convp, lambda e, l=l, k=k: e.dma_start(out=convw[:, l, k, :], in_=prm["ssd_conv_w"][l][:, k * 128:(k + 1) * 128].rearrange("j p -> p j"), allow_slow_non_contiguous=True), w=[])
            S.dma("pool", b_convp, lambda e, l=l: e.dma_start(out=convb[:, l, :], in_=prm["ssd_conv_b"][l].rearrange("(k p) -> p k", p=128), allow_slow_non_contiguous=True), w=[])
            for k in range(2):
                S.dma("pool", b_convp, lambda e, l=l, k=k: e.dma_start(out=scw[:, l, k, :], in_=prm["sc_conv_w"][l][:, k * 128:(k + 1) * 128].rearrange("j p -> p j"), allow_slow_non_contiguous=True), w=[])
            S.dma("pool", b_convp, lambda e, l=l: e.dma_start(out=s5dcol[:, l, :], in_=prm["s5_d"][l].rearrange("(k g) h -> (g h) k", k=2), allow_slow_non_contiguous=True), w=[])
            for h in range(4):
                S.dma("pool", b_convp, lambda e, l=l, h=h: e.dma_start(
                    out=dcol[(h % 2) * 64:(h % 2) * 64 + 64, l, h // 2:h // 2 + 1],
                    in_=prm["ssd_d"][l:l + 1, h:h + 1].partition_broadcast(64)), w=[b_convp])
        wglu = sb("wglu", [128, 2, 256], BF16); b_wglu = Buf("wglu")

        LS = 128
        Er = sb("Er", [128, 8, LS]); Ei = sb("Ei", [128, 8, LS]); T1r = sb("T1r", [128, 8, LS]); T1i = sb("T1i", [128, 8, LS])
        R0 = sb("R0", [128, 8, LS]); b_tab = Buf("s5tab")
        Bm = sb("Bm", [128, 8, 2, 128], BF16); Cm = sb("Cm", [128, 8, 2, 128], BF16); b_BC = Buf("s5BC")
        s5c = sb("s5c", [128, 24, 8]); b_s5c = Buf("s5c")
        LR, LI, DT, RR, CC, SS, AR, AI, QR, QI, CM, SM, TA, TB, TC, TD = range(16)
        Xr = sb("Xr", [128, 8, 4]); Xi = sb("Xi", [128, 8, 4]); Wr = sb("Wr", [128, 8, 4]); Wi = sb("Wi", [128, 8, 4]); b_X = Buf("s5X")
        s5t = sb("s5t", [128, 8, 4, 4]); b_s5t = Buf("s5t")

        def col(i):
            return s5c[:, i, :]

        def s5_tables(l):
            d = S.dma
            d("pool", b_s5c, lambda e: e.dma_start(out=col(LR), in_=prm["s5_lam_re"][l].rearrange("(j g) p -> (g p) j", g=2), allow_slow_non_contiguous=True), w=[b_s5c], r=[b_tab])
            d("pool", b_s5c, lambda e: e.dma_start(out=col(LI), in_=prm["s5_lam_im"][l].rearrange("(j g) p -> (g p) j", g=2), allow_slow_non_contiguous=True), w=[b_s5c])
            for g2 in range(2):
                d("pool", b_s5c, lambda e, g2=g2: e.dma_start(out=s5c[g2 * 64:(g2 + 1) * 64, DT, :], in_=prm["s5_log_dt"][l].rearrange("(j g) -> g j", g=2)[g2:g2 + 1, :].partition_broadcast(64), allow_slow_non_contiguous=True), w=[b_s5c])
            o = lambda eng, fn: S.op(eng, fn, r=[b_s5c, b_halfpi, b_onesf], w=[b_s5c])
            o("act", lambda e: e.activation(out=col(DT), in_=col(DT), func=AF.Exp))
            o("dve", lambda e: e.tensor_tensor(out=col(TA), in0=col(LR), in1=col(DT), op=ALU.mult))
            o("dve", lambda e: e.tensor_tensor(out=col(TB), in0=col(LI), in1=col(DT), op=ALU.mult))
            o("act", lambda e: e.activation(out=col(RR), in_=col(TA), func=AF.Exp))
            o("dve", lambda e: e.tensor_scalar(out=col(TB), in0=col(TB), scalar1=1.0 / 32, scalar2=None, op0=ALU.mult))
            o("dve", lambda e: e.tensor_tensor(out=col(TA), in0=col(TB), in1=col(TB), op=ALU.mult))
            o("dve", lambda e: e.tensor_scalar(out=col(SS), in0=col(TA), scalar1=1.0 / 362880, scalar2=None, op0=ALU.mult))
            for cc in (-1.0 / 5040, 1.0 / 120, -1.0 / 6):
                o("dve", lambda e, cc=cc: e.scalar_tensor_tensor(out=col(SS), in0=col(SS), scalar=cc, in1=col(TA), op0=ALU.add, op1=ALU.mult))
            o("dve", lambda e: e.scalar_tensor_tensor(out=col(SS), in0=col(SS), scalar=1.0, in1=col(TB), op0=ALU.add, op1=ALU.mult))
            o("dve", lambda e: e.tensor_scalar(out=col(CC), in0=col(TA), scalar1=-1.0 / 3628800, scalar2=None, op0=ALU.mult))
            for cc in (1.0 / 40320, -1.0 / 720, 1.0 / 24, -0.5):
                o("dve", lambda e, cc=cc: e.scalar_tensor_tensor(out=col(CC), in0=col(CC), scalar=cc, in1=col(TA), op0=ALU.add, op1=ALU.mult))
            o("dve", lambda e: e.tensor_scalar(out=col(CC), in0=col(CC), scalar1=1.0, scalar2=None, op0=ALU.add))
            for _ in range(5):
                o("dve", lambda e: e.tensor_tensor(out=col(TC), in0=col(CC), in1=col(SS), op=ALU.mult))
                o("dve", lambda e: e.tensor_tensor(out=col(TA), in0=col(CC), in1=col(CC), op=ALU.mult))
                o("dve", lambda e: e.tensor_tensor(out=col(TD), in0=col(SS), in1=col(SS), op=ALU.mult))
                o("dve", lambda e: e.tensor_tensor(out=col(CC), in0=col(TA), in1=col(TD), op=ALU.subtract))
                o("dve", lambda e: e.tensor_scalar(out=col(SS), in0=col(TC), scalar1=2.0, scalar2=None, op0=ALU.mult))
            o("dve", lambda e: e.tensor_tensor(out=col(AR), in0=col(RR), in1=col(CC), op=ALU.mult))
            o("dve", lambda e: e.tensor_tensor(out=col(AI), in0=col(RR), in1=col(SS), op=ALU.mult))
            o("dve", lambda e: e.tensor_scalar(out=col(TA), in0=col(AR), scalar1=-1.0, scalar2=None, op0=ALU.add))
            o("dve", lambda e: e.tensor_tensor(out=col(TB), in0=col(LR), in1=col(LR), op=ALU.mult))
            o("dve", lambda e: e.tensor_tensor(out=col(TC), in0=col(LI), in1=col(LI), op=ALU.mult))
            o("dve", lambda e: e.tensor_tensor(out=col(TB), in0=col(TB), in1=col(TC), op=ALU.add))
            o("dve", lambda e: e.reciprocal(col(TB), col(TB)))
            o("dve", lambda e: e.tensor_tensor(out=col(TC), in0=col(TA), in1=col(LR), op=ALU.mult))
            o("dve", lambda e: e.tensor_tensor(out=col(TD), in0=col(AI), in1=col(LI), op=ALU.mult))
            o("dve", lambda e: e.tensor_tensor(out=col(TC), in0=col(TC), in1=col(TD), op=ALU.add))
            o("dve", lambda e: e.tensor_tensor(out=col(QR), in0=col(TC), in1=col(TB), op=ALU.mult))
            o("dve", lambda e: e.tensor_tensor(out=col(TC), in0=col(AI), in1=col(LR), op=ALU.mult))
            o("dve", lambda e: e.tensor_tensor(out=col(TD), in0=col(TA), in1=col(LI), op=ALU.mult))
            o("dve", lambda e: e.tensor_tensor(out=col(TC), in0=col(TC), in1=col(TD), op=ALU.subtract))
            o("dve", lambda e: e.tensor_tensor(out=col(QI), in0=col(TC), in1=col(TB), op=ALU.mult))
            t = lambda eng, fn: S.op(eng, fn, r=[b_s5c], w=[b_tab])
            t("dve", lambda e: e.memset(Er[:, :, 0:1], 1.0))
            t("dve", lambda e: e.memset(Ei[:, :, 0:1], 0.0))
            o("dve", lambda e: e.tensor_copy(out=col(CM), in_=col(CC)))
            o("dve", lambda e: e.tensor_copy(out=col(SM), in_=col(SS)))
            m = 1
            while m < LS:
                cm_b = s5c[:, CM, :].unsqueeze(2).broadcast_to([128, 8, m]); sm_b = s5c[:, SM, :].unsqueeze(2).broadcast_to([128, 8, m])
                t("dve", lambda e, m=m, cm_b=cm_b: e.tensor_tensor(out=Er[:, :, m:2 * m], in0=Er[:, :, 0:m], in1=cm_b, op=ALU.mult))
                t("dve", lambda e, m=m, sm_b=sm_b: e.tensor_tensor(out=T1r[:, :, 0:m], in0=Ei[:, :, 0:m], in1=sm_b, op=ALU.mult))
                t("dve", lambda e, m=m: e.tensor_tensor(out=Er[:, :, m:2 * m], in0=Er[:, :, m:2 * m], in1=T1r[:, :, 0:m], op=ALU.subtract))
                t("dve", lambda e, m=m, cm_b=cm_b: e.tensor_tensor(out=Ei[:, :, m:2 * m], in0=Ei[:, :, 0:m], in1=cm_b, op=ALU.mult))
                t("dve", lambda e, m=m, sm_b=sm_b: e.tensor_tensor(out=T1r[:, :, 0:m], in0=Er[:, :, 0:m], in1=sm_b, op=ALU.mult))
                t("dve", lambda e, m=m: e.tensor_tensor(out=Ei[:, :, m:2 * m], in0=Ei[:, :, m:2 * m], in1=T1r[:, :, 0:m], op=ALU.add))
                o("dve", lambda e: e.tensor_tensor(out=col(TC), in0=col(CM), in1=col(SM), op=ALU.mult))
                o("dve", lambda e: e.tensor_tensor(out=col(TA), in0=col(CM), in1=col(CM), op=ALU.mult))
                o("dve", lambda e: e.tensor_tensor(out=col(TD), in0=col(SM), in1=col(SM), op=ALU.mult))
                o("dve", lambda e: e.tensor_tensor(out=col(CM), in0=col(TA), in1=col(TD), op=ALU.subtract))
                o("dve", lambda e: e.tensor_scalar(out=col(SM), in0=col(TC), scalar1=2.0, scalar2=None, op0=ALU.mult))
                m *= 2
            qr_b = s5c[:, QR, :].unsqueeze(2).broadcast_to([128, 8, LS]); qi_b = s5c[:, QI, :].unsqueeze(2).broadcast_to([128, 8, LS])
            rr_b = s5c[:, RR, :].unsqueeze(2).broadcast_to([128, 8, LS])
            t("dve", lambda e: e.tensor_tensor(out=T1r[:], in0=Er[:], in1=qr_b, op=ALU.mult))
            t("dve", lambda e: e.tensor_tensor(out=R0[:], in0=Ei[:], in1=qi_b, op=ALU.mult))
            t("dve", lambda e: e.tensor_tensor(out=T1r[:], in0=T1r[:], in1=R0[:], op=ALU.add))
            t("dve", lambda e: e.tensor_tensor(out=T1i[:], in0=Er[:], in1=qi_b, op=ALU.mult))
            t("dve", lambda e: e.tensor_tensor(out=R0[:], in0=Ei[:], in1=qr_b, op=ALU.mult))
            t("dve", lambda e: e.tensor_tensor(out=T1i[:], in0=T1i[:], in1=R0[:], op=ALU.subtract))
            t("dve", lambda e: e.tensor_tensor(out=R0[:], in0=onesf[:, :].unsqueeze(1).broadcast_to([128, 8, LS]), in1=rr_b, op=ALU.mult))
            t("dve", lambda e: e.memset(R0[:, :, 0:1], 0.0))
            S.op("pool", lambda e: e.memset(Bm[:], 0.0), w=[b_BC])
            S.op("pool", lambda e: e.memset(Cm[:], 0.0), w=[b_BC])
            for g in range(16):
                j, g2 = g // 2, g % 2
                gl = g % 8
                for ri, nm in enumerate(["s5_b_re", "s5_b_im"]):
                    d("pool", b_BC, lambda e, g=g, j=j, g2=g2, gl=gl, ri=ri, nm=nm: e.dma_start(
                        out=Bm[gl * 16:(gl + 1) * 16, j, ri, g2 * 64:(g2 + 1) * 64],
                        in_=prm[nm][l, g].rearrange("p h -> h p"), allow_slow_non_contiguous=True), w=[b_BC])
                for ri, nm in enumerate(["s5_c_re", "s5_c_im"]):
                    c0 = (j % 4) * 32 + g2 * 16
                    d("pool", b_BC, lambda e, g=g, j=j, g2=g2, ri=ri, nm=nm, c0=c0: e.dma_start(
                        out=Cm[g2 * 64:(g2 + 1) * 64, j, ri, c0:c0 + 16],
                        in_=prm[nm][l, g].rearrange("h p -> p h"), allow_slow_non_contiguous=True), w=[b_BC])
            d("pool", b_wglu, lambda e: e.dma_start(out=wglu[:], in_=prm["s5_w_glu"][l].rearrange("(k p) m -> p k m", p=128)), w=[b_wglu])
            if l == 0:
                dbg_dump("s5c", s5c[:].rearrange("p a b -> p (a b)"), 192, [b_s5c])
                dbg_dump("Er", Er[:].rearrange("p a b -> p (a b)"), 1024, [b_tab])
                dbg_dump("Ei", Ei[:].rearrange("p a b -> p (a b)"), 1024, [b_tab])
                dbg_dump("T1r", T1r[:].rearrange("p a b -> p (a b)"), 1024, [b_tab])
                dbg_dump("R0", R0[:].rearrange("p a b -> p (a b)"), 1024, [b_tab])

        zr = big16[:, 0:1024].rearrange("p (a b) -> p a b", a=8); zi = big16[:, 1024:2048].rearrange("p (a b) -> p a b", a=8); b_z = b_big[0:8]
        wr_ = big16[:, 2048:3072].rearrange("p (a b) -> p a b", a=8); wi_ = big16[:, 3072:4096].rearrange("p (a b) -> p a b", a=8); b_w = b_big[8:16]
        pp = catT[:, :, :].rearrange("p j (q n) -> p j q n", q=4); b_pp = b_catT
        yA = sb("yA", [128, 2, NB]); b_yA = Buf("yA")
        yAb = sb("yAb", [128, 2, NB], BF16); b_yAb = Buf("yAb")
        gtmp = sb("gtmp", [128, 2, NB]); b_gtmp = Buf("gtmp")

        def s5_block(l, N, nseq, Tq, first):
            nsub = Tq // LS if Tq >= LS else 1
            Ls = min(LS, Tq)
            psy, b_psy = PS[7], PSB[7]
            for sc in range(nsub):
                c0 = sc * Ls
                for half in range(2):
                    jj = 4 * half
                    for ri, bank in enumerate([5, 6]):
                        for j in range(jj, jj + 4):
                            S.op("pe", lambda e, j=j, ri=ri, bank=bank, c0=c0: e.matmul(
                                PS[bank][:, (j % 4) * 128:(j % 4) * 128 + Ls],
                                Bm[:, j, ri, :], uTb[:, j // 4, c0:c0 + Ls], start=True, stop=True),
                                r=[b_BC, b_uTb], w=[PSB[bank]])
                    pv5 = PS[5][:, :].rearrange("p (a b) -> p a b", a=4)[:, :, 0:Ls]
                    pv6 = PS[6][:, :].rearrange("p (a b) -> p a b", a=4)[:, :, 0:Ls]
                    S.op("dve", lambda e, jj=jj, pv5=pv5: e.tensor_tensor(out=zr[:, jj:jj + 4, 0:Ls], in0=T1r[:, jj:jj + 4, 0:Ls], in1=pv5, op=ALU.mult), r=[b_tab, PSB[5]], w=[b_z])
                    S.op("dve", lambda e, jj=jj, pv6=pv6: e.tensor_tensor(out=wr_[:, jj:jj + 4, 0:Ls], in0=T1i[:, jj:jj + 4, 0:Ls], in1=pv6, op=ALU.mult), r=[b_tab, PSB[6]], w=[b_w])
                    S.op("dve", lambda e, jj=jj, pv6=pv6: e.tensor_tensor(out=zi[:, jj:jj + 4, 0:Ls], in0=T1r[:, jj:jj + 4, 0:Ls], in1=pv6, op=ALU.mult), r=[b_tab, PSB[6]], w=[b_z])
                    S.op("dve", lambda e, jj=jj, pv5=pv5: e.tensor_tensor(out=wi_[:, jj:jj + 4, 0:Ls], in0=T1i[:, jj:jj + 4, 0:Ls], in1=pv5, op=ALU.mult), r=[b_tab, PSB[5]], w=[b_w])
                S.op("pool", lambda e: e.tensor_tensor(out=zr[:, :, 0:Ls], in0=zr[:, :, 0:Ls], in1=wr_[:, :, 0:Ls], op=ALU.subtract), r=[b_w], w=[b_z])
                S.op("pool", lambda e: e.tensor_tensor(out=zi[:, :, 0:Ls], in0=zi[:, :, 0:Ls], in1=wi_[:, :, 0:Ls], op=ALU.add), r=[b_w], w=[b_z])
                if not (first and sc == 0):
                    S.op("dve", lambda e: e.tensor_tensor(out=zr[:, :, 0:1], in0=zr[:, :, 0:1], in1=Wr[:, :, 0:1], op=ALU.add), r=[b_X], w=[b_z])
                    S.op("dve", lambda e: e.tensor_tensor(out=zi[:, :, 0:1], in0=zi[:, :, 0:1], in1=Wi[:, :, 0:1], op=ALU.add), r=[b_X], w=[b_z])
                S.op("dve", lambda e: e.tensor_tensor_scan(out=wr_[:, :, 0:Ls].rearrange("p a b -> p (a b)") if Ls == LS else wr_[:, :, 0:Ls],
                                                           data0=R0[:, :, :].rearrange("p a b -> p (a b)"), data1=zr[:, :, :].rearrange("p a b -> p (a b)"),
                                                           initial=0.0, op0=ALU.mult, op1=ALU.add), r=[b_tab, b_z], w=[b_w])
                S.op("dve", lambda e: e.tensor_tensor_scan(out=wi_[:, :, :].rearrange("p a b -> p (a b)"),
                                                           data0=R0[:, :, :].rearrange("p a b -> p (a b)"), data1=zi[:, :, :].rearrange("p a b -> p (a b)"),
                                                           initial=0.0, op0=ALU.mult, op1=ALU.add), r=[b_tab, b_z], w=[b_w])
                S.op("dve", lambda e: e.tensor_tensor(out=pp[:, :, 0, :], in0=Er[:], in1=wr_[:], op=ALU.mult), r=[b_tab, b_w], w=[b_pp])
                S.op("dve", lambda e: e.scalar_tensor_tensor(out=pp[:, :, 1, :], in0=Ei[:], scalar=-1.0, in1=wi_[:], op0=ALU.mult, op1=ALU.mult), r=[b_tab, b_w], w=[b_pp])
                S.op("dve", lambda e: e.scalar_tensor_tensor(out=pp[:, :, 2, :], in0=Er[:], scalar=-1.0, in1=wi_[:], op0=ALU.mult, op1=ALU.mult), r=[b_tab, b_w], w=[b_pp])
                S.op("dve", lambda e: e.scalar_tensor_tensor(out=pp[:, :, 3, :], in0=Ei[:], scalar=-1.0, in1=wr_[:], op0=ALU.mult, op1=ALU.mult), r=[b_tab, b_w], w=[b_pp])
                la = Ls - 1
                xo = lambda eng, fn: S.op(eng, fn, r=[b_tab, b_w, b_s5c, b_s5t], w=[b_s5t])
                xo("dve", lambda e: e.tensor_tensor(out=s5t[:, :, 0, 0:1], in0=Er[:, :, la:la + 1], in1=wr_[:, :, la:la + 1], op=ALU.mult))
                xo("dve", lambda e: e.tensor_tensor(out=s5t[:, :, 1, 0:1], in0=Ei[:, :, la:la + 1], in1=wi_[:, :, la:la + 1], op=ALU.mult))
                xo("dve", lambda e: e.tensor_tensor(out=s5t[:, :, 2, 0:1], in0=Er[:, :, la:la + 1], in1=wi_[:, :, la:la + 1], op=ALU.mult))
                xo("dve", lambda e: e.tensor_tensor(out=s5t[:, :, 3, 0:1], in0=Ei[:, :, la:la + 1], in1=wr_[:, :, la:la + 1], op=ALU.mult))
                xx = lambda eng, fn: S.op(eng, fn, r=[b_s5t, b_s5c], w=[b_X])
                xx("dve", lambda e: e.tensor_tensor(out=Xr[:, :, 0:1], in0=s5t[:, :, 0, 0:1], in1=s5t[:, :, 1, 0:1], op=ALU.subtract))
                xx("dve", lambda e: e.tensor_tensor(out=Xi[:, :, 0:1], in0=s5t[:, :, 2, 0:1], in1=s5t[:, :, 3, 0:1], op=ALU.add))
                arc = s5c[:, AR, :].unsqueeze(2); aic = s5c[:, AI, :].unsqueeze(2)
                xo("dve", lambda e: e.tensor_tensor(out=s5t[:, :, 0, 1:2], in0=Xr[:, :, 0:1], in1=arc, op=ALU.mult))
                xo("dve", lambda e: e.tensor_tensor(out=s5t[:, :, 1, 1:2], in0=Xi[:, :, 0:1], in1=aic, op=ALU.mult))
                xo("dve", lambda e: e.tensor_tensor(out=s5t[:, :, 2, 1:2], in0=Xi[:, :, 0:1], in1=arc, op=ALU.mult))
                xo("dve", lambda e: e.tensor_tensor(out=s5t[:, :, 3, 1:2], in0=Xr[:, :, 0:1], in1=aic, op=ALU.mult))
                xx("dve", lambda e: e.tensor_tensor(out=Wr[:, :, 0:1], in0=s5t[:, :, 0, 1:2], in1=s5t[:, :, 1, 1:2], op=ALU.subtract))
                xx("dve", lambda e: e.tensor_tensor(out=Wi[:, :, 0:1], in0=s5t[:, :, 2, 1:2], in1=s5t[:, :, 3, 1:2], op=ALU.add))
                for hh in range(2):
                    n = 0
                    for j in range(4 * hh, 4 * hh + 4):
                        for pi, ci in [(0, 0), (1, 0), (2, 1), (3, 1)]:
                            S.op("pe", lambda e, j=j, pi=pi, ci=ci, n=n, hh=hh: e.matmul(
                                psy[:, hh * 128:hh * 128 + Ls], Cm[:, j, ci, :], pp[:, j, pi, 0:Ls], start=(n == 0), stop=(n == 15)),
                                r=[b_BC, b_pp], w=[b_psy])
                            n += 1
                    S.op("dve", lambda e, hh=hh, c0=c0: e.scalar_tensor_tensor(
                        out=yA[:, hh, c0:c0 + Ls], in0=uTb[:, hh, c0:c0 + Ls], scalar=s5dcol[:, l, hh:hh + 1], in1=psy[:, hh * 128:hh * 128 + Ls],
                        op0=ALU.mult, op1=ALU.add), r=[b_uTb, b_psy, b_convp], w=[b_yA])
            s5_tail(l, N)

        def s5_tail(l, N):
            S.op("act", lambda e: e.activation(out=gtmp[:, :, 0:N], in_=yA[:, :, 0:N], func=AF.Square), r=[b_yA], w=[b_gtmp])
            S.op("dve", lambda e: e.tensor_scalar(out=gtmp[:, :, 0:N], in0=gtmp[:, :, 0:N], scalar1=0.044715, scalar2=1.0, op0=ALU.mult, op1=ALU.add), r=[b_gtmp], w=[b_gtmp])
            S.op("dve", lambda e: e.tensor_tensor(out=gtmp[:, :, 0:N], in0=gtmp[:, :, 0:N], in1=yA[:, :, 0:N], op=ALU.mult), r=[b_gtmp, b_yA], w=[b_gtmp])
            S.op("act", lambda e: e.activation(out=gtmp[:, :, 0:N], in_=gtmp[:, :, 0:N], func=AF.Sigmoid, scale=1.5957691216057308), r=[b_gtmp], w=[b_gtmp])
            S.op("dve", lambda e: e.tensor_tensor(out=yA[:, :, 0:N], in0=yA[:, :, 0:N], in1=gtmp[:, :, 0:N], op=ALU.mult), r=[b_gtmp], w=[b_yA])
            S.op("pool", lambda e: e.tensor_copy(out=yAb[:, :, 0:N], in_=yA[:, :, 0:N]), r=[b_yA], w=[b_yAb])
            for m in range(2):
                ps, b_ps = mmbank()
                for k in range(2):
                    S.op("pe", lambda e, m=m, k=k, ps=ps: e.matmul(ps[:, 0:N], wglu[:, k, m * 128:(m + 1) * 128], yAb[:, k, 0:N], start=(k == 0), stop=(k == 1)),
                         r=[b_wglu, b_yAb], w=[b_ps])
                S.op("act", lambda e, m=m, ps=ps: e.activation(out=gtmp[:, m, 0:N], in_=ps[:, 0:N], func=AF.Sigmoid), r=[b_ps], w=[b_gtmp])
                S.op("dve", lambda e, m=m: e.tensor_tensor(out=mixT[:, m, 0:N], in0=yA[:, m, 0:N], in1=gtmp[:, m, 0:N], op=ALU.mult), r=[b_yA, b_gtmp], w=[b_mixT[m]])

        if cfg.sample:
            xTs = sb("xTs", [128, 8, 16]); b_xTs = [Buf("xTs%d" % k) for k in range(8)]
            xbcS = sb("xbcS", [128, 6, 4, 7]); b_xbcS = Buf("xbcS")
            cshS = sb("cshS", [128, 2, 4, 6]); b_cshS = Buf("cshS")
            Vown = sb("Vown", [16, 4, 128], BF16); b_Vown = Buf("Vown")
            S.op("pool", lambda e: e.memset(Vown[:], 1.0), w=[b_Vown])
            ncown = sb("ncown", [16, 4]); b_ncown = Buf("ncown")
            triS = sb("triS", [16, 16]); b_triS = Buf("triS")
            maskS = sb("maskS", [128, 16], BF16); b_maskS = Buf("maskS")
            pat4 = sb("pat4", [4, 16]); b_pat4 = Buf("pat4")
            iotaf = sb("iotaf", [128, 1]); b_iotaf = Buf("iotaf")
            S.dma("pool", b_triS, lambda e: e.dma_start(out=triS[:], in_=cst["triS"]), w=[b_triS])
            S.dma("pool", b_maskS, lambda e: e.dma_start(out=maskS[:], in_=cst["maskS"]), w=[b_maskS])
            S.dma("pool", b_pat4, lambda e: e.dma_start(out=pat4[:], in_=cst["pat4"]), w=[b_pat4])
            S.dma("pool", b_iotaf, lambda e: e.dma_start(out=iotaf[:], in_=cst["iota"]), w=[b_iotaf])
            NPGT = NS * NPG
            ptb = sb("ptb", [128, NPGT], I32); idx_all = sb("idx_all", [128, NPGT], I32); b_idx = Buf("idx")
            for sq_ in range(NS):
                S.dma("pool", b_idx, lambda e, sq_=sq_: e.dma_start(out=ptb[:, sq_ * NPG:(sq_ + 1) * NPG], in_=page_table[sq_:sq_ + 1, :].partition_broadcast(128)), w=[b_idx])
            S.op("dve", lambda e: e.tensor_scalar(out=idx_all[:], in0=ptb[:], scalar1=128.0, scalar2=iotaf[:, 0:1], op0=ALU.mult, op1=ALU.add), r=[b_idx, b_iotaf], w=[b_idx])
            Kpg = [sb("Kpg%d" % i, [128, 256]) for i in range(2)]; b_Kpg = [Buf("Kpg%d" % i) for i in range(2)]
            Vpg = [sb("Vpg%d" % i, [128, 256]) for i in range(2)]; b_Vpg = [Buf("Vpg%d" % i) for i in range(2)]
            KTp = [sb("KTp%d" % i, [128, 2, 128], BF16) for i in range(2)]; b_KTp = [Buf("KTp%d" % i) for i in range(2)]
            Vpp = [sb("Vpp%d" % i, [128, 4, 128], BF16) for i in range(2)]; b_Vpp = [Buf("Vpp%d" % i) for i in range(2)]
            for i in range(2):
                S.op("pool", lambda e, i=i: e.memset(Vpp[i][:], 1.0), w=[b_Vpp[i]])
            PTs = [sb("PTs%d" % i, [128, 4, 4], BF16) for i in range(2)]; b_PTs = [Buf("PTs%d" % i) for i in range(2)]
            PTo = sb("PTo", [16, 4, 16], BF16); b_PTo = Buf("PTo")
            lfp = sb("lfp", [128, NPG, 4]); b_lfp = Buf("lfp")
            biasp = sb("biasp", [128, NPG, 4]); b_biasp = Buf("biasp")
            cumtmp = lfp[:, :, :].rearrange("p g h -> p (g h)").rearrange("p (h g) -> p h g", h=4); b_cumtmp = b_lfp
            R0S = sb("R0S", [128, 128]); b_R0S = Buf("R0S")
            hst = sb("hst", [128, 128]); b_hst = Buf("hst")
            pg_rr = [0]

        def load_xTs():
            i = ost_rr[0] % 2; ost_rr[0] += 1
            S.dma("sp", b_ostage[i], lambda e, i=i: e.dma_start(out=ostage[i][0:16, :], in_=xs[:, :]), w=[b_ostage[i]], r=[b_ostage_st[i]])
            for k in range(8):
                ps, b_ps = mmbank()
                S.op("pe", lambda e, k=k, ps=ps, i=i: e.transpose(ps[:, 0:16], ostage[i][0:16, k * 128:(k + 1) * 128], ident[0:16, 0:16]), r=[b_ostage[i], b_ident], w=[b_ps])
                S.op("dve", lambda e, k=k, ps=ps: e.tensor_copy(out=xTs[:, k, 0:16], in_=ps[:, 0:16]), r=[b_ps], w=[b_xTs[k]])

        def s5_block_sample(l):
            v4 = lambda ap: ap.rearrange("p (j s t) -> p j s t", j=8, s=4)
            zrS, ziS, wrS, wiS = v4(big16[:, 0:128]), v4(big16[:, 1024:1152]), v4(big16[:, 2048:2176]), v4(big16[:, 3072:3200])
            psy, b_psy = PS[7], PSB[7]
            for sq_ in range(NS):
                S.dma("pool", b_s5o, lambda e, sq_=sq_: e.dma_start(out=s5o[:], in_=st_s5[l, sq_].rearrange("(j g) p r -> (g p) j r", g=2)), r=[b_s5o_st], w=[b_s5o])
                S.op("dve", lambda e, sq_=sq_: e.tensor_copy(out=Xr[:, :, sq_:sq_ + 1], in_=s5o[:, :, 0:1]), r=[b_s5o], w=[b_X])
                S.op("dve", lambda e, sq_=sq_: e.tensor_copy(out=Xi[:, :, sq_:sq_ + 1], in_=s5o[:, :, 1:2]), r=[b_s5o], w=[b_X])
            arc = s5c[:, AR, :].unsqueeze(2).broadcast_to([128, 8, 4]); aic = s5c[:, AI, :].unsqueeze(2).broadcast_to([128, 8, 4])
            xo = lambda fn: S.op("dve", fn, r=[b_X, b_s5c, b_s5t], w=[b_s5t])
            xx = lambda fn: S.op("dve", fn, r=[b_s5t], w=[b_X])

            def carry():
                xo(lambda e: e.tensor_tensor(out=s5t[:, :, 0, :], in0=Xr[:], in1=arc, op=ALU.mult))
                xo(lambda e: e.tensor_tensor(out=s5t[:, :, 1, :], in0=Xi[:], in1=aic, op=ALU.mult))
                xo(lambda e: e.tensor_tensor(out=s5t[:, :, 2, :], in0=Xi[:], in1=arc, op=ALU.mult))
                xo(lambda e: e.tensor_tensor(out=s5t[:, :, 3, :], in0=Xr[:], in1=aic, op=ALU.mult))
                xx(lambda e: e.tensor_tensor(out=Wr[:], in0=s5t[:, :, 0, :], in1=s5t[:, :, 1, :], op=ALU.subtract))
                xx(lambda e: e.tensor_tensor(out=Wi[:], in0=s5t[:, :, 2, :], in1=s5t[:, :, 3, :], op=ALU.add))
            carry()
            S.op("dve", lambda e: e.tensor_copy(out=v4(R0S[:, :]), in_=R0[:, :, 0:4].unsqueeze(2).broadcast_to([128, 8, 4, 4])), r=[b_tab], w=[b_R0S])
            for half in range(2):
                jj = 4 * half
                for ri, bank in enumerate([5, 6]):
                    for j in range(jj, jj + 4):
                        S.op("pe", lambda e, j=j, ri=ri, bank=bank: e.matmul(PS[bank][:, (j % 4) * 128:(j % 4) * 128 + 16], Bm[:, j, ri, :], uTb[:, j // 4, 0:16], start=True, stop=True),
                             r=[b_BC, b_uTb], w=[PSB[bank]])
                pv5 = PS[5][:, :].rearrange("p (a b) -> p a b", a=4)[:, :, 0:16].rearrange("p a (s t) -> p a s t", s=4)
                pv6 = PS[6][:, :].rearrange("p (a b) -> p a b", a=4)[:, :, 0:16].rearrange("p a (s t) -> p a s t", s=4)
                t1rb = T1r[:, jj:jj + 4, 0:4].unsqueeze(2).broadcast_to([128, 4, 4, 4])
                t1ib = T1i[:, jj:jj + 4, 0:4].unsqueeze(2).broadcast_to([128, 4, 4, 4])
                S.op("dve", lambda e, jj=jj, a_=t1rb, b_=pv5: e.tensor_tensor(out=zrS[:, jj:jj + 4], in0=a_, in1=b_, op=ALU.mult), r=[b_tab, PSB[5]], w=[b_z])
                S.op("dve", lambda e, jj=jj, a_=t1ib, b_=pv6: e.tensor_tensor(out=wrS[:, jj:jj + 4], in0=a_, in1=b_, op=ALU.mult), r=[b_tab, PSB[6]], w=[b_w])
                S.op("dve", lambda e, jj=jj, a_=t1rb, b_=pv6: e.tensor_tensor(out=ziS[:, jj:jj + 4], in0=a_, in1=b_, op=ALU.mult), r=[b_tab, PSB[6]], w=[b_z])
                S.op("dve", lambda e, jj=jj, a_=t1ib, b_=pv5: e.tensor_tensor(out=wiS[:, jj:jj + 4], in0=a_, in1=b_, op=ALU.mult), r=[b_tab, PSB[5]], w=[b_w])
            S.op("dve", lambda e: e.tensor_tensor(out=zrS, in0=zrS, in1=wrS, op=ALU.subtract), r=[b_w], w=[b_z])
            S.op("dve", lambda e: e.tensor_tensor(out=ziS, in0=ziS, in1=wiS, op=ALU.add), r=[b_w], w=[b_z])
            S.op("dve", lambda e: e.tensor_tensor(out=zrS[:, :, :, 0], in0=zrS[:, :, :, 0], in1=Wr[:], op=ALU.add), r=[b_X], w=[b_z])
            S.op("dve", lambda e: e.tensor_tensor(out=ziS[:, :, :, 0], in0=ziS[:, :, :, 0], in1=Wi[:], op=ALU.add), r=[b_X], w=[b_z])
            S.op("dve", lambda e: e.tensor_tensor_scan(out=big16[:, 2048:2176], data0=R0S[:, :], data1=big16[:, 0:128], initial=0.0, op0=ALU.mult, op1=ALU.add), r=[b_R0S, b_z], w=[b_w])
            S.op("dve", lambda e: e.tensor_tensor_scan(out=big16[:, 3072:3200], data0=R0S[:, :], data1=big16[:, 1024:1152], initial=0.0, op0=ALU.mult, op1=ALU.add), r=[b_R0S, b_z], w=[b_w])
            eb = lambda T_: T_[:, :, 0:4].unsqueeze(2).broadcast_to([128, 8, 4, 4])
            ppv = lambda q: pp[:, :, q, 0:16].rearrange("p j (s t) -> p j s t", s=4)
            S.op("dve", lambda e: e.tensor_tensor(out=ppv(0), in0=eb(Er), in1=wrS, op=ALU.mult), r=[b_tab, b_w], w=[b_pp])
            S.op("dve", lambda e: e.tensor_tensor(out=ppv(1), in0=eb(Ei), in1=wiS, op=ALU.mult), r=[b_tab, b_w], w=[b_pp])
            S.op("dve", lambda e: e.tensor_tensor(out=ppv(2), in0=eb(Er), in1=wiS, op=ALU.mult), r=[b_tab, b_w], w=[b_pp])
            S.op("dve", lambda e: e.tensor_tensor(out=ppv(3), in0=eb(Ei), in1=wrS, op=ALU.mult), r=[b_tab, b_w], w=[b_pp])
            for q_ in (1, 2, 3):
                S.op("dve", lambda e, q_=q_: e.tensor_scalar(out=pp[:, :, q_, 0:16], in0=pp[:, :, q_, 0:16], scalar1=-1.0, scalar2=None, op0=ALU.mult), r=[b_pp], w=[b_pp])
            e3r = Er[:, :, 3:4].broadcast_to([128, 8, 4]); e3i = Ei[:, :, 3:4].broadcast_to([128, 8, 4])
            xo2 = lambda fn: S.op("dve", fn, r=[b_tab, b_w, b_s5t], w=[b_s5t])
            xo2(lambda e: e.tensor_tensor(out=s5t[:, :, 0, :], in0=wrS[:, :, :, 3], in1=e3r, op=ALU.mult))
            xo2(lambda e: e.tensor_tensor(out=s5t[:, :, 1, :], in0=wiS[:, :, :, 3], in1=e3i, op=ALU.mult))
            xo2(lambda e: e.tensor_tensor(out=s5t[:, :, 2, :], in0=wiS[:, :, :, 3], in1=e3r, op=ALU.mult))
            xo2(lambda e: e.tensor_tensor(out=s5t[:, :, 3, :], in0=wrS[:, :, :, 3], in1=e3i, op=ALU.mult))
            xx(lambda e: e.tensor_tensor(out=Xr[:], in0=s5t[:, :, 0, :], in1=s5t[:, :, 1, :], op=ALU.subtract))
            xx(lambda e: e.tensor_tensor(out=Xi[:], in0=s5t[:, :, 2, :], in1=s5t[:, :, 3, :], op=ALU.add))
            for hh in range(2):
                n = 0
                for j in range(4 * hh, 4 * hh + 4):
                    for pi, ci in [(0, 0), (1, 0), (2, 1), (3, 1)]:
                        S.op("pe", lambda e, j=j, pi=pi, ci=ci, n=n, hh=hh: e.matmul(psy[:, hh * 128:hh * 128 + 16], Cm[:, j, ci, :], pp[:, j, pi, 0:16], start=(n == 0), stop=(n == 15)),
                             r=[b_BC, b_pp], w=[b_psy])
                        n += 1
                S.op("dve", lambda e, hh=hh: e.scalar_tensor_tensor(out=yA[:, hh, 0:16], in0=uTb[:, hh, 0:16], scalar=s5dcol[:, l, hh:hh + 1], in1=psy[:, hh * 128:hh * 128 + 16],
                                                                   op0=ALU.mult, op1=ALU.add), r=[b_uTb, b_psy, b_convp], w=[b_yA])
            s5_tail(l, 16)
            if l == 0:
                dbg_dump("XrS", Xr[:].rearrange("p a b -> p (a b)"), 32, [b_X])
                dbg_dump("XiS", Xi[:].rearrange("p a b -> p (a b)"), 32, [b_X])
                dbg_dump("WrS", Wr[:].rearrange("p a b -> p (a b)"), 32, [b_X])
                dbg_dump("wrS", big16[:, 2048:2176], 128, [b_w])
                dbg_dump("zrS", big16[:, 0:128], 128, [b_z])
            for sq_ in range(NS):
                S.op("dve", lambda e, sq_=sq_: e.tensor_copy(out=s5o[:, :, 0:1], in_=Xr[:, :, sq_:sq_ + 1]), r=[b_X, b_s5o_st], w=[b_s5o])
                S.op("dve", lambda e, sq_=sq_: e.tensor_copy(out=s5o[:, :, 1:2], in_=Xi[:, :, sq_:sq_ + 1]), r=[b_X], w=[b_s5o])
                finals.append(S.dma("pool", b_s5o_st, lambda e, sq_=sq_: e.dma_start(out=s5_s[l, sq_].rearrange("(j g) p r -> (g p) j r", g=2), in_=s5o[:]), r=[b_s5o], w=[b_s5o_st]))

        def fox_sample(l):
            ck, cv, cl = cache_k, cache_v, cache_logf
            eo = l * cfg.npool * 128
            IOA = bass.IndirectOffsetOnAxis
            S.op("dve", lambda e: e.tensor_tensor_scan(out=cumT[:, 0:16], data0=pat4[:, 0:16], data1=logfT[:, 0:16], initial=0.0, op0=ALU.mult, op1=ALU.add), r=[b_pat4, b_logfT], w=[b_cumT])
            S.op("dve", lambda e: e.tensor_copy(out=augT[0:4, 0:16], in_=cumT[:, 0:16]), r=[b_cumT], w=[b_augT])
            S.op("dve", lambda e: e.tensor_tensor(out=logfT[:, 0:16], in0=cumT[:, 0:16], in1=augT[0:4, 0:16], op=ALU.subtract), r=[b_cumT, b_augT], w=[b_logfT])
            S.op("dve", lambda e: e.tensor_copy(out=augT[32:36, 0:16], in_=logfT[:, 0:16]), r=[b_logfT], w=[b_augT])
            S.op("dve", lambda e: e.memset(PS[5][:, 0:64], 0.0), w=[PSB[5]])
            for sq_ in range(NS):
                for pg in range(NPG):
                    col = sq_ * NPG + pg
                    S.dma("pool", b_lfp, lambda e, pg=pg, col=col: e.indirect_dma_start(out=lfp[:, pg, :], out_offset=None, in_=cl, in_offset=IOA(ap=idx_all[:, col:col + 1], axis=0), element_offset=eo * 4),
                          r=[b_idx], w=[b_lfp] if pg in (0, NPG - 1) else [])
                lff = lfp[:, :, :].rearrange("p g h -> p (g h)")
                S.op("pe", lambda e: e.matmul(PS[7][:, 0:NPG * 4], tristrict[:, :], lff, start=True, stop=True), r=[b_tristrict, b_lfp], w=[PSB[7]])
                S.op("pe", lambda e: e.matmul(PS[7][:, 256:256 + NPG * 4], onesf[:, :], lff, start=True, stop=True), r=[b_onesf, b_lfp], w=[PSB[7]])
                totv = PS[7][:, 256:256 + NPG * 4].rearrange("p (g h) -> p g h", h=4)
                w1v = PS[7][:, 0:NPG * 4].rearrange("p (g h) -> p g h", h=4)
                for h in range(4):
                    S.op("dve", lambda e, h=h: e.tensor_tensor_scan(out=cumtmp[:, h, :], data0=onesf[:, 0:NPG] if NPG <= 128 else None, data1=totv[:, :, h], initial=0.0, op0=ALU.mult, op1=ALU.add),
                         r=[PSB[7], b_onesf], w=[b_cumtmp])
                    S.op("dve", lambda e, h=h: e.tensor_scalar(out=cumtmp[:, h, :], in0=cumtmp[:, h, :], scalar1=-1.0, scalar2=cumtmp[:, h, NPG - 1:NPG], op0=ALU.mult, op1=ALU.add), r=[b_cumtmp], w=[b_cumtmp])
                    S.op("dve", lambda e, h=h: e.tensor_tensor(out=biasp[:, :, h], in0=cumtmp[:, h, :], in1=w1v[:, :, h], op=ALU.add), r=[b_cumtmp, PSB[7]], w=[b_biasp])
                for pg in range(NPG):
                    col = sq_ * NPG + pg
                    i = pg_rr[0] % 2; pg_rr[0] += 1
                    S.dma("pool", b_Kpg[i], lambda e, i=i, col=col: e.indirect_dma_start(out=Kpg[i][:, :], out_offset=None, in_=ck, in_offset=IOA(ap=idx_all[:, col:col + 1], axis=0), element_offset=eo * 256), r=[b_idx], w=[b_Kpg[i]])
                    S.dma("pool", b_Vpg[i], lambda e, i=i, col=col: e.indirect_dma_start(out=Vpg[i][:, :], out_offset=None, in_=cv, in_offset=IOA(ap=idx_all[:, col:col + 1], axis=0), element_offset=eo * 256), r=[b_idx], w=[b_Vpg[i]])
                    ps, b_ps = mmbank()
                    for hf in range(2):
                        S.op("pe", lambda e, i=i, hf=hf, ps=ps: e.transpose(ps[:, hf * 128:(hf + 1) * 128], Kpg[i][:, hf * 128:(hf + 1) * 128], ident[:, :]), r=[b_Kpg[i], b_ident], w=[b_ps])
                    S.op("act", lambda e, i=i, ps=ps: e.activation(out=KTp[i][:, :, :], in_=ps[:, 0:256].rearrange("p (a b) -> p a b", a=2), func=AF.Identity), r=[b_ps], w=[b_KTp[i]])
                    S.op("dve", lambda e, i=i: e.tensor_copy(out=Vpp[i][:, :, 0:64], in_=Vpg[i][:, :].rearrange("p (h d) -> p h d", h=4)), r=[b_Vpg[i]], w=[b_Vpp[i]])
                    for h in range(4):
                        hp = (h % 2) * 64; pair = h // 2
                        sbk = 3 + (h % 2); c4 = pair * 4
                        psS, b_psS = PS[sbk], PSB[sbk]
                        S.op("pe", lambda e, i=i, hp=hp, pair=pair, psS=psS, c4=c4, sq_=sq_: e.matmul(psS[:, c4:c4 + 4], KTp[i][hp:hp + 64, pair, :], qT[hp:hp + 64, pair, sq_ * 4:sq_ * 4 + 4], start=True, stop=False, skip_group_check=True),
                             r=[b_KTp[i], b_qT], w=[b_psS])
                        S.op("pe", lambda e, psS=psS, c4=c4: e.matmul(psS[:, c4:c4 + 4], identb[:, :], zerob[:, 0:4], start=False, stop=False, skip_group_check=True), r=[b_identb, b_maskb], w=[b_psS])
                        S.op("pe", lambda e, h=h, psS=psS, c4=c4, sq_=sq_: e.matmul(psS[:, c4:c4 + 4], sel[0:36, h, :], augT[0:36, sq_ * 4:sq_ * 4 + 4], start=False, stop=True, skip_group_check=True),
                             r=[b_sel, b_augT], w=[b_psS])
                        S.op("act", lambda e, i=i, h=h, psS=psS, c4=c4, pg=pg: e.activation(out=PTs[i][:, h, :], in_=psS[:, c4:c4 + 4], func=AF.Exp, bias=biasp[:, pg, h:h + 1]), r=[b_psS, b_biasp], w=[b_PTs[i]])
                        oc = h * 16 + sq_ * 4
                        S.op("pe", lambda e, i=i, h=h, oc=oc: e.matmul(PS[5][:, oc:oc + 4], Vpp[i][:, h, :], PTs[i][:, h, :], start=False, stop=False, skip_group_check=True), r=[b_Vpp[i], b_PTs[i]], w=[PSB[5]])
            for h in range(4):
                hp = (h % 2) * 64; pair = h // 2
                sbk = 3 + (h % 2)
                psS, b_psS = PS[sbk], PSB[sbk]
                S.op("pe", lambda e, hp=hp, pair=pair, psS=psS: e.matmul(psS[0:16, 16:32], KTb[hp:hp + 64, pair, 0:16], qT[hp:hp + 64, pair, 0:16], start=True, stop=False, skip_group_check=True), r=[b_KTb, b_qT], w=[b_psS])
                S.op("pe", lambda e, psS=psS: e.matmul(psS[0:16, 16:32], identb[:, 0:16], maskS[:, 0:16], start=False, stop=False, skip_group_check=True), r=[b_identb, b_maskS], w=[b_psS])
                S.op("pe", lambda e, h=h, psS=psS: e.matmul(psS[0:16, 16:32], sel[0:36, h, 0:16], augT[0:36, 0:16], start=False, stop=True, skip_group_check=True), r=[b_sel, b_augT], w=[b_psS])
                S.op("act", lambda e, h=h, psS=psS: e.activation(out=PTo[0:16, h, :], in_=psS[0:16, 16:32], func=AF.Exp, bias=ncown[0:16, h:h + 1]), r=[b_psS, b_ncown], w=[b_PTo])
                S.op("pe", lambda e, h=h: e.matmul(PS[5][:, h * 16:(h + 1) * 16], Vown[0:16, h, :], PTo[0:16, h, :], start=False, stop=(h == 3), skip_group_check=True), r=[b_Vown, b_PTo], w=[PSB[5]])
            S.op("dve", lambda e: e.reciprocal(rdn[64:128, 0:64], PS[5][64:128, 0:64]), r=[PSB[5]], w=[b_rdn])
            S.op("dve", lambda e: e.tensor_copy(out=rdn[0:64, 0:64], in_=rdn[64:128, 0:64]), r=[b_rdn], w=[b_rdn])
            for h in range(4):
                hp = (h % 2) * 64; pair = h // 2
                S.op("dve", lambda e, h=h, hp=hp, pair=pair: e.tensor_tensor(out=mixT[hp:hp + 64, 2 + pair, 0:16], in0=PS[5][0:64, h * 16:(h + 1) * 16], in1=rdn[0:64, h * 16:(h + 1) * 16], op=ALU.mult),
                     r=[PSB[5], b_rdn], w=[b_mixT[2 + pair]])

        def ssd_sample(l):
            for sq_ in range(NS):
                conv_ssd(l, lambda k, j, sq_=sq_: xbcS[:, k, sq_, j:j + 4], b_xbcS, sq_ * 4, 4)
            for sq_ in range(NS):
                for i in range(2):
                    S.dma("pool", b_hst, lambda e, sq_=sq_, i=i: e.dma_start(out=hst[:], in_=st_ssd[l, sq_][i * 128:(i + 1) * 128, :]), w=[b_hst])
                    ps, b_ps = mmbank()
                    S.op("pe", lambda e, ps=ps: e.transpose(ps[:, 0:128], hst[:, :], ident[:, :]), r=[b_hst, b_ident], w=[b_ps])
                    S.op("dve", lambda e, ps=ps, i=i: e.tensor_copy(out=HT[:, i * 128:(i + 1) * 128], in_=ps[:, 0:128]), r=[b_ps], w=[b_HT])
                for h in range(4):
                    S.op("pool", lambda e, h=h: e.tensor_copy(out=HTz[:, h, (h % 2) * 64:(h % 2) * 64 + 64], in_=HT[:, h * 64:(h + 1) * 64]), r=[b_HT], w=[b_HTz])
                ssd_chunk(l, sq_ * 4, 4, False)
                for i in range(2):
                    ps, b_ps = mmbank()
                    S.op("pe", lambda e, i=i, ps=ps: e.transpose(ps[:, 0:128], HT[:, i * 128:(i + 1) * 128], ident[:, :]), r=[b_HT, b_ident], w=[b_ps])
                    S.op("act", lambda e, ps=ps: e.activation(out=ysd[:, 0:128], in_=ps[:, 0:128], func=AF.Identity), r=[b_ps, b_ssd_st], w=[b_ysd])
                    finals.append(S.dma("pool", b_ssd_st, lambda e, sq_=sq_, i=i: e.dma_start(out=ssd_s[l, sq_][i * 128:(i + 1) * 128, :], in_=ysd[:, 0:128]), r=[b_ysd], w=[b_ssd_st]))

        def sample_conv_outputs(l):
            for sq_ in range(NS):
                for k in range(6):
                    finals.append(S.dma("pool", cvo_b, lambda e, sq_=sq_, k=k: e.dma_start(out=ssdconv_s[l, sq_][:, k * 128:(k + 1) * 128].rearrange("j p -> p j"), in_=xbcS[:, k, sq_, 4:7], allow_slow_non_contiguous=True), r=[b_xbcS], w=[]))
                for k in range(2):
                    finals.append(S.dma("pool", cvo_b, lambda e, sq_=sq_, k=k: e.dma_start(out=sconv_s[l, sq_][:, k * 128:(k + 1) * 128].rearrange("j p -> p j"), in_=cshS[:, k, sq_, 4:6], allow_slow_non_contiguous=True), r=[b_cshS], w=[cvo_b] if k == 1 else []))

        kv_rr = [0]; ost_rr = [0]

        def load_xT_layer0(c):
            for ts in range(4):
                st, b_st = ostage[ost_rr[0] % 2], b_ostage[ost_rr[0] % 2]; b_st2 = b_ostage_st[ost_rr[0] % 2]; ost_rr[0] += 1
                r0 = c * 512 + ts * 128
                S.dma("sp", b_st, lambda e, st=st, r0=r0: e.dma_start(out=st[:], in_=xp[r0:r0 + 128, :]), w=[b_st], r=[b_st2])
                for k in range(8):
                    ps, b_ps = mmbank()
                    S.op("pe", lambda e, st=st, k=k, ps=ps: e.transpose(ps[:, 0:128], st[:, k * 128:(k + 1) * 128], ident[:, :]), r=[b_st, b_ident], w=[b_ps])
                    S.op("act" if k % 2 else "dve", (lambda e, k=k, ps=ps, ts=ts: e.activation(out=xT[:, k, ts * 128:(ts + 1) * 128], in_=ps[:, 0:128], func=AF.Identity)) if k % 2 else
                         (lambda e, k=k, ps=ps, ts=ts: e.tensor_copy(out=xT[:, k, ts * 128:(ts + 1) * 128], in_=ps[:, 0:128])), r=[b_ps], w=[b_xT[k]])

        def matgroup(kind, l, m0, nm, fn_each):
            slot, b_slot = next_group(kind, l, m0)
            for mi in range(nm):
                fn_each(mi, slot, b_slot)

        def block(l, c, smp=False):
            N = 16 if smp else 512
            NSUB = 1 if smp else 4
            MTK = 16 if smp else 128
            first = (c == 0) and not smp
            last_layer = (l == L - 1)
            t0 = c * 512
            xX, b_xX = (xTs, b_xTs) if smp else (xT, b_xT)
            if smp:
                if l == 0:
                    load_xTs()
            elif l == 0:
                load_xT_layer0(c)
            else:
                S.dma("sp", b_xT[0], lambda e: e.dma_start(out=xT[:], in_=x_scr[c]), r=[b_xscr[c]], w=b_xT)
            KSTOP = int(os.environ.get("K_STOP", "99"))
            rmsnorm_fm(xX, b_xX, 8, N, lambda k: gmix[:, l, k:k + 1], xnT, b_xnT)
            def inproj_each(gi):
                def f(mi, slot, b_slot):
                    nm, c0, wd = MT_IN[gi * 4 + mi]
                    idx = sum(1 for (n2, _, _) in MT_IN[:gi * 4 + mi] if n2 == nm)
                    ps, b_ps = mmbank()
                    for k in range(8):
                        S.op("pe", lambda e, slot=slot, k=k, ps=ps, wd=wd: e.matmul(ps[0:wd, 0:N], slot[:, mi, k, 0:wd], xnT[:, k, 0:N], start=(k == 0), stop=(k == 7)),
                             r=[b_slot, b_xnT[k]], w=[b_ps])
                    if nm == "u":
                        S.op("dve", lambda e: e.tensor_copy(out=uTb[:, idx, 0:N], in_=ps[:, 0:N]), r=[b_ps], w=[b_uTb])
                    elif nm == "q":
                        S.op("act", lambda e: e.activation(out=qT[:, idx, 0:N], in_=ps[:, 0:N], func=AF.Identity, scale=0.125), r=[b_ps], w=[b_qT])
                    elif nm == "k":
                        S.op("dve", lambda e: e.tensor_copy(out=KTb[:, idx, 0:N], in_=ps[:, 0:N]), r=[b_ps], w=[b_KTb])
                    elif nm == "v":
                        pass
                    elif nm == "fr":
                        S.op("act", lambda e: e.activation(out=logfT[:, 0:N], in_=ps[0:4, 0:N], func=AF.Exp, scale=-1.0, bias=nbfcol[:, l:l + 1]), r=[b_ps, b_pcol], w=[b_logfT])
                        S.op("act", lambda e: e.activation(out=logfT[:, 0:N], in_=logfT[:, 0:N], func=AF.Ln, bias=onesf[0:4, 0:1]), r=[b_logfT, b_onesf], w=[b_logfT])
                        S.op("dve", lambda e: e.tensor_scalar(out=logfT[:, 0:N], in0=logfT[:, 0:N], scalar1=-1.0, scalar2=None, op0=ALU.mult), r=[b_logfT], w=[b_logfT])
                    elif nm == "z":
                        S.op("act", lambda e: e.activation(out=szT[:, idx, 0:N], in_=ps[:, 0:N], func=AF.Silu), r=[b_ps], w=[b_szT])
                    elif nm == "xbc":
                        if smp:
                            S.op("act", lambda e: e.activation(out=xbcS[:, idx, :, 3:7], in_=ps[:, 0:16].rearrange("p (s t) -> p s t", s=4), func=AF.Identity), r=[b_ps], w=[b_xbcS])
                        else:
                            S.op("act", lambda e: e.activation(out=xbcT[:, idx, 3:3 + N], in_=ps[:, 0:N], func=AF.Identity), r=[b_ps], w=[b_xbcT])
                    elif nm == "dt":
                        S.op("act", lambda e: e.activation(out=daT[0:4, 0:N], in_=ps[0:4, 0:N], func=AF.Exp, bias=dtbcol[:, l:l + 1]), r=[b_ps, b_pcol], w=[b_daT])
                        S.op("act", lambda e: e.activation(out=daT[0:4, 0:N], in_=daT[0:4, 0:N], func=AF.Ln, bias=onesf[0:4, 0:1]), r=[b_daT, b_onesf], w=[b_daT])
                        S.op("dve", lambda e: e.tensor_scalar(out=daT[32:36, 0:N], in0=daT[0:4, 0:N], scalar1=acol[:, l:l + 1], scalar2=None, op0=ALU.mult), r=[b_daT, b_pcol], w=[b_daT])
                    elif nm == "sb":
                        S.op("act", lambda e: e.activation(out=sbT[:, idx, 0:N], in_=ps[:, 0:N], func=AF.Identity), r=[b_ps], w=[b_sbT])
                    elif nm == "sc":
                        if smp:
                            S.op("act", lambda e: e.activation(out=cshS[:, idx, :, 2:6], in_=ps[:, 0:16].rearrange("p (s t) -> p s t", s=4), func=AF.Identity), r=[b_ps], w=[b_cshS])
                        else:
                            S.op("act", lambda e: e.activation(out=cshT[:, idx, 2:2 + N], in_=ps[:, 0:N], func=AF.Identity), r=[b_ps], w=[b_cshT])
                    elif nm == "sh":
                        if smp:
                            S.op("dve", lambda e: e.tensor_tensor(out=cshS[:, idx, :, 2:6], in0=cshS[:, idx, :, 2:6], in1=ps[:, 0:16].rearrange("p (s t) -> p s t", s=4), op=ALU.mult), r=[b_ps], w=[b_cshS])
                        else:
                            S.op("dve", lambda e: e.tensor_tensor(out=cshT[:, idx, 2:2 + N], in0=cshT[:, idx, 2:2 + N], in1=ps[:, 0:N], op=ALU.mult), r=[b_ps], w=[b_cshT])
                return f
            if smp:
                for sq_ in range(NS):
                    for k in range(6):
                        S.dma("pool", b_xbcS, lambda e, sq_=sq_, k=k: e.dma_start(out=xbcS[:, k, sq_, 0:3], in_=st_ssdconv[l, sq_][:, k * 128:(k + 1) * 128].rearrange("j p -> p j"), allow_slow_non_contiguous=True), w=[b_xbcS])
                    for k in range(2):
                        S.dma("pool", b_cshS, lambda e, sq_=sq_, k=k: e.dma_start(out=cshS[:, k, sq_, 0:2], in_=st_sconv[l, sq_][:, k * 128:(k + 1) * 128].rearrange("j p -> p j"), allow_slow_non_contiguous=True), w=[b_cshS])
            elif first:
                S.op("pool", lambda e: e.memset(xbcT[:, :, 0:3], 0.0), w=[b_xbcT])
                S.op("pool", lambda e: e.memset(cshT[:, :, 0:2], 0.0), w=[b_cshT])
            else:
                S.op("pool", lambda e: e.tensor_copy(out=xbcT[:, :, 0:3], in_=xbcT[:, :, N:N + 3]), r=[b_xbcT], w=[b_xbcT])
                S.op("pool", lambda e: e.tensor_copy(out=cshT[:, :, 0:2], in_=cshT[:, :, N:N + 2]), r=[b_cshT], w=[b_cshT])
            for gi in range(6):
                slot, b_slot = next_group("in", l, gi * 4)
                f = inproj_each(gi)
                for mi in range(4):
                    f(mi, slot, b_slot)
                if gi == 1:
                    for ts in range(NSUB):
                        ps, b_ps = mmbank()
                        for mi in range(4):
                            for k in range(8):
                                S.op("pe", lambda e, slot=slot, k=k, mi=mi, ps=ps, ts=ts: e.matmul(ps[0:MTK, mi * 128:(mi + 1) * 128], xnT[:, k, ts * MTK:(ts + 1) * MTK], slot[:, mi, k, :], start=(k == 0), stop=(k == 7)),
                                     r=[b_slot, b_xnT[k]], w=[b_ps])
                        i = kv_rr[0] % 2; kv_rr[0] += 1
                        S.op("act", lambda e, i=i, ps=ps: e.activation(out=kvst[i][0:MTK, :], in_=ps[0:MTK, :], func=AF.Identity), r=[b_ps, b_kvst_st[i]], w=[b_kvst[i]])
                        if smp:
                            S.op("dve", lambda e, ps=ps: e.tensor_copy(out=Vown[0:16, :, 0:64], in_=ps[0:16, 256:512].rearrange("p (h d) -> p h d", h=4)), r=[b_ps], w=[b_Vown])
                            finals.append(S.dma("pool", b_kvst_st[i], lambda e, i=i: e.dma_start(out=k_s[l, :, :], in_=kvst[i][0:16, 0:256]), r=[b_kvst[i]], w=[]))
                            finals.append(S.dma("pool", b_kvst_st[i], lambda e, i=i: e.dma_start(out=v_s[l, :, :], in_=kvst[i][0:16, 256:512]), r=[b_kvst[i]], w=[b_kvst_st[i]]))
                        else:
                            S.op("dve", lambda e, ps=ps, ts=ts: e.tensor_copy(out=Vpb[:, ts, :, 0:64], in_=ps[:, 256:512].rearrange("p (h d) -> p h d", h=4)), r=[b_ps], w=[b_Vpb])
                            r0 = t0 + ts * 128
                            finals.append(S.dma("pool", b_kvst_st[i], lambda e, i=i, r0=r0: e.dma_start(out=k_p[l, r0:r0 + 128, :], in_=kvst[i][:, 0:256]), r=[b_kvst[i]], w=[]))
                            finals.append(S.dma("pool", b_kvst_st[i], lambda e, i=i, r0=r0: e.dma_start(out=v_p[l, r0:r0 + 128, :], in_=kvst[i][:, 256:512]), r=[b_kvst[i]], w=[b_kvst_st[i]]))
                if gi == 2:
                    for ts in range(NSUB):
                        ps, b_ps = mmbank()
                        for k in range(8):
                            S.op("pe", lambda e, slot=slot, k=k, ps=ps, ts=ts: e.matmul(ps[0:MTK, 0:4], xnT[:, k, ts * MTK:(ts + 1) * MTK], slot[:, 0, k, 0:4], start=(k == 0), stop=(k == 7)),
                                 r=[b_slot, b_xnT[k]], w=[b_ps])
                        S.op("dve", lambda e, ps=ps, ts=ts: e.tensor_tensor(out=lf_tok[0:MTK, ts, :], in0=ps[0:MTK, 0:4], in1=bf_bc[0:MTK, l, :], op=ALU.add), r=[b_ps, b_bfbc, b_lf_st], w=[b_lf_tok])
                        S.op("act", lambda e, ts=ts: e.activation(out=lf_tok[0:MTK, ts, :], in_=lf_tok[0:MTK, ts, :], func=AF.Exp, scale=-1.0), r=[b_lf_tok], w=[b_lf_tok])
                        S.op("act", lambda e, ts=ts: e.activation(out=lf_tok[0:MTK, ts, :], in_=lf_tok[0:MTK, ts, :], func=AF.Ln, bias=onesf[0:MTK, 0:1]), r=[b_lf_tok, b_onesf], w=[b_lf_tok])
                        S.op("dve", lambda e, ts=ts: e.tensor_scalar(out=lf_tok[0:MTK, ts, :], in0=lf_tok[0:MTK, ts, :], scalar1=-1.0, scalar2=None, op0=ALU.mult), r=[b_lf_tok], w=[b_lf_tok])
                        ps2, b_ps2 = mmbank()
                        if smp:
                            S.op("pe", lambda e, ps2=ps2: e.matmul(ps2[0:16, 0:4], triS[0:16, 0:16], lf_tok[0:16, 0, :], start=True, stop=True), r=[b_triS, b_lf_tok], w=[b_ps2])
                            S.op("dve", lambda e, ps2=ps2: e.tensor_scalar(out=ncown[0:16, :], in0=ps2[0:16, 0:4], scalar1=-1.0, scalar2=None, op0=ALU.mult), r=[b_ps2], w=[b_ncown])
                            continue
                        kt = c * 4 + ts
                        S.op("pe", lambda e, ps2=ps2, ts=ts: e.matmul(ps2[:, 0:4], tri[:, :], lf_tok[:, ts, :], start=True, stop=True), r=[b_tri, b_lf_tok], w=[b_ps2])
                        S.op("pe", lambda e, ps2=ps2, ts=ts: e.matmul(ps2[:, 4:8], onesf[:, :], lf_tok[:, ts, :], start=True, stop=True), r=[b_onesf, b_lf_tok], w=[b_ps2])
                        if first and ts == 0:
                            S.op("dve", lambda e, ps2=ps2, kt=kt: e.tensor_scalar(out=negcum[:, kt, :], in0=ps2[:, 0:4], scalar1=-1.0, scalar2=None, op0=ALU.mult), r=[b_ps2], w=[b_negcum])
                            S.op("dve", lambda e, ps2=ps2: e.tensor_copy(out=carry_bc[:], in_=ps2[:, 4:8]), r=[b_ps2], w=[b_carry])
                        else:
                            S.op("dve", lambda e, ps2=ps2, kt=kt: e.scalar_tensor_tensor(out=negcum[:, kt, :], in0=ps2[:, 0:4], scalar=-1.0, in1=carry_bc[:], op0=ALU.mult, op1=ALU.subtract), r=[b_ps2, b_carry], w=[b_negcum])
                            S.op("dve", lambda e, ps2=ps2: e.tensor_tensor(out=carry_bc[:], in0=carry_bc[:], in1=ps2[:, 4:8], op=ALU.add), r=[b_ps2], w=[b_carry])
                    if smp:
                        finals.append(S.dma("pool", b_lf_st, lambda e: e.dma_start(out=logf_s[l, :, :], in_=lf_tok[0:16, 0, :]), r=[b_lf_tok], w=[b_lf_st]))
                    else:
                        finals.append(S.dma("pool", b_lf_st, lambda e: e.dma_start(out=logf_p[l, t0:t0 + 512, :].rearrange("(a p) h -> p a h", p=128), in_=lf_tok[:]), r=[b_lf_tok], w=[b_lf_st]))
            if KSTOP <= 2:
                return
            if smp:
                s5_block_sample(l)
            else:
                s5_block(l, N, 1, 512, first)
            for i in range(2):
                if smp:
                    mo = mixT[:, 6 + i, 0:16].rearrange("p (s t) -> p s t", s=4)
                    S.op("dve", lambda e, i=i, mo=mo: e.tensor_scalar(out=mo, in0=cshS[:, i, :, 0:4], scalar1=scw[:, l, i, 0:1], scalar2=None, op0=ALU.mult), r=[b_cshS, b_convp], w=[b_mixT[6 + i]])
                    for j in (1, 2):
                        S.op("dve", lambda e, i=i, j=j, mo=mo: e.scalar_tensor_tensor(out=mo, in0=cshS[:, i, :, j:j + 4], scalar=scw[:, l, i, j:j + 1], in1=mo, op0=ALU.mult, op1=ALU.add), r=[b_cshS, b_convp], w=[b_mixT[6 + i]])
                else:
                    S.op("dve", lambda e, i=i: e.tensor_scalar(out=mixT[:, 6 + i, 0:N], in0=cshT[:, i, 0:N], scalar1=scw[:, l, i, 0:1], scalar2=None, op0=ALU.mult), r=[b_cshT, b_convp], w=[b_mixT[6 + i]])
                    for j in (1, 2):
                        S.op("dve", lambda e, i=i, j=j: e.scalar_tensor_tensor(out=mixT[:, 6 + i, 0:N], in0=cshT[:, i, j:j + N], scalar=scw[:, l, i, j:j + 1], in1=mixT[:, 6 + i, 0:N], op0=ALU.mult, op1=ALU.add), r=[b_cshT, b_convp], w=[b_mixT[6 + i]])
                S.op("dve", lambda e, i=i: e.tensor_tensor(out=mixT[:, 6 + i, 0:N], in0=mixT[:, 6 + i, 0:N], in1=sbT[:, i, 0:N], op=ALU.mult), r=[b_sbT], w=[b_mixT[6 + i]])
            if smp:
                fox_sample(l)
                ssd_sample(l)
            else:
                fox_block(l, c)
                ssd_block(l, c)
            for g in range(4):
                rmsnorm_fm(mixT[:, 2 * g:2 * g + 2, :], b_mixT[2 * g:2 * g + 2], 2, N, lambda k, g=g: ggrp[:, l, g, k:k + 1], catT[:, 2 * g:2 * g + 2, :], b_catT[2 * g:2 * g + 2])
            for gi in range(2):
                slot, b_slot = next_group("out", l, gi * 4)
                for mi in range(4):
                    m = gi * 4 + mi
                    ps, b_ps = mmbank()
                    for k in range(8):
                        S.op("pe", lambda e, slot=slot, k=k, mi=mi, ps=ps: e.matmul(ps[:, 0:N], slot[:, mi, k, :], catT[:, k, 0:N], start=(k == 0), stop=(k == 7)), r=[b_slot, b_catT[k]], w=[b_ps])
                    S.op("dve", lambda e, m=m, ps=ps: e.tensor_tensor(out=xX[:, m, 0:N], in0=xX[:, m, 0:N], in1=ps[:, 0:N], op=ALU.add), r=[b_ps], w=[b_xX[m]])
            rmsnorm_fm(xX, b_xX, 8, N, lambda k: gffn[:, l, k:k + 1], xnT, b_xnT)
            for half in range(2):
                for gi in range(4 * half, 4 * half + 4):
                    slot, b_slot = next_group("up", l, gi * 4)
                    for mi in range(4):
                        m = gi * 4 + mi
                        ps, b_ps = mmbank()
                        for k in range(8):
                            S.op("pe", lambda e, k=k, mi=mi, ps=ps, slot=slot: e.matmul(ps[:, 0:N], slot[:, mi, k, :], xnT[:, k, 0:N], start=(k == 0), stop=(k == 7)), r=[b_slot, b_xnT[k]], w=[b_ps])
                        S.op("act", lambda e, m=m, ps=ps: e.activation(out=hT[:, m % 16, 0:N], in_=ps[:, 0:N], func=AF.Relu), r=[b_ps], w=[b_hT[m % 16]])
                        S.op("pool" if m % 2 else "dve", lambda e, m=m: e.tensor_tensor(out=hT[:, m % 16, 0:N], in0=hT[:, m % 16, 0:N], in1=hT[:, m % 16, 0:N], op=ALU.mult), r=[b_hT[m % 16]], w=[b_hT[m % 16]])
                for m in range(8):
                    slot, b_slot = next_group("down", l, m * 2 + half)
                    ps, b_ps = mmbank()
                    for k in range(16):
                        S.op("pe", lambda e, k=k, ps=ps, slot=slot: e.matmul(ps[:, 0:N], slot[:, 0, k, :], hT[:, k, 0:N], start=(k == 0), stop=(k == 15)), r=[b_slot, b_hT[k]], w=[b_ps])
                    S.op("dve", lambda e, m=m, ps=ps: e.tensor_tensor(out=xX[:, m, 0:N], in0=xX[:, m, 0:N], in1=ps[:, 0:N], op=ALU.add), r=[b_ps], w=[b_xX[m]])
            if smp:
                if last_layer:
                    rmsnorm_fm(xX, b_xX, 8, N, lambda k: gfin[:, k:k + 1], xX, b_xX)
                    i = ost_rr[0] % 2; ost_rr[0] += 1
                    for k in range(8):
                        ps, b_ps = mmbank()
                        S.op("pe", lambda e, k=k, ps=ps: e.transpose(ps[0:16, 0:128], xX[:, k, 0:16], ident[:, :]), r=[b_xX[k], b_ident], w=[b_ps])
                        S.op("dve", lambda e, k=k, ps=ps, i=i: e.tensor_copy(out=ostage[i][0:16, k * 128:(k + 1) * 128], in_=ps[0:16, 0:128]), r=[b_ps, b_ostage_st[i]], w=[b_ostage[i]])
                    finals.append(S.dma("pool", b_ostage_st[i], lambda e, i=i: e.dma_start(out=y_s[:, :], in_=ostage[i][0:16, :]), r=[b_ostage[i]], w=[b_ostage_st[i]]))
                return
            if not last_layer:
                S.dma("sp", b_xscr[c], lambda e: e.dma_start(out=x_scr[c], in_=xT[:]), r=b_xT, w=[b_xscr[c]])
            else:
                rmsnorm_fm(xT, b_xT, 8, N, lambda k: gfin[:, k:k + 1], xT, b_xT)
                for ts in range(4):
                    i = ost_rr[0] % 2; ost_rr[0] += 1
                    for k in range(8):
                        ps, b_ps = mmbank()
                        S.op("pe", lambda e, k=k, ps=ps, ts=ts: e.transpose(ps[:, 0:128], xT[:, k, ts * 128:(ts + 1) * 128], ident[:, :]), r=[b_xT[k], b_ident], w=[b_ps])
                        S.op("act" if k % 2 else "dve", (lambda e, k=k, ps=ps, i=i: e.activation(out=ostage[i][:, k * 128:(k + 1) * 128], in_=ps[:, 0:128], func=AF.Identity)) if k % 2 else
                             (lambda e, k=k, ps=ps, i=i: e.tensor_copy(out=ostage[i][:, k * 128:(k + 1) * 128], in_=ps[:, 0:128])), r=[b_ps, b_ostage_st[i]], w=[b_ostage[i]])
                    r0 = t0 + ts * 128
                    finals.append(S.dma("pool", b_ostage_st[i], lambda e, i=i, r0=r0: e.dma_start(out=y_p[r0:r0 + 128, :], in_=ostage[i][:]), r=[b_ostage[i]], w=[b_ostage_st[i]]))

        def fox_block(l, c):
            N = 512
            first = (c == 0)
            if c < NCH - 1:
                S.dma("sp", b_ktscr[c], lambda e: e.dma_start(out=kt_scr[c], in_=KTb[:]), r=[b_KTb], w=[b_ktscr[c]])
                S.dma("sp", b_vscr[c], lambda e: e.dma_start(out=v_scr[c], in_=Vpb[:, :, :, 0:64]), r=[b_Vpb], w=[b_vscr[c]])
            if first:
                S.op("dve", lambda e: e.tensor_tensor_scan(out=cumT[:, 0:N], data0=ones4[:, 0:N], data1=logfT[:, 0:N], initial=0.0, op0=ALU.mult, op1=ALU.add),
                     r=[b_ones4, b_logfT], w=[b_cumT])
            else:
                S.op("dve", lambda e: e.tensor_tensor(out=logfT[:, 0:1], in0=logfT[:, 0:1], in1=cumc[:, 0:1], op=ALU.add), r=[b_cumc], w=[b_logfT])
                S.op("dve", lambda e: e.tensor_tensor_scan(out=cumT[:, 0:N], data0=ones4[:, 0:N], data1=logfT[:, 0:N], initial=0.0, op0=ALU.mult, op1=ALU.add),
                     r=[b_ones4, b_logfT], w=[b_cumT])
            S.op("dve", lambda e: e.tensor_copy(out=cumc[:, 0:1], in_=cumT[:, N - 1:N]), r=[b_cumT], w=[b_cumc])
            S.op("dve", lambda e: e.tensor_copy(out=augT[0:4, 0:N], in_=cumT[:, 0:N]), r=[b_cumT], w=[b_augT])
            S.op("dve", lambda e: e.tensor_tensor(out=logfT[:, 0:N], in0=cumT[:, 0:N], in1=augT[0:4, 0:N], op=ALU.subtract), r=[b_cumT, b_augT], w=[b_logfT])
            S.op("dve", lambda e: e.tensor_copy(out=augT[32:36, 0:N], in_=logfT[:, 0:N]), r=[b_logfT], w=[b_augT])
            for pair in range(2):
                obank = {2 * pair: 5, 2 * pair + 1: 6}
                for kb in range(c + 1):
                    diag = (kb == c)
                    if diag:
                        Ks, b_Ks, Vs, b_Vs = KTb, b_KTb, Vpb, b_Vpb
                    else:
                        i = kc_rr[0] % 2; kc_rr[0] += 1
                        Ks, b_Ks, Vs, b_Vs = KTc[i], b_KTc[i], Vpc[i], b_Vpc[i]
                        S.dma("sp", b_Ks, lambda e, Ks=Ks, kb=kb: e.dma_start(out=Ks[:], in_=kt_scr[kb]), r=[b_ktscr[kb]], w=[b_Ks])
                        S.dma("sp", b_Vs, lambda e, Vs=Vs, kb=kb: e.dma_start(out=Vs[:, :, :, 0:64], in_=v_scr[kb]), r=[b_vscr[kb]], w=[b_Vs])
                    for kt in range(4):
                        qlo = kt * 128 if diag else 0
                        k0 = kt * 128
                        for h in (2 * pair, 2 * pair + 1):
                            hp = (h % 2) * 64
                            sbk = 3 + (sbank_rr[0] % 2); sbank_rr[0] += 1
                            psS, b_psS = PS[sbk], PSB[sbk]
                            S.op("pe", lambda e, Ks=Ks, hp=hp, pair=pair, k0=k0, qlo=qlo, psS=psS: e.matmul(
                                psS[:, qlo:N], Ks[hp:hp + 64, pair, k0:k0 + 128], qT[hp:hp + 64, pair, qlo:N], start=True, stop=False, skip_group_check=True),
                                r=[b_Ks, b_qT], w=[b_psS])
                            mk = maskb if diag else zerob
                            S.op("pe", lambda e, qlo=qlo, psS=psS, mk=mk: e.matmul(psS[:, qlo:qlo + 128], identb[:, :], mk[:, :], start=False, stop=False, skip_group_check=True),
                                 r=[b_identb, b_maskb], w=[b_psS])
                            S.op("pe", lambda e, h=h, qlo=qlo, psS=psS: e.matmul(psS[:, qlo:N], sel[0:36, h, :], augT[0:36, qlo:N], start=False, stop=True, skip_group_check=True),
                                 r=[b_sel, b_augT], w=[b_psS])
                            pi = pt_rr[0] % 2; pt_rr[0] += 1
                            kti = kb * 4 + kt
                            S.op("act", lambda e, pi=pi, qlo=qlo, psS=psS, kti=kti, h=h: e.activation(out=PT[pi][:, qlo:N], in_=psS[:, qlo:N], func=AF.Exp, bias=negcum[:, kti, h:h + 1]),
                                 r=[b_psS, b_negcum], w=[b_PT[pi]])
                            ob = obank[h]
                            S.op("pe", lambda e, Vs=Vs, kt=kt, h=h, pi=pi, qlo=qlo, ob=ob, kb=kb, diag=diag: e.matmul(
                                PS[ob][:, qlo:N], Vs[:, kt, h, :], PT[pi][:, qlo:N], start=(kb == 0 and kt == 0), stop=(diag and kt == 3), skip_group_check=True),
                                r=[b_Vs, b_PT[pi]], w=[PSB[ob]])
                for h in (2 * pair, 2 * pair + 1):
                    hp = (h % 2) * 64
                    ob = obank[h]
                    S.op("dve", lambda e, ob=ob: e.reciprocal(rdn[64:128, 0:N], PS[ob][64:128, 0:N]), r=[PSB[ob]], w=[b_rdn])
                    S.op("dve", lambda e: e.tensor_copy(out=rdn[0:64, 0:N], in_=rdn[64:128, 0:N]), r=[b_rdn], w=[b_rdn])
                    S.op("dve", lambda e, ob=ob, hp=hp, pair=pair: e.tensor_tensor(out=mixT[hp:hp + 64, 2 + pair, 0:N], in0=PS[ob][0:64, 0:N], in1=rdn[0:64, 0:N], op=ALU.mult),
                         r=[PSB[ob], b_rdn], w=[b_mixT[2 + pair]])

        def conv_ssd(l, xin_fn, xin_b, c_out0, Tq):
            for k in range(6):
                S.op("dve", lambda e, k=k: e.tensor_scalar(out=gtmp[:, 0, 0:Tq], in0=xin_fn(k, 0), scalar1=convw[:, l, k, 0:1], scalar2=None, op0=ALU.mult),
                     r=[xin_b, b_convp], w=[b_gtmp])
                for j in (1, 2, 3):
                    S.op("dve", lambda e, k=k, j=j: e.scalar_tensor_tensor(out=gtmp[:, 0, 0:Tq], in0=xin_fn(k, j), scalar=convw[:, l, k, j:j + 1], in1=gtmp[:, 0, 0:Tq], op0=ALU.mult, op1=ALU.add),
                         r=[xin_b, b_convp], w=[b_gtmp])
                S.op("act", lambda e, k=k: e.activation(out=xbcA[:, k, c_out0:c_out0 + Tq], in_=gtmp[:, 0, 0:Tq], func=AF.Silu, bias=convb[:, l, k:k + 1]),
                     r=[b_gtmp, b_convp], w=[b_xbcA])

        def ssd_chunk(l, c0, Lc, zero_state):
            p5, b5, p6, b6, p7, b7 = PS[5], PSB[5], PS[6], PSB[6], PS[7], PSB[7]
            for i in range(4):
                S.op("pe", lambda e, i=i: e.matmul(p5[0:Lc, i * 128:(i + 1) * 128], xbcA[:, i, c0:c0 + Lc], identb[:, :], start=True, stop=True), r=[b_xbcA, b_identb], w=[b5])
            S.op("pe", lambda e: e.matmul(p7[0:Lc, 300:336], daT[0:36, c0:c0 + Lc], ident[0:36, 0:36], start=True, stop=True), r=[b_daT, b_ident], w=[b7])
            S.op("act", lambda e: e.activation(out=xs_tok[0:Lc, :], in_=p5[0:Lc, 0:256], func=AF.Identity), r=[b5], w=[b_tok])
            S.op("dve", lambda e: e.tensor_copy(out=B_tok[0:Lc, :], in_=p5[0:Lc, 256:512]), r=[b5], w=[b_tok])
            S.op("dve", lambda e: e.tensor_copy(out=da_tok[0:Lc, :], in_=p7[0:Lc, 300:336]), r=[b7], w=[b_tok])
            S.op("pe", lambda e: e.matmul(p7[0:Lc, 256:260], tri[0:Lc, 0:Lc], da_tok[0:Lc, 32:36], start=True, stop=True), r=[b_tri, b_tok], w=[b7])
            S.op("pe", lambda e: e.matmul(p7[:, 260:264], onesf[0:Lc, 0:128], da_tok[0:Lc, 32:36], start=True, stop=True), r=[b_onesf, b_tok], w=[b7])
            S.op("dve", lambda e: e.tensor_scalar(out=sm[0:Lc, 0:4], in0=p7[0:Lc, 256:260], scalar1=-1.0, scalar2=None, op0=ALU.mult), r=[b7], w=[b_sm])
            S.op("dve", lambda e: e.tensor_tensor(out=sm[0:Lc, 12:16], in0=p7[0:Lc, 260:264], in1=sm[0:Lc, 0:4], op=ALU.add), r=[b7], w=[b_sm])
            S.op("act", lambda e: e.activation(out=sm[0:Lc, 4:8], in_=sm[0:Lc, 12:16], func=AF.Exp), r=[b_sm], w=[b_sm])
            S.op("act", lambda e: e.activation(out=sm[:, 8:12], in_=p7[:, 260:264], func=AF.Exp), r=[b7], w=[b_sm])
            for h in range(4):
                S.op("dve", lambda e, h=h: e.tensor_scalar(out=xdtz[0:Lc, h, (h % 2) * 64:(h % 2) * 64 + 64], in0=xs_tok[0:Lc, h * 64:(h + 1) * 64], scalar1=da_tok[0:Lc, h:h + 1], scalar2=None, op0=ALU.mult),
                     r=[b_tok], w=[b_xdt])
                S.op("pool", lambda e, h=h: e.tensor_scalar(out=xdtd[0:Lc, h * 64:(h + 1) * 64], in0=xdtz[0:Lc, h, (h % 2) * 64:(h % 2) * 64 + 64], scalar1=sm[0:Lc, 4 + h:5 + h], scalar2=None, op0=ALU.mult),
                     r=[b_xdt, b_sm], w=[b_xdt])
            for h in range(4):
                S.op("dve", lambda e, h=h: e.tensor_scalar(out=arep[0:Lc, h, :], in0=onesf[0:Lc, :], scalar1=da_tok[0:Lc, 32 + h:33 + h], scalar2=None, op0=ALU.mult), r=[b_tok, b_onesf], w=[b_arep])
            for h in range(4):
                S.op("pe", lambda e, h=h: e.matmul(p6[:, h * 128:h * 128 + Lc], arep[0:Lc, h, :], tri[0:Lc, 0:Lc], start=True, stop=True), r=[b_arep, b_tri], w=[b6])
            p6v = p6[0:Lc, :].rearrange("p (h n) -> p h n", h=4)[:, :, 0:Lc]
            p6f = p6[:, :].rearrange("p (h n) -> p h n", h=4)[:, :, 0:Lc]
            S.op("act", lambda e: e.activation(out=ea[:, :, 0:Lc], in_=p6f, func=AF.Exp), r=[b6], w=[b_ea])
            S.op("dve", lambda e: e.tensor_tensor(out=dec[0:Lc, :, 0:Lc], in0=p6v, in1=maskf[0:Lc, 0:Lc].unsqueeze(1).broadcast_to([Lc, 4, Lc]), op=ALU.add), r=[b6, b_maskf], w=[b_dec])
            for h in range(4):
                S.op("act", lambda e, h=h: e.activation(out=dec[0:Lc, h, 0:Lc], in_=dec[0:Lc, h, 0:Lc], func=AF.Exp, bias=sm[0:Lc, h:h + 1]), r=[b_dec, b_sm], w=[b_dec])
            for g in range(2):
                S.op("pe", lambda e, g=g: e.matmul(p7[0:Lc, g * 128:g * 128 + Lc], xbcA[:, 2 + g, c0:c0 + Lc], xbcA[:, 4 + g, c0:c0 + Lc], start=True, stop=True), r=[b_xbcA], w=[b7])
            for g in range(2):
                S.op("dve", lambda e, g=g: e.tensor_tensor(out=MTt[0:Lc, 2 * g:2 * g + 2, 0:Lc], in0=dec[0:Lc, 2 * g:2 * g + 2, 0:Lc],
                                                           in1=p7[0:Lc, g * 128:g * 128 + Lc].unsqueeze(1).broadcast_to([Lc, 2, Lc]), op=ALU.mult), r=[b_dec, b7], w=[b_MT])
            if not zero_state:
                for g in range(2):
                    S.op("pool", lambda e, g=g: e.tensor_tensor(out=CdT[:, 2 * g:2 * g + 2, 0:Lc], in0=ea[:, 2 * g:2 * g + 2, 0:Lc],
                                                                in1=xbcA[:, 4 + g, c0:c0 + Lc].unsqueeze(1).broadcast_to([128, 2, Lc]), op=ALU.mult), r=[b_ea, b_xbcA], w=[b_CdT])
            for i in range(2):
                ps, b_ps = mmbank()
                n = 0
                tot = 2 if zero_state else 4
                for h in (2 * i, 2 * i + 1):
                    S.op("pe", lambda e, h=h, ps=ps, n=n, tot=tot: e.matmul(ps[:, 0:Lc], xdtz[0:Lc, h, :], MTt[0:Lc, h, 0:Lc], start=(n == 0), stop=(n == tot - 1)), r=[b_xdt, b_MT], w=[b_ps])
                    n += 1
                    if not zero_state:
                        S.op("pe", lambda e, h=h, ps=ps, n=n, tot=tot: e.matmul(ps[:, 0:Lc], HTz[:, h, :], CdT[:, h, 0:Lc], start=(n == 0), stop=(n == tot - 1)), r=[b_HTz, b_CdT], w=[b_ps])
                        n += 1
                S.op("dve", lambda e, i=i, ps=ps: e.scalar_tensor_tensor(out=ysd[:, 0:Lc], in0=xbcA[:, i, c0:c0 + Lc], scalar=dcol[:, l, i:i + 1], in1=ps[:, 0:Lc], op0=ALU.mult, op1=ALU.add),
                     r=[b_xbcA, b_ps, b_convp], w=[b_ysd])
                S.op("dve", lambda e, i=i: e.tensor_tensor(out=mixT[:, 4 + i, c0:c0 + Lc], in0=ysd[:, 0:Lc], in1=szT[:, i, c0:c0 + Lc], op=ALU.mult), r=[b_ysd, b_szT], w=[b_mixT[4 + i]])
            ps, b_ps = mmbank()
            for g in range(2):
                S.op("pe", lambda e, g=g, ps=ps: e.matmul(ps[:, g * 128:(g + 1) * 128], B_tok[0:Lc, g * 128:(g + 1) * 128], xdtd[0:Lc, g * 128:(g + 1) * 128], start=True, stop=True), r=[b_tok, b_xdt], w=[b_ps])
            if zero_state:
                S.op("dve", lambda e, ps=ps: e.tensor_copy(out=HT[:, :], in_=ps[:, 0:256]), r=[b_ps], w=[b_HT])
            else:
                S.op("dve", lambda e: e.tensor_tensor(out=HT[:, :].rearrange("p (h q) -> p h q", h=4), in0=HT[:, :].rearrange("p (h q) -> p h q", h=4),
                                                       in1=sm[:, 8:12].unsqueeze(2).broadcast_to([128, 4, 64]), op=ALU.mult), r=[b_sm], w=[b_HT])
                S.op("dve", lambda e, ps=ps: e.tensor_tensor(out=HT[:, :], in0=HT[:, :], in1=ps[:, 0:256], op=ALU.add), r=[b_ps], w=[b_HT])
            for h in range(4):
                S.op("pool", lambda e, h=h: e.tensor_copy(out=HTz[:, h, (h % 2) * 64:(h % 2) * 64 + 64], in_=HT[:, h * 64:(h + 1) * 64]), r=[b_HT], w=[b_HTz])

        def ssd_block(l, c):
            first = (c == 0)
            conv_ssd(l, lambda k, j: xbcT[:, k, j:j + 512], b_xbcT, 0, 512)
            for sc in range(4):
                ssd_chunk(l, sc * 128, 128, first and sc == 0)

        cvst = sb("cvst", [128, 8, 3]); b_cvst = Buf("cvst")
        s5o = sb("s5o", [128, 8, 2]); b_s5o = Buf("s5o"); b_s5o_st = Buf("s5o_st"); b_ssd_st = Buf("ssd_st")
        cvo_b = Buf("cvo")
        KSTOP = int(os.environ.get("K_STOP", "99"))
        for l in range(L if KSTOP > 1 else 0):
            if not os.environ.get("K_NOS5"):
                s5_tables(l)
            for c in range(NCH):
                block(l, c)
            if KSTOP <= 3:
                continue
            for i in range(2):
                ps, b_ps = mmbank()
                S.op("pe", lambda e, i=i, ps=ps: e.matmul(ps[:, 0:128], HT[:, i * 128:(i + 1) * 128], ident[:, :], start=True, stop=True), r=[b_HT, b_ident], w=[b_ps])
                S.op("act", lambda e, ps=ps: e.activation(out=ysd[:, 0:128], in_=ps[:, 0:128], func=AF.Identity), r=[b_ps, b_ssd_st], w=[b_ysd])
                finals.append(S.dma("pool", b_ssd_st, lambda e, l=l, i=i: e.dma_start(out=ssd_p[l, i * 128:(i + 1) * 128, :], in_=ysd[:, 0:128]), r=[b_ysd], w=[b_ssd_st]))
            if l == 0:
                dbg_dump("zr", zr[:].rearrange("p a b -> p (a b)"), 1024, [b_z])
                dbg_dump("zi", zi[:].rearrange("p a b -> p (a b)"), 1024, [b_z])
                dbg_dump("wr", wr_[:].rearrange("p a b -> p (a b)"), 1024, [b_w])
                dbg_dump("wi", wi_[:].rearrange("p a b -> p (a b)"), 1024, [b_w])
                dbg_dump("Xr", Xr[:].rearrange("p a b -> p (a b)"), 32, [b_X])
                dbg_dump("Xi", Xi[:].rearrange("p a b -> p (a b)"), 32, [b_X])
            S.op("dve", lambda e: e.tensor_copy(out=s5o[:, :, 0:1], in_=Xr[:, :, 0:1]), r=[b_X, b_s5o_st], w=[b_s5o])
            S.op("dve", lambda e: e.tensor_copy(out=s5o[:, :, 1:2], in_=Xi[:, :, 0:1]), r=[b_X], w=[b_s5o])
            finals.append(S.dma("pool", b_s5o_st, lambda e, l=l: e.dma_start(out=s5_p[l].rearrange("(j g) p r -> (g p) j r", g=2), in_=s5o[:]), r=[b_s5o], w=[b_s5o_st]))
            S.op("dve", lambda e: e.tensor_copy(out=cvst[:, 0:6, :], in_=xbcT[:, :, 512:515]), r=[b_xbcT, cvo_b], w=[b_cvst])
            S.op("dve", lambda e: e.tensor_copy(out=cvst[:, 6:8, 0:2], in_=cshT[:, :, 512:514]), r=[b_cshT], w=[b_cvst])
            for k in range(6):
                finals.append(S.dma("pool", cvo_b, lambda e, l=l, k=k: e.dma_start(out=ssdconv_p[l][:, k * 128:(k + 1) * 128].rearrange("j p -> p j"), in_=cvst[:, k, 0:3], allow_slow_non_contiguous=True), r=[b_cvst], w=[]))
            for k in range(2):
                finals.append(S.dma("pool", cvo_b, lambda e, l=l, k=k: e.dma_start(out=sconv_p[l][:, k * 128:(k + 1) * 128].rearrange("j p -> p j"), in_=cvst[:, 6 + k, 0:2], allow_slow_non_contiguous=True), r=[b_cvst], w=[cvo_b] if k == 1 else []))
            pc_flush_and_next()
            if cfg.sample:
                block(l, NCH, smp=True)
                sample_conv_outputs(l)
        for l in range(L):
            wt_ready[l].val = S.dcnt[id(b_wt[l])]
        print("NOPS", S.nrec, "lastline", S.lastline, "NSEM", S.nsem, flush=True)
        S.emit(finals)
    return nc


def kernel(**inputs):
    cfg = Cfg(sample=True)
    nc = build_program(cfg)
    consts = host_consts()
    in_maps = []
    wnames = ["w_in", "w_out", "w_up", "w_down", "norm_mix_g", "norm_ffn_g", "norm_final_g", "s5_lam_re", "s5_lam_im",
              "s5_log_dt", "s5_b_re", "s5_b_im", "s5_c_re", "s5_c_im", "s5_d", "s5_w_glu", "s5_norm_g", "fox_b_f",
              "fox_norm_g", "ssd_conv_w", "ssd_conv_b", "ssd_dt_bias", "ssd_a_log", "ssd_d", "ssd_norm_g", "sc_conv_w", "sc_norm_g"]
    Ld = 4
    shared = {n: np.ascontiguousarray(inputs[n]) for n in wnames}
    shared["cache_k"] = np.ascontiguousarray(inputs["cache_k"]).reshape(-1, 256)
    shared["cache_v"] = np.ascontiguousarray(inputs["cache_v"]).reshape(-1, 256)
    shared["cache_logf"] = np.ascontiguousarray(inputs["cache_logf"]).reshape(-1, 4)
    for k, v in consts.items():
        shared["c_" + k] = v
    for core in range(8):
        m = dict(shared)
        m["xp"] = np.ascontiguousarray(inputs["x_prompt"][core // 2])
        sl = slice(4 * core, 4 * core + 4)
        m["xs"] = np.ascontiguousarray(inputs["x_sample"][sl]).reshape(16, 1024)
        m["state_s5"] = np.ascontiguousarray(inputs["state_s5"][:, sl])
        m["state_ssd"] = np.ascontiguousarray(inputs["state_ssd"][:, sl]).reshape(Ld, 4, 256, 128)
        m["state_ssd_conv"] = np.ascontiguousarray(inputs["state_ssd_conv"][:, sl])
        m["state_sconv"] = np.ascontiguousarray(inputs["state_sconv"][:, sl])
        m["page_table"] = np.ascontiguousarray(inputs["page_table"][sl]).astype(np.int32)
        in_maps.append(m)
    res = run_bass_kernel_spmd(nc, in_maps, core_ids=list(range(8))).results

    def st(name, shape):
        return np.stack([res[2 * s][name].reshape(shape) for s in range(4)], axis=0)

    def ss(name, shape, axis):
        return np.ascontiguousarray(np.concatenate([res[c][name].reshape(shape) for c in range(8)], axis=axis))
    y_prompt = st("y_p", (4096, 1024))
    k_prompt = np.ascontiguousarray(st("k_p", (Ld, 4096, 4, 64)).transpose(1, 0, 2, 3, 4))
    v_prompt = np.ascontiguousarray(st("v_p", (Ld, 4096, 4, 64)).transpose(1, 0, 2, 3, 4))
    logf_prompt = np.ascontiguousarray(st("logf_p", (Ld, 4096, 4)).transpose(1, 0, 2, 3))
    s5_prompt = np.ascontiguousarray(st("s5_p", (Ld, 16, 64, 2)).transpose(1, 0, 2, 3, 4))
    ssd_prompt = np.ascontiguousarray(st("ssd_p", (Ld, 4, 64, 128)).transpose(1, 0, 2, 3, 4))
    ssd_conv_prompt = np.ascontiguousarray(st("ssdconv_p", (Ld, 3, 768)).transpose(1, 0, 2, 3))
    sconv_prompt = np.ascontiguousarray(st("sconv_p", (Ld, 2, 256)).transpose(1, 0, 2, 3))
    y_sample = ss("y_s", (4, 4, 1024), 0)
    k_sample = ss("k_s", (Ld, 4, 4, 4, 64), 1)
    v_sample = ss("v_s", (Ld, 4, 4, 4, 64), 1)
    logf_sample = ss("logf_s", (Ld, 4, 4, 4), 1)
    s5_sample = ss("s5_s", (Ld, 4, 16, 64, 2), 1)
    ssd_sample = ss("ssd_s", (Ld, 4, 4, 64, 128), 1)
    ssd_conv_sample = ss("ssdconv_s", (Ld, 4, 3, 768), 1)
    sconv_sample = ss("sconv_s", (Ld, 4, 2, 256), 1)
    return (y_prompt, y_sample, k_prompt, v_prompt, logf_prompt, s5_prompt, ssd_prompt, ssd_conv_prompt, sconv_prompt,
            k_sample, v_sample, logf_sample, s5_sample, ssd_sample, ssd_conv_sample, sconv_sample)
```

```python
import os
import numpy as np
import ml_dtypes
from contextlib import ExitStack
import concourse.bass as bass
import concourse.mybir as mybir
from concourse.bass_utils import run_bass_kernel_spmd

F32 = mybir.dt.float32
BF16 = mybir.dt.bfloat16
I32 = mybir.dt.int32
ALU = mybir.AluOpType
AF = mybir.ActivationFunctionType

D = 1024
INC = 2824
EPS = 1e-5
NEG = -30000.0
MT_IN = ([("u", 0 + 128 * i, 128) for i in range(2)] + [("q", 256 + 128 * i, 128) for i in range(2)]
         + [("k", 512 + 128 * i, 128) for i in range(2)] + [("v", 768 + 128 * i, 128) for i in range(2)]
         + [("fr", 1024, 4)] + [("z", 1028 + 128 * i, 128) for i in range(2)]
         + [("xbc", 1284 + 128 * i, 128) for i in range(6)] + [("dt", 2052, 4)]
         + [("sb", 2056 + 128 * i, 128) for i in range(2)] + [("sc", 2312 + 128 * i, 128) for i in range(2)]
         + [("sh", 2568 + 128 * i, 128) for i in range(2)])
NMT_IN = len(MT_IN)


class Buf:
    __slots__ = ("name", "w", "r", "sem", "excl")

    def __init__(self, name, excl=False):
        self.name, self.w, self.r, self.sem, self.excl = name, None, [], None, excl


class Op:
    __slots__ = ("eng", "fn", "deps", "sem", "val", "needed", "isdma", "line")

    def __init__(self, eng, fn, deps, isdma=False, sem=None):
        self.eng, self.fn, self.deps = eng, fn, deps
        self.sem, self.val, self.needed, self.isdma = sem, None, False, isdma
        self.line = 0


def _flat(xs):
    out = []
    for x in xs:
        if isinstance(x, (list, tuple)):
            out.extend(_flat(x))
        elif x is not None:
            out.append(x)
    return out


class Sched:
    ENGS = ("pe", "act", "dve", "pool", "sp")

    def __init__(self, nc, es):
        self.nc, self.es = nc, es
        self.ops = {e: [] for e in self.ENGS}
        self.esem = {e: es.enter_context(nc.semaphore("s_" + e)) for e in self.ENGS}
        self.dcnt = {}
        self.nsem = 0
        self.limit = int(os.environ.get("K_MAXOPS", "1000000000"))
        self.nrec = 0
        self.lastline = None

    def _deps(self, eng, r, w, extra):
        deps = []
        for b in r:
            if b.w is not None:
                deps.append(b.w)
        for b in w:
            if b.w is not None:
                deps.append(b.w)
            deps.extend(b.r)
        deps.extend([x for x in extra if x is not None])
        if eng == "pe":
            deps = [d for d in deps if d.eng != "pe" or d.isdma]
        out, seen = [], set()
        for d in deps:
            if id(d) not in seen:
                seen.add(id(d))
                out.append(d)
                d.needed = True
        return out

    def _upd(self, o, r, w):
        for b in r:
            b.r.append(o)
        for b in w:
            b.w = o
            b.r = []

    def _skip(self):
        import sys as _sys
        self.nrec += 1
        if self.nrec > self.limit:
            return True
        self.lastline = (_sys._getframe(2).f_lineno, _sys._getframe(3).f_lineno)
        return False

    def op(self, eng, fn, r=(), w=(), extra=()):
        r, w, extra = _flat(r), _flat(w), _flat(extra)
        w = w + [b for b in r if b.excl]
        r = [b for b in r if not b.excl]
        if self._skip():
            return None
        o = Op(eng, fn, self._deps(eng, r, w, extra))
        self._upd(o, r, w)
        self.ops[eng].append(o)
        return o

    def dma(self, eng, key, fn, r=(), w=(), extra=()):
        r, w, extra = _flat(r), _flat(w), _flat(extra)
        if self._skip():
            return None
        if key.sem is None:
            key.sem = self.es.enter_context(self.nc.semaphore("d%d" % self.nsem))
            self.nsem += 1
            self.dcnt[id(key)] = 0
        o = Op(eng, fn, self._deps("dma", r, w, extra), isdma=True, sem=key.sem)
        self.dcnt[id(key)] += 16
        o.val = self.dcnt[id(key)]
        o.needed = True
        self._upd(o, r, w)
        self.ops[eng].append(o)
        return o

    def emit(self, final_ops):
        nc = self.nc
        for e in self.ENGS:
            c = 0
            for o in self.ops[e]:
                if o.isdma:
                    continue
                if o.needed:
                    c += 1
                    o.sem, o.val = self.esem[e], c
        engobj = {"pe": "tensor", "act": "scalar", "dve": "vector", "pool": "gpsimd", "sp": "sync"}
        with nc.Block() as block:
            for e in self.ENGS:
                ops = self.ops[e]

                def body(eng, ops=ops, e=e):
                    waited = {}
                    for o in ops:
                        need = {}
                        for d in o.deps:
                            k = id(d.sem)
                            if waited.get(k, 0) >= d.val:
                                continue
                            if k not in need or need[k][1] < d.val:
                                need[k] = (d.sem, d.val)
                        for k, (s, v) in need.items():
                            eng.wait_ge(s, v)
                            waited[k] = v
                        ins = o.fn(eng)
                        if o.isdma:
                            ins.then_inc(o.sem, 16)
                        elif o.needed:
                            ins.then_inc(o.sem, 1)
                    if e == "sp":
                        fin = {}
                        for d in final_ops:
                            if d is None:
                                continue
                            k = id(d.sem)
                            if k not in fin or fin[k][1] < d.val:
                                fin[k] = (d.sem, d.val)
                        for k, (s, v) in fin.items():
                            eng.wait_ge(s, v)

                getattr(block, engobj[e])(body)


def host_consts():
    c = {}
    c["ident"] = np.eye(128, dtype=np.float32)
    s = np.arange(128)
    c["tri"] = (s[:, None] <= s[None, :]).astype(np.float32)
    c["tristrict"] = (s[:, None] > s[None, :]).astype(np.float32)
    c["maskb"] = np.where(s[:, None] <= s[None, :], 0.0, NEG).astype(np.float32)
    sel = np.zeros((36, 4, 128), np.float32)
    for h in range(4):
        sel[h, h, :] = 1.0
        sel[32 + h, h, :] = 1.0
    c["sel"] = sel
    t = np.arange(16)
    same = (t[:, None] // 4) == (t[None, :] // 4)
    c["triS"] = (same & (t[:, None] <= t[None, :])).astype(np.float32)
    mS = np.zeros((128, 16), np.float32)
    mS[:16] = np.where(same & (t[:, None] <= t[None, :]), 0.0, NEG)
    c["maskS"] = mS
    p4 = np.ones((4, 16), np.float32); p4[:, 0::4] = 0.0
    c["pat4"] = p4
    c["iota"] = np.arange(128, dtype=np.float32).reshape(128, 1)
    return c


class Cfg:
    def __init__(self, T=4096, depth=4, ns=4, ts=4, npg=64, npool=2560, sample=True):
        self.T, self.depth, self.ns, self.ts, self.npg, self.npool, self.sample = T, depth, ns, ts, npg, npool, sample
        self.nch = T // 512


def build_program(cfg):
    nc = bass.Bass("TRN2", target_bir_lowering=False)
    T, L, NS, TS, NPG = cfg.T, cfg.depth, cfg.ns, cfg.ts, cfg.npg
    NSAMP = NS * TS
    NCH = cfg.nch
    NKT = T // 128
    es = ExitStack()

    def din(name, shape, dt=F32):
        return nc.dram_tensor(name, list(shape), dt, kind="ExternalInput").ap()

    def dout(name, shape, dt=F32):
        return nc.dram_tensor(name, list(shape), dt, kind="ExternalOutput").ap()

    xp = din("xp", [T, D])
    w_in = din("w_in", [L, D, INC]); w_out = din("w_out", [L, D, D])
    w_up = din("w_up", [L, D, 4096]); w_down = din("w_down", [L, 4096, D])
    prm = {}
    for nm, shp in [("norm_mix_g", [L, D]), ("norm_ffn_g", [L, D]), ("norm_final_g", [D]),
                    ("s5_lam_re", [L, 16, 64]), ("s5_lam_im", [L, 16, 64]), ("s5_log_dt", [L, 16]),
                    ("s5_b_re", [L, 16, 64, 16]), ("s5_b_im", [L, 16, 64, 16]),
                    ("s5_c_re", [L, 16, 16, 64]), ("s5_c_im", [L, 16, 16, 64]), ("s5_d", [L, 16, 16]),
                    ("s5_w_glu", [L, 256, 256]), ("s5_norm_g", [L, 256]), ("fox_b_f", [L, 4]),
                    ("fox_norm_g", [L, 256]), ("ssd_conv_w", [L, 4, 768]), ("ssd_conv_b", [L, 768]),
                    ("ssd_dt_bias", [L, 4]), ("ssd_a_log", [L, 4]), ("ssd_d", [L, 4]),
                    ("ssd_norm_g", [L, 256]), ("sc_conv_w", [L, 3, 256]), ("sc_norm_g", [L, 256])]:
        prm[nm] = din(nm, shp)
    cst = {k: din("c_" + k, v.shape) for k, v in host_consts().items()}
    y_p = dout("y_p", [T, D]); k_p = dout("k_p", [L, T, 256]); v_p = dout("v_p", [L, T, 256])
    logf_p = dout("logf_p", [L, T, 4]); s5_p = dout("s5_p", [L, 16, 64, 2]); ssd_p = dout("ssd_p", [L, 256, 128])
    ssdconv_p = dout("ssdconv_p", [L, 3, 768]); sconv_p = dout("sconv_p", [L, 2, 256])
    if cfg.sample:
        xs = din("xs", [NSAMP, D])
        cache_k = din("cache_k", [L * cfg.npool * 128, 256]); cache_v = din("cache_v", [L * cfg.npool * 128, 256])
        cache_logf = din("cache_logf", [L * cfg.npool * 128, 4])
        st_s5 = din("state_s5", [L, NS, 16, 64, 2]); st_ssd = din("state_ssd", [L, NS, 256, 128])
        st_ssdconv = din("state_ssd_conv", [L, NS, 3, 768]); st_sconv = din("state_sconv", [L, NS, 2, 256])
        page_table = din("page_table", [NS, NPG], I32)
        y_s = dout("y_s", [NSAMP, D]); k_s = dout("k_s", [L, NSAMP, 256]); v_s = dout("v_s", [L, NSAMP, 256])
        logf_s = dout("logf_s", [L, NSAMP, 4]); s5_s = dout("s5_s", [L, NS, 16, 64, 2])
        ssd_s = dout("ssd_s", [L, NS, 256, 128]); ssdconv_s = dout("ssdconv_s", [L, NS, 3, 768])
        sconv_s = dout("sconv_s", [L, NS, 2, 256])
    wt_in = nc.dram_tensor("wt_in", [L, NMT_IN, 128, 8, 128], BF16).ap()
    wt_out = nc.dram_tensor("wt_out", [L, 8, 128, 8, 128], BF16).ap()
    wt_up = nc.dram_tensor("wt_up", [L, 32, 128, 8, 128], BF16).ap()
    wt_down = nc.dram_tensor("wt_down", [L, 8, 128, 32, 128], BF16).ap()
    x_scr = nc.dram_tensor("x_scr", [NCH, 128, 8, 512], F32).ap()
    kt_scr = nc.dram_tensor("kt_scr", [NCH, 128, 2, 512], BF16).ap()
    v_scr = nc.dram_tensor("v_scr", [NCH, 128, 4, 4, 64], BF16).ap()
    DBG = bool(os.environ.get("K_DBG"))
    if DBG:
        dbg = dout("dbg", [128, 16384])
    dbg_state = {"off": 0, "items": []}

    with es:
        S = Sched(nc, es)
        finals = []

        def sb(name, shape, dt=F32):
            return es.enter_context(nc.sbuf_tensor(name, list(shape), dt))

        def dbg_dump(name, ap2d, ncols, bufs):
            if not DBG:
                return
            o = dbg_state["off"]
            npart = ap2d.shape[0]
            b = Buf("dbg_" + name)
            finals.append(S.dma("pool", b, lambda e: e.dma_start(out=dbg[0:npart, o:o + ncols], in_=ap2d), r=bufs, w=[b]))
            dbg_state["items"].append((name, o, ncols, npart))
            dbg_state["off"] = o + ncols
        nc._dbg_items = dbg_state["items"]

        PS = [es.enter_context(nc.psum_tensor("ps%d" % i, [128, 512], F32)) for i in range(8)]
        PSB = [Buf("ps%d" % i, excl=True) for i in range(8)]
        mm_rr = [0]

        def mmbank():
            i = mm_rr[0] % 3
            mm_rr[0] += 1
            return PS[i], PSB[i]

        ident = sb("ident", [128, 128]); b_ident = Buf("ident")
        identb = sb("identb", [128, 128], BF16); b_identb = Buf("identb")
        tri = sb("tri", [128, 128]); b_tri = Buf("tri")
        tristrict = sb("tristrict", [128, 128]); b_tristrict = Buf("tristrict")
        maskb = sb("maskb", [128, 128], BF16); b_maskb = Buf("maskb")
        sel = sb("sel", [36, 4, 128], BF16); b_sel = Buf("sel")
        onesf = sb("onesf", [128, 128]); b_onesf = Buf("onesf")
        onesb = sb("onesb", [128, 128], BF16); b_onesb = Buf("onesb")
        S.dma("pool", b_ident, lambda e: e.dma_start(out=ident[:], in_=cst["ident"]), w=[b_ident])
        S.dma("pool", b_identb, lambda e: e.dma_start(out=identb[:], in_=cst["ident"]), w=[b_identb])
        S.dma("pool", b_tri, lambda e: e.dma_start(out=tri[:], in_=cst["tri"]), w=[b_tri])
        S.dma("pool", b_tristrict, lambda e: e.dma_start(out=tristrict[:], in_=cst["tristrict"]), w=[b_tristrict])
        S.dma("pool", b_maskb, lambda e: e.dma_start(out=maskb[:], in_=cst["maskb"]), w=[b_maskb])
        S.dma("pool", b_sel, lambda e: e.dma_start(out=sel[:], in_=cst["sel"]), w=[b_sel])
        S.op("dve", lambda e: e.memset(onesf[:], 1.0), w=[b_onesf])
        S.op("dve", lambda e: e.memset(onesb[:], 1.0), w=[b_onesb])

        b_wt = [Buf("wt%d" % l) for l in range(L)]
        big16 = sb("big16", [128, 4096], F32); b_big = [Buf("big%d" % k) for k in range(16)]
        pstg = big16[:, 0:2048].bitcast(BF16); b_pstg = b_big[0:8]

        pst2 = [sb("pst2_%d" % i, [128, 4, 128], BF16) for i in range(2)]; b_pst2 = [Buf("pst2_%d" % i) for i in range(2)]
        pc_rr = [0]
        for l in range(L):
            b_wt[l].sem = es.enter_context(nc.semaphore("wt%d" % l)); S.dcnt[id(b_wt[l])] = 0

        def pc_jobs(l):
            jobs = []
            v = lambda w_: w_[l].rearrange("(k p) m -> p k m", p=128)
            for mi, (nm, c0, wd) in enumerate(MT_IN):
                for kh in range(2):
                    jobs.append((v(w_in)[:, kh * 4:(kh + 1) * 4, c0:c0 + wd], wt_in[l, mi, :, kh * 4:(kh + 1) * 4, 0:wd], wd))
            for mi in range(8):
                for kh in range(2):
                    jobs.append((v(w_out)[:, kh * 4:(kh + 1) * 4, mi * 128:(mi + 1) * 128], wt_out[l, mi, :, kh * 4:(kh + 1) * 4, :], 128))
            for mi in range(32):
                for kh in range(2):
                    jobs.append((v(w_up)[:, kh * 4:(kh + 1) * 4, mi * 128:(mi + 1) * 128], wt_up[l, mi, :, kh * 4:(kh + 1) * 4, :], 128))
            for mi in range(8):
                for kq in range(8):
                    jobs.append((v(w_down)[:, kq * 4:(kq + 1) * 4, mi * 128:(mi + 1) * 128], wt_down[l, mi, :, kq * 4:(kq + 1) * 4, :], 128))
            return jobs

        def pc_issue(l, job):
            src, dst, wd = job
            i = pc_rr[0] % 2; pc_rr[0] += 1
            stg = pst2[i][:, :, 0:wd]
            S.dma("pool", b_pst2[i], lambda e: e.dma_start(out=stg, in_=src), w=[b_pst2[i]])
            S.dma("pool", b_wt[l], lambda e: e.dma_start(out=dst, in_=stg), r=[b_pst2[i]], w=[])
        for job in pc_jobs(0):
            pc_issue(0, job)
        pc_pending = {"l": 1, "jobs": pc_jobs(1) if L > 1 else []}

        def pc_drip(n):
            for _ in range(n):
                if not pc_pending["jobs"]:
                    return
                pc_issue(pc_pending["l"], pc_pending["jobs"].pop(0))

        def pc_flush_and_next():
            pc_drip(10 ** 6)
            pc_pending["l"] += 1
            pc_pending["jobs"] = pc_jobs(pc_pending["l"]) if pc_pending["l"] < L else []
        wt_ready = []
        for l in range(L):
            o = Op("pool", None, [], isdma=True, sem=b_wt[l].sem)
            o.val = 0
            wt_ready.append(o)

        NSLOT = 2
        wslot = [sb("wslot%d" % i, [128, 4096], BF16) for i in range(NSLOT)]
        b_wslot = [Buf("wslot%d" % i) for i in range(NSLOT)]
        def groups_for(l):
            g = []
            for i in range(6):
                g.append(("in", l, i * 4, 4))
            for i in range(2):
                g.append(("out", l, i * 4, 4))
            for half in range(2):
                for i in range(4 * half, 4 * half + 4):
                    g.append(("up", l, i * 4, 4))
                for m in range(8):
                    g.append(("down", l, m * 2 + half, 1))
            return g
        nblk = NCH + (1 if cfg.sample else 0)
        glist = []
        for l in range(L):
            for b in range(nblk):
                glist.extend(groups_for(l))
        gstate = {"issued": 0, "next": 0}

        def gview(idx):
            kind, l, m0, n = glist[idx]
            slot = idx % NSLOT
            kk = 16 if kind == "down" else 8
            return wslot[slot][:, 0:n * kk * 128].rearrange("p (m k c) -> p m k c", m=n, k=kk), b_wslot[slot]

        def issue_group(idx):
            kind, l, m0, n = glist[idx]
            dst, b_dst = gview(idx)
            if kind == "down":
                m, half = m0 // 2, m0 % 2
                s_ap = wt_down[l, m:m + 1, :, half * 16:(half + 1) * 16, :].rearrange("m p k c -> p m k c")
            else:
                src = {"in": wt_in, "out": wt_out, "up": wt_up}[kind]
                s_ap = src[l, m0:m0 + n].rearrange("m p k c -> p m k c")
            S.dma("sp", b_dst, lambda e, dst=dst, s_ap=s_ap: e.dma_start(out=dst, in_=s_ap),
                  w=[b_dst], extra=[wt_ready[l]])

        def next_group(kind, l, m0):
            idx = gstate["next"]
            assert glist[idx][:3] == (kind, l, m0), (glist[idx], kind, l, m0)
            while gstate["issued"] < min(len(glist), idx + NSLOT - 1) or gstate["issued"] <= idx:
                issue_group(gstate["issued"])
                gstate["issued"] += 1
            gstate["next"] += 1
            pc_drip(1)
            return gview(idx)

        gmix = sb("gmix", [128, L, 8]); gffn = sb("gffn", [128, L, 8]); gfin = sb("gfin", [128, 8])
        b_gains = Buf("gains")
        for l in range(L):
            S.dma("pool", b_gains, lambda e, l=l: e.dma_start(out=gmix[:, l, :], in_=prm["norm_mix_g"][l].rearrange("(k p) -> p k", p=128), allow_slow_non_contiguous=True), w=[])
            S.dma("pool", b_gains, lambda e, l=l: e.dma_start(out=gffn[:, l, :], in_=prm["norm_ffn_g"][l].rearrange("(k p) -> p k", p=128), allow_slow_non_contiguous=True), w=[])
        o_g = S.dma("pool", b_gains, lambda e: e.dma_start(out=gfin[:], in_=prm["norm_final_g"].rearrange("(k p) -> p k", p=128), allow_slow_non_contiguous=True), w=[b_gains])
        ggrp = sb("ggrp", [128, L, 4, 2]); b_ggrp = Buf("ggrp")
        for l in range(L):
            for gi, nm in enumerate(["s5_norm_g", "fox_norm_g", "ssd_norm_g", "sc_norm_g"]):
                S.dma("pool", b_ggrp, lambda e, l=l, gi=gi, nm=nm: e.dma_start(out=ggrp[:, l, gi, :], in_=prm[nm][l].rearrange("(k p) -> p k", p=128), allow_slow_non_contiguous=True), w=[b_ggrp])

        epsc = sb("epsc", [128, 1]); b_epsc = Buf("epsc")
        S.op("dve", lambda e: e.memset(epsc[:], EPS), w=[b_epsc])
        halfpi = sb("halfpi", [128, 1]); b_halfpi = Buf("halfpi")
        S.op("dve", lambda e: e.memset(halfpi[:], float(np.pi / 2)), w=[b_halfpi])
        maskf = sb("maskf", [128, 128]); b_maskf = Buf("maskf")
        S.dma("pool", b_maskf, lambda e: e.dma_start(out=maskf[:], in_=cst["maskb"]), w=[b_maskf])
        rs = sb("rs", [128, 512]); b_rs = Buf("rs")

        def rmsnorm_fm(src, b_src, nk, N, gain_fn, dst, b_dst):
            ps, b_ps = mmbank()
            for k in range(nk):
                S.op("act", lambda e, k=k: e.activation(out=sq[:, k, 0:N], in_=src[:, k, 0:N], func=AF.Square),
                     r=[b_src[k]], w=[b_sq[k]])
            for k in range(nk):
                S.op("pe", lambda e, k=k: e.matmul(ps[:, 0:N], onesb[:, :], sq[:, k, 0:N], start=(k == 0), stop=(k == nk - 1)),
                     r=[b_sq[k], b_onesb], w=[b_ps])
            S.op("act", lambda e: e.activation(out=rs[:, 0:N], in_=ps[:, 0:N], func=AF.Sqrt, scale=1.0 / (nk * 128), bias=epsc[:, 0:1]),
                 r=[b_ps, b_epsc], w=[b_rs])
            S.op("dve", lambda e: e.reciprocal(rs[:, 0:N], rs[:, 0:N]), r=[b_rs], w=[b_rs])
            for k in range(nk):
                S.op("dve", lambda e, k=k: e.scalar_tensor_tensor(
                    out=dst[:, k, 0:N], in0=src[:, k, 0:N], scalar=gain_fn(k), in1=rs[:, 0:N],
                    op0=ALU.mult, op1=ALU.mult), r=[b_src[k], b_rs, b_gains, b_ggrp], w=[b_dst[k]])

        NB = 512
        xT = sb("xT", [128, 8, NB]); b_xT = [Buf("xT%d" % k) for k in range(8)]
        xnT = sb("xnT", [128, 8, NB], BF16); b_xnT = [Buf("xnT%d" % k) for k in range(8)]
        uTb = sb("uTb", [128, 2, NB], BF16); b_uTb = Buf("uTb")
        qT = sb("qT", [128, 2, NB], BF16); b_qT = Buf("qT")
        KTb = sb("KTb", [128, 2, NB], BF16); b_KTb = Buf("KTb")
        Vpb = sb("Vpb", [128, 4, 4, 128], BF16); b_Vpb = Buf("Vpb")
        S.op("pool", lambda e: e.memset(Vpb[:], 1.0), w=[b_Vpb])
        KTc = [sb("KTc%d" % i, [128, 2, NB], BF16) for i in range(2)]; b_KTc = [Buf("KTc%d" % i) for i in range(2)]
        Vpc = [sb("Vpc%d" % i, [128, 4, 4, 128], BF16) for i in range(2)]; b_Vpc = [Buf("Vpc%d" % i) for i in range(2)]
        for i in range(2):
            S.op("pool", lambda e, i=i: e.memset(Vpc[i][:], 1.0), w=[b_Vpc[i]])
        b_ktscr = [Buf("ktscr%d" % c) for c in range(NCH)]; b_vscr = [Buf("vscr%d" % c) for c in range(NCH)]
        cumT = sb("cumT", [4, NB]); b_cumT = Buf("cumT")
        cumc = sb("cumc", [4, 1]); b_cumc = Buf("cumc")
        augT = sb("augT", [36, NB], BF16); b_augT = Buf("augT")
        S.op("pool", lambda e: e.memset(augT[:], 0.0), w=[b_augT])
        zerob = sb("zerob", [128, 128], BF16)
        S.op("pool", lambda e: e.memset(zerob[:], 0.0), w=[b_maskb])
        ones4 = sb("ones4", [4, NB], BF16); b_ones4 = Buf("ones4")
        S.op("pool", lambda e: e.memset(ones4[:], 1.0), w=[b_ones4])
        PT = [sb("PT%d" % i, [128, NB], BF16) for i in range(2)]; b_PT = [Buf("PT%d" % i) for i in range(2)]
        rdn = rs; b_rdn = b_rs
        kc_rr = [0]; pt_rr = [0]; sbank_rr = [0]
        negcum = sb("negcum", [128, NKT, 4]); b_negcum = Buf("negcum")
        carry_bc = sb("carry_bc", [128, 4]); b_carry = Buf("carry_bc")
        logfT = sb("logfT", [4, NB]); b_logfT = Buf("logfT")
        xbcT = sb("xbcT", [128, 6, 3 + NB], BF16); b_xbcT = Buf("xbcT")
        xbcA = sb("xbcA", [128, 6, NB], BF16); b_xbcA = Buf("xbcA")
        szT = sb("szT", [128, 2, NB], BF16); b_szT = Buf("szT")
        daT = sb("daT", [36, NB]); b_daT = Buf("daT")
        S.op("pool", lambda e: e.memset(daT[:], 0.0), w=[b_daT])
        xs_tok = sb("xs_tok", [128, 256]); B_tok = sb("B_tok", [128, 256], BF16); da_tok = sb("da_tok", [128, 36]); b_tok = Buf("tok")
        xdtz = sb("xdtz", [128, 4, 128], BF16); xdtd = sb("xdtd", [128, 256], BF16); b_xdt = Buf("xdt")
        S.op("pool", lambda e: e.memset(xdtz[:], 0.0), w=[b_xdt])
        arep = sb("arep", [128, 4, 128]); b_arep = Buf("arep")
        sm = sb("sm", [128, 16]); b_sm = Buf("sm")
        dec = sb("dec", [128, 4, 128]); b_dec = Buf("dec")
        ea = sb("ea", [128, 4, 128], BF16); b_ea = Buf("ea")
        MTt = sb("MTt", [128, 4, 128], BF16); b_MT = Buf("MT")
        CdT = sb("CdT", [128, 4, 128], BF16); b_CdT = Buf("CdT")
        HT = sb("HT", [128, 256]); HTz = sb("HTz", [128, 4, 128], BF16); b_HT = Buf("HT"); b_HTz = Buf("HTz")
        S.op("pool", lambda e: e.memset(HTz[:], 0.0), w=[b_HTz])
        ysd = sb("ysd", [128, 128]); b_ysd = Buf("ysd")
        sbT = sb("sbT", [128, 2, NB], BF16); b_sbT = Buf("sbT")
        cshT = sb("cshT", [128, 2, 2 + NB], BF16); b_cshT = Buf("cshT")
        catT = sb("catT", [128, 8, NB], BF16); b_catT = [Buf("catT%d" % k) for k in range(8)]
        mixT = sb("mixT", [128, 8, NB], BF16); b_mixT = [Buf("mixT%d" % k) for k in range(8)]
        hT = big16[:, :].bitcast(BF16).rearrange("p (k n) -> p k n", k=16); b_hT = b_big
        sq = hT; b_sq = b_hT
        kvst = [sb("kvst%d" % i, [128, 512]) for i in range(1)]*2; b_kvst = [Buf("kvst0")]*2
        b_kvst_st = [Buf("kvst_st0")]*2
        ostage = [sb("ostage%d" % i, [128, 1024]) for i in range(1)]*2; b_ostage = [Buf("ost0")]*2
        b_ostage_st = [Buf("ost_st0")]*2
        b_xscr = [Buf("xscr%d" % c) for c in range(NCH)]
        lf_tok = sb("lf_tok", [128, 4, 4]); b_lf_tok = Buf("lf_tok")
        b_lf_st = Buf("lf_st")

        bfcol = sb("bfcol", [4, L]); dtbcol = sb("dtbcol", [4, L]); alogcol = sb("alogcol", [4, L]); b_pcol = Buf("pcol")
        S.dma("pool", b_pcol, lambda e: e.dma_start(out=bfcol[:], in_=prm["fox_b_f"].rearrange("l h -> h l"), allow_slow_non_contiguous=True), w=[])
        S.dma("pool", b_pcol, lambda e: e.dma_start(out=dtbcol[:], in_=prm["ssd_dt_bias"].rearrange("l h -> h l"), allow_slow_non_contiguous=True), w=[])
        S.dma("pool", b_pcol, lambda e: e.dma_start(out=alogcol[:], in_=prm["ssd_a_log"].rearrange("l h -> h l"), allow_slow_non_contiguous=True), w=[b_pcol])
        nbfcol = sb("nbfcol", [4, L]); acol = sb("acol", [4, L])
        S.op("dve", lambda e: e.tensor_scalar(out=nbfcol[:], in0=bfcol[:], scalar1=-1.0, scalar2=None, op0=ALU.mult), r=[b_pcol], w=[b_pcol])
        S.op("act", lambda e: e.activation(out=acol[:], in_=alogcol[:], func=AF.Exp), r=[b_pcol], w=[b_pcol])
        S.op("dve", lambda e: e.tensor_scalar(out=acol[:], in0=acol[:], scalar1=-1.0, scalar2=None, op0=ALU.mult), r=[b_pcol], w=[b_pcol])
        bf_bc = sb("bf_bc", [128, L, 4]); b_bfbc = Buf("bf_bc")
        for l in range(L):
            S.dma("pool", b_bfbc, lambda e, l=l: e.dma_start(out=bf_bc[:, l, :], in_=prm["fox_b_f"][l:l + 1, :].partition_broadcast(128)), w=[b_bfbc])
        convw = sb("convw", [128, L, 6, 4]); convb = sb("convb", [128, L, 6]); scw = sb("scw", [128, L, 2, 3]); b_convp = Buf("convp")
        dcol = sb("dcol", [128, L, 2]); s5dcol = sb("s5dcol", [128, L, 2])
        for l in range(L):
            for k in range(6):
                S.dma("pool", b_convp, lambda e, l=l, k=k: e.dma_start(out=convw[:, l, k, :], in_=prm["ssd_conv_w"][l][:, k * 128:(k + 1) * 128].rearrange("j p -> p j"), allow_slow_non_contiguous=True), w=[])
            S.dma("pool", b_convp, lambda e, l=l: e.dma_start(out=convb[:, l, :], in_=prm["ssd_conv_b"][l].rearrange("(k p) -> p k", p=128), allow_slow_non_contiguous=True), w=[])
            for k in range(2):
                S.dma("pool", b_convp, lambda e, l=l, k=k: e.dma_start(out=scw[:, l, k, :], in_=prm["sc_conv_w"][l][:, k * 128:(k + 1) * 128].rearrange("j p -> p j"), allow_slow_non_contiguous=True), w=[])
            S.dma("pool", b_convp, lambda e, l=l: e.dma_start(out=s5dcol[:, l, :], in_=prm["s5_d"][l].rearrange("(k g) h -> (g h) k", k=2), allow_slow_non_contiguous=True), w=[])
            for h in range(4):
                S.dma("pool", b_convp, lambda e, l=l, h=h: e.dma_start(
                    out=dcol[(h % 2) * 64:(h % 2) * 64 + 64, l, h // 2:h // 2 + 1],
                    in_=prm["ssd_d"][l:l + 1, h:h + 1].partition_broadcast(64)), w=[b_convp])
        wglu = sb("wglu", [128, 2, 256], BF16); b_wglu = Buf("wglu")

        LS = 128
        Er = sb("Er", [128, 8, LS]); Ei = sb("Ei", [128, 8, LS]); T1r = sb("T1r", [128, 8, LS]); T1i = sb("T1i", [128, 8, LS])
        R0 = sb("R0", [128, 8, LS]); b_tab = Buf("s5tab")
        Bm = sb("Bm", [128, 8, 2, 128], BF16); Cm = sb("Cm", [128, 8, 2, 128], BF16); b_BC = Buf("s5BC")
        s5c = sb("s5c", [128, 24, 8]); b_s5c = Buf("s5c")
        LR, LI, DT, RR, CC, SS, AR, AI, QR, QI, CM, SM, TA, TB, TC, TD = range(16)
        Xr = sb("Xr", [128, 8, 4]); Xi = sb("Xi", [128, 8, 4]); Wr = sb("Wr", [128, 8, 4]); Wi = sb("Wi", [128, 8, 4]); b_X = Buf("s5X")
        s5t = sb("s5t", [128, 8, 4, 4]); b_s5t = Buf("s5t")

        def col(i):
            return s5c[:, i, :]

        def s5_tables(l):
            d = S.dma
            d("pool", b_s5c, lambda e: e.dma_start(out=col(LR), in_=prm["s5_lam_re"][l].rearrange("(j g) p -> (g p) j", g=2), allow_slow_non_contiguous=True), w=[b_s5c], r=[b_tab])
            d("pool", b_s5c, lambda e: e.dma_start(out=col(LI), in_=prm["s5_lam_im"][l].rearrange("(j g) p -> (g p) j", g=2), allow_slow_non_contiguous=True), w=[b_s5c])
            for g2 in range(2):
                d("pool", b_s5c, lambda e, g2=g2: e.dma_start(out=s5c[g2 * 64:(g2 + 1) * 64, DT, :], in_=prm["s5_log_dt"][l].rearrange("(j g) -> g j", g=2)[g2:g2 + 1, :].partition_broadcast(64), allow_slow_non_contiguous=True), w=[b_s5c])
            o = lambda eng, fn: S.op(eng, fn, r=[b_s5c, b_halfpi, b_onesf], w=[b_s5c])
            o("act", lambda e: e.activation(out=col(DT), in_=col(DT), func=AF.Exp))
            o("dve", lambda e: e.tensor_tensor(out=col(TA), in0=col(LR), in1=col(DT), op=ALU.mult))
            o("dve", lambda e: e.tensor_tensor(out=col(TB), in0=col(LI), in1=col(DT), op=ALU.mult))
            o("act", lambda e: e.activation(out=col(RR), in_=col(TA), func=AF.Exp))
            o("dve", lambda e: e.tensor_scalar(out=col(TB), in0=col(TB), scalar1=1.0 / 32, scalar2=None, op0=ALU.mult))
            o("dve", lambda e: e.tensor_tensor(out=col(TA), in0=col(TB), in1=col(TB), op=ALU.mult))
            o("dve", lambda e: e.tensor_scalar(out=col(SS), in0=col(TA), scalar1=1.0 / 362880, scalar2=None, op0=ALU.mult))
            for cc in (-1.0 / 5040, 1.0 / 120, -1.0 / 6):
                o("dve", lambda e, cc=cc: e.scalar_tensor_tensor(out=col(SS), in0=col(SS), scalar=cc, in1=col(TA), op0=ALU.add, op1=ALU.mult))
            o("dve", lambda e: e.scalar_tensor_tensor(out=col(SS), in0=col(SS), scalar=1.0, in1=col(TB), op0=ALU.add, op1=ALU.mult))
            o("dve", lambda e: e.tensor_scalar(out=col(CC), in0=col(TA), scalar1=-1.0 / 3628800, scalar2=None, op0=ALU.mult))
            for cc in (1.0 / 40320, -1.0 / 720, 1.0 / 24, -0.5):
                o("dve", lambda e, cc=cc: e.scalar_tensor_tensor(out=col(CC), in0=col(CC), scalar=cc, in1=col(TA), op0=ALU.add, op1=ALU.mult))
            o("dve", lambda e: e.tensor_scalar(out=col(CC), in0=col(CC), scalar1=1.0, scalar2=None, op0=ALU.add))
            for _ in range(5):
                o("dve", lambda e: e.tensor_tensor(out=col(TC), in0=col(CC), in1=col(SS), op=ALU.mult))
                o("dve", lambda e: e.tensor_tensor(out=col(TA), in0=col(CC), in1=col(CC), op=ALU.mult))
                o("dve", lambda e: e.tensor_tensor(out=col(TD), in0=col(SS), in1=col(SS), op=ALU.mult))
                o("dve", lambda e: e.tensor_tensor(out=col(CC), in0=col(TA), in1=col(TD), op=ALU.subtract))
                o("dve", lambda e: e.tensor_scalar(out=col(SS), in0=col(TC), scalar1=2.0, scalar2=None, op0=ALU.mult))
            o("dve", lambda e: e.tensor_tensor(out=col(AR), in0=col(RR), in1=col(CC), op=ALU.mult))
            o("dve", lambda e: e.tensor_tensor(out=col(AI), in0=col(RR), in1=col(SS), op=ALU.mult))
            o("dve", lambda e: e.tensor_scalar(out=col(TA), in0=col(AR), scalar1=-1.0, scalar2=None, op0=ALU.add))
            o("dve", lambda e: e.tensor_tensor(out=col(TB), in0=col(LR), in1=col(LR), op=ALU.mult))
            o("dve", lambda e: e.tensor_tensor(out=col(TC), in0=col(LI), in1=col(LI), op=ALU.mult))
            o("dve", lambda e: e.tensor_tensor(out=col(TB), in0=col(TB), in1=col(TC), op=ALU.add))
            o("dve", lambda e: e.reciprocal(col(TB), col(TB)))
            o("dve", lambda e: e.tensor_tensor(out=col(TC), in0=col(TA), in1=col(LR), op=ALU.mult))
            o("dve", lambda e: e.tensor_tensor(out=col(TD), in0=col(AI), in1=col(LI), op=ALU.mult))
            o("dve", lambda e: e.tensor_tensor(out=col(TC), in0=col(TC), in1=col(TD), op=ALU.add))
            o("dve", lambda e: e.tensor_tensor(out=col(QR), in0=col(TC), in1=col(TB), op=ALU.mult))
            o("dve", lambda e: e.tensor_tensor(out=col(TC), in0=col(AI), in1=col(LR), op=ALU.mult))
            o("dve", lambda e: e.tensor_tensor(out=col(TD), in0=col(TA), in1=col(LI), op=ALU.mult))
            o("dve", lambda e: e.tensor_tensor(out=col(TC), in0=col(TC), in1=col(TD), op=ALU.subtract))
            o("dve", lambda e: e.tensor_tensor(out=col(QI), in0=col(TC), in1=col(TB), op=ALU.mult))
            t = lambda eng, fn: S.op(eng, fn, r=[b_s5c], w=[b_tab])
            t("dve", lambda e: e.memset(Er[:, :, 0:1], 1.0))
            t("dve", lambda e: e.memset(Ei[:, :, 0:1], 0.0))
            o("dve", lambda e: e.tensor_copy(out=col(CM), in_=col(CC)))
            o("dve", lambda e: e.tensor_copy(out=col(SM), in_=col(SS)))
            m = 1
            while m < LS:
                cm_b = s5c[:, CM, :].unsqueeze(2).broadcast_to([128, 8, m]); sm_b = s5c[:, SM, :].unsqueeze(2).broadcast_to([128, 8, m])
                t("dve", lambda e, m=m, cm_b=cm_b: e.tensor_tensor(out=Er[:, :, m:2 * m], in0=Er[:, :, 0:m], in1=cm_b, op=ALU.mult))
                t("dve", lambda e, m=m, sm_b=sm_b: e.tensor_tensor(out=T1r[:, :, 0:m], in0=Ei[:, :, 0:m], in1=sm_b, op=ALU.mult))
                t("dve", lambda e, m=m: e.tensor_tensor(out=Er[:, :, m:2 * m], in0=Er[:, :, m:2 * m], in1=T1r[:, :, 0:m], op=ALU.subtract))
                t("dve", lambda e, m=m, cm_b=cm_b: e.tensor_tensor(out=Ei[:, :, m:2 * m], in0=Ei[:, :, 0:m], in1=cm_b, op=ALU.mult))
                t("dve", lambda e, m=m, sm_b=sm_b: e.tensor_tensor(out=T1r[:, :, 0:m], in0=Er[:, :, 0:m], in1=sm_b, op=ALU.mult))
                t("dve", lambda e, m=m: e.tensor_tensor(out=Ei[:, :, m:2 * m], in0=Ei[:, :, m:2 * m], in1=T1r[:, :, 0:m], op=ALU.add))
                o("dve", lambda e: e.tensor_tensor(out=col(TC), in0=col(CM), in1=col(SM), op=ALU.mult))
                o("dve", lambda e: e.tensor_tensor(out=col(TA), in0=col(CM), in1=col(CM), op=ALU.mult))
                o("dve", lambda e: e.tensor_tensor(out=col(TD), in0=col(SM), in1=col(SM), op=ALU.mult))
                o("dve", lambda e: e.tensor_tensor(out=col(CM), in0=col(TA), in1=col(TD), op=ALU.subtract))
                o("dve", lambda e: e.tensor_scalar(out=col(SM), in0=col(TC), scalar1=2.0, scalar2=None, op0=ALU.mult))
                m *= 2
            qr_b = s5c[:, QR, :].unsqueeze(2).broadcast_to([128, 8, LS]); qi_b = s5c[:, QI, :].unsqueeze(2).broadcast_to([128, 8, LS])
            rr_b = s5c[:, RR, :].unsqueeze(2).broadcast_to([128, 8, LS])
            t("dve", lambda e: e.tensor_tensor(out=T1r[:], in0=Er[:], in1=qr_b, op=ALU.mult))
            t("dve", lambda e: e.tensor_tensor(out=R0[:], in0=Ei[:], in1=qi_b, op=ALU.mult))
            t("dve", lambda e: e.tensor_tensor(out=T1r[:], in0=T1r[:], in1=R0[:], op=ALU.add))
            t("dve", lambda e: e.tensor_tensor(out=T1i[:], in0=Er[:], in1=qi_b, op=ALU.mult))
            t("dve", lambda e: e.tensor_tensor(out=R0[:], in0=Ei[:], in1=qr_b, op=ALU.mult))
            t("dve", lambda e: e.tensor_tensor(out=T1i[:], in0=T1i[:], in1=R0[:], op=ALU.subtract))
            t("dve", lambda e: e.tensor_tensor(out=R0[:], in0=onesf[:, :].unsqueeze(1).broadcast_to([128, 8, LS]), in1=rr_b, op=ALU.mult))
            t("dve", lambda e: e.memset(R0[:, :, 0:1], 0.0))
            S.op("pool", lambda e: e.memset(Bm[:], 0.0), w=[b_BC])
            S.op("pool", lambda e: e.memset(Cm[:], 0.0), w=[b_BC])
            for g in range(16):
                j, g2 = g // 2, g % 2
                gl = g % 8
                for ri, nm in enumerate(["s5_b_re", "s5_b_im"]):
                    d("pool", b_BC, lambda e, g=g, j=j, g2=g2, gl=gl, ri=ri, nm=nm: e.dma_start(
                        out=Bm[gl * 16:(gl + 1) * 16, j, ri, g2 * 64:(g2 + 1) * 64],
                        in_=prm[nm][l, g].rearrange("p h -> h p"), allow_slow_non_contiguous=True), w=[b_BC])
                for ri, nm in enumerate(["s5_c_re", "s5_c_im"]):
                    c0 = (j % 4) * 32 + g2 * 16
                    d("pool", b_BC, lambda e, g=g, j=j, g2=g2, ri=ri, nm=nm, c0=c0: e.dma_start(
                        out=Cm[g2 * 64:(g2 + 1) * 64, j, ri, c0:c0 + 16],
                        in_=prm[nm][l, g].rearrange("h p -> p h"), allow_slow_non_contiguous=True), w=[b_BC])
            d("pool", b_wglu, lambda e: e.dma_start(out=wglu[:], in_=prm["s5_w_glu"][l].rearrange("(k p) m -> p k m", p=128)), w=[b_wglu])
            if l == 0:
                dbg_dump("s5c", s5c[:].rearrange("p a b -> p (a b)"), 192, [b_s5c])
                dbg_dump("Er", Er[:].rearrange("p a b -> p (a b)"), 1024, [b_tab])
                dbg_dump("Ei", Ei[:].rearrange("p a b -> p (a b)"), 1024, [b_tab])
                dbg_dump("T1r", T1r[:].rearrange("p a b -> p (a b)"), 1024, [b_tab])
                dbg_dump("R0", R0[:].rearrange("p a b -> p (a b)"), 1024, [b_tab])

        zr = big16[:, 0:1024].rearrange("p (a b) -> p a b", a=8); zi = big16[:, 1024:2048].rearrange("p (a b) -> p a b", a=8); b_z = b_big[0:8]
        wr_ = big16[:, 2048:3072].rearrange("p (a b) -> p a b", a=8); wi_ = big16[:, 3072:4096].rearrange("p (a b) -> p a b", a=8); b_w = b_big[8:16]
        pp = catT[:, :, :].rearrange("p j (q n) -> p j q n", q=4); b_pp = b_catT
        yA = sb("yA", [128, 2, NB]); b_yA = Buf("yA")
        yAb = sb("yAb", [128, 2, NB], BF16); b_yAb = Buf("yAb")
        gtmp = sb("gtmp", [128, 2, NB]); b_gtmp = Buf("gtmp")

        def s5_block(l, N, nseq, Tq, first):
            nsub = Tq // LS if Tq >= LS else 1
            Ls = min(LS, Tq)
            psy, b_psy = PS[7], PSB[7]
            for sc in range(nsub):
                c0 = sc * Ls
                for half in range(2):
                    jj = 4 * half
                    for ri, bank in enumerate([5, 6]):
                        for j in range(jj, jj + 4):
                            S.op("pe", lambda e, j=j, ri=ri, bank=bank, c0=c0: e.matmul(
                                PS[bank][:, (j % 4) * 128:(j % 4) * 128 + Ls],
                                Bm[:, j, ri, :], uTb[:, j // 4, c0:c0 + Ls], start=True, stop=True),
                                r=[b_BC, b_uTb], w=[PSB[bank]])
                    pv5 = PS[5][:, :].rearrange("p (a b) -> p a b", a=4)[:, :, 0:Ls]
                    pv6 = PS[6][:, :].rearrange("p (a b) -> p a b", a=4)[:, :, 0:Ls]
                    S.op("dve", lambda e, jj=jj, pv5=pv5: e.tensor_tensor(out=zr[:, jj:jj + 4, 0:Ls], in0=T1r[:, jj:jj + 4, 0:Ls], in1=pv5, op=ALU.mult), r=[b_tab, PSB[5]], w=[b_z])
                    S.op("dve", lambda e, jj=jj, pv6=pv6: e.tensor_tensor(out=wr_[:, jj:jj + 4, 0:Ls], in0=T1i[:, jj:jj + 4, 0:Ls], in1=pv6, op=ALU.mult), r=[b_tab, PSB[6]], w=[b_w])
                    S.op("dve", lambda e, jj=jj, pv6=pv6: e.tensor_tensor(out=zi[:, jj:jj + 4, 0:Ls], in0=T1r[:, jj:jj + 4, 0:Ls], in1=pv6, op=ALU.mult), r=[b_tab, PSB[6]], w=[b_z])
                    S.op("dve", lambda e, jj=jj, pv5=pv5: e.tensor_tensor(out=wi_[:, jj:jj + 4, 0:Ls], in0=T1i[:, jj:jj + 4, 0:Ls], in1=pv5, op=ALU.mult), r=[b_tab, PSB[5]], w=[b_w])
                S.op("pool", lambda e: e.tensor_tensor(out=zr[:, :, 0:Ls], in0=zr[:, :, 0:Ls], in1=wr_[:, :, 0:Ls], op=ALU.subtract), r=[b_w], w=[b_z])
                S.op("pool", lambda e: e.tensor_tensor(out=zi[:, :, 0:Ls], in0=zi[:, :, 0:Ls], in1=wi_[:, :, 0:Ls], op=ALU.add), r=[b_w], w=[b_z])
                if not (first and sc == 0):
                    S.op("dve", lambda e: e.tensor_tensor(out=zr[:, :, 0:1], in0=zr[:, :, 0:1], in1=Wr[:, :, 0:1], op=ALU.add), r=[b_X], w=[b_z])
                    S.op("dve", lambda e: e.tensor_tensor(out=zi[:, :, 0:1], in0=zi[:, :, 0:1], in1=Wi[:, :, 0:1], op=ALU.add), r=[b_X], w=[b_z])
                S.op("dve", lambda e: e.tensor_tensor_scan(out=wr_[:, :, 0:Ls].rearrange("p a b -> p (a b)") if Ls == LS else wr_[:, :, 0:Ls],
                                                           data0=R0[:, :, :].rearrange("p a b -> p (a b)"), data1=zr[:, :, :].rearrange("p a b -> p (a b)"),
                                                           initial=0.0, op0=ALU.mult, op1=ALU.add), r=[b_tab, b_z], w=[b_w])
                S.op("dve", lambda e: e.tensor_tensor_scan(out=wi_[:, :, :].rearrange("p a b -> p (a b)"),
                                                           data0=R0[:, :, :].rearrange("p a b -> p (a b)"), data1=zi[:, :, :].rearrange("p a b -> p (a b)"),
                                                           initial=0.0, op0=ALU.mult, op1=ALU.add), r=[b_tab, b_z], w=[b_w])
                S.op("dve", lambda e: e.tensor_tensor(out=pp[:, :, 0, :], in0=Er[:], in1=wr_[:], op=ALU.mult), r=[b_tab, b_w], w=[b_pp])
                S.op("dve", lambda e: e.scalar_tensor_tensor(out=pp[:, :, 1, :], in0=Ei[:], scalar=-1.0, in1=wi_[:], op0=ALU.mult, op1=ALU.mult), r=[b_tab, b_w], w=[b_pp])
                S.op("dve", lambda e: e.scalar_tensor_tensor(out=pp[:, :, 2, :], in0=Er[:], scalar=-1.0, in1=wi_[:], op0=ALU.mult, op1=ALU.mult), r=[b_tab, b_w], w=[b_pp])
                S.op("dve", lambda e: e.scalar_tensor_tensor(out=pp[:, :, 3, :], in0=Ei[:], scalar=-1.0, in1=wr_[:], op0=ALU.mult, op1=ALU.mult), r=[b_tab, b_w], w=[b_pp])
                la = Ls - 1
                xo = lambda eng, fn: S.op(eng, fn, r=[b_tab, b_w, b_s5c, b_s5t], w=[b_s5t])
                xo("dve", lambda e: e.tensor_tensor(out=s5t[:, :, 0, 0:1], in0=Er[:, :, la:la + 1], in1=wr_[:, :, la:la + 1], op=ALU.mult))
                xo("dve", lambda e: e.tensor_tensor(out=s5t[:, :, 1, 0:1], in0=Ei[:, :, la:la + 1], in1=wi_[:, :, la:la + 1], op=ALU.mult))
                xo("dve", lambda e: e.tensor_tensor(out=s5t[:, :, 2, 0:1], in0=Er[:, :, la:la + 1], in1=wi_[:, :, la:la + 1], op=ALU.mult))
                xo("dve", lambda e: e.tensor_tensor(out=s5t[:, :, 3, 0:1], in0=Ei[:, :, la:la + 1], in1=wr_[:, :, la:la + 1], op=ALU.mult))
                xx = lambda eng, fn: S.op(eng, fn, r=[b_s5t, b_s5c], w=[b_X])
                xx("dve", lambda e: e.tensor_tensor(out=Xr[:, :, 0:1], in0=s5t[:, :, 0, 0:1], in1=s5t[:, :, 1, 0:1], op=ALU.subtract))
                xx("dve", lambda e: e.tensor_tensor(out=Xi[:, :, 0:1], in0=s5t[:, :, 2, 0:1], in1=s5t[:, :, 3, 0:1], op=ALU.add))
                arc = s5c[:, AR, :].unsqueeze(2); aic = s5c[:, AI, :].unsqueeze(2)
                xo("dve", lambda e: e.tensor_tensor(out=s5t[:, :, 0, 1:2], in0=Xr[:, :, 0:1], in1=arc, op=ALU.mult))
                xo("dve", lambda e: e.tensor_tensor(out=s5t[:, :, 1, 1:2], in0=Xi[:, :, 0:1], in1=aic, op=ALU.mult))
                xo("dve", lambda e: e.tensor_tensor(out=s5t[:, :, 2, 1:2], in0=Xi[:, :, 0:1], in1=arc, op=ALU.mult))
                xo("dve", lambda e: e.tensor_tensor(out=s5t[:, :, 3, 1:2], in0=Xr[:, :, 0:1], in1=aic, op=ALU.mult))
                xx("dve", lambda e: e.tensor_tensor(out=Wr[:, :, 0:1], in0=s5t[:, :, 0, 1:2], in1=s5t[:, :, 1, 1:2], op=ALU.subtract))
                xx("dve", lambda e: e.tensor_tensor(out=Wi[:, :, 0:1], in0=s5t[:, :, 2, 1:2], in1=s5t[:, :, 3, 1:2], op=ALU.add))
                for hh in range(2):
                    n = 0
                    for j in range(4 * hh, 4 * hh + 4):
                        for pi, ci in [(0, 0), (1, 0), (2, 1), (3, 1)]:
                            S.op("pe", lambda e, j=j, pi=pi, ci=ci, n=n, hh=hh: e.matmul(
                                psy[:, hh * 128:hh * 128 + Ls], Cm[:, j, ci, :], pp[:, j, pi, 0:Ls], start=(n == 0), stop=(n == 15)),
                                r=[b_BC, b_pp], w=[b_psy])
                            n += 1
                    S.op("dve", lambda e, hh=hh, c0=c0: e.scalar_tensor_tensor(
                        out=yA[:, hh, c0:c0 + Ls], in0=uTb[:, hh, c0:c0 + Ls], scalar=s5dcol[:, l, hh:hh + 1], in1=psy[:, hh * 128:hh * 128 + Ls],
                        op0=ALU.mult, op1=ALU.add), r=[b_uTb, b_psy, b_convp], w=[b_yA])
            s5_tail(l, N)

        def s5_tail(l, N):
            S.op("act", lambda e: e.activation(out=gtmp[:, :, 0:N], in_=yA[:, :, 0:N], func=AF.Square), r=[b_yA], w=[b_gtmp])
            S.op("dve", lambda e: e.tensor_scalar(out=gtmp[:, :, 0:N], in0=gtmp[:, :, 0:N], scalar1=0.044715, scalar2=1.0, op0=ALU.mult, op1=ALU.add), r=[b_gtmp], w=[b_gtmp])
            S.op("dve", lambda e: e.tensor_tensor(out=gtmp[:, :, 0:N], in0=gtmp[:, :, 0:N], in1=yA[:, :, 0:N], op=ALU.mult), r=[b_gtmp, b_yA], w=[b_gtmp])
            S.op("act", lambda e: e.activation(out=gtmp[:, :, 0:N], in_=gtmp[:, :, 0:N], func=AF.Sigmoid, scale=1.5957691216057308), r=[b_gtmp], w=[b_gtmp])
            S.op("dve", lambda e: e.tensor_tensor(out=yA[:, :, 0:N], in0=yA[:, :, 0:N], in1=gtmp[:, :, 0:N], op=ALU.mult), r=[b_gtmp], w=[b_yA])
            S.op("pool", lambda e: e.tensor_copy(out=yAb[:, :, 0:N], in_=yA[:, :, 0:N]), r=[b_yA], w=[b_yAb])
            for m in range(2):
                ps, b_ps = mmbank()
                for k in range(2):
                    S.op("pe", lambda e, m=m, k=k, ps=ps: e.matmul(ps[:, 0:N], wglu[:, k, m * 128:(m + 1) * 128], yAb[:, k, 0:N], start=(k == 0), stop=(k == 1)),
                         r=[b_wglu, b_yAb], w=[b_ps])
                S.op("act", lambda e, m=m, ps=ps: e.activation(out=gtmp[:, m, 0:N], in_=ps[:, 0:N], func=AF.Sigmoid), r=[b_ps], w=[b_gtmp])
                S.op("dve", lambda e, m=m: e.tensor_tensor(out=mixT[:, m, 0:N], in0=yA[:, m, 0:N], in1=gtmp[:, m, 0:N], op=ALU.mult), r=[b_yA, b_gtmp], w=[b_mixT[m]])

        if cfg.sample:
            xTs = sb("xTs", [128, 8, 16]); b_xTs = [Buf("xTs%d" % k) for k in range(8)]
            xbcS = sb("xbcS", [128, 6, 4, 7]); b_xbcS = Buf("xbcS")
            cshS = sb("cshS", [128, 2, 4, 6]); b_cshS = Buf("cshS")
            Vown = sb("Vown", [16, 4, 128], BF16); b_Vown = Buf("Vown")
            S.op("pool", lambda e: e.memset(Vown[:], 1.0), w=[b_Vown])
            ncown = sb("ncown", [16, 4]); b_ncown = Buf("ncown")
            triS = sb("triS", [16, 16]); b_triS = Buf("triS")
            maskS = sb("maskS", [128, 16], BF16); b_maskS = Buf("maskS")
            pat4 = sb("pat4", [4, 16]); b_pat4 = Buf("pat4")
            iotaf = sb("iotaf", [128, 1]); b_iotaf = Buf("iotaf")
            S.dma("pool", b_triS, lambda e: e.dma_start(out=triS[:], in_=cst["triS"]), w=[b_triS])
            S.dma("pool", b_maskS, lambda e: e.dma_start(out=maskS[:], in_=cst["maskS"]), w=[b_maskS])
            S.dma("pool", b_pat4, lambda e: e.dma_start(out=pat4[:], in_=cst["pat4"]), w=[b_pat4])
            S.dma("pool", b_iotaf, lambda e: e.dma_start(out=iotaf[:], in_=cst["iota"]), w=[b_iotaf])
            NPGT = NS * NPG
            ptb = sb("ptb", [128, NPGT], I32); idx_all = sb("idx_all", [128, NPGT], I32); b_idx = Buf("idx")
            for sq_ in range(NS):
                S.dma("pool", b_idx, lambda e, sq_=sq_: e.dma_start(out=ptb[:, sq_ * NPG:(sq_ + 1) * NPG], in_=page_table[sq_:sq_ + 1, :].partition_broadcast(128)), w=[b_idx])
            S.op("dve", lambda e: e.tensor_scalar(out=idx_all[:], in0=ptb[:], scalar1=128.0, scalar2=iotaf[:, 0:1], op0=ALU.mult, op1=ALU.add), r=[b_idx, b_iotaf], w=[b_idx])
            Kpg = [sb("Kpg%d" % i, [128, 256]) for i in range(2)]; b_Kpg = [Buf("Kpg%d" % i) for i in range(2)]
            Vpg = [sb("Vpg%d" % i, [128, 256]) for i in range(2)]; b_Vpg = [Buf("Vpg%d" % i) for i in range(2)]
            KTp = [sb("KTp%d" % i, [128, 2, 128], BF16) for i in range(2)]; b_KTp = [Buf("KTp%d" % i) for i in range(2)]
            Vpp = [sb("Vpp%d" % i, [128, 4, 128], BF16) for i in range(2)]; b_Vpp = [Buf("Vpp%d" % i) for i in range(2)]
            for i in range(2):
                S.op("pool", lambda e, i=i: e.memset(Vpp[i][:], 1.0), w=[b_Vpp[i]])
            PTs = [sb("PTs%d" % i, [128, 2, 8], BF16) for i in range(2)]; b_PTs = [Buf("PTs%d" % i) for i in range(2)]
            PTo = sb("PTo", [16, 4, 16], BF16); b_PTo = Buf("PTo")
            lfp = sb("lfp", [128, NPG, 4]); b_lfp = Buf("lfp")
            biasp = sb("biasp", [128, NPG, 4]); b_biasp = Buf("biasp")
            cumtmp = lfp[:, :, :].rearrange("p g h -> p (g h)").rearrange("p (h g) -> p h g", h=4); b_cumtmp = b_lfp
            R0S = sb("R0S", [128, 128]); b_R0S = Buf("R0S")
            hst = sb("hst", [128, 128]); b_hst = Buf("hst")
            pg_rr = [0]

        def load_xTs():
            i = ost_rr[0] % 2; ost_rr[0] += 1
            S.dma("sp", b_ostage[i], lambda e, i=i: e.dma_start(out=ostage[i][0:16, :], in_=xs[:, :]), w=[b_ostage[i]], r=[b_ostage_st[i]])
            for k in range(8):
                ps, b_ps = mmbank()
                S.op("pe", lambda e, k=k, ps=ps, i=i: e.transpose(ps[:, 0:16], ostage[i][0:16, k * 128:(k + 1) * 128], ident[0:16, 0:16]), r=[b_ostage[i], b_ident], w=[b_ps])
                S.op("dve", lambda e, k=k, ps=ps: e.tensor_copy(out=xTs[:, k, 0:16], in_=ps[:, 0:16]), r=[b_ps], w=[b_xTs[k]])

        def s5_block_sample(l):
            v4 = lambda ap: ap.rearrange("p (j s t) -> p j s t", j=8, s=4)
            zrS, ziS, wrS, wiS = v4(big16[:, 0:128]), v4(big16[:, 1024:1152]), v4(big16[:, 2048:2176]), v4(big16[:, 3072:3200])
            psy, b_psy = PS[7], PSB[7]
            for sq_ in range(NS):
                S.dma("pool", b_s5o, lambda e, sq_=sq_: e.dma_start(out=s5o[:], in_=st_s5[l, sq_].rearrange("(j g) p r -> (g p) j r", g=2)), r=[b_s5o_st], w=[b_s5o])
                S.op("dve", lambda e, sq_=sq_: e.tensor_copy(out=Xr[:, :, sq_:sq_ + 1], in_=s5o[:, :, 0:1]), r=[b_s5o], w=[b_X])
                S.op("dve", lambda e, sq_=sq_: e.tensor_copy(out=Xi[:, :, sq_:sq_ + 1], in_=s5o[:, :, 1:2]), r=[b_s5o], w=[b_X])
            arc = s5c[:, AR, :].unsqueeze(2).broadcast_to([128, 8, 4]); aic = s5c[:, AI, :].unsqueeze(2).broadcast_to([128, 8, 4])
            xo = lambda fn: S.op("dve", fn, r=[b_X, b_s5c, b_s5t], w=[b_s5t])
            xx = lambda fn: S.op("dve", fn, r=[b_s5t], w=[b_X])

            def carry():
                xo(lambda e: e.tensor_tensor(out=s5t[:, :, 0, :], in0=Xr[:], in1=arc, op=ALU.mult))
                xo(lambda e: e.tensor_tensor(out=s5t[:, :, 1, :], in0=Xi[:], in1=aic, op=ALU.mult))
                xo(lambda e: e.tensor_tensor(out=s5t[:, :, 2, :], in0=Xi[:], in1=arc, op=ALU.mult))
                xo(lambda e: e.tensor_tensor(out=s5t[:, :, 3, :], in0=Xr[:], in1=aic, op=ALU.mult))
                xx(lambda e: e.tensor_tensor(out=Wr[:], in0=s5t[:, :, 0, :], in1=s5t[:, :, 1, :], op=ALU.subtract))
                xx(lambda e: e.tensor_tensor(out=Wi[:], in0=s5t[:, :, 2, :], in1=s5t[:, :, 3, :], op=ALU.add))
            carry()
            S.op("dve", lambda e: e.tensor_copy(out=v4(R0S[:, :]), in_=R0[:, :, 0:4].unsqueeze(2).broadcast_to([128, 8, 4, 4])), r=[b_tab], w=[b_R0S])
            for half in range(2):
                jj = 4 * half
                for ri, bank in enumerate([5, 6]):
                    for j in range(jj, jj + 4):
                        S.op("pe", lambda e, j=j, ri=ri, bank=bank: e.matmul(PS[bank][:, (j % 4) * 128:(j % 4) * 128 + 16], Bm[:, j, ri, :], uTb[:, j // 4, 0:16], start=True, stop=True),
                             r=[b_BC, b_uTb], w=[PSB[bank]])
                pv5 = PS[5][:, :].rearrange("p (a b) -> p a b", a=4)[:, :, 0:16].rearrange("p a (s t) -> p a s t", s=4)
                pv6 = PS[6][:, :].rearrange("p (a b) -> p a b", a=4)[:, :, 0:16].rearrange("p a (s t) -> p a s t", s=4)
                t1rb = T1r[:, jj:jj + 4, 0:4].unsqueeze(2).broadcast_to([128, 4, 4, 4])
                t1ib = T1i[:, jj:jj + 4, 0:4].unsqueeze(2).broadcast_to([128, 4, 4, 4])
                S.op("dve", lambda e, jj=jj, a_=t1rb, b_=pv5: e.tensor_tensor(out=zrS[:, jj:jj + 4], in0=a_, in1=b_, op=ALU.mult), r=[b_tab, PSB[5]], w=[b_z])
                S.op("dve", lambda e, jj=jj, a_=t1ib, b_=pv6: e.tensor_tensor(out=wrS[:, jj:jj + 4], in0=a_, in1=b_, op=ALU.mult), r=[b_tab, PSB[6]], w=[b_w])
                S.op("dve", lambda e, jj=jj, a_=t1rb, b_=pv6: e.tensor_tensor(out=ziS[:, jj:jj + 4], in0=a_, in1=b_, op=ALU.mult), r=[b_tab, PSB[6]], w=[b_z])
                S.op("dve", lambda e, jj=jj, a_=t1ib, b_=pv5: e.tensor_tensor(out=wiS[:, jj:jj + 4], in0=a_, in1=b_, op=ALU.mult), r=[b_tab, PSB[5]], w=[b_w])
            S.op("dve", lambda e: e.tensor_tensor(out=zrS, in0=zrS, in1=wrS, op=ALU.subtract), r=[b_w], w=[b_z])
            S.op("dve", lambda e: e.tensor_tensor(out=ziS, in0=ziS, in1=wiS, op=ALU.add), r=[b_w], w=[b_z])
            S.op("dve", lambda e: e.tensor_tensor(out=zrS[:, :, :, 0], in0=zrS[:, :, :, 0], in1=Wr[:], op=ALU.add), r=[b_X], w=[b_z])
            S.op("dve", lambda e: e.tensor_tensor(out=ziS[:, :, :, 0], in0=ziS[:, :, :, 0], in1=Wi[:], op=ALU.add), r=[b_X], w=[b_z])
            S.op("dve", lambda e: e.tensor_tensor_scan(out=big16[:, 2048:2176], data0=R0S[:, :], data1=big16[:, 0:128], initial=0.0, op0=ALU.mult, op1=ALU.add), r=[b_R0S, b_z], w=[b_w])
            S.op("dve", lambda e: e.tensor_tensor_scan(out=big16[:, 3072:3200], data0=R0S[:, :], data1=big16[:, 1024:1152], initial=0.0, op0=ALU.mult, op1=ALU.add), r=[b_R0S, b_z], w=[b_w])
            eb = lambda T_: T_[:, :, 0:4].unsqueeze(2).broadcast_to([128, 8, 4, 4])
            ppv = lambda q: pp[:, :, q, 0:16].rearrange("p j (s t) -> p j s t", s=4)
            S.op("dve", lambda e: e.tensor_tensor(out=ppv(0), in0=eb(Er), in1=wrS, op=ALU.mult), r=[b_tab, b_w], w=[b_pp])
            S.op("dve", lambda e: e.tensor_tensor(out=ppv(1), in0=eb(Ei), in1=wiS, op=ALU.mult), r=[b_tab, b_w], w=[b_pp])
            S.op("dve", lambda e: e.tensor_tensor(out=ppv(2), in0=eb(Er), in1=wiS, op=ALU.mult), r=[b_tab, b_w], w=[b_pp])
            S.op("dve", lambda e: e.tensor_tensor(out=ppv(3), in0=eb(Ei), in1=wrS, op=ALU.mult), r=[b_tab, b_w], w=[b_pp])
            for q_ in (1, 2, 3):
                S.op("dve", lambda e, q_=q_: e.tensor_scalar(out=pp[:, :, q_, 0:16], in0=pp[:, :, q_, 0:16], scalar1=-1.0, scalar2=None, op0=ALU.mult), r=[b_pp], w=[b_pp])
            e3r = Er[:, :, 3:4].broadcast_to([128, 8, 4]); e3i = Ei[:, :, 3:4].broadcast_to([128, 8, 4])
            xo2 = lambda fn: S.op("dve", fn, r=[b_tab, b_w, b_s5t], w=[b_s5t])
            xo2(lambda e: e.tensor_tensor(out=s5t[:, :, 0, :], in0=wrS[:, :, :, 3], in1=e3r, op=ALU.mult))
            xo2(lambda e: e.tensor_tensor(out=s5t[:, :, 1, :], in0=wiS[:, :, :, 3], in1=e3i, op=ALU.mult))
            xo2(lambda e: e.tensor_tensor(out=s5t[:, :, 2, :], in0=wiS[:, :, :, 3], in1=e3r, op=ALU.mult))
            xo2(lambda e: e.tensor_tensor(out=s5t[:, :, 3, :], in0=wrS[:, :, :, 3], in1=e3i, op=ALU.mult))
            xx(lambda e: e.tensor_tensor(out=Xr[:], in0=s5t[:, :, 0, :], in1=s5t[:, :, 1, :], op=ALU.subtract))
            xx(lambda e: e.tensor_tensor(out=Xi[:], in0=s5t[:, :, 2, :], in1=s5t[:, :, 3, :], op=ALU.add))
            for hh in range(2):
                n = 0
                for j in range(4 * hh, 4 * hh + 4):
                    for pi, ci in [(0, 0), (1, 0), (2, 1), (3, 1)]:
                        S.op("pe", lambda e, j=j, pi=pi, ci=ci, n=n, hh=hh: e.matmul(psy[:, hh * 128:hh * 128 + 16], Cm[:, j, ci, :], pp[:, j, pi, 0:16], start=(n == 0), stop=(n == 15)),
                             r=[b_BC, b_pp], w=[b_psy])
                        n += 1
                S.op("dve", lambda e, hh=hh: e.scalar_tensor_tensor(out=yA[:, hh, 0:16], in0=uTb[:, hh, 0:16], scalar=s5dcol[:, l, hh:hh + 1], in1=psy[:, hh * 128:hh * 128 + 16],
                                                                   op0=ALU.mult, op1=ALU.add), r=[b_uTb, b_psy, b_convp], w=[b_yA])
            s5_tail(l, 16)
            if l == 0:
                dbg_dump("XrS", Xr[:].rearrange("p a b -> p (a b)"), 32, [b_X])
                dbg_dump("XiS", Xi[:].rearrange("p a b -> p (a b)"), 32, [b_X])
                dbg_dump("WrS", Wr[:].rearrange("p a b -> p (a b)"), 32, [b_X])
                dbg_dump("wrS", big16[:, 2048:2176], 128, [b_w])
                dbg_dump("zrS", big16[:, 0:128], 128, [b_z])
            for sq_ in range(NS):
                S.op("dve", lambda e, sq_=sq_: e.tensor_copy(out=s5o[:, :, 0:1], in_=Xr[:, :, sq_:sq_ + 1]), r=[b_X, b_s5o_st], w=[b_s5o])
                S.op("dve", lambda e, sq_=sq_: e.tensor_copy(out=s5o[:, :, 1:2], in_=Xi[:, :, sq_:sq_ + 1]), r=[b_X], w=[b_s5o])
                finals.append(S.dma("pool", b_s5o_st, lambda e, sq_=sq_: e.dma_start(out=s5_s[l, sq_].rearrange("(j g) p r -> (g p) j r", g=2), in_=s5o[:]), r=[b_s5o], w=[b_s5o_st]))

        def fox_sample(l):
            ck, cv, cl = cache_k, cache_v, cache_logf
            eo = l * cfg.npool * 128
            IOA = bass.IndirectOffsetOnAxis
            S.op("dve", lambda e: e.tensor_tensor_scan(out=cumT[:, 0:16], data0=pat4[:, 0:16], data1=logfT[:, 0:16], initial=0.0, op0=ALU.mult, op1=ALU.add), r=[b_pat4, b_logfT], w=[b_cumT])
            S.op("dve", lambda e: e.tensor_copy(out=augT[0:4, 0:16], in_=cumT[:, 0:16]), r=[b_cumT], w=[b_augT])
            S.op("dve", lambda e: e.tensor_tensor(out=logfT[:, 0:16], in0=cumT[:, 0:16], in1=augT[0:4, 0:16], op=ALU.subtract), r=[b_cumT, b_augT], w=[b_logfT])
            S.op("dve", lambda e: e.tensor_copy(out=augT[32:36, 0:16], in_=logfT[:, 0:16]), r=[b_logfT], w=[b_augT])
            S.op("dve", lambda e: e.memset(PS[5][:, 0:64], 0.0), w=[PSB[5]])
            for sq_ in range(NS):
                for pg in range(NPG):
                    col = sq_ * NPG + pg
                    S.dma("pool", b_lfp, lambda e, pg=pg, col=col: e.indirect_dma_start(out=lfp[:, pg, :], out_offset=None, in_=cl, in_offset=IOA(ap=idx_all[:, col:col + 1], axis=0), element_offset=eo * 4),
                          r=[b_idx], w=[b_lfp] if pg in (0, NPG - 1) else [])
                lff = lfp[:, :, :].rearrange("p g h -> p (g h)")
                S.op("pe", lambda e: e.matmul(PS[7][:, 0:NPG * 4], tristrict[:, :], lff, start=True, stop=True), r=[b_tristrict, b_lfp], w=[PSB[7]])
                S.op("pe", lambda e: e.matmul(PS[7][:, 256:256 + NPG * 4], onesf[:, :], lff, start=True, stop=True), r=[b_onesf, b_lfp], w=[PSB[7]])
                totv = PS[7][:, 256:256 + NPG * 4].rearrange("p (g h) -> p g h", h=4)
                w1v = PS[7][:, 0:NPG * 4].rearrange("p (g h) -> p g h", h=4)
                for h in range(4):
                    S.op("dve", lambda e, h=h: e.tensor_tensor_scan(out=cumtmp[:, h, :], data0=onesf[:, 0:NPG] if NPG <= 128 else None, data1=totv[:, :, h], initial=0.0, op0=ALU.mult, op1=ALU.add),
                         r=[PSB[7], b_onesf], w=[b_cumtmp])
                    S.op("dve", lambda e, h=h: e.tensor_scalar(out=cumtmp[:, h, :], in0=cumtmp[:, h, :], scalar1=-1.0, scalar2=cumtmp[:, h, NPG - 1:NPG], op0=ALU.mult, op1=ALU.add), r=[b_cumtmp], w=[b_cumtmp])
                    S.op("dve", lambda e, h=h: e.tensor_tensor(out=biasp[:, :, h], in0=cumtmp[:, h, :], in1=w1v[:, :, h], op=ALU.add), r=[b_cumtmp, PSB[7]], w=[b_biasp])
                S.op("act", lambda e: e.activation(out=biasp[:, :, :], in_=biasp[:, :, :], func=AF.Exp), r=[b_biasp], w=[b_biasp])
                for pg in range(NPG):
                    col = sq_ * NPG + pg
                    i = pg_rr[0] % 2; pg_rr[0] += 1
                    S.dma("pool", b_Kpg[i], lambda e, i=i, col=col: e.indirect_dma_start(out=Kpg[i][:, :], out_offset=None, in_=ck, in_offset=IOA(ap=idx_all[:, col:col + 1], axis=0), element_offset=eo * 256), r=[b_idx], w=[b_Kpg[i]])
                    S.dma("pool", b_Vpg[i], lambda e, i=i, col=col: e.indirect_dma_start(out=Vpg[i][:, :], out_offset=None, in_=cv, in_offset=IOA(ap=idx_all[:, col:col + 1], axis=0), element_offset=eo * 256), r=[b_idx], w=[b_Vpg[i]])
                    ps, b_ps = mmbank()
                    for hf in range(2):
                        S.op("pe", lambda e, i=i, hf=hf, ps=ps: e.transpose(ps[:, hf * 128:(hf + 1) * 128], Kpg[i][:, hf * 128:(hf + 1) * 128], ident[:, :]), r=[b_Kpg[i], b_ident], w=[b_ps])
                    S.op("act", lambda e, i=i, ps=ps: e.activation(out=KTp[i][:, :, :], in_=ps[:, 0:256].rearrange("p (a b) -> p a b", a=2), func=AF.Identity), r=[b_ps], w=[b_KTp[i]])
                    S.op("dve", lambda e, i=i, pg=pg: e.tensor_tensor(out=Vpp[i][:, :, 0:64], in0=Vpg[i][:, :].rearrange("p (h d) -> p h d", h=4),
                                                                    in1=biasp[:, pg, :].unsqueeze(2).broadcast_to([128, 4, 64]), op=ALU.mult), r=[b_Vpg[i], b_biasp], w=[b_Vpp[i]])
                    S.op("dve", lambda e, i=i, pg=pg: e.tensor_copy(out=Vpp[i][:, :, 64:128], in_=biasp[:, pg, :].unsqueeze(2).broadcast_to([128, 4, 64])), r=[b_biasp], w=[b_Vpp[i]])
                    bpair = (3, 4) if (pg % 2 == 0) else (6, 7)
                    for h in range(4):
                        hp = (h % 2) * 64; pair = h // 2
                        sbk = bpair[h % 2]; c4 = pair * 4
                        psS, b_psS = PS[sbk], PSB[sbk]
                        S.op("pe", lambda e, i=i, hp=hp, pair=pair, psS=psS, c4=c4, sq_=sq_: e.matmul(psS[:, c4:c4 + 4], KTp[i][hp:hp + 64, pair, :], qT[hp:hp + 64, pair, sq_ * 4:sq_ * 4 + 4], start=True, stop=False, skip_group_check=True),
                             r=[b_KTp[i], b_qT], w=[b_psS])
                        S.op("pe", lambda e, psS=psS, c4=c4: e.matmul(psS[:, c4:c4 + 4], identb[:, :], zerob[:, 0:4], start=False, stop=False, skip_group_check=True), r=[b_identb, b_maskb], w=[b_psS])
                        S.op("pe", lambda e, h=h, psS=psS, c4=c4, sq_=sq_: e.matmul(psS[:, c4:c4 + 4], sel[0:36, h, :], augT[0:36, sq_ * 4:sq_ * 4 + 4], start=False, stop=True, skip_group_check=True),
                             r=[b_sel, b_augT], w=[b_psS])
                    for par in range(2):
                        sbk = bpair[par]
                        S.op("act", lambda e, i=i, par=par, sbk=sbk: e.activation(out=PTs[i][:, par, :], in_=PS[sbk][:, 0:8], func=AF.Exp), r=[PSB[sbk]], w=[b_PTs[i]])
                    for h in range(4):
                        oc = h * 16 + sq_ * 4
                        S.op("pe", lambda e, i=i, h=h, oc=oc: e.matmul(PS[5][:, oc:oc + 4], Vpp[i][:, h, :], PTs[i][:, h % 2, (h // 2) * 4:(h // 2) * 4 + 4], start=False, stop=False, skip_group_check=True), r=[b_Vpp[i], b_PTs[i]], w=[PSB[5]])
            for h in range(4):
                hp = (h % 2) * 64; pair = h // 2
                sbk = 3 + (h % 2)
                psS, b_psS = PS[sbk], PSB[sbk]
                S.op("pe", lambda e, hp=hp, pair=pair, psS=psS: e.matmul(psS[0:16, 16:32], KTb[hp:hp + 64, pair, 0:16], qT[hp:hp + 64, pair, 0:16], start=True, stop=False, skip_group_check=True), r=[b_KTb, b_qT], w=[b_psS])
                S.op("pe", lambda e, psS=psS: e.matmul(psS[0:16, 16:32], identb[:, 0:16], maskS[:, 0:16], start=False, stop=False, skip_group_check=True), r=[b_identb, b_maskS], w=[b_psS])
                S.op("pe", lambda e, h=h, psS=psS: e.matmul(psS[0:16, 16:32], sel[0:36, h, 0:16], augT[0:36, 0:16], start=False, stop=True, skip_group_check=True), r=[b_sel, b_augT], w=[b_psS])
                S.op("act", lambda e, h=h, psS=psS: e.activation(out=PTo[0:16, h, :], in_=psS[0:16, 16:32], func=AF.Exp, bias=ncown[0:16, h:h + 1]), r=[b_psS, b_ncown], w=[b_PTo])
                S.op("pe", lambda e, h=h: e.matmul(PS[5][:, h * 16:(h + 1) * 16], Vown[0:16, h, :], PTo[0:16, h, :], start=False, stop=(h == 3), skip_group_check=True), r=[b_Vown, b_PTo], w=[PSB[5]])
            S.op("dve", lambda e: e.reciprocal(rdn[64:128, 0:64], PS[5][64:128, 0:64]), r=[PSB[5]], w=[b_rdn])
            S.op("dve", lambda e: e.tensor_copy(out=rdn[0:64, 0:64], in_=rdn[64:128, 0:64]), r=[b_rdn], w=[b_rdn])
            for h in range(4):
                hp = (h % 2) * 64; pair = h // 2
                S.op("dve", lambda e, h=h, hp=hp, pair=pair: e.tensor_tensor(out=mixT[hp:hp + 64, 2 + pair, 0:16], in0=PS[5][0:64, h * 16:(h + 1) * 16], in1=rdn[0:64, h * 16:(h + 1) * 16], op=ALU.mult),
                     r=[PSB[5], b_rdn], w=[b_mixT[2 + pair]])

        def ssd_sample(l):
            for sq_ in range(NS):
                conv_ssd(l, lambda k, j, sq_=sq_: xbcS[:, k, sq_, j:j + 4], b_xbcS, sq_ * 4, 4)
            for sq_ in range(NS):
                for i in range(2):
                    S.dma("pool", b_hst, lambda e, sq_=sq_, i=i: e.dma_start(out=hst[:], in_=st_ssd[l, sq_][i * 128:(i + 1) * 128, :]), w=[b_hst])
                    ps, b_ps = mmbank()
                    S.op("pe", lambda e, ps=ps: e.transpose(ps[:, 0:128], hst[:, :], ident[:, :]), r=[b_hst, b_ident], w=[b_ps])
                    S.op("dve", lambda e, ps=ps, i=i: e.tensor_copy(out=HT[:, i * 128:(i + 1) * 128], in_=ps[:, 0:128]), r=[b_ps], w=[b_HT])
                for h in range(4):
                    S.op("pool", lambda e, h=h: e.tensor_copy(out=HTz[:, h, (h % 2) * 64:(h % 2) * 64 + 64], in_=HT[:, h * 64:(h + 1) * 64]), r=[b_HT], w=[b_HTz])
                ssd_chunk(l, sq_ * 4, 4, False)
                for i in range(2):
                    ps, b_ps = mmbank()
                    S.op("pe", lambda e, i=i, ps=ps: e.transpose(ps[:, 0:128], HT[:, i * 128:(i + 1) * 128], ident[:, :]), r=[b_HT, b_ident], w=[b_ps])
                    S.op("act", lambda e, ps=ps: e.activation(out=ysd[:, 0:128], in_=ps[:, 0:128], func=AF.Identity), r=[b_ps, b_ssd_st], w=[b_ysd])
                    finals.append(S.dma("pool", b_ssd_st, lambda e, sq_=sq_, i=i: e.dma_start(out=ssd_s[l, sq_][i * 128:(i + 1) * 128, :], in_=ysd[:, 0:128]), r=[b_ysd], w=[b_ssd_st]))

        def sample_conv_outputs(l):
            for sq_ in range(NS):
                for k in range(6):
                    finals.append(S.dma("pool", cvo_b, lambda e, sq_=sq_, k=k: e.dma_start(out=ssdconv_s[l, sq_][:, k * 128:(k + 1) * 128].rearrange("j p -> p j"), in_=xbcS[:, k, sq_, 4:7], allow_slow_non_contiguous=True), r=[b_xbcS], w=[]))
                for k in range(2):
                    finals.append(S.dma("pool", cvo_b, lambda e, sq_=sq_, k=k: e.dma_start(out=sconv_s[l, sq_][:, k * 128:(k + 1) * 128].rearrange("j p -> p j"), in_=cshS[:, k, sq_, 4:6], allow_slow_non_contiguous=True), r=[b_cshS], w=[cvo_b] if k == 1 else []))

        kv_rr = [0]; ost_rr = [0]

        def load_xT_layer0(c):
            for ts in range(4):
                st, b_st = ostage[ost_rr[0] % 2], b_ostage[ost_rr[0] % 2]; b_st2 = b_ostage_st[ost_rr[0] % 2]; ost_rr[0] += 1
                r0 = c * 512 + ts * 128
                S.dma("sp", b_st, lambda e, st=st, r0=r0: e.dma_start(out=st[:], in_=xp[r0:r0 + 128, :]), w=[b_st], r=[b_st2])
                for k in range(8):
                    ps, b_ps = mmbank()
                    S.op("pe", lambda e, st=st, k=k, ps=ps: e.transpose(ps[:, 0:128], st[:, k * 128:(k + 1) * 128], ident[:, :]), r=[b_st, b_ident], w=[b_ps])
                    S.op("act" if k % 2 else "dve", (lambda e, k=k, ps=ps, ts=ts: e.activation(out=xT[:, k, ts * 128:(ts + 1) * 128], in_=ps[:, 0:128], func=AF.Identity)) if k % 2 else
                         (lambda e, k=k, ps=ps, ts=ts: e.tensor_copy(out=xT[:, k, ts * 128:(ts + 1) * 128], in_=ps[:, 0:128])), r=[b_ps], w=[b_xT[k]])

        def matgroup(kind, l, m0, nm, fn_each):
            slot, b_slot = next_group(kind, l, m0)
            for mi in range(nm):
                fn_each(mi, slot, b_slot)

        def block(l, c, smp=False):
            N = 16 if smp else 512
            NSUB = 1 if smp else 4
            MTK = 16 if smp else 128
            first = (c == 0) and not smp
            last_layer = (l == L - 1)
            t0 = c * 512
            xX, b_xX = (xTs, b_xTs) if smp else (xT, b_xT)
            if smp:
                if l == 0:
                    load_xTs()
            elif l == 0:
                load_xT_layer0(c)
            else:
                S.dma("sp", b_xT[0], lambda e: e.dma_start(out=xT[:], in_=x_scr[c]), r=[b_xscr[c]], w=b_xT)
            KSTOP = int(os.environ.get("K_STOP", "99"))
            rmsnorm_fm(xX, b_xX, 8, N, lambda k: gmix[:, l, k:k + 1], xnT, b_xnT)
            def inproj_each(gi):
                def f(mi, slot, b_slot):
                    nm, c0, wd = MT_IN[gi * 4 + mi]
                    idx = sum(1 for (n2, _, _) in MT_IN[:gi * 4 + mi] if n2 == nm)
                    ps, b_ps = mmbank()
                    for k in range(8):
                        S.op("pe", lambda e, slot=slot, k=k, ps=ps, wd=wd: e.matmul(ps[0:wd, 0:N], slot[:, mi, k, 0:wd], xnT[:, k, 0:N], start=(k == 0), stop=(k == 7)),
                             r=[b_slot, b_xnT[k]], w=[b_ps])
                    if nm == "u":
                        S.op("dve", lambda e: e.tensor_copy(out=uTb[:, idx, 0:N], in_=ps[:, 0:N]), r=[b_ps], w=[b_uTb])
                    elif nm == "q":
                        S.op("act", lambda e: e.activation(out=qT[:, idx, 0:N], in_=ps[:, 0:N], func=AF.Identity, scale=0.125), r=[b_ps], w=[b_qT])
                    elif nm == "k":
                        S.op("dve", lambda e: e.tensor_copy(out=KTb[:, idx, 0:N], in_=ps[:, 0:N]), r=[b_ps], w=[b_KTb])
                    elif nm == "v":
                        pass
                    elif nm == "fr":
                        S.op("act", lambda e: e.activation(out=logfT[:, 0:N], in_=ps[0:4, 0:N], func=AF.Exp, scale=-1.0, bias=nbfcol[:, l:l + 1]), r=[b_ps, b_pcol], w=[b_logfT])
                        S.op("act", lambda e: e.activation(out=logfT[:, 0:N], in_=logfT[:, 0:N], func=AF.Ln, bias=onesf[0:4, 0:1]), r=[b_logfT, b_onesf], w=[b_logfT])
                        S.op("dve", lambda e: e.tensor_scalar(out=logfT[:, 0:N], in0=logfT[:, 0:N], scalar1=-1.0, scalar2=None, op0=ALU.mult), r=[b_logfT], w=[b_logfT])
                    elif nm == "z":
                        S.op("act", lambda e: e.activation(out=szT[:, idx, 0:N], in_=ps[:, 0:N], func=AF.Silu), r=[b_ps], w=[b_szT])
                    elif nm == "xbc":
                        if smp:
                            S.op("act", lambda e: e.activation(out=xbcS[:, idx, :, 3:7], in_=ps[:, 0:16].rearrange("p (s t) -> p s t", s=4), func=AF.Identity), r=[b_ps], w=[b_xbcS])
                        else:
                            S.op("act", lambda e: e.activation(out=xbcT[:, idx, 3:3 + N], in_=ps[:, 0:N], func=AF.Identity), r=[b_ps], w=[b_xbcT])
                    elif nm == "dt":
                        S.op("act", lambda e: e.activation(out=daT[0:4, 0:N], in_=ps[0:4, 0:N], func=AF.Exp, bias=dtbcol[:, l:l + 1]), r=[b_ps, b_pcol], w=[b_daT])
                        S.op("act", lambda e: e.activation(out=daT[0:4, 0:N], in_=daT[0:4, 0:N], func=AF.Ln, bias=onesf[0:4, 0:1]), r=[b_daT, b_onesf], w=[b_daT])
                        S.op("dve", lambda e: e.tensor_scalar(out=daT[32:36, 0:N], in0=daT[0:4, 0:N], scalar1=acol[:, l:l + 1], scalar2=None, op0=ALU.mult), r=[b_daT, b_pcol], w=[b_daT])
                    elif nm == "sb":
                        S.op("act", lambda e: e.activation(out=sbT[:, idx, 0:N], in_=ps[:, 0:N], func=AF.Identity), r=[b_ps], w=[b_sbT])
                    elif nm == "sc":
                        if smp:
                            S.op("act", lambda e: e.activation(out=cshS[:, idx, :, 2:6], in_=ps[:, 0:16].rearrange("p (s t) -> p s t", s=4), func=AF.Identity), r=[b_ps], w=[b_cshS])
                        else:
                            S.op("act", lambda e: e.activation(out=cshT[:, idx, 2:2 + N], in_=ps[:, 0:N], func=AF.Identity), r=[b_ps], w=[b_cshT])
                    elif nm == "sh":
                        if smp:
                            S.op("dve", lambda e: e.tensor_tensor(out=cshS[:, idx, :, 2:6], in0=cshS[:, idx, :, 2:6], in1=ps[:, 0:16].rearrange("p (s t) -> p s t", s=4), op=ALU.mult), r=[b_ps], w=[b_cshS])
                        else:
                            S.op("dve", lambda e: e.tensor_tensor(out=cshT[:, idx, 2:2 + N], in0=cshT[:, idx, 2:2 + N], in1=ps[:, 0:N], op=ALU.mult), r=[b_ps], w=[b_cshT])
                return f
            if smp:
                for sq_ in range(NS):
                    for k in range(6):
                        S.dma("pool", b_xbcS, lambda e, sq_=sq_, k=k: e.dma_start(out=xbcS[:, k, sq_, 0:3], in_=st_ssdconv[l, sq_][:, k * 128:(k + 1) * 128].rearrange("j p -> p j"), allow_slow_non_contiguous=True), w=[b_xbcS])
                    for k in range(2):
                        S.dma("pool", b_cshS, lambda e, sq_=sq_, k=k: e.dma_start(out=cshS[:, k, sq_, 0:2], in_=st_sconv[l, sq_][:, k * 128:(k + 1) * 128].rearrange("j p -> p j"), allow_slow_non_contiguous=True), w=[b_cshS])
            elif first:
                S.op("pool", lambda e: e.memset(xbcT[:, :, 0:3], 0.0), w=[b_xbcT])
                S.op("pool", lambda e: e.memset(cshT[:, :, 0:2], 0.0), w=[b_cshT])
            else:
                S.op("pool", lambda e: e.tensor_copy(out=xbcT[:, :, 0:3], in_=xbcT[:, :, N:N + 3]), r=[b_xbcT], w=[b_xbcT])
                S.op("pool", lambda e: e.tensor_copy(out=cshT[:, :, 0:2], in_=cshT[:, :, N:N + 2]), r=[b_cshT], w=[b_cshT])
            for gi in range(6):
                slot, b_slot = next_group("in", l, gi * 4)
                f = inproj_each(gi)
                for mi in range(4):
                    f(mi, slot, b_slot)
                if gi == 1:
                    for ts in range(NSUB):
                        ps, b_ps = mmbank()
                        for mi in range(4):
                            for k in range(8):
                                S.op("pe", lambda e, slot=slot, k=k, mi=mi, ps=ps, ts=ts: e.matmul(ps[0:MTK, mi * 128:(mi + 1) * 128], xnT[:, k, ts * MTK:(ts + 1) * MTK], slot[:, mi, k, :], start=(k == 0), stop=(k == 7)),
                                     r=[b_slot, b_xnT[k]], w=[b_ps])
                        i = kv_rr[0] % 2; kv_rr[0] += 1
                        S.op("act", lambda e, i=i, ps=ps: e.activation(out=kvst[i][0:MTK, :], in_=ps[0:MTK, :], func=AF.Identity), r=[b_ps, b_kvst_st[i]], w=[b_kvst[i]])
                        if smp:
                            S.op("dve", lambda e, ps=ps: e.tensor_copy(out=Vown[0:16, :, 0:64], in_=ps[0:16, 256:512].rearrange("p (h d) -> p h d", h=4)), r=[b_ps], w=[b_Vown])
                            finals.append(S.dma("pool", b_kvst_st[i], lambda e, i=i: e.dma_start(out=k_s[l, :, :], in_=kvst[i][0:16, 0:256]), r=[b_kvst[i]], w=[]))
                            finals.append(S.dma("pool", b_kvst_st[i], lambda e, i=i: e.dma_start(out=v_s[l, :, :], in_=kvst[i][0:16, 256:512]), r=[b_kvst[i]], w=[b_kvst_st[i]]))
                        else:
                            S.op("dve", lambda e, ps=ps, ts=ts: e.tensor_copy(out=Vpb[:, ts, :, 0:64], in_=ps[:, 256:512].rearrange("p (h d) -> p h d", h=4)), r=[b_ps], w=[b_Vpb])
                            r0 = t0 + ts * 128
                            finals.append(S.dma("pool", b_kvst_st[i], lambda e, i=i, r0=r0: e.dma_start(out=k_p[l, r0:r0 + 128, :], in_=kvst[i][:, 0:256]), r=[b_kvst[i]], w=[]))
                            finals.append(S.dma("pool", b_kvst_st[i], lambda e, i=i, r0=r0: e.dma_start(out=v_p[l, r0:r0 + 128, :], in_=kvst[i][:, 256:512]), r=[b_kvst[i]], w=[b_kvst_st[i]]))
                if gi == 2:
                    for ts in range(NSUB):
                        ps, b_ps = mmbank()
                        for k in range(8):
                            S.op("pe", lambda e, slot=slot, k=k, ps=ps, ts=ts: e.matmul(ps[0:MTK, 0:4], xnT[:, k, ts * MTK:(ts + 1) * MTK], slot[:, 0, k, 0:4], start=(k == 0), stop=(k == 7)),
                                 r=[b_slot, b_xnT[k]], w=[b_ps])
                        S.op("dve", lambda e, ps=ps, ts=ts: e.tensor_tensor(out=lf_tok[0:MTK, ts, :], in0=ps[0:MTK, 0:4], in1=bf_bc[0:MTK, l, :], op=ALU.add), r=[b_ps, b_bfbc, b_lf_st], w=[b_lf_tok])
                        S.op("act", lambda e, ts=ts: e.activation(out=lf_tok[0:MTK, ts, :], in_=lf_tok[0:MTK, ts, :], func=AF.Exp, scale=-1.0), r=[b_lf_tok], w=[b_lf_tok])
                        S.op("act", lambda e, ts=ts: e.activation(out=lf_tok[0:MTK, ts, :], in_=lf_tok[0:MTK, ts, :], func=AF.Ln, bias=onesf[0:MTK, 0:1]), r=[b_lf_tok, b_onesf], w=[b_lf_tok])
                        S.op("dve", lambda e, ts=ts: e.tensor_scalar(out=lf_tok[0:MTK, ts, :], in0=lf_tok[0:MTK, ts, :], scalar1=-1.0, scalar2=None, op0=ALU.mult), r=[b_lf_tok], w=[b_lf_tok])
                        ps2, b_ps2 = mmbank()
                        if smp:
                            S.op("pe", lambda e, ps2=ps2: e.matmul(ps2[0:16, 0:4], triS[0:16, 0:16], lf_tok[0:16, 0, :], start=True, stop=True), r=[b_triS, b_lf_tok], w=[b_ps2])
                            S.op("dve", lambda e, ps2=ps2: e.tensor_scalar(out=ncown[0:16, :], in0=ps2[0:16, 0:4], scalar1=-1.0, scalar2=None, op0=ALU.mult), r=[b_ps2], w=[b_ncown])
                            continue
                        kt = c * 4 + ts
                        S.op("pe", lambda e, ps2=ps2, ts=ts: e.matmul(ps2[:, 0:4], tri[:, :], lf_tok[:, ts, :], start=True, stop=True), r=[b_tri, b_lf_tok], w=[b_ps2])
                        S.op("pe", lambda e, ps2=ps2, ts=ts: e.matmul(ps2[:, 4:8], onesf[:, :], lf_tok[:, ts, :], start=True, stop=True), r=[b_onesf, b_lf_tok], w=[b_ps2])
                        if first and ts == 0:
                            S.op("dve", lambda e, ps2=ps2, kt=kt: e.tensor_scalar(out=negcum[:, kt, :], in0=ps2[:, 0:4], scalar1=-1.0, scalar2=None, op0=ALU.mult), r=[b_ps2], w=[b_negcum])
                            S.op("dve", lambda e, ps2=ps2: e.tensor_copy(out=carry_bc[:], in_=ps2[:, 4:8]), r=[b_ps2], w=[b_carry])
                        else:
                            S.op("dve", lambda e, ps2=ps2, kt=kt: e.scalar_tensor_tensor(out=negcum[:, kt, :], in0=ps2[:, 0:4], scalar=-1.0, in1=carry_bc[:], op0=ALU.mult, op1=ALU.subtract), r=[b_ps2, b_carry], w=[b_negcum])
                            S.op("dve", lambda e, ps2=ps2: e.tensor_tensor(out=carry_bc[:], in0=carry_bc[:], in1=ps2[:, 4:8], op=ALU.add), r=[b_ps2], w=[b_carry])
                    if smp:
                        finals.append(S.dma("pool", b_lf_st, lambda e: e.dma_start(out=logf_s[l, :, :], in_=lf_tok[0:16, 0, :]), r=[b_lf_tok], w=[b_lf_st]))
                    else:
                        finals.append(S.dma("pool", b_lf_st, lambda e: e.dma_start(out=logf_p[l, t0:t0 + 512, :].rearrange("(a p) h -> p a h", p=128), in_=lf_tok[:]), r=[b_lf_tok], w=[b_lf_st]))
            if KSTOP <= 2:
                return
            if smp:
                s5_block_sample(l)
            else:
                s5_block(l, N, 1, 512, first)
            for i in range(2):
                if smp:
                    mo = mixT[:, 6 + i, 0:16].rearrange("p (s t) -> p s t", s=4)
                    S.op("dve", lambda e, i=i, mo=mo: e.tensor_scalar(out=mo, in0=cshS[:, i, :, 0:4], scalar1=scw[:, l, i, 0:1], scalar2=None, op0=ALU.mult), r=[b_cshS, b_convp], w=[b_mixT[6 + i]])
                    for j in (1, 2):
                        S.op("dve", lambda e, i=i, j=j, mo=mo: e.scalar_tensor_tensor(out=mo, in0=cshS[:, i, :, j:j + 4], scalar=scw[:, l, i, j:j + 1], in1=mo, op0=ALU.mult, op1=ALU.add), r=[b_cshS, b_convp], w=[b_mixT[6 + i]])
                else:
                    S.op("dve", lambda e, i=i: e.tensor_scalar(out=mixT[:, 6 + i, 0:N], in0=cshT[:, i, 0:N], scalar1=scw[:, l, i, 0:1], scalar2=None, op0=ALU.mult), r=[b_cshT, b_convp], w=[b_mixT[6 + i]])
                    for j in (1, 2):
                        S.op("dve", lambda e, i=i, j=j: e.scalar_tensor_tensor(out=mixT[:, 6 + i, 0:N], in0=cshT[:, i, j:j + N], scalar=scw[:, l, i, j:j + 1], in1=mixT[:, 6 + i, 0:N], op0=ALU.mult, op1=ALU.add), r=[b_cshT, b_convp], w=[b_mixT[6 + i]])
                S.op("dve", lambda e, i=i: e.tensor_tensor(out=mixT[:, 6 + i, 0:N], in0=mixT[:, 6 + i, 0:N], in1=sbT[:, i, 0:N], op=ALU.mult), r=[b_sbT], w=[b_mixT[6 + i]])
            if smp:
                fox_sample(l)
                ssd_sample(l)
            else:
                fox_block(l, c)
                ssd_block(l, c)
            for g in range(4):
                rmsnorm_fm(mixT[:, 2 * g:2 * g + 2, :], b_mixT[2 * g:2 * g + 2], 2, N, lambda k, g=g: ggrp[:, l, g, k:k + 1], catT[:, 2 * g:2 * g + 2, :], b_catT[2 * g:2 * g + 2])
            for gi in range(2):
                slot, b_slot = next_group("out", l, gi * 4)
                for mi in range(4):
                    m = gi * 4 + mi
                    ps, b_ps = mmbank()
                    for k in range(8):
                        S.op("pe", lambda e, slot=slot, k=k, mi=mi, ps=ps: e.matmul(ps[:, 0:N], slot[:, mi, k, :], catT[:, k, 0:N], start=(k == 0), stop=(k == 7)), r=[b_slot, b_catT[k]], w=[b_ps])
                    S.op("dve", lambda e, m=m, ps=ps: e.tensor_tensor(out=xX[:, m, 0:N], in0=xX[:, m, 0:N], in1=ps[:, 0:N], op=ALU.add), r=[b_ps], w=[b_xX[m]])
            rmsnorm_fm(xX, b_xX, 8, N, lambda k: gffn[:, l, k:k + 1], xnT, b_xnT)
            for half in range(2):
                for gi in range(4 * half, 4 * half + 4):
                    slot, b_slot = next_group("up", l, gi * 4)
                    for mi in range(4):
                        m = gi * 4 + mi
                        ps, b_ps = mmbank()
                        for k in range(8):
                            S.op("pe", lambda e, k=k, mi=mi, ps=ps, slot=slot: e.matmul(ps[:, 0:N], slot[:, mi, k, :], xnT[:, k, 0:N], start=(k == 0), stop=(k == 7)), r=[b_slot, b_xnT[k]], w=[b_ps])
                        S.op("act", lambda e, m=m, ps=ps: e.activation(out=hT[:, m % 16, 0:N], in_=ps[:, 0:N], func=AF.Relu), r=[b_ps], w=[b_hT[m % 16]])
                        S.op("pool" if m % 2 else "dve", lambda e, m=m: e.tensor_tensor(out=hT[:, m % 16, 0:N], in0=hT[:, m % 16, 0:N], in1=hT[:, m % 16, 0:N], op=ALU.mult), r=[b_hT[m % 16]], w=[b_hT[m % 16]])
                for m in range(8):
                    slot, b_slot = next_group("down", l, m * 2 + half)
                    ps, b_ps = mmbank()
                    for k in range(16):
                        S.op("pe", lambda e, k=k, ps=ps, slot=slot: e.matmul(ps[:, 0:N], slot[:, 0, k, :], hT[:, k, 0:N], start=(k == 0), stop=(k == 15)), r=[b_slot, b_hT[k]], w=[b_ps])
                    S.op("dve", lambda e, m=m, ps=ps: e.tensor_tensor(out=xX[:, m, 0:N], in0=xX[:, m, 0:N], in1=ps[:, 0:N], op=ALU.add), r=[b_ps], w=[b_xX[m]])
            if smp:
                if last_layer:
                    rmsnorm_fm(xX, b_xX, 8, N, lambda k: gfin[:, k:k + 1], xX, b_xX)
                    i = ost_rr[0] % 2; ost_rr[0] += 1
                    for k in range(8):
                        ps, b_ps = mmbank()
                        S.op("pe", lambda e, k=k, ps=ps: e.transpose(ps[0:16, 0:128], xX[:, k, 0:16], ident[:, :]), r=[b_xX[k], b_ident], w=[b_ps])
                        S.op("dve", lambda e, k=k, ps=ps, i=i: e.tensor_copy(out=ostage[i][0:16, k * 128:(k + 1) * 128], in_=ps[0:16, 0:128]), r=[b_ps, b_ostage_st[i]], w=[b_ostage[i]])
                    finals.append(S.dma("pool", b_ostage_st[i], lambda e, i=i: e.dma_start(out=y_s[:, :], in_=ostage[i][0:16, :]), r=[b_ostage[i]], w=[b_ostage_st[i]]))
                return
            if not last_layer:
                S.dma("sp", b_xscr[c], lambda e: e.dma_start(out=x_scr[c], in_=xT[:]), r=b_xT, w=[b_xscr[c]])
            else:
                rmsnorm_fm(xT, b_xT, 8, N, lambda k: gfin[:, k:k + 1], xT, b_xT)
                for ts in range(4):
                    i = ost_rr[0] % 2; ost_rr[0] += 1
                    for k in range(8):
                        ps, b_ps = mmbank()
                        S.op("pe", lambda e, k=k, ps=ps, ts=ts: e.transpose(ps[:, 0:128], xT[:, k, ts * 128:(ts + 1) * 128], ident[:, :]), r=[b_xT[k], b_ident], w=[b_ps])
                        S.op("act" if k % 2 else "dve", (lambda e, k=k, ps=ps, i=i: e.activation(out=ostage[i][:, k * 128:(k + 1) * 128], in_=ps[:, 0:128], func=AF.Identity)) if k % 2 else
                             (lambda e, k=k, ps=ps, i=i: e.tensor_copy(out=ostage[i][:, k * 128:(k + 1) * 128], in_=ps[:, 0:128])), r=[b_ps, b_ostage_st[i]], w=[b_ostage[i]])
                    r0 = t0 + ts * 128
                    finals.append(S.dma("pool", b_ostage_st[i], lambda e, i=i, r0=r0: e.dma_start(out=y_p[r0:r0 + 128, :], in_=ostage[i][:]), r=[b_ostage[i]], w=[b_ostage_st[i]]))

        def fox_block(l, c):
            N = 512
            first = (c == 0)
            if c < NCH - 1:
                S.dma("sp", b_ktscr[c], lambda e: e.dma_start(out=kt_scr[c], in_=KTb[:]), r=[b_KTb], w=[b_ktscr[c]])
                S.dma("sp", b_vscr[c], lambda e: e.dma_start(out=v_scr[c], in_=Vpb[:, :, :, 0:64]), r=[b_Vpb], w=[b_vscr[c]])
            if first:
                S.op("dve", lambda e: e.tensor_tensor_scan(out=cumT[:, 0:N], data0=ones4[:, 0:N], data1=logfT[:, 0:N], initial=0.0, op0=ALU.mult, op1=ALU.add),
                     r=[b_ones4, b_logfT], w=[b_cumT])
            else:
                S.op("dve", lambda e: e.tensor_tensor(out=logfT[:, 0:1], in0=logfT[:, 0:1], in1=cumc[:, 0:1], op=ALU.add), r=[b_cumc], w=[b_logfT])
                S.op("dve", lambda e: e.tensor_tensor_scan(out=cumT[:, 0:N], data0=ones4[:, 0:N], data1=logfT[:, 0:N], initial=0.0, op0=ALU.mult, op1=ALU.add),
                     r=[b_ones4, b_logfT], w=[b_cumT])
            S.op("dve", lambda e: e.tensor_copy(out=cumc[:, 0:1], in_=cumT[:, N - 1:N]), r=[b_cumT], w=[b_cumc])
            S.op("dve", lambda e: e.tensor_copy(out=augT[0:4, 0:N], in_=cumT[:, 0:N]), r=[b_cumT], w=[b_augT])
            S.op("dve", lambda e: e.tensor_tensor(out=logfT[:, 0:N], in0=cumT[:, 0:N], in1=augT[0:4, 0:N], op=ALU.subtract), r=[b_cumT, b_augT], w=[b_logfT])
            S.op("dve", lambda e: e.tensor_copy(out=augT[32:36, 0:N], in_=logfT[:, 0:N]), r=[b_logfT], w=[b_augT])
            for pair in range(2):
                obank = {2 * pair: 5, 2 * pair + 1: 6}
                for kb in range(c + 1):
                    diag = (kb == c)
                    if diag:
                        Ks, b_Ks, Vs, b_Vs = KTb, b_KTb, Vpb, b_Vpb
                    else:
                        i = kc_rr[0] % 2; kc_rr[0] += 1
                        Ks, b_Ks, Vs, b_Vs = KTc[i], b_KTc[i], Vpc[i], b_Vpc[i]
                        S.dma("sp", b_Ks, lambda e, Ks=Ks, kb=kb: e.dma_start(out=Ks[:], in_=kt_scr[kb]), r=[b_ktscr[kb]], w=[b_Ks])
                        S.dma("sp", b_Vs, lambda e, Vs=Vs, kb=kb: e.dma_start(out=Vs[:, :, :, 0:64], in_=v_scr[kb]), r=[b_vscr[kb]], w=[b_Vs])
                    for kt in range(4):
                        qlo = kt * 128 if diag else 0
                        k0 = kt * 128
                        for h in (2 * pair, 2 * pair + 1):
                            hp = (h % 2) * 64
                            sbk = 3 + (sbank_rr[0] % 2); sbank_rr[0] += 1
                            psS, b_psS = PS[sbk], PSB[sbk]
                            S.op("pe", lambda e, Ks=Ks, hp=hp, pair=pair, k0=k0, qlo=qlo, psS=psS: e.matmul(
                                psS[:, qlo:N], Ks[hp:hp + 64, pair, k0:k0 + 128], qT[hp:hp + 64, pair, qlo:N], start=True, stop=False, skip_group_check=True),
                                r=[b_Ks, b_qT], w=[b_psS])
                            mk = maskb if diag else zerob
                            S.op("pe", lambda e, qlo=qlo, psS=psS, mk=mk: e.matmul(psS[:, qlo:qlo + 128], identb[:, :], mk[:, :], start=False, stop=False, skip_group_check=True),
                                 r=[b_identb, b_maskb], w=[b_psS])
                            S.op("pe", lambda e, h=h, qlo=qlo, psS=psS: e.matmul(psS[:, qlo:N], sel[0:36, h, :], augT[0:36, qlo:N], start=False, stop=True, skip_group_check=True),
                                 r=[b_sel, b_augT], w=[b_psS])
                            pi = pt_rr[0] % 2; pt_rr[0] += 1
                            kti = kb * 4 + kt
                            S.op("act", lambda e, pi=pi, qlo=qlo, psS=psS, kti=kti, h=h: e.activation(out=PT[pi][:, qlo:N], in_=psS[:, qlo:N], func=AF.Exp, bias=negcum[:, kti, h:h + 1]),
                                 r=[b_psS, b_negcum], w=[b_PT[pi]])
                            ob = obank[h]
                            S.op("pe", lambda e, Vs=Vs, kt=kt, h=h, pi=pi, qlo=qlo, ob=ob, kb=kb, diag=diag: e.matmul(
                                PS[ob][:, qlo:N], Vs[:, kt, h, :], PT[pi][:, qlo:N], start=(kb == 0 and kt == 0), stop=(diag and kt == 3), skip_group_check=True),
                                r=[b_Vs, b_PT[pi]], w=[PSB[ob]])
                for h in (2 * pair, 2 * pair + 1):
                    hp = (h % 2) * 64
                    ob = obank[h]
                    S.op("dve", lambda e, ob=ob: e.reciprocal(rdn[64:128, 0:N], PS[ob][64:128, 0:N]), r=[PSB[ob]], w=[b_rdn])
                    S.op("dve", lambda e: e.tensor_copy(out=rdn[0:64, 0:N], in_=rdn[64:128, 0:N]), r=[b_rdn], w=[b_rdn])
                    S.op("dve", lambda e, ob=ob, hp=hp, pair=pair: e.tensor_tensor(out=mixT[hp:hp + 64, 2 + pair, 0:N], in0=PS[ob][0:64, 0:N], in1=rdn[0:64, 0:N], op=ALU.mult),
                         r=[PSB[ob], b_rdn], w=[b_mixT[2 + pair]])

        def conv_ssd(l, xin_fn, xin_b, c_out0, Tq):
            for k in range(6):
                S.op("dve", lambda e, k=k: e.tensor_scalar(out=gtmp[:, 0, 0:Tq], in0=xin_fn(k, 0), scalar1=convw[:, l, k, 0:1], scalar2=None, op0=ALU.mult),
                     r=[xin_b, b_convp], w=[b_gtmp])
                for j in (1, 2, 3):
                    S.op("dve", lambda e, k=k, j=j: e.scalar_tensor_tensor(out=gtmp[:, 0, 0:Tq], in0=xin_fn(k, j), scalar=convw[:, l, k, j:j + 1], in1=gtmp[:, 0, 0:Tq], op0=ALU.mult, op1=ALU.add),
                         r=[xin_b, b_convp], w=[b_gtmp])
                S.op("act", lambda e, k=k: e.activation(out=xbcA[:, k, c_out0:c_out0 + Tq], in_=gtmp[:, 0, 0:Tq], func=AF.Silu, bias=convb[:, l, k:k + 1]),
                     r=[b_gtmp, b_convp], w=[b_xbcA])

        def ssd_chunk(l, c0, Lc, zero_state):
            p5, b5, p6, b6, p7, b7 = PS[5], PSB[5], PS[6], PSB[6], PS[7], PSB[7]
            for i in range(4):
                S.op("pe", lambda e, i=i: e.matmul(p5[0:Lc, i * 128:(i + 1) * 128], xbcA[:, i, c0:c0 + Lc], identb[:, :], start=True, stop=True), r=[b_xbcA, b_identb], w=[b5])
            S.op("pe", lambda e: e.matmul(p7[0:Lc, 300:336], daT[0:36, c0:c0 + Lc], ident[0:36, 0:36], start=True, stop=True), r=[b_daT, b_ident], w=[b7])
            S.op("act", lambda e: e.activation(out=xs_tok[0:Lc, :], in_=p5[0:Lc, 0:256], func=AF.Identity), r=[b5], w=[b_tok])
            S.op("dve", lambda e: e.tensor_copy(out=B_tok[0:Lc, :], in_=p5[0:Lc, 256:512]), r=[b5], w=[b_tok])
            S.op("dve", lambda e: e.tensor_copy(out=da_tok[0:Lc, :], in_=p7[0:Lc, 300:336]), r=[b7], w=[b_tok])
            S.op("pe", lambda e: e.matmul(p7[0:Lc, 256:260], tri[0:Lc, 0:Lc], da_tok[0:Lc, 32:36], start=True, stop=True), r=[b_tri, b_tok], w=[b7])
            S.op("pe", lambda e: e.matmul(p7[:, 260:264], onesf[0:Lc, 0:128], da_tok[0:Lc, 32:36], start=True, stop=True), r=[b_onesf, b_tok], w=[b7])
            S.op("dve", lambda e: e.tensor_scalar(out=sm[0:Lc, 0:4], in0=p7[0:Lc, 256:260], scalar1=-1.0, scalar2=None, op0=ALU.mult), r=[b7], w=[b_sm])
            S.op("dve", lambda e: e.tensor_tensor(out=sm[0:Lc, 12:16], in0=p7[0:Lc, 260:264], in1=sm[0:Lc, 0:4], op=ALU.add), r=[b7], w=[b_sm])
            S.op("act", lambda e: e.activation(out=sm[0:Lc, 4:8], in_=sm[0:Lc, 12:16], func=AF.Exp), r=[b_sm], w=[b_sm])
            S.op("act", lambda e: e.activation(out=sm[:, 8:12], in_=p7[:, 260:264], func=AF.Exp), r=[b7], w=[b_sm])
            for h in range(4):
                S.op("dve", lambda e, h=h: e.tensor_scalar(out=xdtz[0:Lc, h, (h % 2) * 64:(h % 2) * 64 + 64], in0=xs_tok[0:Lc, h * 64:(h + 1) * 64], scalar1=da_tok[0:Lc, h:h + 1], scalar2=None, op0=ALU.mult),
                     r=[b_tok], w=[b_xdt])
                S.op("pool", lambda e, h=h: e.tensor_scalar(out=xdtd[0:Lc, h * 64:(h + 1) * 64], in0=xdtz[0:Lc, h, (h % 2) * 64:(h % 2) * 64 + 64], scalar1=sm[0:Lc, 4 + h:5 + h], scalar2=None, op0=ALU.mult),
                     r=[b_xdt, b_sm], w=[b_xdt])
            for h in range(4):
                S.op("dve", lambda e, h=h: e.tensor_scalar(out=arep[0:Lc, h, :], in0=onesf[0:Lc, :], scalar1=da_tok[0:Lc, 32 + h:33 + h], scalar2=None, op0=ALU.mult), r=[b_tok, b_onesf], w=[b_arep])
            for h in range(4):
                S.op("pe", lambda e, h=h: e.matmul(p6[:, h * 128:h * 128 + Lc], arep[0:Lc, h, :], tri[0:Lc, 0:Lc], start=True, stop=True), r=[b_arep, b_tri], w=[b6])
            p6v = p6[0:Lc, :].rearrange("p (h n) -> p h n", h=4)[:, :, 0:Lc]
            p6f = p6[:, :].rearrange("p (h n) -> p h n", h=4)[:, :, 0:Lc]
            S.op("act", lambda e: e.activation(out=ea[:, :, 0:Lc], in_=p6f, func=AF.Exp), r=[b6], w=[b_ea])
            S.op("dve", lambda e: e.tensor_tensor(out=dec[0:Lc, :, 0:Lc], in0=p6v, in1=maskf[0:Lc, 0:Lc].unsqueeze(1).broadcast_to([Lc, 4, Lc]), op=ALU.add), r=[b6, b_maskf], w=[b_dec])
            for h in range(4):
                S.op("act", lambda e, h=h: e.activation(out=dec[0:Lc, h, 0:Lc], in_=dec[0:Lc, h, 0:Lc], func=AF.Exp, bias=sm[0:Lc, h:h + 1]), r=[b_dec, b_sm], w=[b_dec])
            for g in range(2):
                S.op("pe", lambda e, g=g: e.matmul(p7[0:Lc, g * 128:g * 128 + Lc], xbcA[:, 2 + g, c0:c0 + Lc], xbcA[:, 4 + g, c0:c0 + Lc], start=True, stop=True), r=[b_xbcA], w=[b7])
            for g in range(2):
                S.op("dve", lambda e, g=g: e.tensor_tensor(out=MTt[0:Lc, 2 * g:2 * g + 2, 0:Lc], in0=dec[0:Lc, 2 * g:2 * g + 2, 0:Lc],
                                                           in1=p7[0:Lc, g * 128:g * 128 + Lc].unsqueeze(1).broadcast_to([Lc, 2, Lc]), op=ALU.mult), r=[b_dec, b7], w=[b_MT])
            if not zero_state:
                for g in range(2):
                    S.op("pool", lambda e, g=g: e.tensor_tensor(out=CdT[:, 2 * g:2 * g + 2, 0:Lc], in0=ea[:, 2 * g:2 * g + 2, 0:Lc],
                                                                in1=xbcA[:, 4 + g, c0:c0 + Lc].unsqueeze(1).broadcast_to([128, 2, Lc]), op=ALU.mult), r=[b_ea, b_xbcA], w=[b_CdT])
            for i in range(2):
                ps, b_ps = mmbank()
                n = 0
                tot = 2 if zero_state else 4
                for h in (2 * i, 2 * i + 1):
                    S.op("pe", lambda e, h=h, ps=ps, n=n, tot=tot: e.matmul(ps[:, 0:Lc], xdtz[0:Lc, h, :], MTt[0:Lc, h, 0:Lc], start=(n == 0), stop=(n == tot - 1)), r=[b_xdt, b_MT], w=[b_ps])
                    n += 1
                    if not zero_state:
                        S.op("pe", lambda e, h=h, ps=ps, n=n, tot=tot: e.matmul(ps[:, 0:Lc], HTz[:, h, :], CdT[:, h, 0:Lc], start=(n == 0), stop=(n == tot - 1)), r=[b_HTz, b_CdT], w=[b_ps])
                        n += 1
                S.op("dve", lambda e, i=i, ps=ps: e.scalar_tensor_tensor(out=ysd[:, 0:Lc], in0=xbcA[:, i, c0:c0 + Lc], scalar=dcol[:, l, i:i + 1], in1=ps[:, 0:Lc], op0=ALU.mult, op1=ALU.add),
                     r=[b_xbcA, b_ps, b_convp], w=[b_ysd])
                S.op("dve", lambda e, i=i: e.tensor_tensor(out=mixT[:, 4 + i, c0:c0 + Lc], in0=ysd[:, 0:Lc], in1=szT[:, i, c0:c0 + Lc], op=ALU.mult), r=[b_ysd, b_szT], w=[b_mixT[4 + i]])
            ps, b_ps = mmbank()
            for g in range(2):
                S.op("pe", lambda e, g=g, ps=ps: e.matmul(ps[:, g * 128:(g + 1) * 128], B_tok[0:Lc, g * 128:(g + 1) * 128], xdtd[0:Lc, g * 128:(g + 1) * 128], start=True, stop=True), r=[b_tok, b_xdt], w=[b_ps])
            if zero_state:
                S.op("dve", lambda e, ps=ps: e.tensor_copy(out=HT[:, :], in_=ps[:, 0:256]), r=[b_ps], w=[b_HT])
            else:
                S.op("dve", lambda e: e.tensor_tensor(out=HT[:, :].rearrange("p (h q) -> p h q", h=4), in0=HT[:, :].rearrange("p (h q) -> p h q", h=4),
                                                       in1=sm[:, 8:12].unsqueeze(2).broadcast_to([128, 4, 64]), op=ALU.mult), r=[b_sm], w=[b_HT])
                S.op("dve", lambda e, ps=ps: e.tensor_tensor(out=HT[:, :], in0=HT[:, :], in1=ps[:, 0:256], op=ALU.add), r=[b_ps], w=[b_HT])
            for h in range(4):
                S.op("pool", lambda e, h=h: e.tensor_copy(out=HTz[:, h, (h % 2) * 64:(h % 2) * 64 + 64], in_=HT[:, h * 64:(h + 1) * 64]), r=[b_HT], w=[b_HTz])

        def ssd_block(l, c):
            first = (c == 0)
            conv_ssd(l, lambda k, j: xbcT[:, k, j:j + 512], b_xbcT, 0, 512)
            for sc in range(4):
                ssd_chunk(l, sc * 128, 128, first and sc == 0)

        cvst = sb("cvst", [128, 8, 3]); b_cvst = Buf("cvst")
        s5o = sb("s5o", [128, 8, 2]); b_s5o = Buf("s5o"); b_s5o_st = Buf("s5o_st"); b_ssd_st = Buf("ssd_st")
        cvo_b = Buf("cvo")
        KSTOP = int(os.environ.get("K_STOP", "99"))
        for l in range(L if KSTOP > 1 else 0):
            if not os.environ.get("K_NOS5"):
                s5_tables(l)
            for c in range(NCH):
                block(l, c)
            if KSTOP <= 3:
                continue
            for i in range(2):
                ps, b_ps = mmbank()
                S.op("pe", lambda e, i=i, ps=ps: e.matmul(ps[:, 0:128], HT[:, i * 128:(i + 1) * 128], ident[:, :], start=True, stop=True), r=[b_HT, b_ident], w=[b_ps])
                S.op("act", lambda e, ps=ps: e.activation(out=ysd[:, 0:128], in_=ps[:, 0:128], func=AF.Identity), r=[b_ps, b_ssd_st], w=[b_ysd])
                finals.append(S.dma("pool", b_ssd_st, lambda e, l=l, i=i: e.dma_start(out=ssd_p[l, i * 128:(i + 1) * 128, :], in_=ysd[:, 0:128]), r=[b_ysd], w=[b_ssd_st]))
            if l == 0:
                dbg_dump("zr", zr[:].rearrange("p a b -> p (a b)"), 1024, [b_z])
                dbg_dump("zi", zi[:].rearrange("p a b -> p (a b)"), 1024, [b_z])
                dbg_dump("wr", wr_[:].rearrange("p a b -> p (a b)"), 1024, [b_w])
                dbg_dump("wi", wi_[:].rearrange("p a b -> p (a b)"), 1024, [b_w])
                dbg_dump("Xr", Xr[:].rearrange("p a b -> p (a b)"), 32, [b_X])
                dbg_dump("Xi", Xi[:].rearrange("p a b -> p (a b)"), 32, [b_X])
            S.op("dve", lambda e: e.tensor_copy(out=s5o[:, :, 0:1], in_=Xr[:, :, 0:1]), r=[b_X, b_s5o_st], w=[b_s5o])
            S.op("dve", lambda e: e.tensor_copy(out=s5o[:, :, 1:2], in_=Xi[:, :, 0:1]), r=[b_X], w=[b_s5o])
            finals.append(S.dma("pool", b_s5o_st, lambda e, l=l: e.dma_start(out=s5_p[l].rearrange("(j g) p r -> (g p) j r", g=2), in_=s5o[:]), r=[b_s5o], w=[b_s5o_st]))
            S.op("dve", lambda e: e.tensor_copy(out=cvst[:, 0:6, :], in_=xbcT[:, :, 512:515]), r=[b_xbcT, cvo_b], w=[b_cvst])
            S.op("dve", lambda e: e.tensor_copy(out=cvst[:, 6:8, 0:2], in_=cshT[:, :, 512:514]), r=[b_cshT], w=[b_cvst])
            for k in range(6):
                finals.append(S.dma("pool", cvo_b, lambda e, l=l, k=k: e.dma_start(out=ssdconv_p[l][:, k * 128:(k + 1) * 128].rearrange("j p -> p j"), in_=cvst[:, k, 0:3], allow_slow_non_contiguous=True), r=[b_cvst], w=[]))
            for k in range(2):
                finals.append(S.dma("pool", cvo_b, lambda e, l=l, k=k: e.dma_start(out=sconv_p[l][:, k * 128:(k + 1) * 128].rearrange("j p -> p j"), in_=cvst[:, 6 + k, 0:2], allow_slow_non_contiguous=True), r=[b_cvst], w=[cvo_b] if k == 1 else []))
            pc_flush_and_next()
            if cfg.sample:
                block(l, NCH, smp=True)
                sample_conv_outputs(l)
        for l in range(L):
            wt_ready[l].val = S.dcnt[id(b_wt[l])]
        print("NOPS", S.nrec, "lastline", S.lastline, "NSEM", S.nsem, flush=True)
        S.emit(finals)
    return nc


def kernel(**inputs):
    cfg = Cfg(sample=True)
    nc = build_program(cfg)
    consts = host_consts()
    in_maps = []
    wnames = ["w_in", "w_out", "w_up", "w_down", "norm_mix_g", "norm_ffn_g", "norm_final_g", "s5_lam_re", "s5_lam_im",
              "s5_log_dt", "s5_b_re", "s5_b_im", "s5_c_re", "s5_c_im", "s5_d", "s5_w_glu", "s5_norm_g", "fox_b_f",
              "fox_norm_g", "ssd_conv_w", "ssd_conv_b", "ssd_dt_bias", "ssd_a_log", "ssd_d", "ssd_norm_g", "sc_conv_w", "sc_norm_g"]
    Ld = 4
    shared = {n: np.ascontiguousarray(inputs[n]) for n in wnames}
    shared["cache_k"] = np.ascontiguousarray(inputs["cache_k"]).reshape(-1, 256)
    shared["cache_v"] = np.ascontiguousarray(inputs["cache_v"]).reshape(-1, 256)
    shared["cache_logf"] = np.ascontiguousarray(inputs["cache_logf"]).reshape(-1, 4)
    for k, v in consts.items():
        shared["c_" + k] = v
    for core in range(8):
        m = dict(shared)
        m["xp"] = np.ascontiguousarray(inputs["x_prompt"][core // 2])
        sl = slice(4 * core, 4 * core + 4)
        m["xs"] = np.ascontiguousarray(inputs["x_sample"][sl]).reshape(16, 1024)
        m["state_s5"] = np.ascontiguousarray(inputs["state_s5"][:, sl])
        m["state_ssd"] = np.ascontiguousarray(inputs["state_ssd"][:, sl]).reshape(Ld, 4, 256, 128)
        m["state_ssd_conv"] = np.ascontiguousarray(inputs["state_ssd_conv"][:, sl])
        m["state_sconv"] = np.ascontiguousarray(inputs["state_sconv"][:, sl])
        m["page_table"] = np.ascontiguousarray(inputs["page_table"][sl]).astype(np.int32)
        in_maps.append(m)
    res = run_bass_kernel_spmd(nc, in_maps, core_ids=list(range(8))).results

    def st(name, shape):
        return np.stack([res[2 * s][name].reshape(shape) for s in range(4)], axis=0)

    def ss(name, shape, axis):
        return np.ascontiguousarray(np.concatenate([res[c][name].reshape(shape) for c in range(8)], axis=axis))
    y_prompt = st("y_p", (4096, 1024))
    k_prompt = np.ascontiguousarray(st("k_p", (Ld, 4096, 4, 64)).transpose(1, 0, 2, 3, 4))
    v_prompt = np.ascontiguousarray(st("v_p", (Ld, 4096, 4, 64)).transpose(1, 0, 2, 3, 4))
    logf_prompt = np.ascontiguousarray(st("logf_p", (Ld, 4096, 4)).transpose(1, 0, 2, 3))
    s5_prompt = np.ascontiguousarray(st("s5_p", (Ld, 16, 64, 2)).transpose(1, 0, 2, 3, 4))
    ssd_prompt = np.ascontiguousarray(st("ssd_p", (Ld, 4, 64, 128)).transpose(1, 0, 2, 3, 4))
    ssd_conv_prompt = np.ascontiguousarray(st("ssdconv_p", (Ld, 3, 768)).transpose(1, 0, 2, 3))
    sconv_prompt = np.ascontiguousarray(st("sconv_p", (Ld, 2, 256)).transpose(1, 0, 2, 3))
    y_sample = ss("y_s", (4, 4, 1024), 0)
    k_sample = ss("k_s", (Ld, 4, 4, 4, 64), 1)
    v_sample = ss("v_s", (Ld, 4, 4, 4, 64), 1)
    logf_sample = ss("logf_s", (Ld, 4, 4, 4), 1)
    s5_sample = ss("s5_s", (Ld, 4, 16, 64, 2), 1)
    ssd_sample = ss("ssd_s", (Ld, 4, 4, 64, 128), 1)
    ssd_conv_sample = ss("ssdconv_s", (Ld, 4, 3, 768), 1)
    sconv_sample = ss("sconv_s", (Ld, 4, 2, 256), 1)
    return (y_prompt, y_sample, k_prompt, v_prompt, logf_prompt, s5_prompt, ssd_prompt, ssd_conv_prompt, sconv_prompt,
            k_sample, v_sample, logf_sample, s5_sample, ssd_sample, ssd_conv_sample, sconv_sample)
```

```python
import os
import numpy as np
import ml_dtypes
from contextlib import ExitStack
import concourse.bass as bass
import concourse.mybir as mybir
from concourse.bass_utils import run_bass_kernel_spmd

F32 = mybir.dt.float32
BF16 = mybir.dt.bfloat16
I32 = mybir.dt.int32
ALU = mybir.AluOpType
AF = mybir.ActivationFunctionType

D = 1024
INC = 2824
EPS = 1e-5
NEG = -30000.0
MT_IN = ([("u", 0 + 128 * i, 128) for i in range(2)] + [("q", 256 + 128 * i, 128) for i in range(2)]
         + [("k", 512 + 128 * i, 128) for i in range(2)] + [("v", 768 + 128 * i, 128) for i in range(2)]
         + [("fr", 1024, 4)] + [("z", 1028 + 128 * i, 128) for i in range(2)]
         + [("xbc", 1284 + 128 * i, 128) for i in range(6)] + [("dt", 2052, 4)]
         + [("sb", 2056 + 128 * i, 128) for i in range(2)] + [("sc", 2312 + 128 * i, 128) for i in range(2)]
         + [("sh", 2568 + 128 * i, 128) for i in range(2)])
NMT_IN = len(MT_IN)


class Buf:
    __slots__ = ("name", "w", "r", "sem", "excl")

    def __init__(self, name, excl=False):
        self.name, self.w, self.r, self.sem, self.excl = name, None, [], None, excl


class Op:
    __slots__ = ("eng", "fn", "deps", "sem", "val", "needed", "isdma", "line")

    def __init__(self, eng, fn, deps, isdma=False, sem=None):
        self.eng, self.fn, self.deps = eng, fn, deps
        self.sem, self.val, self.needed, self.isdma = sem, None, False, isdma
        self.line = 0


def _flat(xs):
    out = []
    for x in xs:
        if isinstance(x, (list, tuple)):
            out.extend(_flat(x))
        elif x is not None:
            out.append(x)
    return out


class Sched:
    ENGS = ("pe", "act", "dve", "pool", "sp")

    def __init__(self, nc, es):
        self.nc, self.es = nc, es
        self.ops = {e: [] for e in self.ENGS}
        self.esem = {e: es.enter_context(nc.semaphore("s_" + e)) for e in self.ENGS}
        self.dcnt = {}
        self.nsem = 0
        self.limit = int(os.environ.get("K_MAXOPS", "1000000000"))
        self.nrec = 0
        self.lastline = None

    def _deps(self, eng, r, w, extra):
        deps = []
        for b in r:
            if b.w is not None:
                deps.append(b.w)
        for b in w:
            if b.w is not None:
                deps.append(b.w)
            deps.extend(b.r)
        deps.extend([x for x in extra if x is not None])
        if eng == "pe":
            deps = [d for d in deps if d.eng != "pe" or d.isdma]
        out, seen = [], set()
        for d in deps:
            if id(d) not in seen:
                seen.add(id(d))
                out.append(d)
                d.needed = True
        return out

    def _upd(self, o, r, w):
        for b in r:
            b.r.append(o)
        for b in w:
            b.w = o
            b.r = []

    def _skip(self):
        import sys as _sys
        self.nrec += 1
        if self.nrec > self.limit:
            return True
        self.lastline = (_sys._getframe(2).f_lineno, _sys._getframe(3).f_lineno)
        return False

    def op(self, eng, fn, r=(), w=(), extra=()):
        r, w, extra = _flat(r), _flat(w), _flat(extra)
        w = w + [b for b in r if b.excl]
        r = [b for b in r if not b.excl]
        if self._skip():
            return None
        o = Op(eng, fn, self._deps(eng, r, w, extra))
        self._upd(o, r, w)
        self.ops[eng].append(o)
        return o

    def dma(self, eng, key, fn, r=(), w=(), extra=()):
        r, w, extra = _flat(r), _flat(w), _flat(extra)
        if self._skip():
            return None
        if key.sem is None:
            key.sem = self.es.enter_context(self.nc.semaphore("d%d" % self.nsem))
            self.nsem += 1
            self.dcnt[id(key)] = 0
        o = Op(eng, fn, self._deps("dma", r, w, extra), isdma=True, sem=key.sem)
        self.dcnt[id(key)] += 16
        o.val = self.dcnt[id(key)]
        o.needed = True
        self._upd(o, r, w)
        self.ops[eng].append(o)
        return o

    def emit(self, final_ops):
        nc = self.nc
        for e in self.ENGS:
            c = 0
            for o in self.ops[e]:
                if o.isdma:
                    continue
                if o.needed:
                    c += 1
                    o.sem, o.val = self.esem[e], c
        engobj = {"pe": "tensor", "act": "scalar", "dve": "vector", "pool": "gpsimd", "sp": "sync"}
        with nc.Block() as block:
            for e in self.ENGS:
                ops = self.ops[e]

                def body(eng, ops=ops, e=e):
                    waited = {}
                    for o in ops:
                        need = {}
                        for d in o.deps:
                            k = id(d.sem)
                            if waited.get(k, 0) >= d.val:
                                continue
                            if k not in need or need[k][1] < d.val:
                                need[k] = (d.sem, d.val)
                        for k, (s, v) in need.items():
                            eng.wait_ge(s, v)
                            waited[k] = v
                        ins = o.fn(eng)
                        if o.isdma:
                            ins.then_inc(o.sem, 16)
                        elif o.needed:
                            ins.then_inc(o.sem, 1)
                    if e == "sp":
                        fin = {}
                        for d in final_ops:
                            if d is None:
                                continue
                            k = id(d.sem)
                            if k not in fin or fin[k][1] < d.val:
                                fin[k] = (d.sem, d.val)
                        for k, (s, v) in fin.items():
                            eng.wait_ge(s, v)

                getattr(block, engobj[e])(body)


def host_consts():
    c = {}
    c["ident"] = np.eye(128, dtype=np.float32)
    s = np.arange(128)
    c["tri"] = (s[:, None] <= s[None, :]).astype(np.float32)
    c["tristrict"] = (s[:, None] > s[None, :]).astype(np.float32)
    c["maskb"] = np.where(s[:, None] <= s[None, :], 0.0, NEG).astype(np.float32)
    sel = np.zeros((36, 4, 128), np.float32)
    for h in range(4):
        sel[h, h, :] = 1.0
        sel[32 + h, h, :] = 1.0
    c["sel"] = sel
    t = np.arange(16)
    same = (t[:, None] // 4) == (t[None, :] // 4)
    c["triS"] = (same & (t[:, None] <= t[None, :])).astype(np.float32)
    mS = np.zeros((128, 16), np.float32)
    mS[:16] = np.where(same & (t[:, None] <= t[None, :]), 0.0, NEG)
    c["maskS"] = mS
    p4 = np.ones((4, 16), np.float32); p4[:, 0::4] = 0.0
    c["pat4"] = p4
    c["iota"] = np.arange(128, dtype=np.float32).reshape(128, 1)
    return c


class Cfg:
    def __init__(self, T=4096, depth=4, ns=4, ts=4, npg=64, npool=2560, sample=True):
        self.T, self.depth, self.ns, self.ts, self.npg, self.npool, self.sample = T, depth, ns, ts, npg, npool, sample
        self.nch = T // 512


def build_program(cfg):
    nc = bass.Bass("TRN2", target_bir_lowering=False)
    T, L, NS, TS, NPG = cfg.T, cfg.depth, cfg.ns, cfg.ts, cfg.npg
    NSAMP = NS * TS
    NCH = cfg.nch
    NKT = T // 128
    es = ExitStack()

    def din(name, shape, dt=F32):
        return nc.dram_tensor(name, list(shape), dt, kind="ExternalInput").ap()

    def dout(name, shape, dt=F32):
        return nc.dram_tensor(name, list(shape), dt, kind="ExternalOutput").ap()

    xp = din("xp", [T, D])
    w_in = din("w_in", [L, D, INC]); w_out = din("w_out", [L, D, D])
    w_up = din("w_up", [L, D, 4096]); w_down = din("w_down", [L, 4096, D])
    prm = {}
    for nm, shp in [("norm_mix_g", [L, D]), ("norm_ffn_g", [L, D]), ("norm_final_g", [D]),
                    ("s5_lam_re", [L, 16, 64]), ("s5_lam_im", [L, 16, 64]), ("s5_log_dt", [L, 16]),
                    ("s5_b_re", [L, 16, 64, 16]), ("s5_b_im", [L, 16, 64, 16]),
                    ("s5_c_re", [L, 16, 16, 64]), ("s5_c_im", [L, 16, 16, 64]), ("s5_d", [L, 16, 16]),
                    ("s5_w_glu", [L, 256, 256]), ("s5_norm_g", [L, 256]), ("fox_b_f", [L, 4]),
                    ("fox_norm_g", [L, 256]), ("ssd_conv_w", [L, 4, 768]), ("ssd_conv_b", [L, 768]),
                    ("ssd_dt_bias", [L, 4]), ("ssd_a_log", [L, 4]), ("ssd_d", [L, 4]),
                    ("ssd_norm_g", [L, 256]), ("sc_conv_w", [L, 3, 256]), ("sc_norm_g", [L, 256])]:
        prm[nm] = din(nm, shp)
    cst = {k: din("c_" + k, v.shape) for k, v in host_consts().items()}
    y_p = dout("y_p", [T, D]); k_p = dout("k_p", [L, T, 256]); v_p = dout("v_p", [L, T, 256])
    logf_p = dout("logf_p", [L, T, 4]); s5_p = dout("s5_p", [L, 16, 64, 2]); ssd_p = dout("ssd_p", [L, 256, 128])
    ssdconv_p = dout("ssdconv_p", [L, 3, 768]); sconv_p = dout("sconv_p", [L, 2, 256])
    if cfg.sample:
        xs = din("xs", [NSAMP, D])
        cache_k = din("cache_k", [L * cfg.npool * 128, 256]); cache_v = din("cache_v", [L * cfg.npool * 128, 256])
        cache_logf = din("cache_logf", [L * cfg.npool * 128, 4])
        st_s5 = din("state_s5", [L, NS, 16, 64, 2]); st_ssd = din("state_ssd", [L, NS, 256, 128])
        st_ssdconv = din("state_ssd_conv", [L, NS, 3, 768]); st_sconv = din("state_sconv", [L, NS, 2, 256])
        page_table = din("page_table", [NS, NPG], I32)
        y_s = dout("y_s", [NSAMP, D]); k_s = dout("k_s", [L, NSAMP, 256]); v_s = dout("v_s", [L, NSAMP, 256])
        logf_s = dout("logf_s", [L, NSAMP, 4]); s5_s = dout("s5_s", [L, NS, 16, 64, 2])
        ssd_s = dout("ssd_s", [L, NS, 256, 128]); ssdconv_s = dout("ssdconv_s", [L, NS, 3, 768])
        sconv_s = dout("sconv_s", [L, NS, 2, 256])
    wt_in = nc.dram_tensor("wt_in", [L, NMT_IN, 128, 8, 128], BF16).ap()
    wt_out = nc.dram_tensor("wt_out", [L, 8, 128, 8, 128], BF16).ap()
    wt_up = nc.dram_tensor("wt_up", [L, 32, 128, 8, 128], BF16).ap()
    wt_down = nc.dram_tensor("wt_down", [L, 8, 128, 32, 128], BF16).ap()
    x_scr = nc.dram_tensor("x_scr", [NCH, 128, 8, 512], F32).ap()
    kt_scr = nc.dram_tensor("kt_scr", [NCH, 128, 2, 512], BF16).ap()
    v_scr = nc.dram_tensor("v_scr", [NCH, 128, 4, 4, 64], BF16).ap()
    DBG = bool(os.environ.get("K_DBG"))
    if DBG:
        dbg = dout("dbg", [128, 16384])
    dbg_state = {"off": 0, "items": []}

    with es:
        S = Sched(nc, es)
        finals = []

        def sb(name, shape, dt=F32):
            return es.enter_context(nc.sbuf_tensor(name, list(shape), dt))

        def dbg_dump(name, ap2d, ncols, bufs):
            if not DBG:
                return
            o = dbg_state["off"]
            npart = ap2d.shape[0]
            b = Buf("dbg_" + name)
            finals.append(S.dma("pool", b, lambda e: e.dma_start(out=dbg[0:npart, o:o + ncols], in_=ap2d), r=bufs, w=[b]))
            dbg_state["items"].append((name, o, ncols, npart))
            dbg_state["off"] = o + ncols
        nc._dbg_items = dbg_state["items"]

        PS = [es.enter_context(nc.psum_tensor("ps%d" % i, [128, 512], F32)) for i in range(8)]
        PSB = [Buf("ps%d" % i, excl=True) for i in range(8)]
        mm_rr = [0]

        def mmbank():
            i = mm_rr[0] % 2
            mm_rr[0] += 1
            return PS[i], PSB[i]

        ident = sb("ident", [128, 128]); b_ident = Buf("ident")
        identb = sb("identb", [128, 128], BF16); b_identb = Buf("identb")
        tri = sb("tri", [128, 128]); b_tri = Buf("tri")
        tristrict = sb("tristrict", [128, 128]); b_tristrict = Buf("tristrict")
        maskb = sb("maskb", [128, 128], BF16); b_maskb = Buf("maskb")
        sel = sb("sel", [36, 4, 128], BF16); b_sel = Buf("sel")
        onesf = sb("onesf", [128, 128]); b_onesf = Buf("onesf")
        onesb = sb("onesb", [128, 128], BF16); b_onesb = Buf("onesb")
        S.dma("pool", b_ident, lambda e: e.dma_start(out=ident[:], in_=cst["ident"]), w=[b_ident])
        S.dma("pool", b_identb, lambda e: e.dma_start(out=identb[:], in_=cst["ident"]), w=[b_identb])
        S.dma("pool", b_tri, lambda e: e.dma_start(out=tri[:], in_=cst["tri"]), w=[b_tri])
        S.dma("pool", b_tristrict, lambda e: e.dma_start(out=tristrict[:], in_=cst["tristrict"]), w=[b_tristrict])
        S.dma("pool", b_maskb, lambda e: e.dma_start(out=maskb[:], in_=cst["maskb"]), w=[b_maskb])
        S.dma("pool", b_sel, lambda e: e.dma_start(out=sel[:], in_=cst["sel"]), w=[b_sel])
        S.op("dve", lambda e: e.memset(onesf[:], 1.0), w=[b_onesf])
        S.op("dve", lambda e: e.memset(onesb[:], 1.0), w=[b_onesb])

        b_wt = [Buf("wt%d" % l) for l in range(L)]
        big16 = sb("big16", [128, 4096], F32); b_big = [Buf("big%d" % k) for k in range(16)]
        pstg = big16[:, 0:2048].bitcast(BF16); b_pstg = b_big[0:8]

        pst2 = [sb("pst2_%d" % i, [128, 4, 128], BF16) for i in range(2)]; b_pst2 = [Buf("pst2_%d" % i) for i in range(2)]
        pc_rr = [0]
        for l in range(L):
            b_wt[l].sem = es.enter_context(nc.semaphore("wt%d" % l)); S.dcnt[id(b_wt[l])] = 0

        def pc_jobs(l):
            jobs = []
            v = lambda w_: w_[l].rearrange("(k p) m -> p k m", p=128)
            for mi, (nm, c0, wd) in enumerate(MT_IN):
                for kh in range(2):
                    jobs.append((v(w_in)[:, kh * 4:(kh + 1) * 4, c0:c0 + wd], wt_in[l, mi, :, kh * 4:(kh + 1) * 4, 0:wd], wd))
            for mi in range(8):
                for kh in range(2):
                    jobs.append((v(w_out)[:, kh * 4:(kh + 1) * 4, mi * 128:(mi + 1) * 128], wt_out[l, mi, :, kh * 4:(kh + 1) * 4, :], 128))
            for mi in range(32):
                for kh in range(2):
                    jobs.append((v(w_up)[:, kh * 4:(kh + 1) * 4, mi * 128:(mi + 1) * 128], wt_up[l, mi, :, kh * 4:(kh + 1) * 4, :], 128))
            for mi in range(8):
                for kq in range(8):
                    jobs.append((v(w_down)[:, kq * 4:(kq + 1) * 4, mi * 128:(mi + 1) * 128], wt_down[l, mi, :, kq * 4:(kq + 1) * 4, :], 128))
            return jobs

        def pc_issue(l, job):
            src, dst, wd = job
            i = pc_rr[0] % 2; pc_rr[0] += 1
            stg = pst2[i][:, :, 0:wd]
            S.dma("pool", b_pst2[i], lambda e: e.dma_start(out=stg, in_=src), w=[b_pst2[i]])
            S.dma("pool", b_wt[l], lambda e: e.dma_start(out=dst, in_=stg), r=[b_pst2[i]], w=[])
        for job in pc_jobs(0):
            pc_issue(0, job)
        pc_pending = {"l": 1, "jobs": pc_jobs(1) if L > 1 else []}

        def pc_drip(n):
            for _ in range(n):
                if not pc_pending["jobs"]:
                    return
                pc_issue(pc_pending["l"], pc_pending["jobs"].pop(0))

        def pc_flush_and_next():
            pc_drip(10 ** 6)
            pc_pending["l"] += 1
            pc_pending["jobs"] = pc_jobs(pc_pending["l"]) if pc_pending["l"] < L else []
        wt_ready = []
        for l in range(L):
            o = Op("pool", None, [], isdma=True, sem=b_wt[l].sem)
            o.val = 0
            wt_ready.append(o)

        NSLOT = 2
        wslot = [sb("wslot%d" % i, [128, 4096], BF16) for i in range(NSLOT)]
        b_wslot = [Buf("wslot%d" % i) for i in range(NSLOT)]
        def groups_for(l):
            g = []
            for i in range(6):
                g.append(("in", l, i * 4, 4))
            for i in range(2):
                g.append(("out", l, i * 4, 4))
            for half in range(2):
                for i in range(4 * half, 4 * half + 4):
                    g.append(("up", l, i * 4, 4))
                for m in range(8):
                    g.append(("down", l, m * 2 + half, 1))
            return g
        nblk = NCH + (1 if cfg.sample else 0)
        glist = []
        for l in range(L):
            for b in range(nblk):
                glist.extend(groups_for(l))
        gstate = {"issued": 0, "next": 0}

        def gview(idx):
            kind, l, m0, n = glist[idx]
            slot = idx % NSLOT
            kk = 16 if kind == "down" else 8
            return wslot[slot][:, 0:n * kk * 128].rearrange("p (m k c) -> p m k c", m=n, k=kk), b_wslot[slot]

        def issue_group(idx):
            kind, l, m0, n = glist[idx]
            dst, b_dst = gview(idx)
            if kind == "down":
                m, half = m0 // 2, m0 % 2
                s_ap = wt_down[l, m:m + 1, :, half * 16:(half + 1) * 16, :].rearrange("m p k c -> p m k c")
            else:
                src = {"in": wt_in, "out": wt_out, "up": wt_up}[kind]
                s_ap = src[l, m0:m0 + n].rearrange("m p k c -> p m k c")
            S.dma("sp", b_dst, lambda e, dst=dst, s_ap=s_ap: e.dma_start(out=dst, in_=s_ap),
                  w=[b_dst], extra=[wt_ready[l]])

        def next_group(kind, l, m0):
            idx = gstate["next"]
            assert glist[idx][:3] == (kind, l, m0), (glist[idx], kind, l, m0)
            while gstate["issued"] < min(len(glist), idx + NSLOT - 1) or gstate["issued"] <= idx:
                issue_group(gstate["issued"])
                gstate["issued"] += 1
            gstate["next"] += 1
            pc_drip(1)
            return gview(idx)

        gmix = sb("gmix", [128, L, 8]); gffn = sb("gffn", [128, L, 8]); gfin = sb("gfin", [128, 8])
        b_gains = Buf("gains")
        for l in range(L):
            S.dma("pool", b_gains, lambda e, l=l: e.dma_start(out=gmix[:, l, :], in_=prm["norm_mix_g"][l].rearrange("(k p) -> p k", p=128), allow_slow_non_contiguous=True), w=[])
            S.dma("pool", b_gains, lambda e, l=l: e.dma_start(out=gffn[:, l, :], in_=prm["norm_ffn_g"][l].rearrange("(k p) -> p k", p=128), allow_slow_non_contiguous=True), w=[])
        o_g = S.dma("pool", b_gains, lambda e: e.dma_start(out=gfin[:], in_=prm["norm_final_g"].rearrange("(k p) -> p k", p=128), allow_slow_non_contiguous=True), w=[b_gains])
        ggrp = sb("ggrp", [128, L, 4, 2]); b_ggrp = Buf("ggrp")
        for l in range(L):
            for gi, nm in enumerate(["s5_norm_g", "fox_norm_g", "ssd_norm_g", "sc_norm_g"]):
                S.dma("pool", b_ggrp, lambda e, l=l, gi=gi, nm=nm: e.dma_start(out=ggrp[:, l, gi, :], in_=prm[nm][l].rearrange("(k p) -> p k", p=128), allow_slow_non_contiguous=True), w=[b_ggrp])

        epsc = sb("epsc", [128, 1]); b_epsc = Buf("epsc")
        S.op("dve", lambda e: e.memset(epsc[:], EPS), w=[b_epsc])
        halfpi = sb("halfpi", [128, 1]); b_halfpi = Buf("halfpi")
        S.op("dve", lambda e: e.memset(halfpi[:], float(np.pi / 2)), w=[b_halfpi])
        maskf = sb("maskf", [128, 128]); b_maskf = Buf("maskf")
        S.dma("pool", b_maskf, lambda e: e.dma_start(out=maskf[:], in_=cst["maskb"]), w=[b_maskf])
        rs = sb("rs", [128, 512]); b_rs = Buf("rs")

        def rmsnorm_fm(src, b_src, nk, N, gain_fn, dst, b_dst):
            ps, b_ps = mmbank()
            for k in range(nk):
                S.op("act", lambda e, k=k: e.activation(out=sq[:, k, 0:N], in_=src[:, k, 0:N], func=AF.Square),
                     r=[b_src[k]], w=[b_sq[k]])
            for k in range(nk):
                S.op("pe", lambda e, k=k: e.matmul(ps[:, 0:N], onesb[:, :], sq[:, k, 0:N], start=(k == 0), stop=(k == nk - 1)),
                     r=[b_sq[k], b_onesb], w=[b_ps])
            S.op("act", lambda e: e.activation(out=rs[:, 0:N], in_=ps[:, 0:N], func=AF.Sqrt, scale=1.0 / (nk * 128), bias=epsc[:, 0:1]),
                 r=[b_ps, b_epsc], w=[b_rs])
            S.op("dve", lambda e: e.reciprocal(rs[:, 0:N], rs[:, 0:N]), r=[b_rs], w=[b_rs])
            for k in range(nk):
                S.op("dve", lambda e, k=k: e.scalar_tensor_tensor(
                    out=dst[:, k, 0:N], in0=src[:, k, 0:N], scalar=gain_fn(k), in1=rs[:, 0:N],
                    op0=ALU.mult, op1=ALU.mult), r=[b_src[k], b_rs, b_gains, b_ggrp], w=[b_dst[k]])

        NB = 512
        xT = sb("xT", [128, 8, NB]); b_xT = [Buf("xT%d" % k) for k in range(8)]
        xnT = sb("xnT", [128, 8, NB], BF16); b_xnT = [Buf("xnT%d" % k) for k in range(8)]
        uTb = sb("uTb", [128, 2, NB], BF16); b_uTb = Buf("uTb")
        qT = sb("qT", [128, 2, NB], BF16); b_qT = Buf("qT")
        KTb = sb("KTb", [128, 2, NB], BF16); b_KTb = Buf("KTb")
        Vpb = sb("Vpb", [128, 4, 4, 128], BF16); b_Vpb = Buf("Vpb")
        S.op("pool", lambda e: e.memset(Vpb[:], 1.0), w=[b_Vpb])
        KTc = [sb("KTc%d" % i, [128, 2, NB], BF16) for i in range(2)]; b_KTc = [Buf("KTc%d" % i) for i in range(2)]
        Vpc = [sb("Vpc%d" % i, [128, 4, 4, 128], BF16) for i in range(2)]; b_Vpc = [Buf("Vpc%d" % i) for i in range(2)]
        for i in range(2):
            S.op("pool", lambda e, i=i: e.memset(Vpc[i][:], 1.0), w=[b_Vpc[i]])
        b_ktscr = [Buf("ktscr%d" % c) for c in range(NCH)]; b_vscr = [Buf("vscr%d" % c) for c in range(NCH)]
        cumT = sb("cumT", [4, NB]); b_cumT = Buf("cumT")
        cumc = sb("cumc", [4, 1]); b_cumc = Buf("cumc")
        augT = sb("augT", [36, NB], BF16); b_augT = Buf("augT")
        S.op("pool", lambda e: e.memset(augT[:], 0.0), w=[b_augT])
        zerob = sb("zerob", [128, 128], BF16)
        S.op("pool", lambda e: e.memset(zerob[:], 0.0), w=[b_maskb])
        ones4 = sb("ones4", [4, NB], BF16); b_ones4 = Buf("ones4")
        S.op("pool", lambda e: e.memset(ones4[:], 1.0), w=[b_ones4])
        PT = [sb("PT%d" % i, [128, NB], BF16) for i in range(2)]; b_PT = [Buf("PT%d" % i) for i in range(2)]
        rdn = rs; b_rdn = b_rs
        kc_rr = [0]; pt_rr = [0]; sbank_rr = [0]
        negcum = sb("negcum", [128, NKT, 4]); b_negcum = Buf("negcum")
        carry_bc = sb("carry_bc", [128, 4]); b_carry = Buf("carry_bc")
        logfT = sb("logfT", [4, NB]); b_logfT = Buf("logfT")
        xbcT = sb("xbcT", [128, 6, 3 + NB], BF16); b_xbcT = Buf("xbcT")
        xbcA = sb("xbcA", [128, 6, NB], BF16); b_xbcA = Buf("xbcA")
        szT = sb("szT", [128, 2, NB], BF16); b_szT = Buf("szT")
        daT = sb("daT", [36, NB]); b_daT = Buf("daT")
        S.op("pool", lambda e: e.memset(daT[:], 0.0), w=[b_daT])
        xs_tok = sb("xs_tok", [128, 256]); B_tok = sb("B_tok", [128, 256], BF16); da_tok = sb("da_tok", [128, 36]); b_tok = Buf("tok")
        xdtz = sb("xdtz", [128, 4, 128], BF16); xdtd = sb("xdtd", [128, 256], BF16); b_xdt = Buf("xdt")
        S.op("pool", lambda e: e.memset(xdtz[:], 0.0), w=[b_xdt])
        arep = sb("arep", [128, 4, 128]); b_arep = Buf("arep")
        sm = sb("sm", [128, 16]); b_sm = Buf("sm")
        dec = sb("dec", [128, 4, 128]); b_dec = Buf("dec")
        ea = sb("ea", [128, 4, 128], BF16); b_ea = Buf("ea")
        MTt = sb("MTt", [128, 4, 128], BF16); b_MT = Buf("MT")
        CdT = sb("CdT", [128, 4, 128], BF16); b_CdT = Buf("CdT")
        HT = sb("HT", [128, 256]); HTz = sb("HTz", [128, 4, 128], BF16); b_HT = Buf("HT"); b_HTz = Buf("HTz")
        S.op("pool", lambda e: e.memset(HTz[:], 0.0), w=[b_HTz])
        ysd = sb("ysd", [128, 128]); b_ysd = Buf("ysd")
        sbT = sb("sbT", [128, 2, NB], BF16); b_sbT = Buf("sbT")
        cshT = sb("cshT", [128, 2, 2 + NB], BF16); b_cshT = Buf("cshT")
        catT = sb("catT", [128, 8, NB], BF16); b_catT = [Buf("catT%d" % k) for k in range(8)]
        mixT = sb("mixT", [128, 8, NB], BF16); b_mixT = [Buf("mixT%d" % k) for k in range(8)]
        hT = big16[:, :].bitcast(BF16).rearrange("p (k n) -> p k n", k=16); b_hT = b_big
        sq = hT; b_sq = b_hT
        kvst = [sb("kvst%d" % i, [128, 512]) for i in range(1)]*2; b_kvst = [Buf("kvst0")]*2
        b_kvst_st = [Buf("kvst_st0")]*2
        ostage = [sb("ostage%d" % i, [128, 1024]) for i in range(1)]*2; b_ostage = [Buf("ost0")]*2
        b_ostage_st = [Buf("ost_st0")]*2
        b_xscr = [Buf("xscr%d" % c) for c in range(NCH)]
        lf_tok = sb("lf_tok", [128, 4, 4]); b_lf_tok = Buf("lf_tok")
        b_lf_st = Buf("lf_st")

        bfcol = sb("bfcol", [4, L]); dtbcol = sb("dtbcol", [4, L]); alogcol = sb("alogcol", [4, L]); b_pcol = Buf("pcol")
        S.dma("pool", b_pcol, lambda e: e.dma_start(out=bfcol[:], in_=prm["fox_b_f"].rearrange("l h -> h l"), allow_slow_non_contiguous=True), w=[])
        S.dma("pool", b_pcol, lambda e: e.dma_start(out=dtbcol[:], in_=prm["ssd_dt_bias"].rearrange("l h -> h l"), allow_slow_non_contiguous=True), w=[])
        S.dma("pool", b_pcol, lambda e: e.dma_start(out=alogcol[:], in_=prm["ssd_a_log"].rearrange("l h -> h l"), allow_slow_non_contiguous=True), w=[b_pcol])
        nbfcol = sb("nbfcol", [4, L]); acol = sb("acol", [4, L])
        S.op("dve", lambda e: e.tensor_scalar(out=nbfcol[:], in0=bfcol[:], scalar1=-1.0, scalar2=None, op0=ALU.mult), r=[b_pcol], w=[b_pcol])
        S.op("act", lambda e: e.activation(out=acol[:], in_=alogcol[:], func=AF.Exp), r=[b_pcol], w=[b_pcol])
        S.op("dve", lambda e: e.tensor_scalar(out=acol[:], in0=acol[:], scalar1=-1.0, scalar2=None, op0=ALU.mult), r=[b_pcol], w=[b_pcol])
        bf_bc = sb("bf_bc", [128, L, 4]); b_bfbc = Buf("bf_bc")
        for l in range(L):
            S.dma("pool", b_bfbc, lambda e, l=l: e.dma_start(out=bf_bc[:, l, :], in_=prm["fox_b_f"][l:l + 1, :].partition_broadcast(128)), w=[b_bfbc])
        convw = sb("convw", [128, L, 6, 4]); convb = sb("convb", [128, L, 6]); scw = sb("scw", [128, L, 2, 3]); b_convp = Buf("convp")
        dcol = sb("dcol", [128, L, 2]); s5dcol = sb("s5dcol", [128, L, 2])
        for l in range(L):
            for k in range(6):
                S.dma("pool", b_convp, lambda e, l=l, k=k: e.dma_start(out=convw[:, l, k, :], in_=prm["ssd_conv_w"][l][:, k * 128:(k + 1) * 128].rearrange("j p -> p j"), allow_slow_non_contiguous=True), w=[])
            S.dma("pool", b_convp, lambda e, l=l: e.dma_start(out=convb[:, l, :], in_=prm["ssd_conv_b"][l].rearrange("(k p) -> p k", p=128), allow_slow_non_contiguous=True), w=[])
            for k in range(2):
                S.dma("pool", b_convp, lambda e, l=l, k=k: e.dma_start(out=scw[:, l, k, :], in_=prm["sc_conv_w"][l][:, k * 128:(k + 1) * 128].rearrange("j p -> p j"), allow_slow_non_contiguous=True), w=[])
            S.dma("pool", b_convp, lambda e, l=l: e.dma_start(out=s5dcol[:, l, :], in_=prm["s5_d"][l].rearrange("(k g) h -> (g h) k", k=2), allow_slow_non_contiguous=True), w=[])
            for h in range(4):
                S.dma("pool", b_convp, lambda e, l=l, h=h: e.dma_start(
                    out=dcol[(h % 2) * 64:(h % 2) * 64 + 64, l, h // 2:h // 2 + 1],
                    in_=prm["ssd_d"][l:l + 1, h:h + 1].partition_broadcast(64)), w=[b_convp])
        wglu = sb("wglu", [128, 2, 256], BF16); b_wglu = Buf("wglu")

        LS = 128
        Er = sb("Er", [128, 8, LS]); Ei = sb("Ei", [128, 8, LS]); T1r = sb("T1r", [128, 8, LS]); T1i = sb("T1i", [128, 8, LS])
        R0 = sb("R0", [128, 8, LS]); b_tab = Buf("s5tab")
        Bm = sb("Bm", [128, 8, 2, 128], BF16); Cm = sb("Cm", [128, 8, 2, 128], BF16); b_BC = Buf("s5BC")
        s5c = sb("s5c", [128, 24, 8]); b_s5c = Buf("s5c")
        LR, LI, DT, RR, CC, SS, AR, AI, QR, QI, CM, SM, TA, TB, TC, TD = range(16)
        Xr = sb("Xr", [128, 8, 4]); Xi = sb("Xi", [128, 8, 4]); Wr = sb("Wr", [128, 8, 4]); Wi = sb("Wi", [128, 8, 4]); b_X = Buf("s5X")
        s5t = sb("s5t", [128, 8, 4, 4]); b_s5t = Buf("s5t")

        def col(i):
            return s5c[:, i, :]

        def s5_tables(l):
            d = S.dma
            d("pool", b_s5c, lambda e: e.dma_start(out=col(LR), in_=prm["s5_lam_re"][l].rearrange("(j g) p -> (g p) j", g=2), allow_slow_non_contiguous=True), w=[b_s5c], r=[b_tab])
            d("pool", b_s5c, lambda e: e.dma_start(out=col(LI), in_=prm["s5_lam_im"][l].rearrange("(j g) p -> (g p) j", g=2), allow_slow_non_contiguous=True), w=[b_s5c])
            for g2 in range(2):
                d("pool", b_s5c, lambda e, g2=g2: e.dma_start(out=s5c[g2 * 64:(g2 + 1) * 64, DT, :], in_=prm["s5_log_dt"][l].rearrange("(j g) -> g j", g=2)[g2:g2 + 1, :].partition_broadcast(64), allow_slow_non_contiguous=True), w=[b_s5c])
            o = lambda eng, fn: S.op(eng, fn, r=[b_s5c, b_halfpi, b_onesf], w=[b_s5c])
            o("act", lambda e: e.activation(out=col(DT), in_=col(DT), func=AF.Exp))
            o("dve", lambda e: e.tensor_tensor(out=col(TA), in0=col(LR), in1=col(DT), op=ALU.mult))
            o("dve", lambda e: e.tensor_tensor(out=col(TB), in0=col(LI), in1=col(DT), op=ALU.mult))
            o("act", lambda e: e.activation(out=col(RR), in_=col(TA), func=AF.Exp))
            o("dve", lambda e: e.tensor_scalar(out=col(TB), in0=col(TB), scalar1=1.0 / 32, scalar2=None, op0=ALU.mult))
            o("dve", lambda e: e.tensor_tensor(out=col(TA), in0=col(TB), in1=col(TB), op=ALU.mult))
            o("dve", lambda e: e.tensor_scalar(out=col(SS), in0=col(TA), scalar1=1.0 / 362880, scalar2=None, op0=ALU.mult))
            for cc in (-1.0 / 5040, 1.0 / 120, -1.0 / 6):
                o("dve", lambda e, cc=cc: e.scalar_tensor_tensor(out=col(SS), in0=col(SS), scalar=cc, in1=col(TA), op0=ALU.add, op1=ALU.mult))
            o("dve", lambda e: e.scalar_tensor_tensor(out=col(SS), in0=col(SS), scalar=1.0, in1=col(TB), op0=ALU.add, op1=ALU.mult))
            o("dve", lambda e: e.tensor_scalar(out=col(CC), in0=col(TA), scalar1=-1.0 / 3628800, scalar2=None, op0=ALU.mult))
            for cc in (1.0 / 40320, -1.0 / 720, 1.0 / 24, -0.5):
                o("dve", lambda e, cc=cc: e.scalar_tensor_tensor(out=col(CC), in0=col(CC), scalar=cc, in1=col(TA), op0=ALU.add, op1=ALU.mult))
            o("dve", lambda e: e.tensor_scalar(out=col(CC), in0=col(CC), scalar1=1.0, scalar2=None, op0=ALU.add))
            for _ in range(5):
                o("dve", lambda e: e.tensor_tensor(out=col(TC), in0=col(CC), in1=col(SS), op=ALU.mult))
                o("dve", lambda e: e.tensor_tensor(out=col(TA), in0=col(CC), in1=col(CC), op=ALU.mult))
                o("dve", lambda e: e.tensor_tensor(out=col(TD), in0=col(SS), in1=col(SS), op=ALU.mult))
                o("dve", lambda e: e.tensor_tensor(out=col(CC), in0=col(TA), in1=col(TD), op=ALU.subtract))
                o("dve", lambda e: e.tensor_scalar(out=col(SS), in0=col(TC), scalar1=2.0, scalar2=None, op0=ALU.mult))
            o("dve", lambda e: e.tensor_tensor(out=col(AR), in0=col(RR), in1=col(CC), op=ALU.mult))
            o("dve", lambda e: e.tensor_tensor(out=col(AI), in0=col(RR), in1=col(SS), op=ALU.mult))
            o("dve", lambda e: e.tensor_scalar(out=col(TA), in0=col(AR), scalar1=-1.0, scalar2=None, op0=ALU.add))
            o("dve", lambda e: e.tensor_tensor(out=col(TB), in0=col(LR), in1=col(LR), op=ALU.mult))
            o("dve", lambda e: e.tensor_tensor(out=col(TC), in0=col(LI), in1=col(LI), op=ALU.mult))
            o("dve", lambda e: e.tensor_tensor(out=col(TB), in0=col(TB), in1=col(TC), op=ALU.add))
            o("dve", lambda e: e.reciprocal(col(TB), col(TB)))
            o("dve", lambda e: e.tensor_tensor(out=col(TC), in0=col(TA), in1=col(LR), op=ALU.mult))
            o("dve", lambda e: e.tensor_tensor(out=col(TD), in0=col(AI), in1=col(LI), op=ALU.mult))
            o("dve", lambda e: e.tensor_tensor(out=col(TC), in0=col(TC), in1=col(TD), op=ALU.add))
            o("dve", lambda e: e.tensor_tensor(out=col(QR), in0=col(TC), in1=col(TB), op=ALU.mult))
            o("dve", lambda e: e.tensor_tensor(out=col(TC), in0=col(AI), in1=col(LR), op=ALU.mult))
            o("dve", lambda e: e.tensor_tensor(out=col(TD), in0=col(TA), in1=col(LI), op=ALU.mult))
            o("dve", lambda e: e.tensor_tensor(out=col(TC), in0=col(TC), in1=col(TD), op=ALU.subtract))
            o("dve", lambda e: e.tensor_tensor(out=col(QI), in0=col(TC), in1=col(TB), op=ALU.mult))
            t = lambda eng, fn: S.op(eng, fn, r=[b_s5c], w=[b_tab])
            t("dve", lambda e: e.memset(Er[:, :, 0:1], 1.0))
            t("dve", lambda e: e.memset(Ei[:, :, 0:1], 0.0))
            o("dve", lambda e: e.tensor_copy(out=col(CM), in_=col(CC)))
            o("dve", lambda e: e.tensor_copy(out=col(SM), in_=col(SS)))
            m = 1
            while m < LS:
                cm_b = s5c[:, CM, :].unsqueeze(2).broadcast_to([128, 8, m]); sm_b = s5c[:, SM, :].unsqueeze(2).broadcast_to([128, 8, m])
                t("dve", lambda e, m=m, cm_b=cm_b: e.tensor_tensor(out=Er[:, :, m:2 * m], in0=Er[:, :, 0:m], in1=cm_b, op=ALU.mult))
                t("dve", lambda e, m=m, sm_b=sm_b: e.tensor_tensor(out=T1r[:, :, 0:m], in0=Ei[:, :, 0:m], in1=sm_b, op=ALU.mult))
                t("dve", lambda e, m=m: e.tensor_tensor(out=Er[:, :, m:2 * m], in0=Er[:, :, m:2 * m], in1=T1r[:, :, 0:m], op=ALU.subtract))
                t("dve", lambda e, m=m, cm_b=cm_b: e.tensor_tensor(out=Ei[:, :, m:2 * m], in0=Ei[:, :, 0:m], in1=cm_b, op=ALU.mult))
                t("dve", lambda e, m=m, sm_b=sm_b: e.tensor_tensor(out=T1r[:, :, 0:m], in0=Er[:, :, 0:m], in1=sm_b, op=ALU.mult))
                t("dve", lambda e, m=m: e.tensor_tensor(out=Ei[:, :, m:2 * m], in0=Ei[:, :, m:2 * m], in1=T1r[:, :, 0:m], op=ALU.add))
                o("dve", lambda e: e.tensor_tensor(out=col(TC), in0=col(CM), in1=col(SM), op=ALU.mult))
                o("dve", lambda e: e.tensor_tensor(out=col(TA), in0=col(CM), in1=col(CM), op=ALU.mult))
                o("dve", lambda e: e.tensor_tensor(out=col(TD), in0=col(SM), in1=col(SM), op=ALU.mult))
                o("dve", lambda e: e.tensor_tensor(out=col(CM), in0=col(TA), in1=col(TD), op=ALU.subtract))
                o("dve", lambda e: e.tensor_scalar(out=col(SM), in0=col(TC), scalar1=2.0, scalar2=None, op0=ALU.mult))
                m *= 2
            qr_b = s5c[:, QR, :].unsqueeze(2).broadcast_to([128, 8, LS]); qi_b = s5c[:, QI, :].unsqueeze(2).broadcast_to([128, 8, LS])
            rr_b = s5c[:, RR, :].unsqueeze(2).broadcast_to([128, 8, LS])
            t("dve", lambda e: e.tensor_tensor(out=T1r[:], in0=Er[:], in1=qr_b, op=ALU.mult))
            t("dve", lambda e: e.tensor_tensor(out=R0[:], in0=Ei[:], in1=qi_b, op=ALU.mult))
            t("dve", lambda e: e.tensor_tensor(out=T1r[:], in0=T1r[:], in1=R0[:], op=ALU.add))
            t("dve", lambda e: e.tensor_tensor(out=T1i[:], in0=Er[:], in1=qi_b, op=ALU.mult))
            t("dve", lambda e: e.tensor_tensor(out=R0[:], in0=Ei[:], in1=qr_b, op=ALU.mult))
            t("dve", lambda e: e.tensor_tensor(out=T1i[:], in0=T1i[:], in1=R0[:], op=ALU.subtract))
            t("dve", lambda e: e.tensor_tensor(out=R0[:], in0=onesf[:, :].unsqueeze(1).broadcast_to([128, 8, LS]), in1=rr_b, op=ALU.mult))
            t("dve", lambda e: e.memset(R0[:, :, 0:1], 0.0))
            S.op("pool", lambda e: e.memset(Bm[:], 0.0), w=[b_BC])
            S.op("pool", lambda e: e.memset(Cm[:], 0.0), w=[b_BC])
            for g in range(16):
                j, g2 = g // 2, g % 2
                gl = g % 8
                for ri, nm in enumerate(["s5_b_re", "s5_b_im"]):
                    d("pool", b_BC, lambda e, g=g, j=j, g2=g2, gl=gl, ri=ri, nm=nm: e.dma_start(
                        out=Bm[gl * 16:(gl + 1) * 16, j, ri, g2 * 64:(g2 + 1) * 64],
                        in_=prm[nm][l, g].rearrange("p h -> h p"), allow_slow_non_contiguous=True), w=[b_BC])
                for ri, nm in enumerate(["s5_c_re", "s5_c_im"]):
                    c0 = (j % 4) * 32 + g2 * 16
                    d("pool", b_BC, lambda e, g=g, j=j, g2=g2, ri=ri, nm=nm, c0=c0: e.dma_start(
                        out=Cm[g2 * 64:(g2 + 1) * 64, j, ri, c0:c0 + 16],
                        in_=prm[nm][l, g].rearrange("h p -> p h"), allow_slow_non_contiguous=True), w=[b_BC])
            d("pool", b_wglu, lambda e: e.dma_start(out=wglu[:], in_=prm["s5_w_glu"][l].rearrange("(k p) m -> p k m", p=128)), w=[b_wglu])
            if l == 0:
                dbg_dump("s5c", s5c[:].rearrange("p a b -> p (a b)"), 192, [b_s5c])
                dbg_dump("Er", Er[:].rearrange("p a b -> p (a b)"), 1024, [b_tab])
                dbg_dump("Ei", Ei[:].rearrange("p a b -> p (a b)"), 1024, [b_tab])
                dbg_dump("T1r", T1r[:].rearrange("p a b -> p (a b)"), 1024, [b_tab])
                dbg_dump("R0", R0[:].rearrange("p a b -> p (a b)"), 1024, [b_tab])

        zr = big16[:, 0:1024].rearrange("p (a b) -> p a b", a=8); zi = big16[:, 1024:2048].rearrange("p (a b) -> p a b", a=8); b_z = b_big[0:8]
        wr_ = big16[:, 2048:3072].rearrange("p (a b) -> p a b", a=8); wi_ = big16[:, 3072:4096].rearrange("p (a b) -> p a b", a=8); b_w = b_big[8:16]
        pp = catT[:, :, :].rearrange("p j (q n) -> p j q n", q=4); b_pp = b_catT
        yA = sb("yA", [128, 2, NB]); b_yA = Buf("yA")
        yAb = sb("yAb", [128, 2, NB], BF16); b_yAb = Buf("yAb")
        gtmp = sb("gtmp", [128, 2, NB]); b_gtmp = Buf("gtmp")

        def s5_block(l, N, nseq, Tq, first):
            nsub = Tq // LS if Tq >= LS else 1
            Ls = min(LS, Tq)
            psy, b_psy = PS[7], PSB[7]
            for sc in range(nsub):
                c0 = sc * Ls
                for half in range(2):
                    jj = 4 * half
                    for ri, bank in enumerate([5, 6]):
                        for j in range(jj, jj + 4):
                            S.op("pe", lambda e, j=j, ri=ri, bank=bank, c0=c0: e.matmul(
                                PS[bank][:, (j % 4) * 128:(j % 4) * 128 + Ls],
                                Bm[:, j, ri, :], uTb[:, j // 4, c0:c0 + Ls], start=True, stop=True),
                                r=[b_BC, b_uTb], w=[PSB[bank]])
                    yield
                    pv5 = PS[5][:, :].rearrange("p (a b) -> p a b", a=4)[:, :, 0:Ls]
                    pv6 = PS[6][:, :].rearrange("p (a b) -> p a b", a=4)[:, :, 0:Ls]
                    S.op("dve", lambda e, jj=jj, pv5=pv5: e.tensor_tensor(out=zr[:, jj:jj + 4, 0:Ls], in0=T1r[:, jj:jj + 4, 0:Ls], in1=pv5, op=ALU.mult), r=[b_tab, PSB[5]], w=[b_z])
                    S.op("dve", lambda e, jj=jj, pv6=pv6: e.tensor_tensor(out=wr_[:, jj:jj + 4, 0:Ls], in0=T1i[:, jj:jj + 4, 0:Ls], in1=pv6, op=ALU.mult), r=[b_tab, PSB[6]], w=[b_w])
                    S.op("dve", lambda e, jj=jj, pv6=pv6: e.tensor_tensor(out=zi[:, jj:jj + 4, 0:Ls], in0=T1r[:, jj:jj + 4, 0:Ls], in1=pv6, op=ALU.mult), r=[b_tab, PSB[6]], w=[b_z])
                    S.op("dve", lambda e, jj=jj, pv5=pv5: e.tensor_tensor(out=wi_[:, jj:jj + 4, 0:Ls], in0=T1i[:, jj:jj + 4, 0:Ls], in1=pv5, op=ALU.mult), r=[b_tab, PSB[5]], w=[b_w])
                S.op("pool", lambda e: e.tensor_tensor(out=zr[:, :, 0:Ls], in0=zr[:, :, 0:Ls], in1=wr_[:, :, 0:Ls], op=ALU.subtract), r=[b_w], w=[b_z])
                S.op("pool", lambda e: e.tensor_tensor(out=zi[:, :, 0:Ls], in0=zi[:, :, 0:Ls], in1=wi_[:, :, 0:Ls], op=ALU.add), r=[b_w], w=[b_z])
                if not (first and sc == 0):
                    S.op("dve", lambda e: e.tensor_tensor(out=zr[:, :, 0:1], in0=zr[:, :, 0:1], in1=Wr[:, :, 0:1], op=ALU.add), r=[b_X], w=[b_z])
                    S.op("dve", lambda e: e.tensor_tensor(out=zi[:, :, 0:1], in0=zi[:, :, 0:1], in1=Wi[:, :, 0:1], op=ALU.add), r=[b_X], w=[b_z])
                S.op("dve", lambda e: e.tensor_tensor_scan(out=wr_[:, :, 0:Ls].rearrange("p a b -> p (a b)") if Ls == LS else wr_[:, :, 0:Ls],
                                                           data0=R0[:, :, :].rearrange("p a b -> p (a b)"), data1=zr[:, :, :].rearrange("p a b -> p (a b)"),
                                                           initial=0.0, op0=ALU.mult, op1=ALU.add), r=[b_tab, b_z], w=[b_w])
                S.op("dve", lambda e: e.tensor_tensor_scan(out=wi_[:, :, :].rearrange("p a b -> p (a b)"),
                                                           data0=R0[:, :, :].rearrange("p a b -> p (a b)"), data1=zi[:, :, :].rearrange("p a b -> p (a b)"),
                                                           initial=0.0, op0=ALU.mult, op1=ALU.add), r=[b_tab, b_z], w=[b_w])
                S.op("dve", lambda e: e.tensor_tensor(out=pp[:, :, 0, :], in0=Er[:], in1=wr_[:], op=ALU.mult), r=[b_tab, b_w], w=[b_pp])
                S.op("dve", lambda e: e.scalar_tensor_tensor(out=pp[:, :, 1, :], in0=Ei[:], scalar=-1.0, in1=wi_[:], op0=ALU.mult, op1=ALU.mult), r=[b_tab, b_w], w=[b_pp])
                S.op("dve", lambda e: e.scalar_tensor_tensor(out=pp[:, :, 2, :], in0=Er[:], scalar=-1.0, in1=wi_[:], op0=ALU.mult, op1=ALU.mult), r=[b_tab, b_w], w=[b_pp])
                S.op("dve", lambda e: e.scalar_tensor_tensor(out=pp[:, :, 3, :], in0=Ei[:], scalar=-1.0, in1=wr_[:], op0=ALU.mult, op1=ALU.mult), r=[b_tab, b_w], w=[b_pp])
                la = Ls - 1
                xo = lambda eng, fn: S.op(eng, fn, r=[b_tab, b_w, b_s5c, b_s5t], w=[b_s5t])
                xo("dve", lambda e: e.tensor_tensor(out=s5t[:, :, 0, 0:1], in0=Er[:, :, la:la + 1], in1=wr_[:, :, la:la + 1], op=ALU.mult))
                xo("dve", lambda e: e.tensor_tensor(out=s5t[:, :, 1, 0:1], in0=Ei[:, :, la:la + 1], in1=wi_[:, :, la:la + 1], op=ALU.mult))
                xo("dve", lambda e: e.tensor_tensor(out=s5t[:, :, 2, 0:1], in0=Er[:, :, la:la + 1], in1=wi_[:, :, la:la + 1], op=ALU.mult))
                xo("dve", lambda e: e.tensor_tensor(out=s5t[:, :, 3, 0:1], in0=Ei[:, :, la:la + 1], in1=wr_[:, :, la:la + 1], op=ALU.mult))
                xx = lambda eng, fn: S.op(eng, fn, r=[b_s5t, b_s5c], w=[b_X])
                xx("dve", lambda e: e.tensor_tensor(out=Xr[:, :, 0:1], in0=s5t[:, :, 0, 0:1], in1=s5t[:, :, 1, 0:1], op=ALU.subtract))
                xx("dve", lambda e: e.tensor_tensor(out=Xi[:, :, 0:1], in0=s5t[:, :, 2, 0:1], in1=s5t[:, :, 3, 0:1], op=ALU.add))
                arc = s5c[:, AR, :].unsqueeze(2); aic = s5c[:, AI, :].unsqueeze(2)
                xo("dve", lambda e: e.tensor_tensor(out=s5t[:, :, 0, 1:2], in0=Xr[:, :, 0:1], in1=arc, op=ALU.mult))
                xo("dve", lambda e: e.tensor_tensor(out=s5t[:, :, 1, 1:2], in0=Xi[:, :, 0:1], in1=aic, op=ALU.mult))
                xo("dve", lambda e: e.tensor_tensor(out=s5t[:, :, 2, 1:2], in0=Xi[:, :, 0:1], in1=arc, op=ALU.mult))
                xo("dve", lambda e: e.tensor_tensor(out=s5t[:, :, 3, 1:2], in0=Xr[:, :, 0:1], in1=aic, op=ALU.mult))
                xx("dve", lambda e: e.tensor_tensor(out=Wr[:, :, 0:1], in0=s5t[:, :, 0, 1:2], in1=s5t[:, :, 1, 1:2], op=ALU.subtract))
                xx("dve", lambda e: e.tensor_tensor(out=Wi[:, :, 0:1], in0=s5t[:, :, 2, 1:2], in1=s5t[:, :, 3, 1:2], op=ALU.add))
                yield
                yield
                for hh in range(2):
                    n = 0
                    for j in range(4 * hh, 4 * hh + 4):
                        for pi, ci in [(0, 0), (1, 0), (2, 1), (3, 1)]:
                            S.op("pe", lambda e, j=j, pi=pi, ci=ci, n=n, hh=hh: e.matmul(
                                psy[:, hh * 128:hh * 128 + Ls], Cm[:, j, ci, :], pp[:, j, pi, 0:Ls], start=(n == 0), stop=(n == 15)),
                                r=[b_BC, b_pp], w=[b_psy])
                            n += 1
                    S.op("dve", lambda e, hh=hh, c0=c0: e.scalar_tensor_tensor(
                        out=yA[:, hh, c0:c0 + Ls], in0=uTb[:, hh, c0:c0 + Ls], scalar=s5dcol[:, l, hh:hh + 1], in1=psy[:, hh * 128:hh * 128 + Ls],
                        op0=ALU.mult, op1=ALU.add), r=[b_uTb, b_psy, b_convp], w=[b_yA])
            yield
            s5_tail(l, N)

        def s5_tail(l, N):
            S.op("act", lambda e: e.activation(out=gtmp[:, :, 0:N], in_=yA[:, :, 0:N], func=AF.Square), r=[b_yA], w=[b_gtmp])
            S.op("dve", lambda e: e.tensor_scalar(out=gtmp[:, :, 0:N], in0=gtmp[:, :, 0:N], scalar1=0.044715, scalar2=1.0, op0=ALU.mult, op1=ALU.add), r=[b_gtmp], w=[b_gtmp])
            S.op("dve", lambda e: e.tensor_tensor(out=gtmp[:, :, 0:N], in0=gtmp[:, :, 0:N], in1=yA[:, :, 0:N], op=ALU.mult), r=[b_gtmp, b_yA], w=[b_gtmp])
            S.op("act", lambda e: e.activation(out=gtmp[:, :, 0:N], in_=gtmp[:, :, 0:N], func=AF.Sigmoid, scale=1.5957691216057308), r=[b_gtmp], w=[b_gtmp])
            S.op("dve", lambda e: e.tensor_tensor(out=yA[:, :, 0:N], in0=yA[:, :, 0:N], in1=gtmp[:, :, 0:N], op=ALU.mult), r=[b_gtmp], w=[b_yA])
            S.op("pool", lambda e: e.tensor_copy(out=yAb[:, :, 0:N], in_=yA[:, :, 0:N]), r=[b_yA], w=[b_yAb])
            for m in range(2):
                ps, b_ps = mmbank()
                for k in range(2):
                    S.op("pe", lambda e, m=m, k=k, ps=ps: e.matmul(ps[:, 0:N], wglu[:, k, m * 128:(m + 1) * 128], yAb[:, k, 0:N], start=(k == 0), stop=(k == 1)),
                         r=[b_wglu, b_yAb], w=[b_ps])
                S.op("act", lambda e, m=m, ps=ps: e.activation(out=gtmp[:, m, 0:N], in_=ps[:, 0:N], func=AF.Sigmoid), r=[b_ps], w=[b_gtmp])
                S.op("dve", lambda e, m=m: e.tensor_tensor(out=mixT[:, m, 0:N], in0=yA[:, m, 0:N], in1=gtmp[:, m, 0:N], op=ALU.mult), r=[b_yA, b_gtmp], w=[b_mixT[m]])

        if cfg.sample:
            xTs = sb("xTs", [128, 8, 16]); b_xTs = [Buf("xTs%d" % k) for k in range(8)]
            xbcS = sb("xbcS", [128, 6, 4, 7]); b_xbcS = Buf("xbcS")
            cshS = sb("cshS", [128, 2, 4, 6]); b_cshS = Buf("cshS")
            Vown = sb("Vown", [16, 4, 128], BF16); b_Vown = Buf("Vown")
            S.op("pool", lambda e: e.memset(Vown[:], 1.0), w=[b_Vown])
            ncown = sb("ncown", [16, 4]); b_ncown = Buf("ncown")
            triS = sb("triS", [16, 16]); b_triS = Buf("triS")
            maskS = sb("maskS", [128, 16], BF16); b_maskS = Buf("maskS")
            pat4 = sb("pat4", [4, 16]); b_pat4 = Buf("pat4")
            iotaf = sb("iotaf", [128, 1]); b_iotaf = Buf("iotaf")
            S.dma("pool", b_triS, lambda e: e.dma_start(out=triS[:], in_=cst["triS"]), w=[b_triS])
            S.dma("pool", b_maskS, lambda e: e.dma_start(out=maskS[:], in_=cst["maskS"]), w=[b_maskS])
            S.dma("pool", b_pat4, lambda e: e.dma_start(out=pat4[:], in_=cst["pat4"]), w=[b_pat4])
            S.dma("pool", b_iotaf, lambda e: e.dma_start(out=iotaf[:], in_=cst["iota"]), w=[b_iotaf])
            NPGT = NS * NPG
            ptb = sb("ptb", [128, NPGT], I32); idx_all = sb("idx_all", [128, NPGT], I32); b_idx = Buf("idx")
            for sq_ in range(NS):
                S.dma("pool", b_idx, lambda e, sq_=sq_: e.dma_start(out=ptb[:, sq_ * NPG:(sq_ + 1) * NPG], in_=page_table[sq_:sq_ + 1, :].partition_broadcast(128)), w=[b_idx])
            S.op("dve", lambda e: e.tensor_scalar(out=idx_all[:], in0=ptb[:], scalar1=128.0, scalar2=iotaf[:, 0:1], op0=ALU.mult, op1=ALU.add), r=[b_idx, b_iotaf], w=[b_idx])
            idx2 = sb("idx2", [128, NS // 2], I32); b_idx2 = Buf("idx2")
            S.op("pool", lambda e: e.memset(idx2[:], 0), w=[b_idx2])
            for j_ in range(NS // 2):
                for hf_ in range(2):
                    S.dma("pool", b_idx2, lambda e, j_=j_, hf_=hf_: e.dma_start(out=idx2[hf_ * 64:hf_ * 64 + NPG, j_:j_ + 1], in_=page_table[2 * j_ + hf_:2 * j_ + hf_ + 1, :].rearrange("s g -> g s"), allow_slow_non_contiguous=True), w=[b_idx2])
            cl_pg = cache_logf.rearrange("(g t) h -> g (t h)", t=128)
            Kpg = [sb("Kpg%d" % i, [128, 256]) for i in range(2)]; b_Kpg = [Buf("Kpg%d" % i) for i in range(2)]
            Vpg = [sb("Vpg%d" % i, [128, 256]) for i in range(2)]; b_Vpg = [Buf("Vpg%d" % i) for i in range(2)]
            KTp = [sb("KTp%d" % i, [128, 2, 128], BF16) for i in range(2)]; b_KTp = [Buf("KTp%d" % i) for i in range(2)]
            Vpp = [sb("Vpp%d" % i, [128, 4, 128], BF16) for i in range(2)]; b_Vpp = [Buf("Vpp%d" % i) for i in range(2)]
            for i in range(2):
                S.op("pool", lambda e, i=i: e.memset(Vpp[i][:], 1.0), w=[b_Vpp[i]])
            PTs = [sb("PTs%d" % i, [128, 2, 8], BF16) for i in range(2)]; b_PTs = [Buf("PTs%d" % i) for i in range(2)]
            PTo = sb("PTo", [16, 4, 16], BF16); b_PTo = Buf("PTo")
            lfp = sb("lfp", [128, NPG, 4]); b_lfp = Buf("lfp")
            biasp = sb("biasp", [128, NPG, 4]); b_biasp = Buf("biasp")
            cumtmp = lfp[:, :, :].rearrange("p g h -> p (g h)").rearrange("p (h g) -> p h g", h=4); b_cumtmp = b_lfp
            R0S = sb("R0S", [128, 128]); b_R0S = Buf("R0S")
            hst = sb("hst", [128, 128]); b_hst = Buf("hst")
            pg_rr = [0]

        def load_xTs():
            i = ost_rr[0] % 2; ost_rr[0] += 1
            S.dma("sp", b_ostage[i], lambda e, i=i: e.dma_start(out=ostage[i][0:16, :], in_=xs[:, :]), w=[b_ostage[i]], r=[b_ostage_st[i]])
            for k in range(8):
                ps, b_ps = mmbank()
                S.op("pe", lambda e, k=k, ps=ps, i=i: e.transpose(ps[:, 0:16], ostage[i][0:16, k * 128:(k + 1) * 128], ident[0:16, 0:16]), r=[b_ostage[i], b_ident], w=[b_ps])
                S.op("dve", lambda e, k=k, ps=ps: e.tensor_copy(out=xTs[:, k, 0:16], in_=ps[:, 0:16]), r=[b_ps], w=[b_xTs[k]])

        def s5_block_sample(l):
            v4 = lambda ap: ap.rearrange("p (j s t) -> p j s t", j=8, s=4)
            zrS, ziS, wrS, wiS = v4(big16[:, 0:128]), v4(big16[:, 1024:1152]), v4(big16[:, 2048:2176]), v4(big16[:, 3072:3200])
            psy, b_psy = PS[7], PSB[7]
            for sq_ in range(NS):
                S.dma("pool", b_s5o, lambda e, sq_=sq_: e.dma_start(out=s5o[:], in_=st_s5[l, sq_].rearrange("(j g) p r -> (g p) j r", g=2)), r=[b_s5o_st], w=[b_s5o])
                S.op("dve", lambda e, sq_=sq_: e.tensor_copy(out=Xr[:, :, sq_:sq_ + 1], in_=s5o[:, :, 0:1]), r=[b_s5o], w=[b_X])
                S.op("dve", lambda e, sq_=sq_: e.tensor_copy(out=Xi[:, :, sq_:sq_ + 1], in_=s5o[:, :, 1:2]), r=[b_s5o], w=[b_X])
            arc = s5c[:, AR, :].unsqueeze(2).broadcast_to([128, 8, 4]); aic = s5c[:, AI, :].unsqueeze(2).broadcast_to([128, 8, 4])
            xo = lambda fn: S.op("dve", fn, r=[b_X, b_s5c, b_s5t], w=[b_s5t])
            xx = lambda fn: S.op("dve", fn, r=[b_s5t], w=[b_X])

            def carry():
                xo(lambda e: e.tensor_tensor(out=s5t[:, :, 0, :], in0=Xr[:], in1=arc, op=ALU.mult))
                xo(lambda e: e.tensor_tensor(out=s5t[:, :, 1, :], in0=Xi[:], in1=aic, op=ALU.mult))
                xo(lambda e: e.tensor_tensor(out=s5t[:, :, 2, :], in0=Xi[:], in1=arc, op=ALU.mult))
                xo(lambda e: e.tensor_tensor(out=s5t[:, :, 3, :], in0=Xr[:], in1=aic, op=ALU.mult))
                xx(lambda e: e.tensor_tensor(out=Wr[:], in0=s5t[:, :, 0, :], in1=s5t[:, :, 1, :], op=ALU.subtract))
                xx(lambda e: e.tensor_tensor(out=Wi[:], in0=s5t[:, :, 2, :], in1=s5t[:, :, 3, :], op=ALU.add))
            carry()
            S.op("dve", lambda e: e.tensor_copy(out=v4(R0S[:, :]), in_=R0[:, :, 0:4].unsqueeze(2).broadcast_to([128, 8, 4, 4])), r=[b_tab], w=[b_R0S])
            for half in range(2):
                jj = 4 * half
                for ri, bank in enumerate([5, 6]):
                    for j in range(jj, jj + 4):
                        S.op("pe", lambda e, j=j, ri=ri, bank=bank: e.matmul(PS[bank][:, (j % 4) * 128:(j % 4) * 128 + 16], Bm[:, j, ri, :], uTb[:, j // 4, 0:16], start=True, stop=True),
                             r=[b_BC, b_uTb], w=[PSB[bank]])
                pv5 = PS[5][:, :].rearrange("p (a b) -> p a b", a=4)[:, :, 0:16].rearrange("p a (s t) -> p a s t", s=4)
                pv6 = PS[6][:, :].rearrange("p (a b) -> p a b", a=4)[:, :, 0:16].rearrange("p a (s t) -> p a s t", s=4)
                t1rb = T1r[:, jj:jj + 4, 0:4].unsqueeze(2).broadcast_to([128, 4, 4, 4])
                t1ib = T1i[:, jj:jj + 4, 0:4].unsqueeze(2).broadcast_to([128, 4, 4, 4])
                S.op("dve", lambda e, jj=jj, a_=t1rb, b_=pv5: e.tensor_tensor(out=zrS[:, jj:jj + 4], in0=a_, in1=b_, op=ALU.mult), r=[b_tab, PSB[5]], w=[b_z])
                S.op("dve", lambda e, jj=jj, a_=t1ib, b_=pv6: e.tensor_tensor(out=wrS[:, jj:jj + 4], in0=a_, in1=b_, op=ALU.mult), r=[b_tab, PSB[6]], w=[b_w])
                S.op("dve", lambda e, jj=jj, a_=t1rb, b_=pv6: e.tensor_tensor(out=ziS[:, jj:jj + 4], in0=a_, in1=b_, op=ALU.mult), r=[b_tab, PSB[6]], w=[b_z])
                S.op("dve", lambda e, jj=jj, a_=t1ib, b_=pv5: e.tensor_tensor(out=wiS[:, jj:jj + 4], in0=a_, in1=b_, op=ALU.mult), r=[b_tab, PSB[5]], w=[b_w])
            S.op("dve", lambda e: e.tensor_tensor(out=zrS, in0=zrS, in1=wrS, op=ALU.subtract), r=[b_w], w=[b_z])
            S.op("dve", lambda e: e.tensor_tensor(out=ziS, in0=ziS, in1=wiS, op=ALU.add), r=[b_w], w=[b_z])
            S.op("dve", lambda e: e.tensor_tensor(out=zrS[:, :, :, 0], in0=zrS[:, :, :, 0], in1=Wr[:], op=ALU.add), r=[b_X], w=[b_z])
            S.op("dve", lambda e: e.tensor_tensor(out=ziS[:, :, :, 0], in0=ziS[:, :, :, 0], in1=Wi[:], op=ALU.add), r=[b_X], w=[b_z])
            S.op("dve", lambda e: e.tensor_tensor_scan(out=big16[:, 2048:2176], data0=R0S[:, :], data1=big16[:, 0:128], initial=0.0, op0=ALU.mult, op1=ALU.add), r=[b_R0S, b_z], w=[b_w])
            S.op("dve", lambda e: e.tensor_tensor_scan(out=big16[:, 3072:3200], data0=R0S[:, :], data1=big16[:, 1024:1152], initial=0.0, op0=ALU.mult, op1=ALU.add), r=[b_R0S, b_z], w=[b_w])
            eb = lambda T_: T_[:, :, 0:4].unsqueeze(2).broadcast_to([128, 8, 4, 4])
            ppv = lambda q: pp[:, :, q, 0:16].rearrange("p j (s t) -> p j s t", s=4)
            S.op("dve", lambda e: e.tensor_tensor(out=ppv(0), in0=eb(Er), in1=wrS, op=ALU.mult), r=[b_tab, b_w], w=[b_pp])
            S.op("dve", lambda e: e.tensor_tensor(out=ppv(1), in0=eb(Ei), in1=wiS, op=ALU.mult), r=[b_tab, b_w], w=[b_pp])
            S.op("dve", lambda e: e.tensor_tensor(out=ppv(2), in0=eb(Er), in1=wiS, op=ALU.mult), r=[b_tab, b_w], w=[b_pp])
            S.op("dve", lambda e: e.tensor_tensor(out=ppv(3), in0=eb(Ei), in1=wrS, op=ALU.mult), r=[b_tab, b_w], w=[b_pp])
            for q_ in (1, 2, 3):
                S.op("dve", lambda e, q_=q_: e.tensor_scalar(out=pp[:, :, q_, 0:16], in0=pp[:, :, q_, 0:16], scalar1=-1.0, scalar2=None, op0=ALU.mult), r=[b_pp], w=[b_pp])
            e3r = Er[:, :, 3:4].broadcast_to([128, 8, 4]); e3i = Ei[:, :, 3:4].broadcast_to([128, 8, 4])
            xo2 = lambda fn: S.op("dve", fn, r=[b_tab, b_w, b_s5t], w=[b_s5t])
            xo2(lambda e: e.tensor_tensor(out=s5t[:, :, 0, :], in0=wrS[:, :, :, 3], in1=e3r, op=ALU.mult))
            xo2(lambda e: e.tensor_tensor(out=s5t[:, :, 1, :], in0=wiS[:, :, :, 3], in1=e3i, op=ALU.mult))
            xo2(lambda e: e.tensor_tensor(out=s5t[:, :, 2, :], in0=wiS[:, :, :, 3], in1=e3r, op=ALU.mult))
            xo2(lambda e: e.tensor_tensor(out=s5t[:, :, 3, :], in0=wrS[:, :, :, 3], in1=e3i, op=ALU.mult))
            xx(lambda e: e.tensor_tensor(out=Xr[:], in0=s5t[:, :, 0, :], in1=s5t[:, :, 1, :], op=ALU.subtract))
            xx(lambda e: e.tensor_tensor(out=Xi[:], in0=s5t[:, :, 2, :], in1=s5t[:, :, 3, :], op=ALU.add))
            for hh in range(2):
                n = 0
                for j in range(4 * hh, 4 * hh + 4):
                    for pi, ci in [(0, 0), (1, 0), (2, 1), (3, 1)]:
                        S.op("pe", lambda e, j=j, pi=pi, ci=ci, n=n, hh=hh: e.matmul(psy[:, hh * 128:hh * 128 + 16], Cm[:, j, ci, :], pp[:, j, pi, 0:16], start=(n == 0), stop=(n == 15)),
                             r=[b_BC, b_pp], w=[b_psy])
                        n += 1
                S.op("dve", lambda e, hh=hh: e.scalar_tensor_tensor(out=yA[:, hh, 0:16], in0=uTb[:, hh, 0:16], scalar=s5dcol[:, l, hh:hh + 1], in1=psy[:, hh * 128:hh * 128 + 16],
                                                                   op0=ALU.mult, op1=ALU.add), r=[b_uTb, b_psy, b_convp], w=[b_yA])
            s5_tail(l, 16)
            if l == 0:
                dbg_dump("XrS", Xr[:].rearrange("p a b -> p (a b)"), 32, [b_X])
                dbg_dump("XiS", Xi[:].rearrange("p a b -> p (a b)"), 32, [b_X])
                dbg_dump("WrS", Wr[:].rearrange("p a b -> p (a b)"), 32, [b_X])
                dbg_dump("wrS", big16[:, 2048:2176], 128, [b_w])
                dbg_dump("zrS", big16[:, 0:128], 128, [b_z])
            for sq_ in range(NS):
                S.op("dve", lambda e, sq_=sq_: e.tensor_copy(out=s5o[:, :, 0:1], in_=Xr[:, :, sq_:sq_ + 1]), r=[b_X, b_s5o_st], w=[b_s5o])
                S.op("dve", lambda e, sq_=sq_: e.tensor_copy(out=s5o[:, :, 1:2], in_=Xi[:, :, sq_:sq_ + 1]), r=[b_X], w=[b_s5o])
                finals.append(S.dma("pool", b_s5o_st, lambda e, sq_=sq_: e.dma_start(out=s5_s[l, sq_].rearrange("(j g) p r -> (g p) j r", g=2), in_=s5o[:]), r=[b_s5o], w=[b_s5o_st]))

        def fox_sample(l):
            ck, cv, cl = cache_k, cache_v, cache_logf
            eo = l * cfg.npool * 128
            IOA = bass.IndirectOffsetOnAxis
            S.op("dve", lambda e: e.tensor_tensor_scan(out=cumT[:, 0:16], data0=pat4[:, 0:16], data1=logfT[:, 0:16], initial=0.0, op0=ALU.mult, op1=ALU.add), r=[b_pat4, b_logfT], w=[b_cumT])
            S.op("dve", lambda e: e.tensor_copy(out=augT[0:4, 0:16], in_=cumT[:, 0:16]), r=[b_cumT], w=[b_augT])
            S.op("dve", lambda e: e.tensor_tensor(out=logfT[:, 0:16], in0=cumT[:, 0:16], in1=augT[0:4, 0:16], op=ALU.subtract), r=[b_cumT, b_augT], w=[b_logfT])
            S.op("dve", lambda e: e.tensor_copy(out=augT[32:36, 0:16], in_=logfT[:, 0:16]), r=[b_logfT], w=[b_augT])
            S.op("dve", lambda e: e.memset(PS[5][:, 0:64], 0.0), w=[PSB[5]])
            for sq_ in range(NS):
                half_ = sq_ % 2
                if half_ == 0:
                    S.dma("pool", b_rs, lambda e, sq_=sq_: e.indirect_dma_start(out=rs[:, :], out_offset=None, in_=cl_pg, in_offset=IOA(ap=idx2[:, sq_ // 2:sq_ // 2 + 1], axis=0), element_offset=l * cfg.npool * 512),
                          r=[b_idx2], w=[b_rs])
                pst, b_pst = mmbank()
                r0_ = half_ * 64
                for h in range(4):
                    S.op("pe", lambda e, h=h, pst=pst, r0_=r0_: e.transpose(pst[:, h * 64:h * 64 + NPG], rs[r0_:r0_ + NPG, :].rearrange("p (t h) -> p t h", h=4)[:, :, h], ident[r0_:r0_ + NPG, r0_:r0_ + NPG]),
                         r=[b_rs, b_ident], w=[b_pst])
                lfp_hg = lfp[:, :, :].rearrange("p g h -> p (g h)").rearrange("p (h g) -> p h g", h=4)
                bias_hg = biasp[:, :, :].rearrange("p g h -> p (g h)").rearrange("p (h g) -> p h g", h=4)
                S.op("dve", lambda e, pst=pst: e.tensor_copy(out=lfp_hg, in_=pst[:, 0:256].rearrange("p (h g) -> p h g", h=4)[:, :, 0:NPG]), r=[b_pst], w=[b_lfp])
                lff = lfp[:, :, :].rearrange("p g h -> p (g h)")
                S.op("pe", lambda e: e.matmul(PS[7][:, 0:NPG * 4], tristrict[:, :], lff, start=True, stop=True), r=[b_tristrict, b_lfp], w=[PSB[7]])
                S.op("pe", lambda e: e.matmul(PS[7][:, 256:256 + NPG * 4], onesf[:, :], lff, start=True, stop=True), r=[b_onesf, b_lfp], w=[PSB[7]])
                totv = PS[7][:, 256:256 + NPG * 4].rearrange("p (h g) -> p h g", h=4)
                w1v = PS[7][:, 0:NPG * 4].rearrange("p (h g) -> p h g", h=4)
                for h in range(4):
                    S.op("dve", lambda e, h=h: e.tensor_tensor_scan(out=cumtmp[:, h, :], data0=onesf[:, 0:NPG], data1=totv[:, h, :], initial=0.0, op0=ALU.mult, op1=ALU.add),
                         r=[PSB[7], b_onesf], w=[b_cumtmp])
                    S.op("dve", lambda e, h=h: e.tensor_scalar(out=cumtmp[:, h, :], in0=cumtmp[:, h, :], scalar1=-1.0, scalar2=cumtmp[:, h, NPG - 1:NPG], op0=ALU.mult, op1=ALU.add), r=[b_cumtmp], w=[b_cumtmp])
                    S.op("dve", lambda e, h=h: e.tensor_tensor(out=bias_hg[:, h, :], in0=cumtmp[:, h, :], in1=w1v[:, h, :], op=ALU.add), r=[b_cumtmp, PSB[7]], w=[b_biasp])
                S.op("act", lambda e: e.activation(out=biasp[:, :, :], in_=biasp[:, :, :], func=AF.Exp), r=[b_biasp], w=[b_biasp])
                for pg in range(NPG):
                    col = sq_ * NPG + pg
                    i = pg_rr[0] % 2; pg_rr[0] += 1
                    S.dma("pool", b_Kpg[i], lambda e, i=i, col=col: e.indirect_dma_start(out=Kpg[i][:, :], out_offset=None, in_=ck, in_offset=IOA(ap=idx_all[:, col:col + 1], axis=0), element_offset=eo * 256), r=[b_idx], w=[b_Kpg[i]])
                    S.dma("pool", b_Vpg[i], lambda e, i=i, col=col: e.indirect_dma_start(out=Vpg[i][:, :], out_offset=None, in_=cv, in_offset=IOA(ap=idx_all[:, col:col + 1], axis=0), element_offset=eo * 256), r=[b_idx], w=[b_Vpg[i]])
                    ps, b_ps = mmbank()
                    for hf in range(2):
                        S.op("pe", lambda e, i=i, hf=hf, ps=ps: e.transpose(ps[:, hf * 128:(hf + 1) * 128], Kpg[i][:, hf * 128:(hf + 1) * 128], ident[:, :]), r=[b_Kpg[i], b_ident], w=[b_ps])
                    S.op("act", lambda e, i=i, ps=ps: e.activation(out=KTp[i][:, :, :], in_=ps[:, 0:256].rearrange("p (a b) -> p a b", a=2), func=AF.Identity), r=[b_ps], w=[b_KTp[i]])
                    S.op("dve", lambda e, i=i, pg=pg: e.tensor_tensor(out=Vpp[i][:, :, 0:64], in0=Vpg[i][:, :].rearrange("p (h d) -> p h d", h=4),
                                                                    in1=bias_hg[:, :, pg].unsqueeze(2).broadcast_to([128, 4, 64]), op=ALU.mult), r=[b_Vpg[i], b_biasp], w=[b_Vpp[i]])
                    S.op("dve", lambda e, i=i, pg=pg: e.tensor_copy(out=Vpp[i][:, :, 64:128], in_=bias_hg[:, :, pg].unsqueeze(2).broadcast_to([128, 4, 64])), r=[b_biasp], w=[b_Vpp[i]])
                    bpair = (3, 4) if (pg % 2 == 0) else (6, 7)
                    for h in range(4):
                        hp = (h % 2) * 64; pair = h // 2
                        sbk = bpair[h % 2]; c4 = pair * 4
                        psS, b_psS = PS[sbk], PSB[sbk]
                        S.op("pe", lambda e, i=i, hp=hp, pair=pair, psS=psS, c4=c4, sq_=sq_: e.matmul(psS[:, c4:c4 + 4], KTp[i][hp:hp + 64, pair, :], qT[hp:hp + 64, pair, sq_ * 4:sq_ * 4 + 4], start=True, stop=False, skip_group_check=True),
                             r=[b_KTp[i], b_qT], w=[b_psS])
                        S.op("pe", lambda e, psS=psS, c4=c4: e.matmul(psS[:, c4:c4 + 4], identb[:, :], zerob[:, 0:4], start=False, stop=False, skip_group_check=True), r=[b_identb, b_maskb], w=[b_psS])
                        S.op("pe", lambda e, h=h, psS=psS, c4=c4, sq_=sq_: e.matmul(psS[:, c4:c4 + 4], sel[0:36, h, :], augT[0:36, sq_ * 4:sq_ * 4 + 4], start=False, stop=True, skip_group_check=True),
                             r=[b_sel, b_augT], w=[b_psS])
                    for par in range(2):
                        sbk = bpair[par]
                        S.op("act", lambda e, i=i, par=par, sbk=sbk: e.activation(out=PTs[i][:, par, :], in_=PS[sbk][:, 0:8], func=AF.Exp), r=[PSB[sbk]], w=[b_PTs[i]])
                    for h in range(4):
                        oc = h * 16 + sq_ * 4
                        S.op("pe", lambda e, i=i, h=h, oc=oc: e.matmul(PS[5][:, oc:oc + 4], Vpp[i][:, h, :], PTs[i][:, h % 2, (h // 2) * 4:(h // 2) * 4 + 4], start=False, stop=False, skip_group_check=True), r=[b_Vpp[i], b_PTs[i]], w=[PSB[5]])
            for h in range(4):
                hp = (h % 2) * 64; pair = h // 2
                sbk = 3 + (h % 2)
                psS, b_psS = PS[sbk], PSB[sbk]
                S.op("pe", lambda e, hp=hp, pair=pair, psS=psS: e.matmul(psS[0:16, 16:32], KTb[hp:hp + 64, pair, 0:16], qT[hp:hp + 64, pair, 0:16], start=True, stop=False, skip_group_check=True), r=[b_KTb, b_qT], w=[b_psS])
                S.op("pe", lambda e, psS=psS: e.matmul(psS[0:16, 16:32], identb[:, 0:16], maskS[:, 0:16], start=False, stop=False, skip_group_check=True), r=[b_identb, b_maskS], w=[b_psS])
                S.op("pe", lambda e, h=h, psS=psS: e.matmul(psS[0:16, 16:32], sel[0:36, h, 0:16], augT[0:36, 0:16], start=False, stop=True, skip_group_check=True), r=[b_sel, b_augT], w=[b_psS])
                S.op("act", lambda e, h=h, psS=psS: e.activation(out=PTo[0:16, h, :], in_=psS[0:16, 16:32], func=AF.Exp, bias=ncown[0:16, h:h + 1]), r=[b_psS, b_ncown], w=[b_PTo])
                S.op("pe", lambda e, h=h: e.matmul(PS[5][:, h * 16:(h + 1) * 16], Vown[0:16, h, :], PTo[0:16, h, :], start=False, stop=(h == 3), skip_group_check=True), r=[b_Vown, b_PTo], w=[PSB[5]])
            S.op("dve", lambda e: e.reciprocal(rdn[64:128, 0:64], PS[5][64:128, 0:64]), r=[PSB[5]], w=[b_rdn])
            S.op("dve", lambda e: e.tensor_copy(out=rdn[0:64, 0:64], in_=rdn[64:128, 0:64]), r=[b_rdn], w=[b_rdn])
            for h in range(4):
                hp = (h % 2) * 64; pair = h // 2
                S.op("dve", lambda e, h=h, hp=hp, pair=pair: e.tensor_tensor(out=mixT[hp:hp + 64, 2 + pair, 0:16], in0=PS[5][0:64, h * 16:(h + 1) * 16], in1=rdn[0:64, h * 16:(h + 1) * 16], op=ALU.mult),
                     r=[PSB[5], b_rdn], w=[b_mixT[2 + pair]])

        def ssd_sample(l):
            for sq_ in range(NS):
                conv_ssd(l, lambda k, j, sq_=sq_: xbcS[:, k, sq_, j:j + 4], b_xbcS, sq_ * 4, 4)
            for sq_ in range(NS):
                for i in range(2):
                    S.dma("pool", b_hst, lambda e, sq_=sq_, i=i: e.dma_start(out=hst[:], in_=st_ssd[l, sq_][i * 128:(i + 1) * 128, :]), w=[b_hst])
                    ps, b_ps = mmbank()
                    S.op("pe", lambda e, ps=ps: e.transpose(ps[:, 0:128], hst[:, :], ident[:, :]), r=[b_hst, b_ident], w=[b_ps])
                    S.op("dve", lambda e, ps=ps, i=i: e.tensor_copy(out=HT[:, i * 128:(i + 1) * 128], in_=ps[:, 0:128]), r=[b_ps], w=[b_HT])
                for h in range(4):
                    S.op("pool", lambda e, h=h: e.tensor_copy(out=HTz[:, h, (h % 2) * 64:(h % 2) * 64 + 64], in_=HT[:, h * 64:(h + 1) * 64]), r=[b_HT], w=[b_HTz])
                for _ in ssd_chunk(l, sq_ * 4, 4, False):
                    pass
                for i in range(2):
                    ps, b_ps = mmbank()
                    S.op("pe", lambda e, i=i, ps=ps: e.transpose(ps[:, 0:128], HT[:, i * 128:(i + 1) * 128], ident[:, :]), r=[b_HT, b_ident], w=[b_ps])
                    S.op("act", lambda e, ps=ps: e.activation(out=ysd[:, 0:128], in_=ps[:, 0:128], func=AF.Identity), r=[b_ps, b_ssd_st], w=[b_ysd])
                    finals.append(S.dma("pool", b_ssd_st, lambda e, sq_=sq_, i=i: e.dma_start(out=ssd_s[l, sq_][i * 128:(i + 1) * 128, :], in_=ysd[:, 0:128]), r=[b_ysd], w=[b_ssd_st]))

        def sample_conv_outputs(l):
            for sq_ in range(NS):
                for k in range(6):
                    finals.append(S.dma("pool", cvo_b, lambda e, sq_=sq_, k=k: e.dma_start(out=ssdconv_s[l, sq_][:, k * 128:(k + 1) * 128].rearrange("j p -> p j"), in_=xbcS[:, k, sq_, 4:7], allow_slow_non_contiguous=True), r=[b_xbcS], w=[]))
                for k in range(2):
                    finals.append(S.dma("pool", cvo_b, lambda e, sq_=sq_, k=k: e.dma_start(out=sconv_s[l, sq_][:, k * 128:(k + 1) * 128].rearrange("j p -> p j"), in_=cshS[:, k, sq_, 4:6], allow_slow_non_contiguous=True), r=[b_cshS], w=[cvo_b] if k == 1 else []))

        kv_rr = [0]; ost_rr = [0]

        def load_xT_layer0(c):
            for ts in range(4):
                st, b_st = ostage[ost_rr[0] % 2], b_ostage[ost_rr[0] % 2]; b_st2 = b_ostage_st[ost_rr[0] % 2]; ost_rr[0] += 1
                r0 = c * 512 + ts * 128
                S.dma("sp", b_st, lambda e, st=st, r0=r0: e.dma_start(out=st[:], in_=xp[r0:r0 + 128, :]), w=[b_st], r=[b_st2])
                for k in range(8):
                    ps, b_ps = mmbank()
                    S.op("pe", lambda e, st=st, k=k, ps=ps: e.transpose(ps[:, 0:128], st[:, k * 128:(k + 1) * 128], ident[:, :]), r=[b_st, b_ident], w=[b_ps])
                    S.op("act" if k % 2 else "dve", (lambda e, k=k, ps=ps, ts=ts: e.activation(out=xT[:, k, ts * 128:(ts + 1) * 128], in_=ps[:, 0:128], func=AF.Identity)) if k % 2 else
                         (lambda e, k=k, ps=ps, ts=ts: e.tensor_copy(out=xT[:, k, ts * 128:(ts + 1) * 128], in_=ps[:, 0:128])), r=[b_ps], w=[b_xT[k]])

        def matgroup(kind, l, m0, nm, fn_each):
            slot, b_slot = next_group(kind, l, m0)
            for mi in range(nm):
                fn_each(mi, slot, b_slot)

        def sconv_part(l, smp, N):
            for i in range(2):
                if smp:
                    mo = mixT[:, 6 + i, 0:16].rearrange("p (s t) -> p s t", s=4)
                    S.op("dve", lambda e, i=i, mo=mo: e.tensor_scalar(out=mo, in0=cshS[:, i, :, 0:4], scalar1=scw[:, l, i, 0:1], scalar2=None, op0=ALU.mult), r=[b_cshS, b_convp], w=[b_mixT[6 + i]])
                    for j in (1, 2):
                        S.op("dve", lambda e, i=i, j=j, mo=mo: e.scalar_tensor_tensor(out=mo, in0=cshS[:, i, :, j:j + 4], scalar=scw[:, l, i, j:j + 1], in1=mo, op0=ALU.mult, op1=ALU.add), r=[b_cshS, b_convp], w=[b_mixT[6 + i]])
                else:
                    S.op("dve", lambda e, i=i: e.tensor_scalar(out=mixT[:, 6 + i, 0:N], in0=cshT[:, i, 0:N], scalar1=scw[:, l, i, 0:1], scalar2=None, op0=ALU.mult), r=[b_cshT, b_convp], w=[b_mixT[6 + i]])
                    for j in (1, 2):
                        S.op("dve", lambda e, i=i, j=j: e.scalar_tensor_tensor(out=mixT[:, 6 + i, 0:N], in0=cshT[:, i, j:j + N], scalar=scw[:, l, i, j:j + 1], in1=mixT[:, 6 + i, 0:N], op0=ALU.mult, op1=ALU.add), r=[b_cshT, b_convp], w=[b_mixT[6 + i]])
                S.op("dve", lambda e, i=i: e.tensor_tensor(out=mixT[:, 6 + i, 0:N], in0=mixT[:, 6 + i, 0:N], in1=sbT[:, i, 0:N], op=ALU.mult), r=[b_sbT], w=[b_mixT[6 + i]])

        def block(l, c, smp=False):
            N = 16 if smp else 512
            NSUB = 1 if smp else 4
            MTK = 16 if smp else 128
            first = (c == 0) and not smp
            last_layer = (l == L - 1)
            t0 = c * 512
            xX, b_xX = (xTs, b_xTs) if smp else (xT, b_xT)
            if smp:
                if l == 0:
                    load_xTs()
            elif l == 0:
                load_xT_layer0(c)
            else:
                S.dma("sp", b_xT[0], lambda e: e.dma_start(out=xT[:], in_=x_scr[c]), r=[b_xscr[c]], w=b_xT)
            KSTOP = int(os.environ.get("K_STOP", "99"))
            rmsnorm_fm(xX, b_xX, 8, N, lambda k: gmix[:, l, k:k + 1], xnT, b_xnT)
            def inproj_each(gi):
                def f(mi, slot, b_slot):
                    nm, c0, wd = MT_IN[gi * 4 + mi]
                    idx = sum(1 for (n2, _, _) in MT_IN[:gi * 4 + mi] if n2 == nm)
                    ps, b_ps = mmbank()
                    for k in range(8):
                        S.op("pe", lambda e, slot=slot, k=k, ps=ps, wd=wd: e.matmul(ps[0:wd, 0:N], slot[:, mi, k, 0:wd], xnT[:, k, 0:N], start=(k == 0), stop=(k == 7)),
                             r=[b_slot, b_xnT[k]], w=[b_ps])
                    if nm == "u":
                        S.op("dve", lambda e: e.tensor_copy(out=uTb[:, idx, 0:N], in_=ps[:, 0:N]), r=[b_ps], w=[b_uTb])
                    elif nm == "q":
                        S.op("act", lambda e: e.activation(out=qT[:, idx, 0:N], in_=ps[:, 0:N], func=AF.Identity, scale=0.125), r=[b_ps], w=[b_qT])
                    elif nm == "k":
                        S.op("dve", lambda e: e.tensor_copy(out=KTb[:, idx, 0:N], in_=ps[:, 0:N]), r=[b_ps], w=[b_KTb])
                    elif nm == "v":
                        pass
                    elif nm == "fr":
                        S.op("act", lambda e: e.activation(out=logfT[:, 0:N], in_=ps[0:4, 0:N], func=AF.Exp, scale=-1.0, bias=nbfcol[:, l:l + 1]), r=[b_ps, b_pcol], w=[b_logfT])
                        S.op("act", lambda e: e.activation(out=logfT[:, 0:N], in_=logfT[:, 0:N], func=AF.Ln, bias=onesf[0:4, 0:1]), r=[b_logfT, b_onesf], w=[b_logfT])
                        S.op("dve", lambda e: e.tensor_scalar(out=logfT[:, 0:N], in0=logfT[:, 0:N], scalar1=-1.0, scalar2=None, op0=ALU.mult), r=[b_logfT], w=[b_logfT])
                    elif nm == "z":
                        S.op("act", lambda e: e.activation(out=szT[:, idx, 0:N], in_=ps[:, 0:N], func=AF.Silu), r=[b_ps], w=[b_szT])
                    elif nm == "xbc":
                        if smp:
                            S.op("act", lambda e: e.activation(out=xbcS[:, idx, :, 3:7], in_=ps[:, 0:16].rearrange("p (s t) -> p s t", s=4), func=AF.Identity), r=[b_ps], w=[b_xbcS])
                        else:
                            S.op("act", lambda e: e.activation(out=xbcT[:, idx, 3:3 + N], in_=ps[:, 0:N], func=AF.Identity), r=[b_ps], w=[b_xbcT])
                    elif nm == "dt":
                        S.op("act", lambda e: e.activation(out=daT[0:4, 0:N], in_=ps[0:4, 0:N], func=AF.Exp, bias=dtbcol[:, l:l + 1]), r=[b_ps, b_pcol], w=[b_daT])
                        S.op("act", lambda e: e.activation(out=daT[0:4, 0:N], in_=daT[0:4, 0:N], func=AF.Ln, bias=onesf[0:4, 0:1]), r=[b_daT, b_onesf], w=[b_daT])
                        S.op("dve", lambda e: e.tensor_scalar(out=daT[32:36, 0:N], in0=daT[0:4, 0:N], scalar1=acol[:, l:l + 1], scalar2=None, op0=ALU.mult), r=[b_daT, b_pcol], w=[b_daT])
                    elif nm == "sb":
                        S.op("act", lambda e: e.activation(out=sbT[:, idx, 0:N], in_=ps[:, 0:N], func=AF.Identity), r=[b_ps], w=[b_sbT])
                    elif nm == "sc":
                        if smp:
                            S.op("act", lambda e: e.activation(out=cshS[:, idx, :, 2:6], in_=ps[:, 0:16].rearrange("p (s t) -> p s t", s=4), func=AF.Identity), r=[b_ps], w=[b_cshS])
                        else:
                            S.op("act", lambda e: e.activation(out=cshT[:, idx, 2:2 + N], in_=ps[:, 0:N], func=AF.Identity), r=[b_ps], w=[b_cshT])
                    elif nm == "sh":
                        if smp:
                            S.op("dve", lambda e: e.tensor_tensor(out=cshS[:, idx, :, 2:6], in0=cshS[:, idx, :, 2:6], in1=ps[:, 0:16].rearrange("p (s t) -> p s t", s=4), op=ALU.mult), r=[b_ps], w=[b_cshS])
                        else:
                            S.op("dve", lambda e: e.tensor_tensor(out=cshT[:, idx, 2:2 + N], in0=cshT[:, idx, 2:2 + N], in1=ps[:, 0:N], op=ALU.mult), r=[b_ps], w=[b_cshT])
                return f
            if smp:
                for sq_ in range(NS):
                    for k in range(6):
                        S.dma("pool", b_xbcS, lambda e, sq_=sq_, k=k: e.dma_start(out=xbcS[:, k, sq_, 0:3], in_=st_ssdconv[l, sq_][:, k * 128:(k + 1) * 128].rearrange("j p -> p j"), allow_slow_non_contiguous=True), w=[b_xbcS])
                    for k in range(2):
                        S.dma("pool", b_cshS, lambda e, sq_=sq_, k=k: e.dma_start(out=cshS[:, k, sq_, 0:2], in_=st_sconv[l, sq_][:, k * 128:(k + 1) * 128].rearrange("j p -> p j"), allow_slow_non_contiguous=True), w=[b_cshS])
            elif first:
                S.op("pool", lambda e: e.memset(xbcT[:, :, 0:3], 0.0), w=[b_xbcT])
                S.op("pool", lambda e: e.memset(cshT[:, :, 0:2], 0.0), w=[b_cshT])
            else:
                S.op("pool", lambda e: e.tensor_copy(out=xbcT[:, :, 0:3], in_=xbcT[:, :, N:N + 3]), r=[b_xbcT], w=[b_xbcT])
                S.op("pool", lambda e: e.tensor_copy(out=cshT[:, :, 0:2], in_=cshT[:, :, N:N + 2]), r=[b_cshT], w=[b_cshT])
            for gi in range(6):
                slot, b_slot = next_group("in", l, gi * 4)
                f = inproj_each(gi)
                for mi in range(4):
                    f(mi, slot, b_slot)
                if gi == 1:
                    for ts in range(NSUB):
                        ps, b_ps = mmbank()
                        for mi in range(4):
                            for k in range(8):
                                S.op("pe", lambda e, slot=slot, k=k, mi=mi, ps=ps, ts=ts: e.matmul(ps[0:MTK, mi * 128:(mi + 1) * 128], xnT[:, k, ts * MTK:(ts + 1) * MTK], slot[:, mi, k, :], start=(k == 0), stop=(k == 7)),
                                     r=[b_slot, b_xnT[k]], w=[b_ps])
                        i = kv_rr[0] % 2; kv_rr[0] += 1
                        S.op("act", lambda e, i=i, ps=ps: e.activation(out=kvst[i][0:MTK, :], in_=ps[0:MTK, :], func=AF.Identity), r=[b_ps, b_kvst_st[i]], w=[b_kvst[i]])
                        if smp:
                            S.op("dve", lambda e, ps=ps: e.tensor_copy(out=Vown[0:16, :, 0:64], in_=ps[0:16, 256:512].rearrange("p (h d) -> p h d", h=4)), r=[b_ps], w=[b_Vown])
                            finals.append(S.dma("pool", b_kvst_st[i], lambda e, i=i: e.dma_start(out=k_s[l, :, :], in_=kvst[i][0:16, 0:256]), r=[b_kvst[i]], w=[]))
                            finals.append(S.dma("pool", b_kvst_st[i], lambda e, i=i: e.dma_start(out=v_s[l, :, :], in_=kvst[i][0:16, 256:512]), r=[b_kvst[i]], w=[b_kvst_st[i]]))
                        else:
                            S.op("dve", lambda e, ps=ps, ts=ts: e.tensor_copy(out=Vpb[:, ts, :, 0:64], in_=ps[:, 256:512].rearrange("p (h d) -> p h d", h=4)), r=[b_ps], w=[b_Vpb])
                            r0 = t0 + ts * 128
                            finals.append(S.dma("pool", b_kvst_st[i], lambda e, i=i, r0=r0: e.dma_start(out=k_p[l, r0:r0 + 128, :], in_=kvst[i][:, 0:256]), r=[b_kvst[i]], w=[]))
                            finals.append(S.dma("pool", b_kvst_st[i], lambda e, i=i, r0=r0: e.dma_start(out=v_p[l, r0:r0 + 128, :], in_=kvst[i][:, 256:512]), r=[b_kvst[i]], w=[b_kvst_st[i]]))
                if gi == 2:
                    for ts in range(NSUB):
                        ps, b_ps = mmbank()
                        for k in range(8):
                            S.op("pe", lambda e, slot=slot, k=k, ps=ps, ts=ts: e.matmul(ps[0:MTK, 0:4], xnT[:, k, ts * MTK:(ts + 1) * MTK], slot[:, 0, k, 0:4], start=(k == 0), stop=(k == 7)),
                                 r=[b_slot, b_xnT[k]], w=[b_ps])
                        S.op("dve", lambda e, ps=ps, ts=ts: e.tensor_tensor(out=lf_tok[0:MTK, ts, :], in0=ps[0:MTK, 0:4], in1=bf_bc[0:MTK, l, :], op=ALU.add), r=[b_ps, b_bfbc, b_lf_st], w=[b_lf_tok])
                        S.op("act", lambda e, ts=ts: e.activation(out=lf_tok[0:MTK, ts, :], in_=lf_tok[0:MTK, ts, :], func=AF.Exp, scale=-1.0), r=[b_lf_tok], w=[b_lf_tok])
                        S.op("act", lambda e, ts=ts: e.activation(out=lf_tok[0:MTK, ts, :], in_=lf_tok[0:MTK, ts, :], func=AF.Ln, bias=onesf[0:MTK, 0:1]), r=[b_lf_tok, b_onesf], w=[b_lf_tok])
                        S.op("dve", lambda e, ts=ts: e.tensor_scalar(out=lf_tok[0:MTK, ts, :], in0=lf_tok[0:MTK, ts, :], scalar1=-1.0, scalar2=None, op0=ALU.mult), r=[b_lf_tok], w=[b_lf_tok])
                        ps2, b_ps2 = mmbank()
                        if smp:
                            S.op("pe", lambda e, ps2=ps2: e.matmul(ps2[0:16, 0:4], triS[0:16, 0:16], lf_tok[0:16, 0, :], start=True, stop=True), r=[b_triS, b_lf_tok], w=[b_ps2])
                            S.op("dve", lambda e, ps2=ps2: e.tensor_scalar(out=ncown[0:16, :], in0=ps2[0:16, 0:4], scalar1=-1.0, scalar2=None, op0=ALU.mult), r=[b_ps2], w=[b_ncown])
                            continue
                        kt = c * 4 + ts
                        S.op("pe", lambda e, ps2=ps2, ts=ts: e.matmul(ps2[:, 0:4], tri[:, :], lf_tok[:, ts, :], start=True, stop=True), r=[b_tri, b_lf_tok], w=[b_ps2])
                        S.op("pe", lambda e, ps2=ps2, ts=ts: e.matmul(ps2[:, 4:8], onesf[:, :], lf_tok[:, ts, :], start=True, stop=True), r=[b_onesf, b_lf_tok], w=[b_ps2])
                        if first and ts == 0:
                            S.op("dve", lambda e, ps2=ps2, kt=kt: e.tensor_scalar(out=negcum[:, kt, :], in0=ps2[:, 0:4], scalar1=-1.0, scalar2=None, op0=ALU.mult), r=[b_ps2], w=[b_negcum])
                            S.op("dve", lambda e, ps2=ps2: e.tensor_copy(out=carry_bc[:], in_=ps2[:, 4:8]), r=[b_ps2], w=[b_carry])
                        else:
                            S.op("dve", lambda e, ps2=ps2, kt=kt: e.scalar_tensor_tensor(out=negcum[:, kt, :], in0=ps2[:, 0:4], scalar=-1.0, in1=carry_bc[:], op0=ALU.mult, op1=ALU.subtract), r=[b_ps2, b_carry], w=[b_negcum])
                            S.op("dve", lambda e, ps2=ps2: e.tensor_tensor(out=carry_bc[:], in0=carry_bc[:], in1=ps2[:, 4:8], op=ALU.add), r=[b_ps2], w=[b_carry])
                    if smp:
                        finals.append(S.dma("pool", b_lf_st, lambda e: e.dma_start(out=logf_s[l, :, :], in_=lf_tok[0:16, 0, :]), r=[b_lf_tok], w=[b_lf_st]))
                    else:
                        finals.append(S.dma("pool", b_lf_st, lambda e: e.dma_start(out=logf_p[l, t0:t0 + 512, :].rearrange("(a p) h -> p a h", p=128), in_=lf_tok[:]), r=[b_lf_tok], w=[b_lf_st]))
            if KSTOP <= 2:
                return
            if smp:
                s5_block_sample(l)
                sconv_part(l, smp, N)
                fox_sample(l)
                ssd_sample(l)
            else:
                def streamA():
                    yield from s5_block(l, N, 1, 512, first)
                    sconv_part(l, smp, N)
                    yield from ssd_block(l, c)
                genA = streamA(); genB = fox_block(l, c)
                nB = 16 * (c + 1); nA = 60
                per = max(1, -(-nB // nA))
                doneA = doneB = False
                while not (doneA and doneB):
                    if not doneA:
                        try:
                            next(genA)
                        except StopIteration:
                            doneA = True
                    for _ in range(per if not doneA else 10 ** 6):
                        if doneB:
                            break
                        try:
                            next(genB)
                        except StopIteration:
                            doneB = True
            for g in range(4):
                rmsnorm_fm(mixT[:, 2 * g:2 * g + 2, :], b_mixT[2 * g:2 * g + 2], 2, N, lambda k, g=g: ggrp[:, l, g, k:k + 1], catT[:, 2 * g:2 * g + 2, :], b_catT[2 * g:2 * g + 2])
            for gi in range(2):
                slot, b_slot = next_group("out", l, gi * 4)
                for mi in range(4):
                    m = gi * 4 + mi
                    ps, b_ps = mmbank()
                    for k in range(8):
                        S.op("pe", lambda e, slot=slot, k=k, mi=mi, ps=ps: e.matmul(ps[:, 0:N], slot[:, mi, k, :], catT[:, k, 0:N], start=(k == 0), stop=(k == 7)), r=[b_slot, b_catT[k]], w=[b_ps])
                    S.op("dve", lambda e, m=m, ps=ps: e.tensor_tensor(out=xX[:, m, 0:N], in0=xX[:, m, 0:N], in1=ps[:, 0:N], op=ALU.add), r=[b_ps], w=[b_xX[m]])
            rmsnorm_fm(xX, b_xX, 8, N, lambda k: gffn[:, l, k:k + 1], xnT, b_xnT)
            for half in range(2):
                for gi in range(4 * half, 4 * half + 4):
                    slot, b_slot = next_group("up", l, gi * 4)
                    for mi in range(4):
                        m = gi * 4 + mi
                        ps, b_ps = mmbank()
                        for k in range(8):
                            S.op("pe", lambda e, k=k, mi=mi, ps=ps, slot=slot: e.matmul(ps[:, 0:N], slot[:, mi, k, :], xnT[:, k, 0:N], start=(k == 0), stop=(k == 7)), r=[b_slot, b_xnT[k]], w=[b_ps])
                        S.op("act", lambda e, m=m, ps=ps: e.activation(out=hT[:, m % 16, 0:N], in_=ps[:, 0:N], func=AF.Relu), r=[b_ps], w=[b_hT[m % 16]])
                        S.op("pool" if m % 2 else "dve", lambda e, m=m: e.tensor_tensor(out=hT[:, m % 16, 0:N], in0=hT[:, m % 16, 0:N], in1=hT[:, m % 16, 0:N], op=ALU.mult), r=[b_hT[m % 16]], w=[b_hT[m % 16]])
                for m in range(8):
                    slot, b_slot = next_group("down", l, m * 2 + half)
                    ps, b_ps = mmbank()
                    for k in range(16):
                        S.op("pe", lambda e, k=k, ps=ps, slot=slot: e.matmul(ps[:, 0:N], slot[:, 0, k, :], hT[:, k, 0:N], start=(k == 0), stop=(k == 15)), r=[b_slot, b_hT[k]], w=[b_ps])
                    S.op("dve", lambda e, m=m, ps=ps: e.tensor_tensor(out=xX[:, m, 0:N], in0=xX[:, m, 0:N], in1=ps[:, 0:N], op=ALU.add), r=[b_ps], w=[b_xX[m]])
            if smp:
                if last_layer:
                    rmsnorm_fm(xX, b_xX, 8, N, lambda k: gfin[:, k:k + 1], xX, b_xX)
                    i = ost_rr[0] % 2; ost_rr[0] += 1
                    for k in range(8):
                        ps, b_ps = mmbank()
                        S.op("pe", lambda e, k=k, ps=ps: e.transpose(ps[0:16, 0:128], xX[:, k, 0:16], ident[:, :]), r=[b_xX[k], b_ident], w=[b_ps])
                        S.op("dve", lambda e, k=k, ps=ps, i=i: e.tensor_copy(out=ostage[i][0:16, k * 128:(k + 1) * 128], in_=ps[0:16, 0:128]), r=[b_ps, b_ostage_st[i]], w=[b_ostage[i]])
                    finals.append(S.dma("pool", b_ostage_st[i], lambda e, i=i: e.dma_start(out=y_s[:, :], in_=ostage[i][0:16, :]), r=[b_ostage[i]], w=[b_ostage_st[i]]))
                return
            if not last_layer:
                S.dma("sp", b_xscr[c], lambda e: e.dma_start(out=x_scr[c], in_=xT[:]), r=b_xT, w=[b_xscr[c]])
            else:
                rmsnorm_fm(xT, b_xT, 8, N, lambda k: gfin[:, k:k + 1], xT, b_xT)
                for ts in range(4):
                    i = ost_rr[0] % 2; ost_rr[0] += 1
                    for k in range(8):
                        ps, b_ps = mmbank()
                        S.op("pe", lambda e, k=k, ps=ps, ts=ts: e.transpose(ps[:, 0:128], xT[:, k, ts * 128:(ts + 1) * 128], ident[:, :]), r=[b_xT[k], b_ident], w=[b_ps])
                        S.op("act" if k % 2 else "dve", (lambda e, k=k, ps=ps, i=i: e.activation(out=ostage[i][:, k * 128:(k + 1) * 128], in_=ps[:, 0:128], func=AF.Identity)) if k % 2 else
                             (lambda e, k=k, ps=ps, i=i: e.tensor_copy(out=ostage[i][:, k * 128:(k + 1) * 128], in_=ps[:, 0:128])), r=[b_ps, b_ostage_st[i]], w=[b_ostage[i]])
                    r0 = t0 + ts * 128
                    finals.append(S.dma("pool", b_ostage_st[i], lambda e, i=i, r0=r0: e.dma_start(out=y_p[r0:r0 + 128, :], in_=ostage[i][:]), r=[b_ostage[i]], w=[b_ostage_st[i]]))

        def fox_block(l, c):
            N = 512
            first = (c == 0)
            if c < NCH - 1:
                S.dma("sp", b_ktscr[c], lambda e: e.dma_start(out=kt_scr[c], in_=KTb[:]), r=[b_KTb], w=[b_ktscr[c]])
                S.dma("sp", b_vscr[c], lambda e: e.dma_start(out=v_scr[c], in_=Vpb[:, :, :, 0:64]), r=[b_Vpb], w=[b_vscr[c]])
            if first:
                S.op("dve", lambda e: e.tensor_tensor_scan(out=cumT[:, 0:N], data0=ones4[:, 0:N], data1=logfT[:, 0:N], initial=0.0, op0=ALU.mult, op1=ALU.add),
                     r=[b_ones4, b_logfT], w=[b_cumT])
            else:
                S.op("dve", lambda e: e.tensor_tensor(out=logfT[:, 0:1], in0=logfT[:, 0:1], in1=cumc[:, 0:1], op=ALU.add), r=[b_cumc], w=[b_logfT])
                S.op("dve", lambda e: e.tensor_tensor_scan(out=cumT[:, 0:N], data0=ones4[:, 0:N], data1=logfT[:, 0:N], initial=0.0, op0=ALU.mult, op1=ALU.add),
                     r=[b_ones4, b_logfT], w=[b_cumT])
            S.op("dve", lambda e: e.tensor_copy(out=cumc[:, 0:1], in_=cumT[:, N - 1:N]), r=[b_cumT], w=[b_cumc])
            S.op("dve", lambda e: e.tensor_copy(out=augT[0:4, 0:N], in_=cumT[:, 0:N]), r=[b_cumT], w=[b_augT])
            S.op("dve", lambda e: e.tensor_tensor(out=logfT[:, 0:N], in0=cumT[:, 0:N], in1=augT[0:4, 0:N], op=ALU.subtract), r=[b_cumT, b_augT], w=[b_logfT])
            S.op("dve", lambda e: e.tensor_copy(out=augT[32:36, 0:N], in_=logfT[:, 0:N]), r=[b_logfT], w=[b_augT])
            for h in range(4):
                pair = h // 2
                hp = (h % 2) * 64
                ob = 2
                for kb in range(c + 1):
                    diag = (kb == c)
                    if diag:
                        Ks, b_Ks, Vs, b_Vs = KTb, b_KTb, Vpb, b_Vpb
                    else:
                        i = kc_rr[0] % 2; kc_rr[0] += 1
                        Ks, b_Ks, Vs, b_Vs = KTc[i], b_KTc[i], Vpc[i], b_Vpc[i]
                        S.dma("sp", b_Ks, lambda e, Ks=Ks, kb=kb, pair=pair: e.dma_start(out=Ks[:, pair, :], in_=kt_scr[kb, :, pair, :]), r=[b_ktscr[kb]], w=[b_Ks])
                        S.dma("sp", b_Vs, lambda e, Vs=Vs, kb=kb, h=h: e.dma_start(out=Vs[:, :, h, 0:64], in_=v_scr[kb, :, :, h, :]), r=[b_vscr[kb]], w=[b_Vs])
                    for kt in range(4):
                        qlo = kt * 128 if diag else 0
                        k0 = kt * 128
                        sbk = 3 + (sbank_rr[0] % 2); sbank_rr[0] += 1
                        psS, b_psS = PS[sbk], PSB[sbk]
                        S.op("pe", lambda e, Ks=Ks, hp=hp, pair=pair, k0=k0, qlo=qlo, psS=psS: e.matmul(
                            psS[:, qlo:N], Ks[hp:hp + 64, pair, k0:k0 + 128], qT[hp:hp + 64, pair, qlo:N], start=True, stop=False, skip_group_check=True),
                            r=[b_Ks, b_qT], w=[b_psS])
                        mk = maskb if diag else zerob
                        S.op("pe", lambda e, qlo=qlo, psS=psS, mk=mk: e.matmul(psS[:, qlo:qlo + 128], identb[:, :], mk[:, :], start=False, stop=False, skip_group_check=True),
                             r=[b_identb, b_maskb], w=[b_psS])
                        S.op("pe", lambda e, h=h, qlo=qlo, psS=psS: e.matmul(psS[:, qlo:N], sel[0:36, h, :], augT[0:36, qlo:N], start=False, stop=True, skip_group_check=True),
                             r=[b_sel, b_augT], w=[b_psS])
                        pi = pt_rr[0] % 2; pt_rr[0] += 1
                        kti = kb * 4 + kt
                        S.op("act", lambda e, pi=pi, qlo=qlo, psS=psS, kti=kti, h=h: e.activation(out=PT[pi][:, qlo:N], in_=psS[:, qlo:N], func=AF.Exp, bias=negcum[:, kti, h:h + 1]),
                             r=[b_psS, b_negcum], w=[b_PT[pi]])
                        S.op("pe", lambda e, Vs=Vs, kt=kt, h=h, pi=pi, qlo=qlo, ob=ob, kb=kb, diag=diag: e.matmul(
                            PS[ob][:, qlo:N], Vs[:, kt, h, :], PT[pi][:, qlo:N], start=(kb == 0 and kt == 0), stop=(diag and kt == 3), skip_group_check=True),
                            r=[b_Vs, b_PT[pi]], w=[PSB[ob]])
                        yield
                S.op("dve", lambda e, ob=ob: e.reciprocal(rdn[64:128, 0:N], PS[ob][64:128, 0:N]), r=[PSB[ob]], w=[b_rdn])
                S.op("dve", lambda e: e.tensor_copy(out=rdn[0:64, 0:N], in_=rdn[64:128, 0:N]), r=[b_rdn], w=[b_rdn])
                S.op("dve", lambda e, ob=ob, hp=hp, pair=pair: e.tensor_tensor(out=mixT[hp:hp + 64, 2 + pair, 0:N], in0=PS[ob][0:64, 0:N], in1=rdn[0:64, 0:N], op=ALU.mult),
                     r=[PSB[ob], b_rdn], w=[b_mixT[2 + pair]])

        def conv_ssd(l, xin_fn, xin_b, c_out0, Tq):
            for k in range(6):
                S.op("dve", lambda e, k=k: e.tensor_scalar(out=gtmp[:, 0, 0:Tq], in0=xin_fn(k, 0), scalar1=convw[:, l, k, 0:1], scalar2=None, op0=ALU.mult),
                     r=[xin_b, b_convp], w=[b_gtmp])
                for j in (1, 2, 3):
                    S.op("dve", lambda e, k=k, j=j: e.scalar_tensor_tensor(out=gtmp[:, 0, 0:Tq], in0=xin_fn(k, j), scalar=convw[:, l, k, j:j + 1], in1=gtmp[:, 0, 0:Tq], op0=ALU.mult, op1=ALU.add),
                         r=[xin_b, b_convp], w=[b_gtmp])
                S.op("act", lambda e, k=k: e.activation(out=xbcA[:, k, c_out0:c_out0 + Tq], in_=gtmp[:, 0, 0:Tq], func=AF.Silu, bias=convb[:, l, k:k + 1]),
                     r=[b_gtmp, b_convp], w=[b_xbcA])

        def ssd_chunk(l, c0, Lc, zero_state):
            p5, b5, p6, b6, p7, b7 = PS[5], PSB[5], PS[6], PSB[6], PS[7], PSB[7]
            for i in range(4):
                S.op("pe", lambda e, i=i: e.matmul(p5[0:Lc, i * 128:(i + 1) * 128], xbcA[:, i, c0:c0 + Lc], identb[:, :], start=True, stop=True), r=[b_xbcA, b_identb], w=[b5])
            S.op("pe", lambda e: e.matmul(p7[0:Lc, 300:336], daT[0:36, c0:c0 + Lc], ident[0:36, 0:36], start=True, stop=True), r=[b_daT, b_ident], w=[b7])
            S.op("act", lambda e: e.activation(out=xs_tok[0:Lc, :], in_=p5[0:Lc, 0:256], func=AF.Identity), r=[b5], w=[b_tok])
            S.op("dve", lambda e: e.tensor_copy(out=B_tok[0:Lc, :], in_=p5[0:Lc, 256:512]), r=[b5], w=[b_tok])
            S.op("dve", lambda e: e.tensor_copy(out=da_tok[0:Lc, :], in_=p7[0:Lc, 300:336]), r=[b7], w=[b_tok])
            yield
            S.op("pe", lambda e: e.matmul(p7[0:Lc, 256:260], tri[0:Lc, 0:Lc], da_tok[0:Lc, 32:36], start=True, stop=True), r=[b_tri, b_tok], w=[b7])
            S.op("pe", lambda e: e.matmul(p7[:, 260:264], onesf[0:Lc, 0:128], da_tok[0:Lc, 32:36], start=True, stop=True), r=[b_onesf, b_tok], w=[b7])
            S.op("dve", lambda e: e.tensor_scalar(out=sm[0:Lc, 0:4], in0=p7[0:Lc, 256:260], scalar1=-1.0, scalar2=None, op0=ALU.mult), r=[b7], w=[b_sm])
            S.op("dve", lambda e: e.tensor_tensor(out=sm[0:Lc, 12:16], in0=p7[0:Lc, 260:264], in1=sm[0:Lc, 0:4], op=ALU.add), r=[b7], w=[b_sm])
            S.op("act", lambda e: e.activation(out=sm[0:Lc, 4:8], in_=sm[0:Lc, 12:16], func=AF.Exp), r=[b_sm], w=[b_sm])
            S.op("act", lambda e: e.activation(out=sm[:, 8:12], in_=p7[:, 260:264], func=AF.Exp), r=[b7], w=[b_sm])
            for h in range(4):
                S.op("dve", lambda e, h=h: e.tensor_scalar(out=xdtz[0:Lc, h, (h % 2) * 64:(h % 2) * 64 + 64], in0=xs_tok[0:Lc, h * 64:(h + 1) * 64], scalar1=da_tok[0:Lc, h:h + 1], scalar2=None, op0=ALU.mult),
                     r=[b_tok], w=[b_xdt])
                S.op("pool", lambda e, h=h: e.tensor_scalar(out=xdtd[0:Lc, h * 64:(h + 1) * 64], in0=xdtz[0:Lc, h, (h % 2) * 64:(h % 2) * 64 + 64], scalar1=sm[0:Lc, 4 + h:5 + h], scalar2=None, op0=ALU.mult),
                     r=[b_xdt, b_sm], w=[b_xdt])
            for h in range(4):
                S.op("dve", lambda e, h=h: e.tensor_scalar(out=arep[0:Lc, h, :], in0=onesf[0:Lc, :], scalar1=da_tok[0:Lc, 32 + h:33 + h], scalar2=None, op0=ALU.mult), r=[b_tok, b_onesf], w=[b_arep])
            yield
            for h in range(4):
                S.op("pe", lambda e, h=h: e.matmul(p6[:, h * 128:h * 128 + Lc], arep[0:Lc, h, :], tri[0:Lc, 0:Lc], start=True, stop=True), r=[b_arep, b_tri], w=[b6])
            p6v = p6[0:Lc, :].rearrange("p (h n) -> p h n", h=4)[:, :, 0:Lc]
            p6f = p6[:, :].rearrange("p (h n) -> p h n", h=4)[:, :, 0:Lc]
            S.op("act", lambda e: e.activation(out=ea[:, :, 0:Lc], in_=p6f, func=AF.Exp), r=[b6], w=[b_ea])
            S.op("dve", lambda e: e.tensor_tensor(out=dec[0:Lc, :, 0:Lc], in0=p6v, in1=maskf[0:Lc, 0:Lc].unsqueeze(1).broadcast_to([Lc, 4, Lc]), op=ALU.add), r=[b6, b_maskf], w=[b_dec])
            for h in range(4):
                S.op("act", lambda e, h=h: e.activation(out=dec[0:Lc, h, 0:Lc], in_=dec[0:Lc, h, 0:Lc], func=AF.Exp, bias=sm[0:Lc, h:h + 1]), r=[b_dec, b_sm], w=[b_dec])
            for g in range(2):
                S.op("pe", lambda e, g=g: e.matmul(p7[0:Lc, g * 128:g * 128 + Lc], xbcA[:, 2 + g, c0:c0 + Lc], xbcA[:, 4 + g, c0:c0 + Lc], start=True, stop=True), r=[b_xbcA], w=[b7])
            for g in range(2):
                S.op("dve", lambda e, g=g: e.tensor_tensor(out=MTt[0:Lc, 2 * g:2 * g + 2, 0:Lc], in0=dec[0:Lc, 2 * g:2 * g + 2, 0:Lc],
                                                           in1=p7[0:Lc, g * 128:g * 128 + Lc].unsqueeze(1).broadcast_to([Lc, 2, Lc]), op=ALU.mult), r=[b_dec, b7], w=[b_MT])
            if not zero_state:
                for g in range(2):
                    S.op("pool", lambda e, g=g: e.tensor_tensor(out=CdT[:, 2 * g:2 * g + 2, 0:Lc], in0=ea[:, 2 * g:2 * g + 2, 0:Lc],
                                                                in1=xbcA[:, 4 + g, c0:c0 + Lc].unsqueeze(1).broadcast_to([128, 2, Lc]), op=ALU.mult), r=[b_ea, b_xbcA], w=[b_CdT])
            yield
            yield
            for i in range(2):
                ps, b_ps = mmbank()
                n = 0
                tot = 2 if zero_state else 4
                for h in (2 * i, 2 * i + 1):
                    S.op("pe", lambda e, h=h, ps=ps, n=n, tot=tot: e.matmul(ps[:, 0:Lc], xdtz[0:Lc, h, :], MTt[0:Lc, h, 0:Lc], start=(n == 0), stop=(n == tot - 1)), r=[b_xdt, b_MT], w=[b_ps])
                    n += 1
                    if not zero_state:
                        S.op("pe", lambda e, h=h, ps=ps, n=n, tot=tot: e.matmul(ps[:, 0:Lc], HTz[:, h, :], CdT[:, h, 0:Lc], start=(n == 0), stop=(n == tot - 1)), r=[b_HTz, b_CdT], w=[b_ps])
                        n += 1
                S.op("dve", lambda e, i=i, ps=ps: e.scalar_tensor_tensor(out=ysd[:, 0:Lc], in0=xbcA[:, i, c0:c0 + Lc], scalar=dcol[:, l, i:i + 1], in1=ps[:, 0:Lc], op0=ALU.mult, op1=ALU.add),
                     r=[b_xbcA, b_ps, b_convp], w=[b_ysd])
                S.op("dve", lambda e, i=i: e.tensor_tensor(out=mixT[:, 4 + i, c0:c0 + Lc], in0=ysd[:, 0:Lc], in1=szT[:, i, c0:c0 + Lc], op=ALU.mult), r=[b_ysd, b_szT], w=[b_mixT[4 + i]])
            yield
            ps, b_ps = mmbank()
            for g in range(2):
                S.op("pe", lambda e, g=g, ps=ps: e.matmul(ps[:, g * 128:(g + 1) * 128], B_tok[0:Lc, g * 128:(g + 1) * 128], xdtd[0:Lc, g * 128:(g + 1) * 128], start=True, stop=True), r=[b_tok, b_xdt], w=[b_ps])
            if zero_state:
                S.op("dve", lambda e, ps=ps: e.tensor_copy(out=HT[:, :], in_=ps[:, 0:256]), r=[b_ps], w=[b_HT])
            else:
                S.op("dve", lambda e: e.tensor_tensor(out=HT[:, :].rearrange("p (h q) -> p h q", h=4), in0=HT[:, :].rearrange("p (h q) -> p h q", h=4),
                                                       in1=sm[:, 8:12].unsqueeze(2).broadcast_to([128, 4, 64]), op=ALU.mult), r=[b_sm], w=[b_HT])
                S.op("dve", lambda e, ps=ps: e.tensor_tensor(out=HT[:, :], in0=HT[:, :], in1=ps[:, 0:256], op=ALU.add), r=[b_ps], w=[b_HT])
            for h in range(4):
                S.op("pool", lambda e, h=h: e.tensor_copy(out=HTz[:, h, (h % 2) * 64:(h % 2) * 64 + 64], in_=HT[:, h * 64:(h + 1) * 64]), r=[b_HT], w=[b_HTz])

        def ssd_block(l, c):
            first = (c == 0)
            conv_ssd(l, lambda k, j: xbcT[:, k, j:j + 512], b_xbcT, 0, 512)
            for sc in range(4):
                yield from ssd_chunk(l, sc * 128, 128, first and sc == 0)

        cvst = sb("cvst", [128, 8, 3]); b_cvst = Buf("cvst")
        s5o = sb("s5o", [128, 8, 2]); b_s5o = Buf("s5o"); b_s5o_st = Buf("s5o_st"); b_ssd_st = Buf("ssd_st")
        cvo_b = Buf("cvo")
        KSTOP = int(os.environ.get("K_STOP", "99"))
        for l in range(L if KSTOP > 1 else 0):
            if not os.environ.get("K_NOS5"):
                s5_tables(l)
            for c in range(NCH):
                block(l, c)
            if KSTOP <= 3:
                continue
            for i in range(2):
                ps, b_ps = mmbank()
                S.op("pe", lambda e, i=i, ps=ps: e.matmul(ps[:, 0:128], HT[:, i * 128:(i + 1) * 128], ident[:, :], start=True, stop=True), r=[b_HT, b_ident], w=[b_ps])
                S.op("act", lambda e, ps=ps: e.activation(out=ysd[:, 0:128], in_=ps[:, 0:128], func=AF.Identity), r=[b_ps, b_ssd_st], w=[b_ysd])
                finals.append(S.dma("pool", b_ssd_st, lambda e, l=l, i=i: e.dma_start(out=ssd_p[l, i * 128:(i + 1) * 128, :], in_=ysd[:, 0:128]), r=[b_ysd], w=[b_ssd_st]))
            if l == 0:
                dbg_dump("zr", zr[:].rearrange("p a b -> p (a b)"), 1024, [b_z])
                dbg_dump("zi", zi[:].rearrange("p a b -> p (a b)"), 1024, [b_z])
                dbg_dump("wr", wr_[:].rearrange("p a b -> p (a b)"), 1024, [b_w])
                dbg_dump("wi", wi_[:].rearrange("p a b -> p (a b)"), 1024, [b_w])
                dbg_dump("Xr", Xr[:].rearrange("p a b -> p (a b)"), 32, [b_X])
                dbg_dump("Xi", Xi[:].rearrange("p a b -> p (a b)"), 32, [b_X])
            S.op("dve", lambda e: e.tensor_copy(out=s5o[:, :, 0:1], in_=Xr[:, :, 0:1]), r=[b_X, b_s5o_st], w=[b_s5o])
            S.op("dve", lambda e: e.tensor_copy(out=s5o[:, :, 1:2], in_=Xi[:, :, 0:1]), r=[b_X], w=[b_s5o])
            finals.append(S.dma("pool", b_s5o_st, lambda e, l=l: e.dma_start(out=s5_p[l].rearrange("(j g) p r -> (g p) j r", g=2), in_=s5o[:]), r=[b_s5o], w=[b_s5o_st]))
            S.op("dve", lambda e: e.tensor_copy(out=cvst[:, 0:6, :], in_=xbcT[:, :, 512:515]), r=[b_xbcT, cvo_b], w=[b_cvst])
            S.op("dve", lambda e: e.tensor_copy(out=cvst[:, 6:8, 0:2], in_=cshT[:, :, 512:514]), r=[b_cshT], w=[b_cvst])
            for k in range(6):
                finals.append(S.dma("pool", cvo_b, lambda e, l=l, k=k: e.dma_start(out=ssdconv_p[l][:, k * 128:(k + 1) * 128].rearrange("j p -> p j"), in_=cvst[:, k, 0:3], allow_slow_non_contiguous=True), r=[b_cvst], w=[]))
            for k in range(2):
                finals.append(S.dma("pool", cvo_b, lambda e, l=l, k=k: e.dma_start(out=sconv_p[l][:, k * 128:(k + 1) * 128].rearrange("j p -> p j"), in_=cvst[:, 6 + k, 0:2], allow_slow_non_contiguous=True), r=[b_cvst], w=[cvo_b] if k == 1 else []))
            pc_flush_and_next()
            if cfg.sample:
                block(l, NCH, smp=True)
                sample_conv_outputs(l)
        for l in range(L):
            wt_ready[l].val = S.dcnt[id(b_wt[l])]
        print("NOPS", S.nrec, "lastline", S.lastline, "NSEM", S.nsem, flush=True)
        S.emit(finals)
    return nc


def kernel(**inputs):
    cfg = Cfg(sample=True)
    nc = build_program(cfg)
    consts = host_consts()
    in_maps = []
    wnames = ["w_in", "w_out", "w_up", "w_down", "norm_mix_g", "norm_ffn_g", "norm_final_g", "s5_lam_re", "s5_lam_im",
              "s5_log_dt", "s5_b_re", "s5_b_im", "s5_c_re", "s5_c_im", "s5_d", "s5_w_glu", "s5_norm_g", "fox_b_f",
              "fox_norm_g", "ssd_conv_w", "ssd_conv_b", "ssd_dt_bias", "ssd_a_log", "ssd_d", "ssd_norm_g", "sc_conv_w", "sc_norm_g"]
    Ld = 4
    shared = {n: np.ascontiguousarray(inputs[n]) for n in wnames}
    shared["cache_k"] = np.ascontiguousarray(inputs["cache_k"]).reshape(-1, 256)
    shared["cache_v"] = np.ascontiguousarray(inputs["cache_v"]).reshape(-1, 256)
    shared["cache_logf"] = np.ascontiguousarray(inputs["cache_logf"]).reshape(-1, 4)
    for k, v in consts.items():
        shared["c_" + k] = v
    for core in range(8):
        m = dict(shared)
        m["xp"] = np.ascontiguousarray(inputs["x_prompt"][core // 2])
        sl = slice(4 * core, 4 * core + 4)
        m["xs"] = np.ascontiguousarray(inputs["x_sample"][sl]).reshape(16, 1024)
        m["state_s5"] = np.ascontiguousarray(inputs["state_s5"][:, sl])
        m["state_ssd"] = np.ascontiguousarray(inputs["state_ssd"][:, sl]).reshape(Ld, 4, 256, 128)
        m["state_ssd_conv"] = np.ascontiguousarray(inputs["state_ssd_conv"][:, sl])
        m["state_sconv"] = np.ascontiguousarray(inputs["state_sconv"][:, sl])
        m["page_table"] = np.ascontiguousarray(inputs["page_table"][sl]).astype(np.int32)
        in_maps.append(m)
    res = run_bass_kernel_spmd(nc, in_maps, core_ids=list(range(8))).results

    def st(name, shape):
        return np.stack([res[2 * s][name].reshape(shape) for s in range(4)], axis=0)

    def ss(name, shape, axis):
        return np.ascontiguousarray(np.concatenate([res[c][name].reshape(shape) for c in range(8)], axis=axis))
    y_prompt = st("y_p", (4096, 1024))
    k_prompt = np.ascontiguousarray(st("k_p", (Ld, 4096, 4, 64)).transpose(1, 0, 2, 3, 4))
    v_prompt = np.ascontiguousarray(st("v_p", (Ld, 4096, 4, 64)).transpose(1, 0, 2, 3, 4))
    logf_prompt = np.ascontiguousarray(st("logf_p", (Ld, 4096, 4)).transpose(1, 0, 2, 3))
    s5_prompt = np.ascontiguousarray(st("s5_p", (Ld, 16, 64, 2)).transpose(1, 0, 2, 3, 4))
    ssd_prompt = np.ascontiguousarray(st("ssd_p", (Ld, 4, 64, 128)).transpose(1, 0, 2, 3, 4))
    ssd_conv_prompt = np.ascontiguousarray(st("ssdconv_p", (Ld, 3, 768)).transpose(1, 0, 2, 3))
    sconv_prompt = np.ascontiguousarray(st("sconv_p", (Ld, 2, 256)).transpose(1, 0, 2, 3))
    y_sample = ss("y_s", (4, 4, 1024), 0)
    k_sample = ss("k_s", (Ld, 4, 4, 4, 64), 1)
    v_sample = ss("v_s", (Ld, 4, 4, 4, 64), 1)
    logf_sample = ss("logf_s", (Ld, 4, 4, 4), 1)
    s5_sample = ss("s5_s", (Ld, 4, 16, 64, 2), 1)
    ssd_sample = ss("ssd_s", (Ld, 4, 4, 64, 128), 1)
    ssd_conv_sample = ss("ssdconv_s", (Ld, 4, 3, 768), 1)
    sconv_sample = ss("sconv_s", (Ld, 4, 2, 256), 1)
    return (y_prompt, y_sample, k_prompt, v_prompt, logf_prompt, s5_prompt, ssd_prompt, ssd_conv_prompt, sconv_prompt,
            k_sample, v_sample, logf_sample, s5_sample, ssd_sample, ssd_conv_sample, sconv_sample)
```
